# Optimizing a Trainium2 kernel written in Bass

```python
import jax, jax.numpy as jnp
from jax import lax
import numpy as np

D_MODEL = 1024
BATCH = 4
SEQ = 4096
DEPTH = 1

HEAD_DIM = 64
ATTN_HEADS = 8
ATTN_KV_HEADS = 2
ATTN_GROUP = ATTN_HEADS // ATTN_KV_HEADS
RET_HEADS = 8
ATTN_WIDTH = ATTN_HEADS * HEAD_DIM
KV_WIDTH = ATTN_KV_HEADS * HEAD_DIM
RET_WIDTH = RET_HEADS * HEAD_DIM
MIX_WIDTH = ATTN_WIDTH + RET_WIDTH
IN_PROJ = ATTN_WIDTH + 2 * KV_WIDTH + 4 * RET_WIDTH
WINDOW = 128
BLOCK = 128
RET_CHUNK = 128
ROPE_THETA = 10000.0
D_FF = -(-8 * D_MODEL // (3 * 256)) * 256
EPS = 1e-6
NEG_INF = -1e30

kernel_name = "hymba_swa_retention_encoder_block"


def rms_norm(x, w):
    xf = x.astype(jnp.float32)
    y = xf * lax.rsqrt(jnp.mean(xf * xf, axis=-1, keepdims=True) + EPS)
    return (y * w.astype(jnp.float32)).astype(x.dtype)


def rope(x, pos):
    d = x.shape[-1]
    inv_freq = ROPE_THETA ** (-jnp.arange(0, d, 2, dtype=jnp.float32) / d)
    ang = pos[:, None] * inv_freq[None, :]
    cos = jnp.concatenate([jnp.cos(ang), jnp.cos(ang)], -1)[None, :, None, :]
    sin = jnp.concatenate([jnp.sin(ang), jnp.sin(ang)], -1)[None, :, None, :]
    xf = x.astype(jnp.float32)
    x1, x2 = xf[..., : d // 2], xf[..., d // 2:]
    rot = jnp.concatenate([-x2, x1], axis=-1)
    return (xf * cos + rot * sin).astype(x.dtype)


def windowed_gqa_sink(q, k, v, sink):
    B, S, H, D = q.shape
    nb = S // BLOCK
    pad = ((0, 0), (BLOCK, BLOCK), (0, 0), (0, 0))
    kp = jnp.pad(k, pad).reshape(B, nb + 2, BLOCK, ATTN_KV_HEADS, D)
    vp = jnp.pad(v, pad).reshape(B, nb + 2, BLOCK, ATTN_KV_HEADS, D)
    kb = jnp.concatenate([kp[:, :-2], kp[:, 1:-1], kp[:, 2:]], axis=2)
    vb = jnp.concatenate([vp[:, :-2], vp[:, 1:-1], vp[:, 2:]], axis=2)
    qb = q.reshape(B, nb, BLOCK, ATTN_KV_HEADS, ATTN_GROUP, D)
    scale = D ** -0.5
    s = jnp.einsum('bnqkgd,bnjkd->bnkgqj', qb, kb).astype(jnp.float32) * scale
    q_pos = jnp.arange(nb)[:, None] * BLOCK + jnp.arange(BLOCK)[None, :]
    k_pos = jnp.arange(nb)[:, None] * BLOCK - BLOCK + jnp.arange(3 * BLOCK)[None, :]
    valid = (jnp.abs(q_pos[:, :, None] - k_pos[:, None, :]) <= WINDOW) \
        & (k_pos >= 0)[:, None, :] & (k_pos < S)[:, None, :]
    s = jnp.where(valid[None, :, None, None], s, NEG_INF)
    sink_l = sink.astype(jnp.float32).reshape(ATTN_KV_HEADS, ATTN_GROUP)[None, None, :, :, None, None]
    m = jnp.maximum(jnp.max(s, axis=-1, keepdims=True), sink_l)
    p = jnp.exp(s - m)
    p = p / (jnp.sum(p, axis=-1, keepdims=True) + jnp.exp(sink_l - m))
    o = jnp.einsum('bnkgqj,bnjkd->bnqkgd', p.astype(v.dtype), vb)
    return o.reshape(B, S, H * D)


def retention_direction(q, k, v, log_gamma, strict):
    B, H, S, D = q.shape
    C = RET_CHUNK
    nc = S // C
    qc = q.reshape(B, H, nc, C, D)
    kc = k.reshape(B, H, nc, C, D)
    vc = v.reshape(B, H, nc, C, D)
    idx = jnp.arange(C, dtype=jnp.float32)
    diff = idx[:, None] - idx[None, :]
    mask = diff > 0 if strict else diff >= 0
    lg = log_gamma[:, None, None]
    dmat = jnp.where(mask[None], jnp.exp(lg * jnp.maximum(diff, 0.0)[None]), 0.0)
    xi = jnp.exp(log_gamma[:, None] * (idx + 1.0)[None])
    zeta = jnp.exp(log_gamma[:, None] * (C - 1.0 - idx)[None])
    g_chunk = jnp.exp(log_gamma * C)[None, :, None, None]
    inner = jnp.einsum('bhcnd,bhcmd->bhcnm', qc, kc) * dmat[None, :, None]
    out_inner = jnp.einsum('bhcnm,bhcme->bhcne', inner, vc)
    kv = jnp.einsum('bhcmd,bhcme->bhcde', kc * zeta[None, :, None, :, None], vc)

    def step(r, kv_c):
        return g_chunk * r + kv_c, r

    _, states = lax.scan(step, jnp.zeros((B, H, D, D), jnp.float32), jnp.moveaxis(kv, 2, 0))
    states = jnp.moveaxis(states, 0, 2)
    cross = jnp.einsum('bhcnd,bhcde->bhcne', qc * xi[None, :, None, :, None], states)
    return (out_inner + cross).reshape(B, H, S, D)


def bidirectional_retention(q, k, v, log_fwd, log_bwd):
    fwd = retention_direction(q, k, v, log_fwd, False)
    bwd = retention_direction(jnp.flip(q, 2), jnp.flip(k, 2), jnp.flip(v, 2), log_bwd, True)
    return fwd + jnp.flip(bwd, 2)


def setup_inputs(seed: int = 0) -> dict:
    key = jax.random.key(seed)
    ks = jax.random.split(key, 16)
    f32 = jnp.float32
    nrm = lambda k, shape, fan_in: jax.random.normal(k, shape, f32) * fan_in ** -0.5
    base_decay = np.log(1.0 - 2.0 ** (-5.0 - np.arange(RET_HEADS))).astype(np.float32)
    base_decay = jnp.asarray(base_decay)[None, :]
    return {
        "x": jax.random.normal(ks[0], (BATCH, SEQ, D_MODEL), f32),
        "attn_norm_w": 1.0 + 0.02 * jax.random.normal(ks[1], (DEPTH, D_MODEL), f32),
        "w_in": nrm(ks[2], (DEPTH, D_MODEL, IN_PROJ), D_MODEL),
        "q_norm_w": 1.0 + 0.02 * jax.random.normal(ks[3], (DEPTH, HEAD_DIM), f32),
        "k_norm_w": 1.0 + 0.02 * jax.random.normal(ks[4], (DEPTH, HEAD_DIM), f32),
        "attn_sink": 0.5 * jax.random.normal(ks[5], (DEPTH, ATTN_HEADS), f32),
        "ret_log_decay_fwd": base_decay * (1.0 + 0.05 * jax.random.normal(ks[6], (DEPTH, RET_HEADS), f32)),
        "ret_log_decay_bwd": base_decay * (1.0 + 0.05 * jax.random.normal(ks[7], (DEPTH, RET_HEADS), f32)),
        "ret_norm_w": 1.0 + 0.02 * jax.random.normal(ks[8], (DEPTH, RET_WIDTH), f32),
        "w_out": nrm(ks[9], (DEPTH, MIX_WIDTH, D_MODEL), MIX_WIDTH),
        "ffn_norm_w": 1.0 + 0.02 * jax.random.normal(ks[10], (DEPTH, D_MODEL), f32),
        "w_gate": nrm(ks[11], (DEPTH, D_MODEL, D_FF), D_MODEL),
        "w_up": nrm(ks[12], (DEPTH, D_MODEL, D_FF), D_MODEL),
        "w_down": nrm(ks[13], (DEPTH, D_FF, D_MODEL), D_FF),
    }


def reference(x, attn_norm_w, w_in, q_norm_w, k_norm_w, attn_sink, ret_log_decay_fwd,
              ret_log_decay_bwd, ret_norm_w, w_out, ffn_norm_w, w_gate, w_up, w_down):
    B, S, _ = x.shape
    pos = jnp.arange(S, dtype=jnp.float32)
    split_at = np.cumsum([ATTN_WIDTH, KV_WIDTH, KV_WIDTH, RET_WIDTH, RET_WIDTH, RET_WIDTH]).tolist()
    h = x
    for l in range(DEPTH):
        n = rms_norm(h, attn_norm_w[l])
        proj = n @ w_in[l]
        aq, ak, av, rq, rk, rv, rg = jnp.split(proj, split_at, axis=-1)
        aq = rope(rms_norm(aq.reshape(B, S, ATTN_HEADS, HEAD_DIM), q_norm_w[l]), pos)
        ak = rope(rms_norm(ak.reshape(B, S, ATTN_KV_HEADS, HEAD_DIM), k_norm_w[l]), pos)
        av = av.reshape(B, S, ATTN_KV_HEADS, HEAD_DIM)
        y_attn = windowed_gqa_sink(aq, ak, av, attn_sink[l])
        rq = rope(rq.reshape(B, S, RET_HEADS, HEAD_DIM), pos)
        rk = rope(rk.reshape(B, S, RET_HEADS, HEAD_DIM), pos) * (HEAD_DIM ** -0.5)
        rv = rv.reshape(B, S, RET_HEADS, HEAD_DIM)
        to_bhsd = lambda t: jnp.transpose(t, (0, 2, 1, 3)).astype(jnp.float32)
        ret = bidirectional_retention(to_bhsd(rq), to_bhsd(rk), to_bhsd(rv),
                                      -jnp.abs(ret_log_decay_fwd[l].astype(jnp.float32)),
                                      -jnp.abs(ret_log_decay_bwd[l].astype(jnp.float32)))
        ret = jnp.transpose(ret, (0, 2, 1, 3)).astype(h.dtype)
        ret = rms_norm(ret, ret_norm_w[l].reshape(RET_HEADS, HEAD_DIM)).reshape(B, S, RET_WIDTH)
        y_ret = jax.nn.silu(rg) * ret
        h = h + jnp.concatenate([y_attn, y_ret], axis=-1) @ w_out[l]
        m = rms_norm(h, ffn_norm_w[l])
        h = h + (jax.nn.silu(m @ w_gate[l]) * (m @ w_up[l])) @ w_down[l]
    return h
```

```python
import contextlib
import os
import sys
import numpy as np
import concourse.bass as bass
import concourse.mybir as mybir
from concourse.bass_utils import run_bass_kernel_spmd

F32 = mybir.dt.float32
BF16 = mybir.dt.bfloat16
AF = mybir.ActivationFunctionType
ALU = mybir.AluOpType
AX = mybir.AxisListType

D_MODEL = 1024
SEQ = 4096
NT_ALL = 32
NT_OWN = 16
IN_PROJ = 2816
D_FF = 2816
EPS = 1e-6
C_AQ, C_AK, C_AV, C_RQ, C_RK, C_RV, C_RG = 0, 512, 640, 768, 1280, 1792, 2304

CF = {}
_o = 0
for _n, _w in [("anwP", 8), ("rnw", 512), ("qnw", 64), ("qnws", 64), ("knw", 64), ("knws", 64),
               ("sink", 8), ("decP", 8), ("decR", 16), ("c127m", 1), ("cm", 1)]:
    CF[_n] = (_o, _o + _w)
    _o += _w
NCF = _o
NCB = 128 + 256 + 256


class Op:
    __slots__ = ("eng", "fn", "deps", "signal", "sigval", "dma", "idx", "cost", "seg", "fence", "lat", "fin", "ph", "line", "crit", "estsrc", "prio")


class DmaTok:
    __slots__ = ("sem", "val", "op")

    def __init__(self, sem, val, op):
        self.sem = sem
        self.val = val
        self.op = op


class Buf:
    def __init__(self, name, ap=None, share=None):
        self.name = name
        self.ap = ap
        self._w = []
        self._r = []
        self.share = share
        self.sem = None
        self.cnt = 0

    @property
    def writers(self):
        return (self.share or self)._w

    @writers.setter
    def writers(self, v):
        (self.share or self)._w = v

    @property
    def readers(self):
        return (self.share or self)._r

    @readers.setter
    def readers(self, v):
        (self.share or self)._r = v

    def nfree(self):
        if self.ap is None:
            return 512
        n = 1
        for d in self.ap.shape[1:]:
            n *= d
        return n


_COST = {"dve": (0.10, 1.0 / 870), "act": (0.17, 1.0 / 1200), "pool": (0.15, 1.0 / 450), "pe": (0.02, 1.0 / 2600), "sp": (0.05, 0.0)}


class Rec:
    ENGS = ("pe", "act", "dve", "pool", "sp")

    def __init__(self, sems):
        self.lists = {e: [] for e in self.ENGS}
        self.free_sems = list(sems)
        self.engsem = {e: self.free_sems.pop() for e in ("pe", "act", "dve", "pool")}
        self.final = []
        self.nops = 0
        self.seg = 0

    def _deps(self, reads, writes, extra):
        deps = []
        for b in reads:
            for t in b.writers:
                deps.append((t, "raw"))
        for b in writes:
            for t in b.readers:
                deps.append((t, "war"))
            for t in b.writers:
                deps.append((t, "waw"))
        for t in extra:
            deps.append((t, "raw"))
        return deps

    def _new(self, eng, fn, deps, dma, cost):
        o = Op()
        o.eng = eng
        o.fn = fn
        o.deps = deps
        o.signal = False
        o.sigval = None
        o.dma = dma
        o.idx = self.nops
        self.nops += 1
        o.cost = cost
        o.seg = self.seg
        o.ph = getattr(self, "phase", "setup")
        o.line = sys._getframe(2).f_lineno if os.environ.get("KSCHEDSTAT", "") else 0
        o.fence = False
        o.lat = 0.0
        o.fin = 0.0
        o.crit = None
        o.estsrc = None
        o.prio = getattr(self, "prio", 0)
        self.lists[eng].append(o)
        return o

    def op(self, eng, fn, reads=(), writes=(), extra=(), n=None):
        if n is None:
            n = writes[0].nfree() if writes else 64
        a, b = _COST[eng]
        o = self._new(eng, fn, self._deps(reads, writes, extra), False, a + b * n)
        for bf in reads:
            bf.readers.append(o)
        for bf in writes:
            bf.writers = [o]
            bf.readers = []
        return o

    def dma(self, q, out_ap, in_ap, semowner, reads=(), writes=(), extra=(), final=False):
        if semowner.sem is None:
            semowner.sem = self.free_sems.pop()
        sem = semowner.sem
        semowner.cnt += 1
        fn = lambda e, out_ap=out_ap, in_ap=in_ap, sem=sem: e.dma_start(out=out_ap, in_=in_ap).then_inc(sem, 16)
        o = self._new(q, fn, self._deps(reads, writes, extra), True, 0.06 if q == "sp" else 1.0)
        nel = 1
        for d in out_ap.shape:
            nel *= d
        o.lat = 2.0 + nel * 4 / 180e3
        tok = DmaTok(sem, semowner.cnt * 16, o)
        for bf in reads:
            bf.readers.append(tok)
        for bf in writes:
            bf.writers = [tok]
            bf.readers = []
        if final:
            self.final.append(tok)
        return tok

    def schedule(self):
        allops = []
        for e in self.ENGS:
            allops.extend(self.lists[e])
        allops.sort(key=lambda o: o.idx)
        preds = {}
        succs = {}
        for o in allops:
            ps = {}
            for (t, kind) in o.deps:
                if isinstance(t, DmaTok):
                    ps[id(t.op)] = (t.op, True)
                else:
                    if id(t) not in ps:
                        ps[id(t)] = (t, False)
            preds[id(o)] = list(ps.values())
            for (p, _) in ps.values():
                succs.setdefault(id(p), []).append(o)
        bl = {}
        for o in reversed(allops):
            m = 0.0
            for sc in succs.get(id(o), ()):
                m = max(m, bl[id(sc)] + (0.3 if sc.eng != o.eng else 0.03))
            bl[id(o)] = o.cost + m
        use_bl = os.environ.get("KBL", "1") == "1"
        slack = float(os.environ.get("KSLACK", "0.05"))
        free = {e: 0.0 for e in self.ENGS}
        neworder = {e: [] for e in self.ENGS}
        nseg = max(o.seg for o in allops) + 1
        for sg in range(nseg):
            ops = [o for o in allops if o.seg == sg]
            inseg = set(id(o) for o in ops)
            indeg = {}
            est = {}
            avail = {e: [] for e in self.ENGS}
            remaining = {e: 0 for e in self.ENGS}
            for o in ops:
                remaining[o.eng] += 1
                d = 0
                t0 = 0.0
                for (p, isdma) in preds[id(o)]:
                    if id(p) in inseg:
                        d += 1
                    else:
                        t0 = max(t0, p.fin + (p.lat if isdma else 0.3))
                indeg[id(o)] = d
                est[id(o)] = t0
                if d == 0:
                    avail[o.eng].append(o)
            nleft = len(ops)
            while nleft:
                best = None
                for e in self.ENGS:
                    cands = []
                    tmin = None
                    for o in avail[e]:
                        if o.fence and remaining[e] > 1:
                            continue
                        st = max(free[e], est[id(o)])
                        cands.append((st, o))
                        if tmin is None or st < tmin:
                            tmin = st
                    if not cands:
                        continue
                    if use_bl:
                        st, o = max(((st, o) for (st, o) in cands if st <= tmin + slack), key=lambda x: (bl[id(x[1])], -x[1].idx))
                    else:
                        st, o = min(cands, key=lambda x: (x[0], x[1].idx))
                    key = (st, o.idx)
                    if best is None or key < best[0]:
                        best = (key, o)
                assert best is not None, "scheduler stuck"
                o = best[1]
                st = best[0][0]
                e = o.eng
                avail[e].remove(o)
                remaining[e] -= 1
                nleft -= 1
                o.fin = st + o.cost
                o.crit = (neworder[e][-1] if (neworder[e] and free[e] >= est[id(o)]) else o.estsrc)
                free[e] = o.fin
                neworder[e].append(o)
                for sc in succs.get(id(o), ()):
                    if id(sc) not in inseg:
                        continue
                    isdma = any((p is o and dm) for (p, dm) in preds[id(sc)])
                    lat = o.lat if isdma else (0.3 if sc.eng != e else 0.03)
                    if o.fin + lat > est[id(sc)]:
                        est[id(sc)] = o.fin + lat
                        sc.estsrc = o
                    indeg[id(sc)] -= 1
                    if indeg[id(sc)] == 0:
                        avail[sc.eng].append(sc)
        self.lists = neworder
        self.sim_total = max(free.values())
        if os.environ.get("KSCHEDSTAT", "") == "1":
            stat = {}
            for o in allops:
                d = stat.setdefault(o.ph, {"end": 0.0, "start": 1e18})
                d["end"] = max(d["end"], o.fin)
                d["start"] = min(d["start"], o.fin - o.cost)
                d[o.eng] = d.get(o.eng, 0.0) + o.cost
            for ph, d in stat.items():
                print("SCHED", ph, {k: round(v, 1) for k, v in d.items()})
            cp = os.environ.get("KSCHEDCRIT", "")
            if cp:
                ph_, n_ = cp.split(",")
                cur = max((o for o in allops if o.ph == ph_), key=lambda o: o.fin)
                chain = []
                while cur is not None and len(chain) < int(n_):
                    chain.append(cur)
                    cur = cur.crit
                for o in reversed(chain):
                    print("CRIT %8.2f %6.2f %-4s L%d%s" % (o.fin - o.cost, o.cost, o.eng, o.line, " dma" if o.dma else ""))
            w = os.environ.get("KSCHEDWIN", "")
            if w:
                a_, b_ = (float(x) for x in w.split(","))
                sel = [o for o in allops if a_ <= o.fin - o.cost < b_]
                sel.sort(key=lambda o: o.fin - o.cost)
                for o in sel:
                    print("OP %8.2f %6.2f %-4s L%d%s" % (o.fin - o.cost, o.cost, o.eng, o.line, " dma" if o.dma else ""))

    def finalize(self):
        if os.environ.get("KNOSCHED", "") != "1":
            self.schedule()
        for e in self.ENGS:
            for o in self.lists[e]:
                for (t, kind) in o.deps:
                    if isinstance(t, Op):
                        if t.eng != o.eng or o.dma or o.eng in ("act", "dve", "pool"):
                            t.signal = True
        for e in self.ENGS:
            cnt = 0
            for o in self.lists[e]:
                if o.signal:
                    cnt += 1
                    o.sigval = cnt

    def emit(self, e, eng):
        waited = {}
        for o in self.lists[e]:
            need = {}
            for (t, kind) in o.deps:
                if isinstance(t, DmaTok):
                    key, val = t.sem, t.val
                else:
                    if t.eng == e and not (o.dma or e in ("act", "dve", "pool")):
                        continue
                    key, val = self.engsem[t.eng], t.sigval
                if need.get(key, (None, 0))[1] < val:
                    need[key] = (key, val)
            for key, val in need.values():
                if waited.get(key, 0) < val:
                    eng.wait_ge(key, val)
                    waited[key] = val
            ins = o.fn(eng)
            if o.signal:
                ins.then_inc(self.engsem[e], 1)
        if e == "sp":
            for t in self.final:
                if waited.get(t.sem, 0) < t.val:
                    eng.wait_ge(t.sem, t.val)
                    waited[t.sem] = t.val


def v3(ap, a):
    return ap.rearrange("p (a b) -> p a b", a=a)


class _Stop(Exception):
    pass


def build_program():
    nc = bass.Bass("TRN2", target_bir_lowering=False)
    KSTOP = os.environ.get("KSTOP", "")
    xs = nc.dram_tensor("xs", [SEQ, D_MODEL], F32, kind="ExternalInput").ap()
    cs_d = nc.dram_tensor("cs", [SEQ, 128], F32, kind="ExternalInput").ap()
    cstf_d = nc.dram_tensor("cstf", [128, NCF], F32, kind="ExternalInput").ap()
    csts_d = nc.dram_tensor("csts", [128, 512], F32, kind="ExternalInput").ap()
    cstb_d = nc.dram_tensor("cstb", [128, NCB], F32, kind="ExternalInput").ap()
    fnw_d = nc.dram_tensor("fnw", [128, D_MODEL], F32, kind="ExternalInput").ap()
    w_in_d = nc.dram_tensor("w_in", [D_MODEL, IN_PROJ], F32, kind="ExternalInput").ap()
    w_out_d = nc.dram_tensor("w_out", [D_MODEL, D_MODEL], F32, kind="ExternalInput").ap()
    w_gate_d = nc.dram_tensor("w_gate", [D_MODEL, D_FF], F32, kind="ExternalInput").ap()
    w_up_d = nc.dram_tensor("w_up", [D_MODEL, D_FF], F32, kind="ExternalInput").ap()
    w_down_d = nc.dram_tensor("w_down", [D_FF, D_MODEL], F32, kind="ExternalInput").ap()
    out_d = nc.dram_tensor("out", [NT_OWN * 128, D_MODEL], F32, kind="ExternalOutput").ap()

    NB16 = 26148
    NF32 = 7872
    with contextlib.ExitStack() as es:
        Wt = es.enter_context(nc.sbuf_tensor("Wt", [128, 30720], BF16))
        Ht = es.enter_context(nc.sbuf_tensor("Ht", [128, NT_OWN * 1024], F32))
        ABt = es.enter_context(nc.sbuf_tensor("ABt", [128, NB16], BF16))
        AFt = es.enter_context(nc.sbuf_tensor("AFt", [128, NF32], F32))
        banks = [es.enter_context(nc.psum_tensor(f"pb{i}", [128, 512], F32)) for i in range(8)]
        sems = [es.enter_context(nc.semaphore(f"s{i}")) for i in range(60)]
        block = es.enter_context(nc.Block())
        R = Rec(sems)

        class Arena:
            def __init__(self, t, n):
                self.t, self.n, self.off = t, n, 0

            def alloc(self, name, n):
                assert self.off + n <= self.n, (name, self.off, n, self.n)
                ap = self.t[:, self.off:self.off + n]
                self.off += n
                return Buf(name, ap)

        AB = Arena(ABt, NB16)
        AFa = Arena(AFt, NF32)

        W_in = Buf("w_in", v3(Wt[:, 0:22528], 8))
        W_out = Buf("w_out", v3(Wt[:, 22528:30720], 8))
        hbuf = [Buf(f"h{c}", Ht[:, c * 1024:(c + 1) * 1024]) for c in range(NT_OWN)]
        kvf = [Buf(f"kvf{c}", v3(Ht[:, 12 * 1024 + c * 256: 12 * 1024 + (c + 1) * 256], 4)) for c in range(NT_OWN)]

        F = [Buf(f"F{i}", banks[i][:, :]) for i in range(6)]
        T0 = Buf("T0", banks[6][:, :])
        T1a = Buf("T1a", banks[7][:, 0:256])
        T1b = Buf("T1b", banks[7][:, 256:512], share=T1a)
        T0b = banks[6][:, :].bitcast(BF16)
        T1ab = banks[7][:, 0:256].bitcast(BF16)
        T1bb = banks[7][:, 256:512].bitcast(BF16)

        cstb = AB.alloc("cstb", NCB)
        ident = cstb.ap[:, 0:128]
        mprev = cstb.ap[:, 128:384]
        mnext = cstb.ap[:, 384:640]
        RbS = [AB.alloc(f"RbS{c}", 256) for c in range(NT_OWN)]
        RfS = [AB.alloc(f"RfS{c}", 256) for c in range(NT_OWN)]
        KT = [AB.alloc(f"KT{c}", 256) for c in range(NT_OWN + 1)]
        Vb = [AB.alloc(f"V{c}", 130) for c in range(NT_OWN + 1)]
        xb = AB.alloc("xb", 1024)
        xT = AB.alloc("xT", 1024)
        qb = AB.alloc("qb", 512)
        rqb = AB.alloc("rqb", 512)
        rkb = AB.alloc("rkb", 512)
        rvb = AB.alloc("rvb", 512)
        kdup = AB.alloc("kdup", 256)
        qT = AB.alloc("qT", 512)
        rqT = AB.alloc("rqT", 512)
        rqxf = AB.alloc("rqxf", 512)
        rqxb = AB.alloc("rqxb", 512)
        rkT = AB.alloc("rkT", 512)
        PT = AB.alloc("PT", 768)
        retS = AB.alloc("retS", 1024)
        yb = AB.alloc("yb", 1024)
        yT = AB.alloc("yT", 1024)
        kzb, kzf = qb, rqb

        cstf = AFa.alloc("cstf", NCF)

        def cf(name):
            a, b = CF[name]
            return cstf.ap[:, a:b]

        Xif = AFa.alloc("Xif", 512)
        Xib = AFa.alloc("Xib", 512)
        csl = [AFa.alloc(f"cs{i}", 128) for i in range(2)]
        tabq = AFa.alloc("tabq", 128)
        ta = AFa.alloc("ta", 512)
        DT = AFa.alloc("DT", 1024)
        td = AFa.alloc("td", 512)
        te = AFa.alloc("te", 512)
        tg = AFa.alloc("tg", 512)
        tf = AFa.alloc("tf", 512)
        akf = Buf("akf", tf.ap[:, 0:128])
        tk1 = Buf("tk1", tf.ap[:, 128:256])
        tk2 = Buf("tk2", tf.ap[:, 256:384])
        tabk = Buf("tabk", tf.ap[:, 384:512])
        Rb = AFa.alloc("Rb", 256)
        Rf = AFa.alloc("Rf", 256)
        rtmp = AFa.alloc("rtmp", 256)
        lgP = AFa.alloc("lgP", 8)
        lgR = AFa.alloc("lgR", 16)
        g128 = AFa.alloc("g128", 8)
        Zfb = AFa.alloc("Zfb", 16)
        esink = AFa.alloc("esink", 8)
        st = [AFa.alloc(f"st{i}", 8) for i in range(2)]
        s8a = AFa.alloc("s8a", 8)
        s8g = AFa.alloc("s8g", 8)
        s8h = AFa.alloc("s8h", 8)
        tb2 = AFa.alloc("tb2", 512)
        junk = AFa.alloc("junk", 512)
        junk_ap = junk.ap.bitcast(BF16)
        s8b = AFa.alloc("s8b", 8)
        s8c = AFa.alloc("s8c", 8)
        s8d = AFa.alloc("s8d", 8)
        s8e = AFa.alloc("s8e", 8)
        s8f = AFa.alloc("s8f", 8)
        epsb = AFa.alloc("epsb", 1)
        onesb = AFa.alloc("onesb", 1)
        fsc = AFa.alloc("fsc", 4)
        ssq2 = AFa.alloc("ssq2", 16)
        lnr2 = AFa.alloc("lnr2", 16)
        rstd2 = AFa.alloc("rstd2", 16)

        print("arena use: bf16", AB.off, "/", NB16, " f32", AFa.off, "/", NF32)
        W_in_v = w_in_d.rearrange("(k p) n -> p k n", p=128)
        W_out_v = w_out_d.rearrange("(k p) n -> p k n", p=128)

        def _record():
            R.dma("sp", cstf.ap, cstf_d, cstf, writes=[cstf])
            R.dma("sp", ta.ap, csts_d, ta, writes=[ta])
            R.dma("pool", cstb.ap, cstb_d, cstb, writes=[cstb])
            wcols = [("rkrv", C_RK, C_RG), ("akav", C_AK, C_RQ), ("aq", C_AQ, C_AK), ("rq", C_RQ, C_RK), ("rg", C_RG, IN_PROJ)]
            Wg_buf = {}
            for name, a, b in wcols:
                bb = Buf("w_in_" + name)
                Wg_buf[name] = bb
                R.dma("pool", W_in.ap[:, :, a:b], W_in_v[:, :, a:b], bb, writes=[bb])
            R.dma("pool", W_out.ap, W_out_v, W_out, writes=[W_out])
            for name, a, b in wcols:
                for k in range(8):
                    R.op("dve", lambda e, k=k, a=a, b=b: e.tensor_scalar(out=W_in.ap[:, k, a:b], in0=W_in.ap[:, k, a:b],
                                                                         scalar1=cf("anwP")[:, k:k + 1], scalar2=None, op0=ALU.mult),
                         reads=[cstf, Wg_buf[name]], writes=[Wg_buf[name]], n=(b - a) // 3)

            iota1 = ta.ap[:, 0:128]
            iota2 = ta.ap[:, 128:256]
            reluP = ta.ap[:, 256:384]
            reluN = ta.ap[:, 384:512]

            R.op("act", lambda e: e.activation(out=lgP.ap, in_=cf("decP"), func=AF.Abs), reads=[cstf], writes=[lgP])
            R.op("act", lambda e: e.activation(out=lgR.ap, in_=cf("decR"), func=AF.Abs), reads=[cstf], writes=[lgR])
            R.op("dve", lambda e: e.tensor_scalar(out=lgP.ap, in0=lgP.ap, scalar1=-1.0, scalar2=None, op0=ALU.mult), reads=[lgP], writes=[lgP])
            R.op("dve", lambda e: e.tensor_scalar(out=lgR.ap, in0=lgR.ap, scalar1=-1.0, scalar2=None, op0=ALU.mult), reads=[lgR], writes=[lgR])
            R.op("act", lambda e: e.activation(out=g128.ap, in_=lgP.ap, func=AF.Exp, scale=128.0), reads=[lgP], writes=[g128])
            for t in range(4):
                R.op("act", lambda e, t=t: e.activation(out=Xif.ap[:, t * 128:(t + 1) * 128], in_=iota1, func=AF.Exp,
                                                        scale=lgP.ap[:, t:t + 1]), reads=[lgP, ta], writes=[Xif])
                R.op("act", lambda e, t=t: e.activation(out=Xib.ap[:, t * 128:(t + 1) * 128], in_=iota2, func=AF.Exp,
                                                        scale=lgP.ap[:, 4 + t:5 + t]), reads=[lgP, ta], writes=[Xib])
            R.op("act", lambda e: e.activation(out=Zfb.ap[:, 0:8], in_=lgR.ap[:, 0:8], func=AF.Exp, scale=cf("c127m")),
                 reads=[lgR, cstf], writes=[Zfb])
            R.op("act", lambda e: e.activation(out=Zfb.ap[:, 8:16], in_=lgR.ap[:, 8:16], func=AF.Exp, scale=cf("cm")),
                 reads=[lgR, cstf], writes=[Zfb])
            R.op("act", lambda e: e.activation(out=esink.ap, in_=cf("sink"), func=AF.Exp), reads=[cstf], writes=[esink])
            for h in range(8):
                R.op("dve", lambda e, h=h: e.tensor_scalar(out=te.ap[:, 0:128], in0=reluP, scalar1=lgR.ap[:, h:h + 1],
                                                           scalar2=None, op0=ALU.mult), reads=[ta, lgR], writes=[te])
                R.op("dve", lambda e, h=h: e.scalar_tensor_tensor(out=td.ap[:, 0:128], in0=reluN, scalar=lgR.ap[:, 8 + h:9 + h],
                                                                  in1=te.ap[:, 0:128], op0=ALU.mult, op1=ALU.add),
                     reads=[ta, lgR, te], writes=[td])
                R.op("act", lambda e, h=h: e.activation(out=DT.ap[:, h * 128:(h + 1) * 128], in_=td.ap[:, 0:128], func=AF.Exp),
                     reads=[td], writes=[DT])
            for c in range(NT_OWN + 1):
                R.op("pool", lambda e, c=c: e.memset(v3(Vb[c].ap, 2)[:, :, 64:65], 1.0), writes=[Vb[c]])
            R.op("pool", lambda e: e.memset(Rb.ap, 0.0), writes=[Rb])
            R.op("pool", lambda e: e.memset(Rf.ap, 0.0), writes=[Rf])
            R.op("pool", lambda e: e.memset(epsb.ap, EPS), writes=[epsb])
            R.op("pool", lambda e: e.memset(onesb.ap, 1.0), writes=[onesb])

            if KSTOP == "setup":
                raise _Stop()
            def load_tile(L, xbuf, slot):
                R.dma("sp", xbuf.ap, xs[L * 128:(L + 1) * 128, :], xbuf, writes=[xbuf])
                R.dma("sp", csl[slot].ap, cs_d[L * 128:(L + 1) * 128, :], csl[slot], writes=[csl[slot]])

            def prep_tile(xbuf, slot, xb=xb, xT=xT):
                s = st[slot]
                R.op("act", lambda e: e.activation(out=junk_ap, in_=xbuf.ap, func=AF.Square, accum_out=s.ap[:, 0:1]),
                     reads=[xbuf], writes=[junk, s], n=1024)
                R.op("act", lambda e: e.activation(out=s.ap[:, 1:2], in_=s.ap[:, 0:1], func=AF.Ln, scale=1.0 / D_MODEL, bias=epsb.ap),
                     reads=[s, epsb], writes=[s])
                R.op("act", lambda e: e.activation(out=s.ap[:, 2:3], in_=s.ap[:, 1:2], func=AF.Exp, scale=-0.5), reads=[s], writes=[s])
                R.op("dve", lambda e: e.tensor_scalar(out=s.ap[:, 5:6], in0=s.ap[:, 2:3], scalar1=-1.0, scalar2=None,
                                                      op0=ALU.mult), reads=[s], writes=[s])
                R.op("dve", lambda e: e.tensor_scalar(out=s.ap[:, 3:4], in0=s.ap[:, 2:3], scalar1=0.125, scalar2=None,
                                                      op0=ALU.mult), reads=[s], writes=[s])
                R.op("dve", lambda e: e.tensor_scalar(out=s.ap[:, 4:5], in0=s.ap[:, 2:3], scalar1=0.5, scalar2=None,
                                                      op0=ALU.mult), reads=[s], writes=[s])
                R.op("dve", lambda e: e.tensor_copy(out=xb.ap, in_=xbuf.ap), reads=[xbuf], writes=[xb], n=600)
                for k in range(8):
                    R.op("pe", lambda e, k=k: e.transpose(out=T0b[:, k * 128:(k + 1) * 128], in_=xb.ap[:, k * 128:(k + 1) * 128],
                                                          identity=ident), reads=[xb, cstb], writes=[T0], n=128)
                R.op("act", lambda e: e.activation(out=xT.ap, in_=T0b, func=AF.Identity), reads=[T0], writes=[xT])
                return s

            def inproj(bank, c0, n, wnames, xT=xT, also=()):
                xT3 = v3(xT.ap, 8)
                for k in range(8):
                    R.op("pe", lambda e, k=k: e.matmul(bank.ap[:, 0:n], lhsT=xT3[:, k, :], rhs=W_in.ap[:, k, c0:c0 + n],
                                                       start=(k == 0), stop=(k == 7)),
                         reads=[xT] + [Wg_buf[w] for w in wnames], writes=[bank] + list(also), n=n)

            def rope(eng_a, eng_b, src, dst_t1, dst_t2, A, B, H):
                s3 = v3(src.ap[:, 0:H * 64], H)
                t13 = v3(dst_t1.ap[:, 0:H * 64], H)
                t23 = v3(dst_t2.ap[:, 0:H * 64], H)
                Ab = A.unsqueeze(1).broadcast_to([128, H, 64])
                Bl = B[:, 0:32].unsqueeze(1).broadcast_to([128, H, 32])
                Bh = B[:, 32:64].unsqueeze(1).broadcast_to([128, H, 32])
                return s3, t13, t23, Ab, Bl, Bh

            R.phase = "pre"
            xring = [hbuf[4], hbuf[5], hbuf[6]]
            order = list(range(NT_ALL - 1, -1, -1))
            load_tile(order[0], xring[0], 0)
            for i, L in enumerate(order):
                if KSTOP == "pre1" and i == 1:
                    raise _Stop()
                xbuf = xring[i % 3]
                slot = i % 2
                if i + 1 < len(order):
                    load_tile(order[i + 1], xring[(i + 1) % 3], (i + 1) % 2)
                own = L < NT_OWN
                needkv = L <= NT_OWN
                par = i % 2
                xb_ = (xb, retS)[par]
                xT_ = (xT, yT)[par]
                ta_ = (ta, tg)[par]
                rvb_ = (rvb, rkb)[par]
                kzb_ = (qb, rqxf)[par]
                kzf_ = (rqb, rqxb)[par]
                Brk = (F[0], F[3])[par]
                Brv = (F[1], F[4])[par]
                Bkv = (F[2], F[5])[par]
                s = prep_tile(xbuf, slot, xb_, xT_)
                cst = csl[slot]
                cosv = cst.ap[:, 0:64]
                sinv = cst.ap[:, 64:128]
                inproj(Brk, C_RK, 512, ["rkrv"], xT_)
                inproj(Brv, C_RV, 512, ["rkrv"], xT_)
                if needkv:
                    inproj(T1b, C_AK, 256, ["akav"], xT_, also=[T1a])
                R.op("act", lambda e, s=s, ta_=ta_, Brk=Brk: e.activation(out=ta_.ap, in_=Brk.ap, func=AF.Identity, scale=s.ap[:, 3:4]),
                     reads=[Brk, s], writes=[ta_])
                R.op("act", lambda e, s=s, rvb_=rvb_, Brv=Brv: e.activation(out=rvb_.ap, in_=Brv.ap, func=AF.Identity, scale=s.ap[:, 2:3]),
                     reads=[Brv, s], writes=[rvb_])
                s3, t13, t23, Ab, Bl, Bh = rope(None, None, ta_, td, te, cosv, sinv, 8)
                R.op("dve", lambda e, s3=s3, t13=t13, Ab=Ab: e.tensor_tensor(out=t13, in0=s3, in1=Ab, op=ALU.mult),
                     reads=[ta_, cst], writes=[td])
                R.op("pool", lambda e, s3=s3, t23=t23, Bl=Bl: e.tensor_tensor(out=t23[:, :, 0:32], in0=s3[:, :, 32:64], in1=Bl, op=ALU.mult),
                     reads=[ta_, cst], writes=[te])
                R.op("pool", lambda e, s3=s3, t23=t23, Bh=Bh: e.tensor_tensor(out=t23[:, :, 32:64], in0=s3[:, :, 0:32], in1=Bh, op=ALU.mult),
                     reads=[ta_, cst], writes=[te])
                R.op("dve", lambda e: e.tensor_tensor(out=td.ap, in0=td.ap, in1=te.ap, op=ALU.add), reads=[td, te], writes=[td])
                Zb_b = Zfb.ap[:, 8:16].unsqueeze(2).broadcast_to([128, 8, 64])
                Zf_b = Zfb.ap[:, 0:8].unsqueeze(2).broadcast_to([128, 8, 64])
                R.op("dve", lambda e, Zb_b=Zb_b, kzb_=kzb_: e.tensor_tensor(out=v3(kzb_.ap, 8), in0=v3(td.ap, 8), in1=Zb_b, op=ALU.mult),
                     reads=[td, Zfb], writes=[kzb_])
                if own:
                    R.op("pool", lambda e, Zf_b=Zf_b, kzf_=kzf_: e.tensor_tensor(out=v3(kzf_.ap, 8), in0=v3(td.ap, 8), in1=Zf_b, op=ALU.mult),
                         reads=[td, Zfb], writes=[kzf_])
                for t in range(4):
                    R.op("pe", lambda e, t=t, Bkv=Bkv, kzb_=kzb_, rvb_=rvb_: e.matmul(Bkv.ap[:, t * 128:(t + 1) * 128], lhsT=kzb_.ap[:, t * 128:(t + 1) * 128],
                                                       rhs=rvb_.ap[:, t * 128:(t + 1) * 128], start=True, stop=True),
                         reads=[kzb_, rvb_], writes=[Bkv], n=128)
                if own:
                    for t in range(4):
                        R.op("pe", lambda e, t=t, Brk=Brk, kzf_=kzf_, rvb_=rvb_: e.matmul(Brk.ap[:, t * 128:(t + 1) * 128], lhsT=kzf_.ap[:, t * 128:(t + 1) * 128],
                                                           rhs=rvb_.ap[:, t * 128:(t + 1) * 128], start=True, stop=True),
                             reads=[kzf_, rvb_], writes=[Brk], n=128)
                Rb3 = v3(Rb.ap, 4)
                rt3 = v3(rtmp.ap, 4)
                F33 = v3(Bkv.ap, 4)
                F43 = v3(Brk.ap, 4)
                if own:
                    R.op("dve", lambda e, L=L: e.tensor_copy(out=RbS[L].ap, in_=Rb.ap), reads=[Rb], writes=[RbS[L]])
                gb_b = g128.ap[:, 4:8].unsqueeze(2).broadcast_to([128, 4, 64])
                R.op("dve", lambda e, gb_b=gb_b, Rb3=Rb3, rt3=rt3: e.tensor_tensor(out=rt3, in0=Rb3, in1=gb_b, op=ALU.mult),
                     reads=[Rb, g128], writes=[rtmp])
                R.op("dve", lambda e, Rb3=Rb3, rt3=rt3, F33=F33: e.tensor_tensor(out=Rb3[0:64], in0=rt3[0:64], in1=F33[0:64, :, 0:64], op=ALU.add),
                     reads=[rtmp, Bkv], writes=[Rb])
                R.op("dve", lambda e, Rb3=Rb3, rt3=rt3, F33=F33: e.tensor_tensor(out=Rb3[64:128], in0=rt3[64:128], in1=F33[64:128, :, 64:128], op=ALU.add),
                     reads=[rtmp, Bkv, Rb], writes=[Rb])
                if own:
                    R.op("act", lambda e, L=L, F43=F43: e.activation(out=kvf[L].ap[0:64], in_=F43[0:64, :, 0:64], func=AF.Identity),
                         reads=[Brk], writes=[kvf[L]])
                    R.op("act", lambda e, L=L, F43=F43: e.activation(out=kvf[L].ap[64:128], in_=F43[64:128, :, 64:128], func=AF.Identity),
                         reads=[Brk, kvf[L]], writes=[kvf[L]])
                if needkv:
                    R.op("act", lambda e, s=s: e.activation(out=akf.ap, in_=T1b.ap[:, 0:128], func=AF.Identity, scale=s.ap[:, 2:3]),
                         reads=[T1b, s], writes=[akf])
                    R.op("act", lambda e, s=s, L=L: e.activation(out=v3(Vb[L].ap, 2)[:, :, 0:64], in_=v3(T1b.ap[:, 128:256], 2),
                                                                  func=AF.Identity, scale=s.ap[:, 2:3]),
                         reads=[T1b, s, Vb[L]], writes=[Vb[L]])
                    R.op("dve", lambda e: e.tensor_tensor(out=tk1.ap, in0=akf.ap, in1=akf.ap, op=ALU.mult), reads=[akf], writes=[tk1])
                    R.op("dve", lambda e: e.tensor_reduce(out=s8a.ap[:, 0:2], in_=v3(tk1.ap, 2), axis=AX.X, op=ALU.add),
                         reads=[tk1], writes=[s8a])
                    R.op("act", lambda e: e.activation(out=s8a.ap[:, 2:4], in_=s8a.ap[:, 0:2], func=AF.Ln, scale=1.0 / 64, bias=epsb.ap),
                         reads=[s8a, epsb], writes=[s8a])
                    R.op("act", lambda e: e.activation(out=s8a.ap[:, 4:6], in_=s8a.ap[:, 2:4], func=AF.Exp, scale=-0.5), reads=[s8a], writes=[s8a])
                    R.op("pool", lambda e, cosv=cosv: e.tensor_tensor(out=tabk.ap[:, 0:64], in0=cosv, in1=cf("knw"), op=ALU.mult),
                         reads=[cst, cstf], writes=[tabk])
                    R.op("pool", lambda e, sinv=sinv: e.tensor_tensor(out=tabk.ap[:, 64:128], in0=sinv, in1=cf("knws"), op=ALU.mult),
                         reads=[cst, cstf, tabk], writes=[tabk])
                    s3, t13, t23, Ab, Bl, Bh = rope(None, None, akf, tk1, tk2, tabk.ap[:, 0:64], tabk.ap[:, 64:128], 2)
                    R.op("pool", lambda e, s3=s3, t13=t13, Ab=Ab: e.tensor_tensor(out=t13, in0=s3, in1=Ab, op=ALU.mult),
                         reads=[akf, tabk, s8a], writes=[tk1])
                    R.op("pool", lambda e, s3=s3, t23=t23, Bl=Bl: e.tensor_tensor(out=t23[:, :, 0:32], in0=s3[:, :, 32:64], in1=Bl, op=ALU.mult),
                         reads=[akf, tabk], writes=[tk2])
                    R.op("pool", lambda e, s3=s3, t23=t23, Bh=Bh: e.tensor_tensor(out=t23[:, :, 32:64], in0=s3[:, :, 0:32], in1=Bh, op=ALU.mult),
                         reads=[akf, tabk, tk2], writes=[tk2])
                    R.op("dve", lambda e: e.tensor_tensor(out=tk1.ap, in0=tk1.ap, in1=tk2.ap, op=ALU.add), reads=[tk1, tk2], writes=[tk1])
                    kd4 = kdup.ap.rearrange("p (g u d) -> p g u d", g=2, u=2)
                    rk2b = s8a.ap[:, 4:6].unsqueeze(2).broadcast_to([128, 2, 64])
                    for u in range(2):
                        R.op("dve", lambda e, u=u, kd4=kd4, rk2b=rk2b: e.tensor_tensor(out=kd4[:, :, u, :], in0=v3(tk1.ap, 2), in1=rk2b, op=ALU.mult),
                             reads=[tk1, s8a, kdup], writes=[kdup])
                    for g in range(2):
                        R.op("pe", lambda e, g=g: e.transpose(out=T1ab[:, g * 128:(g + 1) * 128], in_=kdup.ap[:, g * 128:(g + 1) * 128],
                                                              identity=ident), reads=[kdup, cstb], writes=[T1a, T1b], n=128)
                    R.op("act", lambda e, L=L: e.activation(out=KT[L].ap, in_=T1ab[:, 0:256], func=AF.Identity), reads=[T1a], writes=[KT[L]])

            if KSTOP == "pre":
                raise _Stop()
            Rf3 = v3(Rf.ap, 4)
            rt3 = v3(rtmp.ap, 4)
            gf_b = g128.ap[:, 0:4].unsqueeze(2).broadcast_to([128, 4, 64])
            scan_last = None
            for c in range(NT_OWN):
                R.op("dve", lambda e, c=c: e.tensor_copy(out=RfS[c].ap, in_=Rf.ap), reads=[Rf], writes=[RfS[c]])
                R.op("dve", lambda e: e.tensor_tensor(out=rt3, in0=Rf3, in1=gf_b, op=ALU.mult), reads=[Rf, g128], writes=[rtmp])
                scan_last = R.op("dve", lambda e, c=c: e.tensor_tensor(out=Rf3, in0=rt3, in1=kvf[c].ap, op=ALU.add),
                                 reads=[rtmp, kvf[c]], writes=[Rf])

            if KSTOP == "scan":
                raise _Stop()
            R.phase = "main"

            def load_main(c):
                extra = [scan_last] if c >= 12 else []
                R.dma("sp", hbuf[c].ap, xs[c * 128:(c + 1) * 128, :], hbuf[c], writes=[hbuf[c]], extra=extra)
                R.dma("sp", csl[c % 2].ap, cs_d[c * 128:(c + 1) * 128, :], csl[c % 2], writes=[csl[c % 2]])

            load_main(0)
            for c in range(NT_OWN):
                if KSTOP == "main1" and c == 1:
                    raise _Stop()
                xbuf = hbuf[c]
                slot = c % 2
                s = prep_tile(xbuf, slot)
                cst = csl[slot]
                cosv = cst.ap[:, 0:64]
                sinv = cst.ap[:, 64:128]
                R.op("pool", lambda e, cosv=cosv: e.tensor_tensor(out=tabq.ap[:, 0:64], in0=cosv, in1=cf("qnw"), op=ALU.mult),
                     reads=[cst, cstf], writes=[tabq])
                R.op("pool", lambda e, sinv=sinv: e.tensor_tensor(out=tabq.ap[:, 64:128], in0=sinv, in1=cf("qnws"), op=ALU.mult),
                     reads=[cst, cstf, tabq], writes=[tabq])
                BQ, BG = (F[4], F[0]) if os.environ.get("KSWAP", "0") == "1" else (F[0], F[4])
                inproj(BQ, C_AQ, 512, ["aq"])
                inproj(F[1], C_RQ, 512, ["rq"])
                inproj(F[2], C_RK, 512, ["rkrv"])
                inproj(F[3], C_RV, 512, ["rkrv"])
                inproj(BG, C_RG, 512, ["rg"])
                if c + 1 < NT_OWN:
                    load_main(c + 1)
                R.op("act", lambda e, s=s: e.activation(out=ta.ap, in_=BQ.ap, func=AF.Identity, scale=s.ap[:, 2:3]),
                     reads=[BQ, s], writes=[ta])
                R.op("dve", lambda e: e.tensor_tensor(out=td.ap, in0=ta.ap, in1=ta.ap, op=ALU.mult), reads=[ta], writes=[td])
                R.op("dve", lambda e: e.tensor_reduce(out=s8a.ap, in_=v3(td.ap, 8), axis=AX.X, op=ALU.add), reads=[td], writes=[s8a])
                R.op("act", lambda e: e.activation(out=s8b.ap, in_=s8a.ap, func=AF.Ln, scale=1.0 / 64, bias=epsb.ap),
                     reads=[s8a, epsb], writes=[s8b])
                R.op("act", lambda e: e.activation(out=s8c.ap, in_=s8b.ap, func=AF.Exp, scale=-0.5), reads=[s8b], writes=[s8c])
                s3, t13, t23, Ab, Bl, Bh = rope(None, None, ta, td, te, tabq.ap[:, 0:64], tabq.ap[:, 64:128], 8)
                R.op("dve", lambda e, s3=s3, t13=t13, Ab=Ab: e.tensor_tensor(out=t13, in0=s3, in1=Ab, op=ALU.mult),
                     reads=[ta, tabq, s8a], writes=[td])
                R.op("pool", lambda e, s3=s3, t23=t23, Bl=Bl: e.tensor_tensor(out=t23[:, :, 0:32], in0=s3[:, :, 32:64], in1=Bl, op=ALU.mult),
                     reads=[ta, tabq], writes=[te])
                R.op("pool", lambda e, s3=s3, t23=t23, Bh=Bh: e.tensor_tensor(out=t23[:, :, 32:64], in0=s3[:, :, 0:32], in1=Bh, op=ALU.mult),
                     reads=[ta, tabq, te], writes=[te])
                R.op("dve", lambda e: e.tensor_tensor(out=td.ap, in0=td.ap, in1=te.ap, op=ALU.add), reads=[td, te], writes=[td])
                rq8b = s8c.ap.unsqueeze(2).broadcast_to([128, 8, 64])
                R.op("dve", lambda e, rq8b=rq8b: e.tensor_tensor(out=v3(qb.ap, 8), in0=v3(td.ap, 8), in1=rq8b, op=ALU.mult),
                     reads=[td, s8c], writes=[qb])
                for t in range(4):
                    R.op("pe", lambda e, t=t: e.transpose(out=T1ab[:, t * 128:(t + 1) * 128], in_=qb.ap[:, t * 128:(t + 1) * 128],
                                                          identity=ident), reads=[qb, cstb], writes=[T1a], n=128)
                R.op("act", lambda e: e.activation(out=qT.ap, in_=T1ab, func=AF.Identity), reads=[T1a], writes=[qT])
                if KSTOP == "m_q1":
                    raise _Stop()
                R.op("act", lambda e, s=s: e.activation(out=tb2.ap, in_=F[1].ap, func=AF.Identity, scale=s.ap[:, 2:3]),
                     reads=[F[1], s], writes=[tb2] + ([akf, tk1, tk2, tabk] if c == 0 else []), n=512)
                s3, t13, t23, Ab, Bl, Bh = rope(None, None, tb2, td, te, cosv, sinv, 8)
                R.op("dve", lambda e, s3=s3, t13=t13, Ab=Ab: e.tensor_tensor(out=t13, in0=s3, in1=Ab, op=ALU.mult),
                     reads=[tb2, cst], writes=[td])
                R.op("pool", lambda e, s3=s3, t23=t23, Bl=Bl: e.tensor_tensor(out=t23[:, :, 0:32], in0=s3[:, :, 32:64], in1=Bl, op=ALU.mult),
                     reads=[tb2, cst], writes=[te])
                R.op("pool", lambda e, s3=s3, t23=t23, Bh=Bh: e.tensor_tensor(out=t23[:, :, 32:64], in0=s3[:, :, 0:32], in1=Bh, op=ALU.mult),
                     reads=[tb2, cst, te], writes=[te])
                R.op("dve", lambda e: e.tensor_tensor(out=rqb.ap, in0=td.ap, in1=te.ap, op=ALU.add), reads=[td, te], writes=[rqb])
                for t in range(4):
                    R.op("pe", lambda e, t=t: e.transpose(out=T1bb[:, t * 128:(t + 1) * 128], in_=rqb.ap[:, t * 128:(t + 1) * 128],
                                                          identity=ident), reads=[rqb, cstb], writes=[T1b], n=128)
                R.op("act", lambda e: e.activation(out=rqT.ap, in_=T1bb, func=AF.Identity), reads=[T1b], writes=[rqT])
                R.op("dve", lambda e: e.tensor_tensor(out=rqxf.ap, in0=rqT.ap, in1=Xif.ap, op=ALU.mult), reads=[rqT, Xif], writes=[rqxf])
                R.op("dve", lambda e: e.tensor_tensor(out=rqxb.ap, in0=rqT.ap, in1=Xib.ap, op=ALU.mult), reads=[rqT, Xib], writes=[rqxb])
                if KSTOP == "m_q2":
                    raise _Stop()
                R.op("act", lambda e, s=s: e.activation(out=ta.ap, in_=F[2].ap, func=AF.Identity, scale=s.ap[:, 3:4]),
                     reads=[F[2], s], writes=[ta])
                s3, t13, t23, Ab, Bl, Bh = rope(None, None, ta, td, te, cosv, sinv, 8)
                R.op("dve", lambda e, s3=s3, t13=t13, Ab=Ab: e.tensor_tensor(out=t13, in0=s3, in1=Ab, op=ALU.mult),
                     reads=[ta, cst], writes=[td])
                R.op("pool", lambda e, s3=s3, t23=t23, Bl=Bl: e.tensor_tensor(out=t23[:, :, 0:32], in0=s3[:, :, 32:64], in1=Bl, op=ALU.mult),
                     reads=[ta, cst], writes=[te])
                R.op("pool", lambda e, s3=s3, t23=t23, Bh=Bh: e.tensor_tensor(out=t23[:, :, 32:64], in0=s3[:, :, 0:32], in1=Bh, op=ALU.mult),
                     reads=[ta, cst, te], writes=[te])
                R.op("dve", lambda e: e.tensor_tensor(out=rkb.ap, in0=td.ap, in1=te.ap, op=ALU.add), reads=[td, te], writes=[rkb])
                for t in range(4):
                    R.op("pe", lambda e, t=t: e.transpose(out=T1ab[:, t * 128:(t + 1) * 128], in_=rkb.ap[:, t * 128:(t + 1) * 128],
                                                          identity=ident), reads=[rkb, cstb], writes=[T1a], n=128)
                R.op("act", lambda e: e.activation(out=rkT.ap, in_=T1ab, func=AF.Identity), reads=[T1a], writes=[rkT])
                if KSTOP == "m_q3":
                    raise _Stop()
                R.op("act", lambda e, s=s: e.activation(out=rvb.ap, in_=F[3].ap, func=AF.Identity, scale=s.ap[:, 2:3]),
                     reads=[F[3], s], writes=[rvb])
                R.op("act", lambda e, s=s: e.activation(out=tg.ap, in_=BG.ap, func=AF.Exp, scale=s.ap[:, 5:6]),
                     reads=[BG, s], writes=[tg])
                R.op("act", lambda e: e.activation(out=tg.ap, in_=tg.ap, func=AF.Ln, bias=onesb.ap), reads=[tg, onesb], writes=[tg])
                R.op("act", lambda e: e.activation(out=tg.ap, in_=tg.ap, func=AF.Exp, scale=-1.0), reads=[tg], writes=[tg])
                R.op("dve", lambda e, s=s: e.scalar_tensor_tensor(out=tg.ap, in0=BG.ap, scalar=s.ap[:, 2:3], in1=tg.ap,
                                                                  op0=ALU.mult, op1=ALU.mult), reads=[BG, s, tg], writes=[tg])
                R.op("pool", lambda e: e.tensor_tensor(out=tg.ap, in0=tg.ap, in1=cf("rnw"), op=ALU.mult), reads=[tg, cstf], writes=[tg])

                if KSTOP == "m_q":
                    raise _Stop()
                kbs = [kb for kb in (c - 1, c, c + 1) if kb >= 0]
                nkb = len(kbs)
                qT3 = v3(qT.ap, 4)
                Obank = [F[4], F[5]]
                it = 0
                for g in range(2):
                    for ee in range(2):
                        bx, by = (F[0], F[1]) if it % 2 == 0 else (F[2], F[3])
                        it += 1
                        regs = [bx.ap[:, 0:256], bx.ap[:, 256:512], by.ap[:, 0:256]]
                        rbuf = [bx, bx, by]
                        for j, kb in enumerate(kbs):
                            KT3 = v3(KT[kb].ap, 2)
                            masked = (kb != c)
                            R.op("pe", lambda e, j=j, KT3=KT3, g=g, ee=ee, masked=masked, regs=regs: e.matmul(
                                v3(regs[j], 2), lhsT=KT3[64 * ee:64 * ee + 64, g, :], rhs=qT3[64 * ee:64 * ee + 64, 2 * g:2 * g + 2, :],
                                start=True, stop=(not masked)), reads=[KT[kb], qT], writes=[rbuf[j]], n=256)
                            if masked:
                                mk = mprev if kb < c else mnext
                                R.op("pe", lambda e, j=j, mk=mk, regs=regs: e.matmul(regs[j], lhsT=ident, rhs=mk, start=False, stop=True),
                                     reads=[cstb], writes=[rbuf[j]], n=256)
                        n1 = min(nkb, 2)
                        R.op("act", lambda e, n1=n1, bx=bx: e.activation(out=PT.ap[:, 0:n1 * 256], in_=bx.ap[:, 0:n1 * 256], func=AF.Exp, scale=0.125),
                             reads=[bx], writes=[PT])
                        if nkb == 3:
                            R.op("act", lambda e, by=by: e.activation(out=PT.ap[:, 512:768], in_=by.ap[:, 0:256], func=AF.Exp, scale=0.125),
                                 reads=[by, PT], writes=[PT])
                        for tt in range(2):
                            h = 2 * (2 * g + tt) + ee
                            ob = Obank[h // 4]
                            hl = h % 4
                            for j, kb in enumerate(kbs):
                                R.op("pe", lambda e, j=j, kb=kb, tt=tt, ob=ob, hl=hl, g=g, nkb=nkb: e.matmul(
                                    ob.ap[:, hl * 65:(hl + 1) * 65], lhsT=PT.ap[:, j * 256 + tt * 128: j * 256 + (tt + 1) * 128],
                                    rhs=v3(Vb[kb].ap, 2)[:, g, :], start=(j == 0), stop=(j == nkb - 1)),
                                    reads=[PT, Vb[kb]], writes=[ob], n=800)
                if KSTOP == "m_att0":
                    raise _Stop()
                for gb in range(2):
                    O3 = Obank[gb].ap[:, 0:260].rearrange("p (h d) -> p h d", h=4)
                    R.op("dve", lambda e, gb=gb, O3=O3: e.tensor_tensor(out=s8d.ap[:, gb * 4:(gb + 1) * 4], in0=O3[:, :, 64],
                                                                        in1=esink.ap[:, gb * 4:(gb + 1) * 4], op=ALU.add),
                         reads=[Obank[gb], esink, s8d], writes=[s8d])
                R.op("dve", lambda e: e.reciprocal(out=s8e.ap, in_=s8d.ap), reads=[s8d], writes=[s8e])
                for gb in range(2):
                    O3 = Obank[gb].ap[:, 0:260].rearrange("p (h d) -> p h d", h=4)
                    rdb = s8e.ap[:, gb * 4:(gb + 1) * 4].unsqueeze(2).broadcast_to([128, 4, 64])
                    R.op("dve", lambda e, gb=gb, O3=O3, rdb=rdb: e.tensor_tensor(out=v3(yb.ap[:, gb * 256:(gb + 1) * 256], 4), in0=O3[:, :, 0:64],
                                                                                 in1=rdb, op=ALU.mult),
                         reads=[Obank[gb], s8e, yb], writes=[yb], n=256)

                if KSTOP == "m_att":
                    raise _Stop()
                rkT3 = v3(rkT.ap, 4)
                rqT3 = v3(rqT.ap, 4)
                rxf3 = v3(rqxf.ap, 4)
                rxb3 = v3(rqxb.ap, 4)
                for ee in range(2):
                    for t in range(4):
                        bank = F[ee]
                        col = t * 128
                        R.op("pe", lambda e, t=t, ee=ee, bank=bank, col=col: e.matmul(
                            bank.ap[:, col:col + 128], lhsT=rkT3[64 * ee:64 * ee + 64, t, :], rhs=rqT3[64 * ee:64 * ee + 64, t, :],
                            start=True, stop=True), reads=[rkT, rqT], writes=[bank], n=128)
                if KSTOP == "m_r0":
                    raise _Stop()
                retS4 = retS.ap.rearrange("p (t e n) -> p t e n", t=4, e=2)
                DT4 = DT.ap.rearrange("p (t e n) -> p t e n", t=4, e=2)
                for gb in range(2):
                    R.op("dve", lambda e, gb=gb, retS4=retS4, DT4=DT4: e.tensor_tensor(out=retS4[:, :, gb, :], in0=v3(F[gb].ap, 4),
                                                                                     in1=DT4[:, :, gb, :], op=ALU.mult),
                         reads=[F[gb], DT, retS], writes=[retS])
                if KSTOP == "m_r1":
                    raise _Stop()
                for h in range(8):
                    t, ee = h // 2, h % 2
                    Rf3s = v3(RfS[c].ap, 4)
                    Rb3s = v3(RbS[c].ap, 4)
                    R.op("pe", lambda e, h=h: e.matmul(F[2].ap[:, h * 64:(h + 1) * 64], lhsT=retS.ap[:, h * 128:(h + 1) * 128],
                                                       rhs=rvb.ap[:, h * 64:(h + 1) * 64], start=True, stop=False),
                         reads=[retS, rvb], writes=[F[2]], n=200)
                    R.op("pe", lambda e, h=h, t=t, ee=ee, Rf3s=Rf3s: e.matmul(F[2].ap[:, h * 64:(h + 1) * 64], lhsT=rxf3[64 * ee:64 * ee + 64, t, :],
                                                                             rhs=Rf3s[64 * ee:64 * ee + 64, t, :], start=False, stop=False),
                         reads=[rqxf, RfS[c]], writes=[F[2]], n=100)
                    R.op("pe", lambda e, h=h, t=t, ee=ee, Rb3s=Rb3s: e.matmul(F[2].ap[:, h * 64:(h + 1) * 64], lhsT=rxb3[64 * ee:64 * ee + 64, t, :],
                                                                             rhs=Rb3s[64 * ee:64 * ee + 64, t, :], start=False, stop=True),
                         reads=[rqxb, RbS[c]], writes=[F[2]], n=100)
                if KSTOP == "m_r2":
                    raise _Stop()
                R.op("act", lambda e: e.activation(out=tf.ap, in_=F[2].ap, func=AF.Square), reads=[F[2]], writes=[tf])
                R.op("dve", lambda e: e.tensor_reduce(out=s8f.ap, in_=v3(tf.ap, 8), axis=AX.X, op=ALU.add), reads=[tf], writes=[s8f])
                R.op("act", lambda e: e.activation(out=s8g.ap, in_=s8f.ap, func=AF.Ln, scale=1.0 / 64, bias=epsb.ap),
                     reads=[s8f, epsb], writes=[s8g])
                R.op("act", lambda e: e.activation(out=s8h.ap, in_=s8g.ap, func=AF.Exp, scale=-0.5), reads=[s8g], writes=[s8h])
                R.op("dve", lambda e: e.tensor_tensor(out=tf.ap, in0=F[2].ap, in1=tg.ap, op=ALU.mult), reads=[F[2], tg, s8f], writes=[tf])
                rr8b = s8h.ap.unsqueeze(2).broadcast_to([128, 8, 64])
                R.op("dve", lambda e, rr8b=rr8b: e.tensor_tensor(out=v3(yb.ap[:, 512:1024], 8), in0=v3(tf.ap, 8), in1=rr8b, op=ALU.mult),
                     reads=[tf, s8h, yb], writes=[yb], n=512)

                if KSTOP == "m_ret":
                    raise _Stop()
                for k in range(8):
                    R.op("pe", lambda e, k=k: e.transpose(out=T0b[:, k * 128:(k + 1) * 128], in_=yb.ap[:, k * 128:(k + 1) * 128],
                                                          identity=ident), reads=[yb, cstb], writes=[T0], n=128)
                R.op("act", lambda e: e.activation(out=yT.ap, in_=T0b, func=AF.Identity), reads=[T0], writes=[yT])
                yT3 = v3(yT.ap, 8)
                for half in range(2):
                    bank = F[3] if half == 0 else F[5]
                    for k in range(8):
                        R.op("pe", lambda e, k=k, half=half, bank=bank: e.matmul(bank.ap, lhsT=yT3[:, k, :],
                                                                                  rhs=W_out.ap[:, k, half * 512:(half + 1) * 512],
                                                                                  start=(k == 0), stop=(k == 7)),
                             reads=[yT, W_out], writes=[bank])
                    R.op("dve", lambda e, half=half, bank=bank, xbuf=xbuf: e.tensor_tensor(out=xbuf.ap[:, half * 512:(half + 1) * 512],
                                                                                           in0=xbuf.ap[:, half * 512:(half + 1) * 512],
                                                                                           in1=bank.ap, op=ALU.add),
                         reads=[bank, xbuf], writes=[xbuf], n=512)
                R.op("act", lambda e, c=c, xbuf=xbuf: e.activation(out=tf.ap.bitcast(BF16), in_=xbuf.ap, func=AF.Square, accum_out=ssq2.ap[:, c:c + 1]),
                     reads=[xbuf, ssq2], writes=[tf, ssq2], n=1024)
                R.op("act", lambda e, c=c: e.activation(out=lnr2.ap[:, c:c + 1], in_=ssq2.ap[:, c:c + 1], func=AF.Ln, scale=1.0 / D_MODEL, bias=epsb.ap),
                     reads=[ssq2, epsb, lnr2], writes=[lnr2])
                R.op("act", lambda e, c=c: e.activation(out=rstd2.ap[:, c:c + 1], in_=lnr2.ap[:, c:c + 1], func=AF.Exp, scale=-0.5),
                     reads=[lnr2, rstd2], writes=[rstd2])

            if KSTOP == "main":
                raise _Stop()
            R.phase = "ffn"
            fences = []
            f_ = R.op("dve", lambda e: e.tensor_copy(out=fsc.ap[:, 0:1], in_=epsb.ap), reads=[epsb], writes=[Buf("fscD", fsc.ap[:, 0:1])])
            fences.append(f_)
            f_ = R.op("act", lambda e: e.activation(out=fsc.ap[:, 1:2], in_=epsb.ap, func=AF.Identity), reads=[epsb], writes=[Buf("fscA", fsc.ap[:, 1:2])])
            fences.append(f_)
            f_ = R.op("pool", lambda e: e.memset(fsc.ap[:, 2:3], 0.0), writes=[Buf("fscP", fsc.ap[:, 2:3])])
            fences.append(f_)
            f_ = R.op("pe", lambda e: e.matmul(T0.ap[:, 0:1], lhsT=ident, rhs=ident[:, 0:1], start=True, stop=True),
                      reads=[cstb], writes=[T0], n=1)
            fences.append(f_)
            for f_ in fences:
                f_.fence = True
            R.seg = 1
            bar = fences

            def pbuf(name, ap):
                b = Buf(name, ap)
                b.readers = list(bar)
                return b

            AB.off = 0
            AFa.off = 0

            def pb_alloc(arena, name, n):
                b = arena.alloc(name, n)
                b.readers = list(bar)
                return b

            mTg = [pb_alloc(AB, f"mTg{g_}", 4096) for g_ in range(4)]
            mT = []
            for c in range(NT_OWN):
                b_ = Buf(f"mT{c}", v3(mTg[c // 4].ap, 8)[:, :, (c % 4) * 128:(c % 4 + 1) * 128])
                b_.readers = list(bar)
                mT.append(b_)
            hTb = [[pb_alloc(AB, f"hT{s_}_{ci}", 512) for ci in range(4)] for s_ in range(2)]
            mb = pb_alloc(AB, "mb", 1024)
            junk2 = pb_alloc(AB, "junk2", 1024)
            identB = pb_alloc(AB, "identB", 128)
            fnw = pb_alloc(AFa, "fnw", 1024)
            sgt = [pb_alloc(AFa, f"sgt{i}", 512) for i in range(2)]
            st2 = pb_alloc(AFa, "st2", 8)
            T0f = pbuf("T0f", banks[6][:, :])
            T1f = pbuf("T1f", banks[7][:, :])
            slots = [pbuf(f"slot{i}", Wt[:, i * 12288:(i + 1) * 12288]) for i in range(2)]
            early = []
            for wb in Wg_buf.values():
                early.extend(wb.readers)
                early.extend(wb.writers)
            slots[0].readers = early

            R.dma("sp", fnw.ap, fnw_d, fnw, writes=[fnw])
            R.dma("pool", identB.ap, cstb_d[:, 0:128], identB, writes=[identB])

            passes = [(0, 4), (4, 4), (8, 4), (12, 4), (16, 3), (19, 3)]
            Wg_v = w_gate_d.rearrange("(k p) n -> p k n", p=128)
            Wu_v = w_up_d.rearrange("(k p) n -> p k n", p=128)

            def load_pass(r):
                f0, C = passes[r]
                sl = slots[r % 2]
                g3 = v3(sl.ap[:, 0:4096], 8)
                u3 = v3(sl.ap[:, 4096:8192], 8)
                d3 = v3(sl.ap[:, 8192:12288], 4)
                R.dma("pool", g3[:, :, 0:C * 128], Wg_v[:, :, f0 * 128:(f0 + C) * 128], sl, writes=[sl])
                R.dma("pool", u3[:, :, 0:C * 128], Wu_v[:, :, f0 * 128:(f0 + C) * 128], sl, writes=[sl])
                R.dma("pool", d3[:, 0:C, :], w_down_d[f0 * 128:(f0 + C) * 128, :].rearrange("(c p) n -> p c n", p=128), sl, writes=[sl])

            load_pass(0)
            load_pass(1)

            def prologue(tgi):
                for t in range(4):
                    c = tgi * 4 + t
                    R.op("dve", lambda e, c=c: e.scalar_tensor_tensor(out=mb.ap, in0=hbuf[c].ap, scalar=rstd2.ap[:, c:c + 1], in1=fnw.ap,
                                                                      op0=ALU.mult, op1=ALU.mult), reads=[hbuf[c], rstd2, fnw], writes=[mb])
                    for k in range(8):
                        R.op("pe", lambda e, k=k: e.transpose(out=T0b[:, k * 128:(k + 1) * 128], in_=mb.ap[:, k * 128:(k + 1) * 128],
                                                              identity=identB.ap), reads=[mb, identB], writes=[T0f], n=128)
                    R.op("act", lambda e, c=c: e.activation(out=mT[c].ap, in_=v3(T0b, 8), func=AF.Identity), reads=[T0f], writes=[mT[c]])

            prologue(0)
            gi = 0
            for r, (f0, C) in enumerate(passes):
                sl = slots[r % 2]
                g3 = v3(sl.ap[:, 0:4096], 8)
                u3 = v3(sl.ap[:, 4096:8192], 8)
                d3 = v3(sl.ap[:, 8192:12288], 4)
                last = (r == len(passes) - 1)
                for tgi in range(4):
                    if r == 0 and tgi + 1 < 4:
                        prologue(tgi + 1)
                    hs = hTb[tgi % 2]
                    for ci in range(C):
                        gbank = F[0] if gi % 2 == 0 else F[1]
                        ubank = F[2] if gi % 2 == 0 else F[3]
                        sg_ = sgt[gi % 2]
                        gi += 1
                        mg3 = v3(mTg[tgi].ap, 8)
                        for (bank, w3) in ((gbank, g3), (ubank, u3)):
                            for k in range(8):
                                R.op("pe", lambda e, bank=bank, w3=w3, ci=ci, k=k, mg3=mg3: e.matmul(
                                    bank.ap, lhsT=w3[:, k, ci * 128:(ci + 1) * 128],
                                    rhs=mg3[:, k, :], start=(k == 0), stop=(k == 7)),
                                    reads=[sl] + mT[tgi * 4:tgi * 4 + 4], writes=[bank])
                        R.op("act", lambda e, gbank=gbank, sg_=sg_: e.activation(out=sg_.ap, in_=gbank.ap, func=AF.Silu),
                             reads=[gbank], writes=[sg_])
                        R.op("dve", lambda e, ubank=ubank, sg_=sg_, hs=hs, ci=ci: e.tensor_tensor(out=hs[ci].ap, in0=sg_.ap, in1=ubank.ap, op=ALU.mult),
                             reads=[sg_, ubank], writes=[hs[ci]])
                    for t in range(4):
                        c = tgi * 4 + t
                        dbanks = (F[4], F[5]) if t % 2 == 0 else (T1f, T0f)
                        for half in range(2):
                            bank = dbanks[half]
                            for ci in range(C):
                                R.op("pe", lambda e, bank=bank, ci=ci, t=t, half=half, hs=hs, C=C, d3=d3: e.matmul(
                                    bank.ap, lhsT=hs[ci].ap[:, t * 128:(t + 1) * 128], rhs=d3[:, ci, half * 512:(half + 1) * 512],
                                    start=(ci == 0), stop=(ci == C - 1)), reads=[hs[ci], sl], writes=[bank])
                            R.op("dve", lambda e, bank=bank, c=c, half=half: e.tensor_tensor(
                                out=hbuf[c].ap[:, half * 512:(half + 1) * 512], in0=hbuf[c].ap[:, half * 512:(half + 1) * 512],
                                in1=bank.ap, op=ALU.add), reads=[bank, hbuf[c]], writes=[hbuf[c]])
                        if last:
                            R.dma("sp", out_d[c * 128:(c + 1) * 128, :], hbuf[c].ap, hbuf[c], reads=[hbuf[c]], final=True)
                if r + 2 < len(passes):
                    load_pass(r + 2)

        try:
            _record()
        except _Stop:
            pass

        R.finalize()
        _NC_CACHE['R'] = R
        print('sched sim total us:', getattr(R, 'sim_total', None))

        @block.sync
        def _(eng):
            R.emit("sp", eng)

        @block.gpsimd
        def _(eng):
            R.emit("pool", eng)

        @block.scalar
        def _(eng):
            R.emit("act", eng)

        @block.vector
        def _(eng):
            R.emit("dve", eng)

        @block.tensor
        def _(eng):
            R.emit("pe", eng)
    return nc


_NC_CACHE = {}


def _rope_tables(hf):
    l = np.arange(SEQ, dtype=np.float32)
    pos = l if hf == 0 else (np.float32(SEQ - 1) - l)
    inv_freq = (np.float32(10000.0) ** (-(np.arange(0, 64, 2, dtype=np.float32)) / np.float32(64))).astype(np.float32)
    ang = (pos[:, None] * inv_freq[None, :]).astype(np.float32)
    cos = np.cos(ang.astype(np.float64)).astype(np.float32)
    sin = np.sin(ang.astype(np.float64)).astype(np.float32)
    cs = np.concatenate([cos, cos, -sin, sin], axis=1)
    return np.ascontiguousarray(cs, dtype=np.float32)


def _const_tables():
    i = np.arange(128, dtype=np.float32)
    iota1 = np.tile((i + 1.0)[None, :], (128, 1))
    iota2 = np.tile((128.0 - i)[None, :], (128, 1))
    m = i[:, None]
    n = i[None, :]
    reluP = np.maximum(n - m, 0.0)
    reluN = np.maximum(m - n, 0.0)
    csts = np.concatenate([iota1, iota2, reluP, reluN], axis=1).astype(np.float32)
    ident = np.eye(128, dtype=np.float32)
    j = i[:, None]
    q = i[None, :]
    mprev = np.where(j >= q, 0.0, -30000.0).astype(np.float32)
    mnext = np.where(j <= q, 0.0, -30000.0).astype(np.float32)
    cstb = np.concatenate([ident, mprev, mprev, mnext, mnext], axis=1).astype(np.float32)
    return np.ascontiguousarray(csts), np.ascontiguousarray(cstb)


def _make_in_maps(x, attn_norm_w, w_in, q_norm_w, k_norm_w, attn_sink, ret_log_decay_fwd, ret_log_decay_bwd,
           ret_norm_w, w_out, ffn_norm_w, w_gate, w_up, w_down):
    x = np.asarray(x, dtype=np.float32)
    f = lambda a: np.asarray(a, dtype=np.float32)
    attn_norm_w, q_norm_w, k_norm_w, attn_sink = f(attn_norm_w)[0], f(q_norm_w)[0], f(k_norm_w)[0], f(attn_sink)[0]
    dfw, dbw = f(ret_log_decay_fwd)[0], f(ret_log_decay_bwd)[0]
    ret_norm_w, ffn_norm_w = f(ret_norm_w)[0], f(ffn_norm_w)[0]
    w_in_, w_out_, w_gate_, w_up_, w_down_ = (np.ascontiguousarray(f(a)[0]) for a in (w_in, w_out, w_gate, w_up, w_down))

    csts, cstb = _const_tables()
    fnw = np.ascontiguousarray(np.tile(ffn_norm_w[None, :], (128, 1)))
    rope_tabs = [_rope_tables(0), _rope_tables(1)]
    swap = lambda w: np.concatenate([w[32:], w[:32]])
    in_maps = []
    for c in range(8):
        b, hf = c // 2, c % 2
        xs = x[b] if hf == 0 else x[b, ::-1]
        dF, dB = (dfw, dbw) if hf == 0 else (dbw, dfw)
        cstf = np.zeros((128, NCF), dtype=np.float32)

        def put(name, row):
            a, bnd = CF[name]
            cstf[:, a:bnd] = row[None, :]
        a, bnd = CF["anwP"]
        cstf[:, a:bnd] = attn_norm_w.reshape(8, 128).T
        put("rnw", ret_norm_w)
        put("qnw", q_norm_w)
        put("qnws", swap(q_norm_w))
        put("knw", k_norm_w)
        put("knws", swap(k_norm_w))
        put("sink", attn_sink)
        put("decR", np.concatenate([dF, dB]))
        a, bnd = CF["decP"]
        for t in range(4):
            cstf[0:64, a + t] = dF[2 * t]
            cstf[64:128, a + t] = dF[2 * t + 1]
            cstf[0:64, a + 4 + t] = dB[2 * t]
            cstf[64:128, a + 4 + t] = dB[2 * t + 1]
        a, _b = CF["c127m"]
        cstf[:, a] = 127.0 - np.arange(128, dtype=np.float32)
        a, _b = CF["cm"]
        cstf[:, a] = np.arange(128, dtype=np.float32)
        in_maps.append({
            "xs": np.ascontiguousarray(xs), "cs": rope_tabs[hf], "cstf": cstf, "csts": csts, "cstb": cstb, "fnw": fnw,
            "w_in": w_in_, "w_out": w_out_, "w_gate": w_gate_, "w_up": w_up_, "w_down": w_down_,
        })
    return in_maps


def kernel(**inputs):
    in_maps = _make_in_maps(**inputs)
    if "nc" not in _NC_CACHE:
        _NC_CACHE["nc"] = build_program()
    nc = _NC_CACHE["nc"]
    res = run_bass_kernel_spmd(nc, in_maps, core_ids=list(range(8)))
    out = np.empty((4, SEQ, D_MODEL), dtype=np.float32)
    for c in range(8):
        b, hf = c // 2, c % 2
        o = np.asarray(res.results[c]["out"], dtype=np.float32)
        if hf == 0:
            out[b, 0:2048] = o
        else:
            out[b, 2048:4096] = o[::-1]
    return out
```

```python
import contextlib
import os
import sys
import numpy as np
import concourse.bass as bass
import concourse.mybir as mybir
from concourse.bass_utils import run_bass_kernel_spmd

F32 = mybir.dt.float32
BF16 = mybir.dt.bfloat16
AF = mybir.ActivationFunctionType
ALU = mybir.AluOpType
AX = mybir.AxisListType

D_MODEL = 1024
SEQ = 4096
NT_ALL = 32
NT_OWN = 16
IN_PROJ = 2816
D_FF = 2816
EPS = 1e-6
C_AQ, C_AK, C_AV, C_RQ, C_RK, C_RV, C_RG = 0, 512, 640, 768, 1280, 1792, 2304

CF = {}
_o = 0
for _n, _w in [("anwP", 8), ("rnw", 512), ("qnw", 64), ("qnws", 64), ("knw", 64), ("knws", 64),
               ("sink", 8), ("decP", 8), ("decR", 16), ("c127m", 1), ("cm", 1)]:
    CF[_n] = (_o, _o + _w)
    _o += _w
NCF = _o
NCB = 128 + 256 + 256


class Op:
    __slots__ = ("eng", "fn", "deps", "signal", "sigval", "dma", "idx", "cost", "seg", "fence", "lat", "fin", "ph", "line", "crit", "estsrc", "prio")


class DmaTok:
    __slots__ = ("sem", "val", "op")

    def __init__(self, sem, val, op):
        self.sem = sem
        self.val = val
        self.op = op


class Buf:
    def __init__(self, name, ap=None, share=None):
        self.name = name
        self.ap = ap
        self._w = []
        self._r = []
        self.share = share
        self.sem = None
        self.cnt = 0

    @property
    def writers(self):
        return (self.share or self)._w

    @writers.setter
    def writers(self, v):
        (self.share or self)._w = v

    @property
    def readers(self):
        return (self.share or self)._r

    @readers.setter
    def readers(self, v):
        (self.share or self)._r = v

    def nfree(self):
        if self.ap is None:
            return 512
        n = 1
        for d in self.ap.shape[1:]:
            n *= d
        return n


_COST = {"dve": (0.10, 1.0 / 870), "act": (0.17, 1.0 / 1200), "pool": (0.15, 1.0 / 450), "pe": (0.02, 1.0 / 2600), "sp": (0.05, 0.0)}


class Rec:
    ENGS = ("pe", "act", "dve", "pool", "sp")

    def __init__(self, sems):
        self.lists = {e: [] for e in self.ENGS}
        self.free_sems = list(sems)
        self.engsem = {e: self.free_sems.pop() for e in ("pe", "act", "dve", "pool")}
        self.final = []
        self.nops = 0
        self.seg = 0

    def _deps(self, reads, writes, extra):
        deps = []
        for b in reads:
            for t in b.writers:
                deps.append((t, "raw"))
        for b in writes:
            for t in b.readers:
                deps.append((t, "war"))
            for t in b.writers:
                deps.append((t, "waw"))
        for t in extra:
            deps.append((t, "raw"))
        return deps

    def _new(self, eng, fn, deps, dma, cost):
        o = Op()
        o.eng = eng
        o.fn = fn
        o.deps = deps
        o.signal = False
        o.sigval = None
        o.dma = dma
        o.idx = self.nops
        self.nops += 1
        o.cost = cost
        o.seg = self.seg
        o.ph = getattr(self, "phase", "setup")
        o.line = sys._getframe(2).f_lineno if os.environ.get("KSCHEDSTAT", "") else 0
        o.fence = False
        o.lat = 0.0
        o.fin = 0.0
        o.crit = None
        o.estsrc = None
        o.prio = getattr(self, "prio", 0)
        self.lists[eng].append(o)
        return o

    def op(self, eng, fn, reads=(), writes=(), extra=(), n=None):
        if n is None:
            n = writes[0].nfree() if writes else 64
        a, b = _COST[eng]
        o = self._new(eng, fn, self._deps(reads, writes, extra), False, a + b * n)
        for bf in reads:
            bf.readers.append(o)
        for bf in writes:
            bf.writers = [o]
            bf.readers = []
        return o

    def dma(self, q, out_ap, in_ap, semowner, reads=(), writes=(), extra=(), final=False, order_after=()):
        if semowner.sem is None:
            semowner.sem = self.free_sems.pop()
        sem = semowner.sem
        semowner.cnt += 1
        fn = lambda e, out_ap=out_ap, in_ap=in_ap, sem=sem: e.dma_start(out=out_ap, in_=in_ap).then_inc(sem, 16)
        deps_ = self._deps(reads, writes, extra)
        for t_ in order_after:
            deps_.append((t_.op, "order"))
        o = self._new(q, fn, deps_, True, 0.06 if q == "sp" else 1.0)
        nel = 1
        for d in out_ap.shape:
            nel *= d
        o.lat = 2.0 + nel * 4 / 180e3
        tok = DmaTok(sem, semowner.cnt * 16, o)
        for bf in reads:
            bf.readers.append(tok)
        for bf in writes:
            bf.writers = [tok]
            bf.readers = []
        if final:
            self.final.append(tok)
        return tok

    def schedule(self):
        allops = []
        for e in self.ENGS:
            allops.extend(self.lists[e])
        allops.sort(key=lambda o: o.idx)
        preds = {}
        succs = {}
        for o in allops:
            ps = {}
            for (t, kind) in o.deps:
                if isinstance(t, DmaTok):
                    ps[id(t.op)] = (t.op, True)
                else:
                    if id(t) not in ps:
                        ps[id(t)] = (t, False)
            preds[id(o)] = list(ps.values())
            for (p, _) in ps.values():
                succs.setdefault(id(p), []).append(o)
        bl = {}
        for o in reversed(allops):
            m = 0.0
            for sc in succs.get(id(o), ()):
                m = max(m, bl[id(sc)] + (0.3 if sc.eng != o.eng else 0.03))
            bl[id(o)] = o.cost + m
        use_bl = os.environ.get("KBL", "1") == "1"
        slack = float(os.environ.get("KSLACK", "0.05"))
        free = {e: 0.0 for e in self.ENGS}
        neworder = {e: [] for e in self.ENGS}
        nseg = max(o.seg for o in allops) + 1
        for sg in range(nseg):
            ops = [o for o in allops if o.seg == sg]
            inseg = set(id(o) for o in ops)
            indeg = {}
            est = {}
            avail = {e: [] for e in self.ENGS}
            remaining = {e: 0 for e in self.ENGS}
            for o in ops:
                remaining[o.eng] += 1
                d = 0
                t0 = 0.0
                for (p, isdma) in preds[id(o)]:
                    if id(p) in inseg:
                        d += 1
                    else:
                        t0 = max(t0, p.fin + (p.lat if isdma else 0.3))
                indeg[id(o)] = d
                est[id(o)] = t0
                if d == 0:
                    avail[o.eng].append(o)
            nleft = len(ops)
            while nleft:
                best = None
                for e in self.ENGS:
                    cands = []
                    tmin = None
                    for o in avail[e]:
                        if o.fence and remaining[e] > 1:
                            continue
                        st = max(free[e], est[id(o)])
                        cands.append((st, o))
                        if tmin is None or st < tmin:
                            tmin = st
                    if not cands:
                        continue
                    if use_bl:
                        st, o = max(((st, o) for (st, o) in cands if st <= tmin + slack), key=lambda x: (bl[id(x[1])], -x[1].idx))
                    else:
                        st, o = min(cands, key=lambda x: (x[0], x[1].idx))
                    key = (st, o.idx)
                    if best is None or key < best[0]:
                        best = (key, o)
                assert best is not None, "scheduler stuck"
                o = best[1]
                st = best[0][0]
                e = o.eng
                avail[e].remove(o)
                remaining[e] -= 1
                nleft -= 1
                o.fin = st + o.cost
                o.crit = (neworder[e][-1] if (neworder[e] and free[e] >= est[id(o)]) else o.estsrc)
                free[e] = o.fin
                neworder[e].append(o)
                for sc in succs.get(id(o), ()):
                    if id(sc) not in inseg:
                        continue
                    isdma = any((p is o and dm) for (p, dm) in preds[id(sc)])
                    lat = o.lat if isdma else (0.3 if sc.eng != e else 0.03)
                    if o.fin + lat > est[id(sc)]:
                        est[id(sc)] = o.fin + lat
                        sc.estsrc = o
                    indeg[id(sc)] -= 1
                    if indeg[id(sc)] == 0:
                        avail[sc.eng].append(sc)
        self.lists = neworder
        self.sim_total = max(free.values())
        if os.environ.get("KSCHEDSTAT", "") == "1":
            stat = {}
            for o in allops:
                d = stat.setdefault(o.ph, {"end": 0.0, "start": 1e18})
                d["end"] = max(d["end"], o.fin)
                d["start"] = min(d["start"], o.fin - o.cost)
                d[o.eng] = d.get(o.eng, 0.0) + o.cost
            for ph, d in stat.items():
                print("SCHED", ph, {k: round(v, 1) for k, v in d.items()})
            cp = os.environ.get("KSCHEDCRIT", "")
            if cp:
                ph_, n_ = cp.split(",")
                tmax_ = float(os.environ.get("KSCHEDTMAX", "1e18"))
                cur = max((o for o in allops if o.ph == ph_ and o.fin <= tmax_), key=lambda o: o.fin)
                chain = []
                while cur is not None and len(chain) < int(n_):
                    chain.append(cur)
                    cur = cur.crit
                for o in reversed(chain):
                    print("CRIT %8.2f %6.2f %-4s L%d%s" % (o.fin - o.cost, o.cost, o.eng, o.line, " dma" if o.dma else ""))
            w = os.environ.get("KSCHEDWIN", "")
            if w:
                a_, b_ = (float(x) for x in w.split(","))
                sel = [o for o in allops if a_ <= o.fin - o.cost < b_]
                sel.sort(key=lambda o: o.fin - o.cost)
                for o in sel:
                    print("OP %8.2f %6.2f %-4s L%d%s" % (o.fin - o.cost, o.cost, o.eng, o.line, " dma" if o.dma else ""))

    def finalize(self):
        if os.environ.get("KNOSCHED", "") != "1":
            self.schedule()
        for e in self.ENGS:
            for o in self.lists[e]:
                for (t, kind) in o.deps:
                    if kind == "order":
                        continue
                    if isinstance(t, Op):
                        if t.eng != o.eng or o.dma or o.eng in ("act", "dve", "pool"):
                            t.signal = True
        for e in self.ENGS:
            cnt = 0
            for o in self.lists[e]:
                if o.signal:
                    cnt += 1
                    o.sigval = cnt

    def emit(self, e, eng):
        waited = {}
        for o in self.lists[e]:
            need = {}
            for (t, kind) in o.deps:
                if kind == "order":
                    continue
                if isinstance(t, DmaTok):
                    key, val = t.sem, t.val
                else:
                    if t.eng == e and not (o.dma or e in ("act", "dve", "pool")):
                        continue
                    key, val = self.engsem[t.eng], t.sigval
                if need.get(key, (None, 0))[1] < val:
                    need[key] = (key, val)
            for key, val in need.values():
                if waited.get(key, 0) < val:
                    eng.wait_ge(key, val)
                    waited[key] = val
            ins = o.fn(eng)
            if o.signal:
                ins.then_inc(self.engsem[e], 1)
        if e == "sp":
            for t in self.final:
                if waited.get(t.sem, 0) < t.val:
                    eng.wait_ge(t.sem, t.val)
                    waited[t.sem] = t.val


def v3(ap, a):
    return ap.rearrange("p (a b) -> p a b", a=a)


class _Stop(Exception):
    pass


def build_program():
    nc = bass.Bass("TRN2", target_bir_lowering=False)
    KSTOP = os.environ.get("KSTOP", "")
    xs = nc.dram_tensor("xs", [SEQ, D_MODEL], F32, kind="ExternalInput").ap()
    cs_d = nc.dram_tensor("cs", [SEQ, 128], F32, kind="ExternalInput").ap()
    cstf_d = nc.dram_tensor("cstf", [128, NCF], F32, kind="ExternalInput").ap()
    csts_d = nc.dram_tensor("csts", [128, 512], F32, kind="ExternalInput").ap()
    cstb_d = nc.dram_tensor("cstb", [128, NCB], F32, kind="ExternalInput").ap()
    fnw_d = nc.dram_tensor("fnw", [128, D_MODEL], F32, kind="ExternalInput").ap()
    w_in_d = nc.dram_tensor("w_in", [D_MODEL, IN_PROJ], F32, kind="ExternalInput").ap()
    w_out_d = nc.dram_tensor("w_out", [D_MODEL, D_MODEL], F32, kind="ExternalInput").ap()
    w_gate_d = nc.dram_tensor("w_gate", [D_MODEL, D_FF], F32, kind="ExternalInput").ap()
    w_up_d = nc.dram_tensor("w_up", [D_MODEL, D_FF], F32, kind="ExternalInput").ap()
    w_down_d = nc.dram_tensor("w_down", [D_FF, D_MODEL], F32, kind="ExternalInput").ap()
    out_d = nc.dram_tensor("out", [NT_OWN * 128, D_MODEL], F32, kind="ExternalOutput").ap()

    NB16 = 26148
    NF32 = 7872
    with contextlib.ExitStack() as es:
        Wt = es.enter_context(nc.sbuf_tensor("Wt", [128, 30720], BF16))
        Ht = es.enter_context(nc.sbuf_tensor("Ht", [128, NT_OWN * 1024], F32))
        ABt = es.enter_context(nc.sbuf_tensor("ABt", [128, NB16], BF16))
        AFt = es.enter_context(nc.sbuf_tensor("AFt", [128, NF32], F32))
        banks = [es.enter_context(nc.psum_tensor(f"pb{i}", [128, 512], F32)) for i in range(8)]
        sems = [es.enter_context(nc.semaphore(f"s{i}")) for i in range(60)]
        block = es.enter_context(nc.Block())
        R = Rec(sems)

        class Arena:
            def __init__(self, t, n):
                self.t, self.n, self.off = t, n, 0

            def alloc(self, name, n):
                assert self.off + n <= self.n, (name, self.off, n, self.n)
                ap = self.t[:, self.off:self.off + n]
                self.off += n
                return Buf(name, ap)

        AB = Arena(ABt, NB16)
        AFa = Arena(AFt, NF32)

        W_in = Buf("w_in", v3(Wt[:, 0:22528], 8))
        W_out = Buf("w_out", v3(Wt[:, 22528:30720], 8))
        hbuf = [Buf(f"h{c}", Ht[:, c * 1024:(c + 1) * 1024]) for c in range(NT_OWN)]
        kvf = [Buf(f"kvf{c}", v3(Ht[:, 12 * 1024 + c * 256: 12 * 1024 + (c + 1) * 256], 4)) for c in range(NT_OWN)]

        F = [Buf(f"F{i}", banks[i][:, :]) for i in range(6)]
        T0 = Buf("T0", banks[6][:, :])
        T1a = Buf("T1a", banks[7][:, 0:256])
        T1b = Buf("T1b", banks[7][:, 256:512], share=T1a)
        T0b = banks[6][:, :].bitcast(BF16)
        T1ab = banks[7][:, 0:256].bitcast(BF16)
        T1bb = banks[7][:, 256:512].bitcast(BF16)

        cstb = AB.alloc("cstb", NCB)
        ident = cstb.ap[:, 0:128]
        mprev = cstb.ap[:, 128:384]
        mnext = cstb.ap[:, 384:640]
        RbS = [AB.alloc(f"RbS{c}", 256) for c in range(NT_OWN)]
        RfS = [AB.alloc(f"RfS{c}", 256) for c in range(NT_OWN)]
        KT = [AB.alloc(f"KT{c}", 256) for c in range(NT_OWN + 1)]
        Vb = [AB.alloc(f"V{c}", 130) for c in range(NT_OWN + 1)]
        xb = AB.alloc("xb", 1024)
        xT = AB.alloc("xT", 1024)
        qb = AB.alloc("qb", 512)
        rqb = AB.alloc("rqb", 512)
        rkb = AB.alloc("rkb", 512)
        rvb = AB.alloc("rvb", 512)
        kdup = AB.alloc("kdup", 256)
        qT = AB.alloc("qT", 512)
        rqT = AB.alloc("rqT", 512)
        rqxf = AB.alloc("rqxf", 512)
        rqxb = AB.alloc("rqxb", 512)
        rkT = AB.alloc("rkT", 512)
        PT = AB.alloc("PT", 768)
        retS = AB.alloc("retS", 1024)
        yb = AB.alloc("yb", 1024)
        yT = AB.alloc("yT", 1024)
        kzb, kzf = qb, rqb

        cstf = AFa.alloc("cstf", NCF)

        def cf(name):
            a, b = CF[name]
            return cstf.ap[:, a:b]

        Xif = AFa.alloc("Xif", 512)
        Xib = AFa.alloc("Xib", 512)
        csl = [AFa.alloc(f"cs{i}", 128) for i in range(2)]
        tabq = AFa.alloc("tabq", 128)
        ta = AFa.alloc("ta", 512)
        DT = AFa.alloc("DT", 1024)
        td = AFa.alloc("td", 512)
        te = AFa.alloc("te", 512)
        tg = AFa.alloc("tg", 512)
        tf = AFa.alloc("tf", 512)
        akf = Buf("akf", tf.ap[:, 0:128])
        tk1 = Buf("tk1", tf.ap[:, 128:256])
        tk2 = Buf("tk2", tf.ap[:, 256:384])
        tabk = Buf("tabk", tf.ap[:, 384:512])
        Rb = AFa.alloc("Rb", 256)
        Rf = AFa.alloc("Rf", 256)
        rtmp = AFa.alloc("rtmp", 256)
        lgP = AFa.alloc("lgP", 8)
        lgR = AFa.alloc("lgR", 16)
        g128 = AFa.alloc("g128", 8)
        Zfb = AFa.alloc("Zfb", 16)
        esink = AFa.alloc("esink", 8)
        st = [AFa.alloc(f"st{i}", 8) for i in range(2)]
        s8a = AFa.alloc("s8a", 8)
        s8g = AFa.alloc("s8g", 8)
        s8h = AFa.alloc("s8h", 8)
        tb2 = AFa.alloc("tb2", 512)
        junk = AFa.alloc("junk", 512)
        junk_ap = junk.ap.bitcast(BF16)
        s8b = AFa.alloc("s8b", 8)
        s8c = AFa.alloc("s8c", 8)
        s8d = AFa.alloc("s8d", 8)
        s8e = AFa.alloc("s8e", 8)
        s8f = AFa.alloc("s8f", 8)
        epsb = AFa.alloc("epsb", 1)
        onesb = AFa.alloc("onesb", 1)
        fsc = AFa.alloc("fsc", 4)
        ssq2 = AFa.alloc("ssq2", 16)
        lnr2 = AFa.alloc("lnr2", 16)
        rstd2 = AFa.alloc("rstd2", 16)

        print("arena use: bf16", AB.off, "/", NB16, " f32", AFa.off, "/", NF32)
        W_in_v = w_in_d.rearrange("(k p) n -> p k n", p=128)
        W_out_v = w_out_d.rearrange("(k p) n -> p k n", p=128)

        def _record():
            R.dma("sp", cstf.ap, cstf_d, cstf, writes=[cstf])
            R.dma("sp", ta.ap, csts_d, ta, writes=[ta])
            R.dma("pool", cstb.ap, cstb_d, cstb, writes=[cstb])
            wcols = [("rkrv", C_RK, C_RG), ("akav", C_AK, C_RQ), ("aq", C_AQ, C_AK), ("rq", C_RQ, C_RK), ("rg", C_RG, IN_PROJ)]
            Wg_buf = {}
            prev_w = None
            for name, a, b in wcols:
                bb = Buf("w_in_" + name)
                Wg_buf[name] = bb
                ex = [prev_w] if prev_w else []
                tk_ = None
                for k in range(8):
                    tk_ = R.dma("pool", W_in.ap[:, k, a:b], W_in_v[:, k, a:b], bb, writes=([bb] if k == 0 else []), extra=ex,
                                order_after=([tk_] if tk_ else []))
                bb.writers = [tk_]
                prev_w = tk_
            tk_ = None
            for k in range(8):
                tk_ = R.dma("pool", W_out.ap[:, k, :], W_out_v[:, k, :], W_out, writes=([W_out] if k == 0 else []), extra=[prev_w],
                            order_after=([tk_] if tk_ else []))
            W_out.writers = [tk_]
            for name, a, b in wcols:
                for k in range(8):
                    R.op("dve", lambda e, k=k, a=a, b=b: e.tensor_scalar(out=W_in.ap[:, k, a:b], in0=W_in.ap[:, k, a:b],
                                                                         scalar1=cf("anwP")[:, k:k + 1], scalar2=None, op0=ALU.mult),
                         reads=[cstf, Wg_buf[name]], writes=[Wg_buf[name]], n=(b - a) // 3)

            iota1 = ta.ap[:, 0:128]
            iota2 = ta.ap[:, 128:256]
            reluP = ta.ap[:, 256:384]
            reluN = ta.ap[:, 384:512]

            R.op("act", lambda e: e.activation(out=lgP.ap, in_=cf("decP"), func=AF.Abs), reads=[cstf], writes=[lgP])
            R.op("act", lambda e: e.activation(out=lgR.ap, in_=cf("decR"), func=AF.Abs), reads=[cstf], writes=[lgR])
            R.op("dve", lambda e: e.tensor_scalar(out=lgP.ap, in0=lgP.ap, scalar1=-1.0, scalar2=None, op0=ALU.mult), reads=[lgP], writes=[lgP])
            R.op("dve", lambda e: e.tensor_scalar(out=lgR.ap, in0=lgR.ap, scalar1=-1.0, scalar2=None, op0=ALU.mult), reads=[lgR], writes=[lgR])
            R.op("act", lambda e: e.activation(out=g128.ap, in_=lgP.ap, func=AF.Exp, scale=128.0), reads=[lgP], writes=[g128])
            for t in range(4):
                R.op("act", lambda e, t=t: e.activation(out=Xif.ap[:, t * 128:(t + 1) * 128], in_=iota1, func=AF.Exp,
                                                        scale=lgP.ap[:, t:t + 1]), reads=[lgP, ta], writes=[Xif])
                R.op("act", lambda e, t=t: e.activation(out=Xib.ap[:, t * 128:(t + 1) * 128], in_=iota2, func=AF.Exp,
                                                        scale=lgP.ap[:, 4 + t:5 + t]), reads=[lgP, ta], writes=[Xib])
            R.op("act", lambda e: e.activation(out=Zfb.ap[:, 0:8], in_=lgR.ap[:, 0:8], func=AF.Exp, scale=cf("c127m")),
                 reads=[lgR, cstf], writes=[Zfb])
            R.op("act", lambda e: e.activation(out=Zfb.ap[:, 8:16], in_=lgR.ap[:, 8:16], func=AF.Exp, scale=cf("cm")),
                 reads=[lgR, cstf], writes=[Zfb])
            R.op("act", lambda e: e.activation(out=esink.ap, in_=cf("sink"), func=AF.Exp), reads=[cstf], writes=[esink])
            for h in range(8):
                R.op("dve", lambda e, h=h: e.tensor_scalar(out=te.ap[:, 0:128], in0=reluP, scalar1=lgR.ap[:, h:h + 1],
                                                           scalar2=None, op0=ALU.mult), reads=[ta, lgR], writes=[te])
                R.op("dve", lambda e, h=h: e.scalar_tensor_tensor(out=td.ap[:, 0:128], in0=reluN, scalar=lgR.ap[:, 8 + h:9 + h],
                                                                  in1=te.ap[:, 0:128], op0=ALU.mult, op1=ALU.add),
                     reads=[ta, lgR, te], writes=[td])
                R.op("act", lambda e, h=h: e.activation(out=DT.ap[:, h * 128:(h + 1) * 128], in_=td.ap[:, 0:128], func=AF.Exp),
                     reads=[td], writes=[DT])
            for c in range(NT_OWN + 1):
                R.op("pool", lambda e, c=c: e.memset(v3(Vb[c].ap, 2)[:, :, 64:65], 1.0), writes=[Vb[c]])
            R.op("pool", lambda e: e.memset(Rb.ap, 0.0), writes=[Rb])
            R.op("pool", lambda e: e.memset(Rf.ap, 0.0), writes=[Rf])
            R.op("pool", lambda e: e.memset(epsb.ap, EPS), writes=[epsb])
            R.op("pool", lambda e: e.memset(onesb.ap, 1.0), writes=[onesb])

            if KSTOP == "setup":
                raise _Stop()
            def load_tile(L, xbuf, slot):
                R.dma("sp", xbuf.ap, xs[L * 128:(L + 1) * 128, :], xbuf, writes=[xbuf])
                R.dma("sp", csl[slot].ap, cs_d[L * 128:(L + 1) * 128, :], csl[slot], writes=[csl[slot]])

            def prep_tile(xbuf, slot, xb=xb, xT=xT):
                s = st[slot]
                R.op("act", lambda e: e.activation(out=junk_ap, in_=xbuf.ap, func=AF.Square, accum_out=s.ap[:, 0:1]),
                     reads=[xbuf], writes=[junk, s], n=1024)
                R.op("act", lambda e: e.activation(out=s.ap[:, 1:2], in_=s.ap[:, 0:1], func=AF.Ln, scale=1.0 / D_MODEL, bias=epsb.ap),
                     reads=[s, epsb], writes=[s])
                R.op("act", lambda e: e.activation(out=s.ap[:, 2:3], in_=s.ap[:, 1:2], func=AF.Exp, scale=-0.5), reads=[s], writes=[s])
                R.op("dve", lambda e: e.tensor_scalar(out=s.ap[:, 5:6], in0=s.ap[:, 2:3], scalar1=-1.0, scalar2=None,
                                                      op0=ALU.mult), reads=[s], writes=[s])
                R.op("dve", lambda e: e.tensor_scalar(out=s.ap[:, 3:4], in0=s.ap[:, 2:3], scalar1=0.125, scalar2=None,
                                                      op0=ALU.mult), reads=[s], writes=[s])
                R.op("dve", lambda e: e.tensor_scalar(out=s.ap[:, 4:5], in0=s.ap[:, 2:3], scalar1=0.5, scalar2=None,
                                                      op0=ALU.mult), reads=[s], writes=[s])
                R.op("dve", lambda e: e.tensor_copy(out=xb.ap, in_=xbuf.ap), reads=[xbuf], writes=[xb], n=600)
                for k in range(8):
                    R.op("pe", lambda e, k=k: e.transpose(out=T0b[:, k * 128:(k + 1) * 128], in_=xb.ap[:, k * 128:(k + 1) * 128],
                                                          identity=ident), reads=[xb, cstb], writes=[T0], n=128)
                R.op("act", lambda e: e.activation(out=xT.ap, in_=T0b, func=AF.Identity), reads=[T0], writes=[xT])
                return s

            def inproj(bank, c0, n, wnames, xT=xT, also=()):
                xT3 = v3(xT.ap, 8)
                for k in range(8):
                    R.op("pe", lambda e, k=k: e.matmul(bank.ap[:, 0:n], lhsT=xT3[:, k, :], rhs=W_in.ap[:, k, c0:c0 + n],
                                                       start=(k == 0), stop=(k == 7)),
                         reads=[xT] + [Wg_buf[w] for w in wnames], writes=[bank] + list(also), n=n)

            def rope(eng_a, eng_b, src, dst_t1, dst_t2, A, B, H):
                s3 = v3(src.ap[:, 0:H * 64], H)
                t13 = v3(dst_t1.ap[:, 0:H * 64], H)
                t23 = v3(dst_t2.ap[:, 0:H * 64], H)
                Ab = A.unsqueeze(1).broadcast_to([128, H, 64])
                Bl = B[:, 0:32].unsqueeze(1).broadcast_to([128, H, 32])
                Bh = B[:, 32:64].unsqueeze(1).broadcast_to([128, H, 32])
                return s3, t13, t23, Ab, Bl, Bh

            R.phase = "pre"
            xring = [hbuf[4], hbuf[5], hbuf[6]]
            order = list(range(NT_ALL - 1, -1, -1))
            load_tile(order[0], xring[0], 0)
            for i, L in enumerate(order):
                if KSTOP == "pre1" and i == 1:
                    raise _Stop()
                xbuf = xring[i % 3]
                slot = i % 2
                if i + 1 < len(order):
                    load_tile(order[i + 1], xring[(i + 1) % 3], (i + 1) % 2)
                own = L < NT_OWN
                needkv = L <= NT_OWN
                par = i % 2
                xb_ = (xb, retS)[par]
                xT_ = (xT, yT)[par]
                ta_ = (ta, tg)[par]
                rvb_ = (rvb, rkb)[par]
                kzb_ = (qb, rqxf)[par]
                kzf_ = (rqb, rqxb)[par]
                Brk = (F[0], F[3])[par]
                Brv = (F[1], F[4])[par]
                Bkv = (F[2], F[5])[par]
                s = prep_tile(xbuf, slot, xb_, xT_)
                cst = csl[slot]
                cosv = cst.ap[:, 0:64]
                sinv = cst.ap[:, 64:128]
                inproj(Brk, C_RK, 512, ["rkrv"], xT_)
                inproj(Brv, C_RV, 512, ["rkrv"], xT_)
                if needkv:
                    inproj(T1b, C_AK, 256, ["akav"], xT_, also=[T1a])
                R.op("act", lambda e, s=s, ta_=ta_, Brk=Brk: e.activation(out=ta_.ap, in_=Brk.ap, func=AF.Identity, scale=s.ap[:, 3:4]),
                     reads=[Brk, s], writes=[ta_])
                R.op("act", lambda e, s=s, rvb_=rvb_, Brv=Brv: e.activation(out=rvb_.ap, in_=Brv.ap, func=AF.Identity, scale=s.ap[:, 2:3]),
                     reads=[Brv, s], writes=[rvb_])
                s3, t13, t23, Ab, Bl, Bh = rope(None, None, ta_, td, te, cosv, sinv, 8)
                R.op("dve", lambda e, s3=s3, t13=t13, Ab=Ab: e.tensor_tensor(out=t13, in0=s3, in1=Ab, op=ALU.mult),
                     reads=[ta_, cst], writes=[td])
                R.op("pool", lambda e, s3=s3, t23=t23, Bl=Bl: e.tensor_tensor(out=t23[:, :, 0:32], in0=s3[:, :, 32:64], in1=Bl, op=ALU.mult),
                     reads=[ta_, cst], writes=[te])
                R.op("pool", lambda e, s3=s3, t23=t23, Bh=Bh: e.tensor_tensor(out=t23[:, :, 32:64], in0=s3[:, :, 0:32], in1=Bh, op=ALU.mult),
                     reads=[ta_, cst], writes=[te])
                R.op("dve", lambda e: e.tensor_tensor(out=td.ap, in0=td.ap, in1=te.ap, op=ALU.add), reads=[td, te], writes=[td])
                Zb_b = Zfb.ap[:, 8:16].unsqueeze(2).broadcast_to([128, 8, 64])
                Zf_b = Zfb.ap[:, 0:8].unsqueeze(2).broadcast_to([128, 8, 64])
                R.op("dve", lambda e, Zb_b=Zb_b, kzb_=kzb_: e.tensor_tensor(out=v3(kzb_.ap, 8), in0=v3(td.ap, 8), in1=Zb_b, op=ALU.mult),
                     reads=[td, Zfb], writes=[kzb_])
                if own:
                    R.op("pool", lambda e, Zf_b=Zf_b, kzf_=kzf_: e.tensor_tensor(out=v3(kzf_.ap, 8), in0=v3(td.ap, 8), in1=Zf_b, op=ALU.mult),
                         reads=[td, Zfb], writes=[kzf_])
                for t in range(4):
                    R.op("pe", lambda e, t=t, Bkv=Bkv, kzb_=kzb_, rvb_=rvb_: e.matmul(Bkv.ap[:, t * 128:(t + 1) * 128], lhsT=kzb_.ap[:, t * 128:(t + 1) * 128],
                                                       rhs=rvb_.ap[:, t * 128:(t + 1) * 128], start=True, stop=True),
                         reads=[kzb_, rvb_], writes=[Bkv], n=128)
                if own:
                    for t in range(4):
                        R.op("pe", lambda e, t=t, Brk=Brk, kzf_=kzf_, rvb_=rvb_: e.matmul(Brk.ap[:, t * 128:(t + 1) * 128], lhsT=kzf_.ap[:, t * 128:(t + 1) * 128],
                                                           rhs=rvb_.ap[:, t * 128:(t + 1) * 128], start=True, stop=True),
                             reads=[kzf_, rvb_], writes=[Brk], n=128)
                Rb3 = v3(Rb.ap, 4)
                rt3 = v3(rtmp.ap, 4)
                F33 = v3(Bkv.ap, 4)
                F43 = v3(Brk.ap, 4)
                if own:
                    R.op("dve", lambda e, L=L: e.tensor_copy(out=RbS[L].ap, in_=Rb.ap), reads=[Rb], writes=[RbS[L]])
                gb_b = g128.ap[:, 4:8].unsqueeze(2).broadcast_to([128, 4, 64])
                R.op("dve", lambda e, gb_b=gb_b, Rb3=Rb3, rt3=rt3: e.tensor_tensor(out=rt3, in0=Rb3, in1=gb_b, op=ALU.mult),
                     reads=[Rb, g128], writes=[rtmp])
                R.op("dve", lambda e, Rb3=Rb3, rt3=rt3, F33=F33: e.tensor_tensor(out=Rb3[0:64], in0=rt3[0:64], in1=F33[0:64, :, 0:64], op=ALU.add),
                     reads=[rtmp, Bkv], writes=[Rb])
                R.op("dve", lambda e, Rb3=Rb3, rt3=rt3, F33=F33: e.tensor_tensor(out=Rb3[64:128], in0=rt3[64:128], in1=F33[64:128, :, 64:128], op=ALU.add),
                     reads=[rtmp, Bkv, Rb], writes=[Rb])
                if own:
                    R.op("act", lambda e, L=L, F43=F43: e.activation(out=kvf[L].ap[0:64], in_=F43[0:64, :, 0:64], func=AF.Identity),
                         reads=[Brk], writes=[kvf[L]])
                    R.op("act", lambda e, L=L, F43=F43: e.activation(out=kvf[L].ap[64:128], in_=F43[64:128, :, 64:128], func=AF.Identity),
                         reads=[Brk, kvf[L]], writes=[kvf[L]])
                if needkv:
                    R.op("act", lambda e, s=s: e.activation(out=akf.ap, in_=T1b.ap[:, 0:128], func=AF.Identity, scale=s.ap[:, 2:3]),
                         reads=[T1b, s], writes=[akf])
                    R.op("act", lambda e, s=s, L=L: e.activation(out=v3(Vb[L].ap, 2)[:, :, 0:64], in_=v3(T1b.ap[:, 128:256], 2),
                                                                  func=AF.Identity, scale=s.ap[:, 2:3]),
                         reads=[T1b, s, Vb[L]], writes=[Vb[L]])
                    R.op("dve", lambda e: e.tensor_tensor(out=tk1.ap, in0=akf.ap, in1=akf.ap, op=ALU.mult), reads=[akf], writes=[tk1])
                    R.op("dve", lambda e: e.tensor_reduce(out=s8a.ap[:, 0:2], in_=v3(tk1.ap, 2), axis=AX.X, op=ALU.add),
                         reads=[tk1], writes=[s8a])
                    R.op("act", lambda e: e.activation(out=s8a.ap[:, 2:4], in_=s8a.ap[:, 0:2], func=AF.Ln, scale=1.0 / 64, bias=epsb.ap),
                         reads=[s8a, epsb], writes=[s8a])
                    R.op("act", lambda e: e.activation(out=s8a.ap[:, 4:6], in_=s8a.ap[:, 2:4], func=AF.Exp, scale=-0.5), reads=[s8a], writes=[s8a])
                    R.op("pool", lambda e, cosv=cosv: e.tensor_tensor(out=tabk.ap[:, 0:64], in0=cosv, in1=cf("knw"), op=ALU.mult),
                         reads=[cst, cstf], writes=[tabk])
                    R.op("pool", lambda e, sinv=sinv: e.tensor_tensor(out=tabk.ap[:, 64:128], in0=sinv, in1=cf("knws"), op=ALU.mult),
                         reads=[cst, cstf, tabk], writes=[tabk])
                    s3, t13, t23, Ab, Bl, Bh = rope(None, None, akf, tk1, tk2, tabk.ap[:, 0:64], tabk.ap[:, 64:128], 2)
                    R.op("pool", lambda e, s3=s3, t13=t13, Ab=Ab: e.tensor_tensor(out=t13, in0=s3, in1=Ab, op=ALU.mult),
                         reads=[akf, tabk, s8a], writes=[tk1])
                    R.op("pool", lambda e, s3=s3, t23=t23, Bl=Bl: e.tensor_tensor(out=t23[:, :, 0:32], in0=s3[:, :, 32:64], in1=Bl, op=ALU.mult),
                         reads=[akf, tabk], writes=[tk2])
                    R.op("pool", lambda e, s3=s3, t23=t23, Bh=Bh: e.tensor_tensor(out=t23[:, :, 32:64], in0=s3[:, :, 0:32], in1=Bh, op=ALU.mult),
                         reads=[akf, tabk, tk2], writes=[tk2])
                    R.op("dve", lambda e: e.tensor_tensor(out=tk1.ap, in0=tk1.ap, in1=tk2.ap, op=ALU.add), reads=[tk1, tk2], writes=[tk1])
                    kd4 = kdup.ap.rearrange("p (g u d) -> p g u d", g=2, u=2)
                    rk2b = s8a.ap[:, 4:6].unsqueeze(2).broadcast_to([128, 2, 64])
                    for u in range(2):
                        R.op("dve", lambda e, u=u, kd4=kd4, rk2b=rk2b: e.tensor_tensor(out=kd4[:, :, u, :], in0=v3(tk1.ap, 2), in1=rk2b, op=ALU.mult),
                             reads=[tk1, s8a, kdup], writes=[kdup])
                    for g in range(2):
                        R.op("pe", lambda e, g=g: e.transpose(out=T1ab[:, g * 128:(g + 1) * 128], in_=kdup.ap[:, g * 128:(g + 1) * 128],
                                                              identity=ident), reads=[kdup, cstb], writes=[T1a, T1b], n=128)
                    R.op("act", lambda e, L=L: e.activation(out=KT[L].ap, in_=T1ab[:, 0:256], func=AF.Identity), reads=[T1a], writes=[KT[L]])

            if KSTOP == "pre":
                raise _Stop()
            Rf3 = v3(Rf.ap, 4)
            rt3 = v3(rtmp.ap, 4)
            gf_b = g128.ap[:, 0:4].unsqueeze(2).broadcast_to([128, 4, 64])
            scan_last = None
            for c in range(NT_OWN):
                R.op("dve", lambda e, c=c: e.tensor_copy(out=RfS[c].ap, in_=Rf.ap), reads=[Rf], writes=[RfS[c]])
                R.op("dve", lambda e: e.tensor_tensor(out=rt3, in0=Rf3, in1=gf_b, op=ALU.mult), reads=[Rf, g128], writes=[rtmp])
                scan_last = R.op("dve", lambda e, c=c: e.tensor_tensor(out=Rf3, in0=rt3, in1=kvf[c].ap, op=ALU.add),
                                 reads=[rtmp, kvf[c]], writes=[Rf])

            if KSTOP == "scan":
                raise _Stop()
            R.phase = "main"

            def load_main(c):
                extra = [scan_last] if c >= 12 else []
                R.dma("sp", hbuf[c].ap, xs[c * 128:(c + 1) * 128, :], hbuf[c], writes=[hbuf[c]], extra=extra)
                R.dma("sp", csl[c % 2].ap, cs_d[c * 128:(c + 1) * 128, :], csl[c % 2], writes=[csl[c % 2]])

            load_main(0)
            for c in range(NT_OWN):
                if KSTOP == "main1" and c == 1:
                    raise _Stop()
                xbuf = hbuf[c]
                slot = c % 2
                s = prep_tile(xbuf, slot)
                cst = csl[slot]
                cosv = cst.ap[:, 0:64]
                sinv = cst.ap[:, 64:128]
                R.op("pool", lambda e, cosv=cosv: e.tensor_tensor(out=tabq.ap[:, 0:64], in0=cosv, in1=cf("qnw"), op=ALU.mult),
                     reads=[cst, cstf], writes=[tabq])
                R.op("pool", lambda e, sinv=sinv: e.tensor_tensor(out=tabq.ap[:, 64:128], in0=sinv, in1=cf("qnws"), op=ALU.mult),
                     reads=[cst, cstf, tabq], writes=[tabq])
                BQ, BG = (F[4], F[0]) if os.environ.get("KSWAP", "0") == "1" else (F[0], F[4])
                inproj(BQ, C_AQ, 512, ["aq"])
                inproj(F[1], C_RQ, 512, ["rq"])
                inproj(F[2], C_RK, 512, ["rkrv"])
                inproj(F[3], C_RV, 512, ["rkrv"])
                inproj(BG, C_RG, 512, ["rg"])
                if c + 1 < NT_OWN:
                    load_main(c + 1)
                R.op("act", lambda e, s=s: e.activation(out=ta.ap, in_=BQ.ap, func=AF.Identity, scale=s.ap[:, 2:3]),
                     reads=[BQ, s], writes=[ta])
                R.op("dve", lambda e: e.tensor_tensor(out=td.ap, in0=ta.ap, in1=ta.ap, op=ALU.mult), reads=[ta], writes=[td])
                R.op("dve", lambda e: e.tensor_reduce(out=s8a.ap, in_=v3(td.ap, 8), axis=AX.X, op=ALU.add), reads=[td], writes=[s8a])
                R.op("act", lambda e: e.activation(out=s8b.ap, in_=s8a.ap, func=AF.Ln, scale=1.0 / 64, bias=epsb.ap),
                     reads=[s8a, epsb], writes=[s8b])
                R.op("act", lambda e: e.activation(out=s8c.ap, in_=s8b.ap, func=AF.Exp, scale=-0.5), reads=[s8b], writes=[s8c])
                s3, t13, t23, Ab, Bl, Bh = rope(None, None, ta, td, te, tabq.ap[:, 0:64], tabq.ap[:, 64:128], 8)
                R.op("dve", lambda e, s3=s3, t13=t13, Ab=Ab: e.tensor_tensor(out=t13, in0=s3, in1=Ab, op=ALU.mult),
                     reads=[ta, tabq, s8a], writes=[td])
                R.op("pool", lambda e, s3=s3, t23=t23, Bl=Bl: e.tensor_tensor(out=t23[:, :, 0:32], in0=s3[:, :, 32:64], in1=Bl, op=ALU.mult),
                     reads=[ta, tabq], writes=[te])
                R.op("pool", lambda e, s3=s3, t23=t23, Bh=Bh: e.tensor_tensor(out=t23[:, :, 32:64], in0=s3[:, :, 0:32], in1=Bh, op=ALU.mult),
                     reads=[ta, tabq, te], writes=[te])
                R.op("dve", lambda e: e.tensor_tensor(out=td.ap, in0=td.ap, in1=te.ap, op=ALU.add), reads=[td, te], writes=[td])
                rq8b = s8c.ap.unsqueeze(2).broadcast_to([128, 8, 64])
                R.op("dve", lambda e, rq8b=rq8b: e.tensor_tensor(out=v3(qb.ap, 8), in0=v3(td.ap, 8), in1=rq8b, op=ALU.mult),
                     reads=[td, s8c], writes=[qb])
                for t in range(4):
                    R.op("pe", lambda e, t=t: e.transpose(out=T1ab[:, t * 128:(t + 1) * 128], in_=qb.ap[:, t * 128:(t + 1) * 128],
                                                          identity=ident), reads=[qb, cstb], writes=[T1a], n=128)
                R.op("act", lambda e: e.activation(out=qT.ap, in_=T1ab, func=AF.Identity), reads=[T1a], writes=[qT])
                if KSTOP == "m_q1":
                    raise _Stop()
                R.op("act", lambda e, s=s: e.activation(out=tb2.ap, in_=F[1].ap, func=AF.Identity, scale=s.ap[:, 2:3]),
                     reads=[F[1], s], writes=[tb2] + ([akf, tk1, tk2, tabk] if c == 0 else []), n=512)
                s3, t13, t23, Ab, Bl, Bh = rope(None, None, tb2, td, te, cosv, sinv, 8)
                R.op("dve", lambda e, s3=s3, t13=t13, Ab=Ab: e.tensor_tensor(out=t13, in0=s3, in1=Ab, op=ALU.mult),
                     reads=[tb2, cst], writes=[td])
                R.op("pool", lambda e, s3=s3, t23=t23, Bl=Bl: e.tensor_tensor(out=t23[:, :, 0:32], in0=s3[:, :, 32:64], in1=Bl, op=ALU.mult),
                     reads=[tb2, cst], writes=[te])
                R.op("pool", lambda e, s3=s3, t23=t23, Bh=Bh: e.tensor_tensor(out=t23[:, :, 32:64], in0=s3[:, :, 0:32], in1=Bh, op=ALU.mult),
                     reads=[tb2, cst, te], writes=[te])
                R.op("dve", lambda e: e.tensor_tensor(out=rqb.ap, in0=td.ap, in1=te.ap, op=ALU.add), reads=[td, te], writes=[rqb])
                for t in range(4):
                    R.op("pe", lambda e, t=t: e.transpose(out=T1bb[:, t * 128:(t + 1) * 128], in_=rqb.ap[:, t * 128:(t + 1) * 128],
                                                          identity=ident), reads=[rqb, cstb], writes=[T1b], n=128)
                R.op("act", lambda e: e.activation(out=rqT.ap, in_=T1bb, func=AF.Identity), reads=[T1b], writes=[rqT])
                R.op("dve", lambda e: e.tensor_tensor(out=rqxf.ap, in0=rqT.ap, in1=Xif.ap, op=ALU.mult), reads=[rqT, Xif], writes=[rqxf])
                R.op("dve", lambda e: e.tensor_tensor(out=rqxb.ap, in0=rqT.ap, in1=Xib.ap, op=ALU.mult), reads=[rqT, Xib], writes=[rqxb])
                if KSTOP == "m_q2":
                    raise _Stop()
                R.op("act", lambda e, s=s: e.activation(out=ta.ap, in_=F[2].ap, func=AF.Identity, scale=s.ap[:, 3:4]),
                     reads=[F[2], s], writes=[ta])
                s3, t13, t23, Ab, Bl, Bh = rope(None, None, ta, td, te, cosv, sinv, 8)
                R.op("dve", lambda e, s3=s3, t13=t13, Ab=Ab: e.tensor_tensor(out=t13, in0=s3, in1=Ab, op=ALU.mult),
                     reads=[ta, cst], writes=[td])
                R.op("pool", lambda e, s3=s3, t23=t23, Bl=Bl: e.tensor_tensor(out=t23[:, :, 0:32], in0=s3[:, :, 32:64], in1=Bl, op=ALU.mult),
                     reads=[ta, cst], writes=[te])
                R.op("pool", lambda e, s3=s3, t23=t23, Bh=Bh: e.tensor_tensor(out=t23[:, :, 32:64], in0=s3[:, :, 0:32], in1=Bh, op=ALU.mult),
                     reads=[ta, cst, te], writes=[te])
                R.op("dve", lambda e: e.tensor_tensor(out=rkb.ap, in0=td.ap, in1=te.ap, op=ALU.add), reads=[td, te], writes=[rkb])
                for t in range(4):
                    R.op("pe", lambda e, t=t: e.transpose(out=T1ab[:, t * 128:(t + 1) * 128], in_=rkb.ap[:, t * 128:(t + 1) * 128],
                                                          identity=ident), reads=[rkb, cstb], writes=[T1a], n=128)
                R.op("act", lambda e: e.activation(out=rkT.ap, in_=T1ab, func=AF.Identity), reads=[T1a], writes=[rkT])
                if KSTOP == "m_q3":
                    raise _Stop()
                R.op("act", lambda e, s=s: e.activation(out=rvb.ap, in_=F[3].ap, func=AF.Identity, scale=s.ap[:, 2:3]),
                     reads=[F[3], s], writes=[rvb])
                R.op("act", lambda e, s=s: e.activation(out=tg.ap, in_=BG.ap, func=AF.Exp, scale=s.ap[:, 5:6]),
                     reads=[BG, s], writes=[tg])
                R.op("act", lambda e: e.activation(out=tg.ap, in_=tg.ap, func=AF.Ln, bias=onesb.ap), reads=[tg, onesb], writes=[tg])
                R.op("act", lambda e: e.activation(out=tg.ap, in_=tg.ap, func=AF.Exp, scale=-1.0), reads=[tg], writes=[tg])
                R.op("dve", lambda e, s=s: e.scalar_tensor_tensor(out=tg.ap, in0=BG.ap, scalar=s.ap[:, 2:3], in1=tg.ap,
                                                                  op0=ALU.mult, op1=ALU.mult), reads=[BG, s, tg], writes=[tg])
                R.op("pool", lambda e: e.tensor_tensor(out=tg.ap, in0=tg.ap, in1=cf("rnw"), op=ALU.mult), reads=[tg, cstf], writes=[tg])

                if KSTOP == "m_q":
                    raise _Stop()
                kbs = [kb for kb in (c - 1, c, c + 1) if kb >= 0]
                nkb = len(kbs)
                qT3 = v3(qT.ap, 4)
                Obank = [F[4], F[5]]
                it = 0
                for g in range(2):
                    for ee in range(2):
                        bx, by = (F[0], F[1]) if it % 2 == 0 else (F[2], F[3])
                        it += 1
                        regs = [bx.ap[:, 0:256], bx.ap[:, 256:512], by.ap[:, 0:256]]
                        rbuf = [bx, bx, by]
                        for j, kb in enumerate(kbs):
                            KT3 = v3(KT[kb].ap, 2)
                            masked = (kb != c)
                            R.op("pe", lambda e, j=j, KT3=KT3, g=g, ee=ee, masked=masked, regs=regs: e.matmul(
                                v3(regs[j], 2), lhsT=KT3[64 * ee:64 * ee + 64, g, :], rhs=qT3[64 * ee:64 * ee + 64, 2 * g:2 * g + 2, :],
                                start=True, stop=(not masked)), reads=[KT[kb], qT], writes=[rbuf[j]], n=256)
                            if masked:
                                mk = mprev if kb < c else mnext
                                R.op("pe", lambda e, j=j, mk=mk, regs=regs: e.matmul(regs[j], lhsT=ident, rhs=mk, start=False, stop=True),
                                     reads=[cstb], writes=[rbuf[j]], n=256)
                        n1 = min(nkb, 2)
                        R.op("act", lambda e, n1=n1, bx=bx: e.activation(out=PT.ap[:, 0:n1 * 256], in_=bx.ap[:, 0:n1 * 256], func=AF.Exp, scale=0.125),
                             reads=[bx], writes=[PT])
                        if nkb == 3:
                            R.op("act", lambda e, by=by: e.activation(out=PT.ap[:, 512:768], in_=by.ap[:, 0:256], func=AF.Exp, scale=0.125),
                                 reads=[by, PT], writes=[PT])
                        for tt in range(2):
                            h = 2 * (2 * g + tt) + ee
                            ob = Obank[h // 4]
                            hl = h % 4
                            for j, kb in enumerate(kbs):
                                R.op("pe", lambda e, j=j, kb=kb, tt=tt, ob=ob, hl=hl, g=g, nkb=nkb: e.matmul(
                                    ob.ap[:, hl * 65:(hl + 1) * 65], lhsT=PT.ap[:, j * 256 + tt * 128: j * 256 + (tt + 1) * 128],
                                    rhs=v3(Vb[kb].ap, 2)[:, g, :], start=(j == 0), stop=(j == nkb - 1)),
                                    reads=[PT, Vb[kb]], writes=[ob], n=800)
                if KSTOP == "m_att0":
                    raise _Stop()
                for gb in range(2):
                    O3 = Obank[gb].ap[:, 0:260].rearrange("p (h d) -> p h d", h=4)
                    R.op("dve", lambda e, gb=gb, O3=O3: e.tensor_tensor(out=s8d.ap[:, gb * 4:(gb + 1) * 4], in0=O3[:, :, 64],
                                                                        in1=esink.ap[:, gb * 4:(gb + 1) * 4], op=ALU.add),
                         reads=[Obank[gb], esink, s8d], writes=[s8d])
                R.op("dve", lambda e: e.reciprocal(out=s8e.ap, in_=s8d.ap), reads=[s8d], writes=[s8e])
                for gb in range(2):
                    O3 = Obank[gb].ap[:, 0:260].rearrange("p (h d) -> p h d", h=4)
                    rdb = s8e.ap[:, gb * 4:(gb + 1) * 4].unsqueeze(2).broadcast_to([128, 4, 64])
                    R.op("dve", lambda e, gb=gb, O3=O3, rdb=rdb: e.tensor_tensor(out=v3(yb.ap[:, gb * 256:(gb + 1) * 256], 4), in0=O3[:, :, 0:64],
                                                                                 in1=rdb, op=ALU.mult),
                         reads=[Obank[gb], s8e, yb], writes=[yb], n=256)

                if KSTOP == "m_att":
                    raise _Stop()
                rkT3 = v3(rkT.ap, 4)
                rqT3 = v3(rqT.ap, 4)
                rxf3 = v3(rqxf.ap, 4)
                rxb3 = v3(rqxb.ap, 4)
                for ee in range(2):
                    for t in range(4):
                        bank = F[ee]
                        col = t * 128
                        R.op("pe", lambda e, t=t, ee=ee, bank=bank, col=col: e.matmul(
                            bank.ap[:, col:col + 128], lhsT=rkT3[64 * ee:64 * ee + 64, t, :], rhs=rqT3[64 * ee:64 * ee + 64, t, :],
                            start=True, stop=True), reads=[rkT, rqT], writes=[bank], n=128)
                if KSTOP == "m_r0":
                    raise _Stop()
                retS4 = retS.ap.rearrange("p (t e n) -> p t e n", t=4, e=2)
                DT4 = DT.ap.rearrange("p (t e n) -> p t e n", t=4, e=2)
                for gb in range(2):
                    R.op("dve", lambda e, gb=gb, retS4=retS4, DT4=DT4: e.tensor_tensor(out=retS4[:, :, gb, :], in0=v3(F[gb].ap, 4),
                                                                                     in1=DT4[:, :, gb, :], op=ALU.mult),
                         reads=[F[gb], DT, retS], writes=[retS])
                if KSTOP == "m_r1":
                    raise _Stop()
                for h in range(8):
                    t, ee = h // 2, h % 2
                    Rf3s = v3(RfS[c].ap, 4)
                    Rb3s = v3(RbS[c].ap, 4)
                    R.op("pe", lambda e, h=h: e.matmul(F[2].ap[:, h * 64:(h + 1) * 64], lhsT=retS.ap[:, h * 128:(h + 1) * 128],
                                                       rhs=rvb.ap[:, h * 64:(h + 1) * 64], start=True, stop=False),
                         reads=[retS, rvb], writes=[F[2]], n=200)
                    R.op("pe", lambda e, h=h, t=t, ee=ee, Rf3s=Rf3s: e.matmul(F[2].ap[:, h * 64:(h + 1) * 64], lhsT=rxf3[64 * ee:64 * ee + 64, t, :],
                                                                             rhs=Rf3s[64 * ee:64 * ee + 64, t, :], start=False, stop=False),
                         reads=[rqxf, RfS[c]], writes=[F[2]], n=100)
                    R.op("pe", lambda e, h=h, t=t, ee=ee, Rb3s=Rb3s: e.matmul(F[2].ap[:, h * 64:(h + 1) * 64], lhsT=rxb3[64 * ee:64 * ee + 64, t, :],
                                                                             rhs=Rb3s[64 * ee:64 * ee + 64, t, :], start=False, stop=True),
                         reads=[rqxb, RbS[c]], writes=[F[2]], n=100)
                if KSTOP == "m_r2":
                    raise _Stop()
                R.op("act", lambda e: e.activation(out=tf.ap, in_=F[2].ap, func=AF.Square), reads=[F[2]], writes=[tf])
                R.op("dve", lambda e: e.tensor_reduce(out=s8f.ap, in_=v3(tf.ap, 8), axis=AX.X, op=ALU.add), reads=[tf], writes=[s8f])
                R.op("act", lambda e: e.activation(out=s8g.ap, in_=s8f.ap, func=AF.Ln, scale=1.0 / 64, bias=epsb.ap),
                     reads=[s8f, epsb], writes=[s8g])
                R.op("act", lambda e: e.activation(out=s8h.ap, in_=s8g.ap, func=AF.Exp, scale=-0.5), reads=[s8g], writes=[s8h])
                R.op("dve", lambda e: e.tensor_tensor(out=tf.ap, in0=F[2].ap, in1=tg.ap, op=ALU.mult), reads=[F[2], tg, s8f], writes=[tf])
                rr8b = s8h.ap.unsqueeze(2).broadcast_to([128, 8, 64])
                R.op("dve", lambda e, rr8b=rr8b: e.tensor_tensor(out=v3(yb.ap[:, 512:1024], 8), in0=v3(tf.ap, 8), in1=rr8b, op=ALU.mult),
                     reads=[tf, s8h, yb], writes=[yb], n=512)

                if KSTOP == "m_ret":
                    raise _Stop()
                for k in range(8):
                    R.op("pe", lambda e, k=k: e.transpose(out=T0b[:, k * 128:(k + 1) * 128], in_=yb.ap[:, k * 128:(k + 1) * 128],
                                                          identity=ident), reads=[yb, cstb], writes=[T0], n=128)
                R.op("act", lambda e: e.activation(out=yT.ap, in_=T0b, func=AF.Identity), reads=[T0], writes=[yT])
                yT3 = v3(yT.ap, 8)
                for half in range(2):
                    bank = F[3] if half == 0 else F[5]
                    for k in range(8):
                        R.op("pe", lambda e, k=k, half=half, bank=bank: e.matmul(bank.ap, lhsT=yT3[:, k, :],
                                                                                  rhs=W_out.ap[:, k, half * 512:(half + 1) * 512],
                                                                                  start=(k == 0), stop=(k == 7)),
                             reads=[yT, W_out], writes=[bank])
                    R.op("dve", lambda e, half=half, bank=bank, xbuf=xbuf: e.tensor_tensor(out=xbuf.ap[:, half * 512:(half + 1) * 512],
                                                                                           in0=xbuf.ap[:, half * 512:(half + 1) * 512],
                                                                                           in1=bank.ap, op=ALU.add),
                         reads=[bank, xbuf], writes=[xbuf], n=512)
                R.op("act", lambda e, c=c, xbuf=xbuf: e.activation(out=tf.ap.bitcast(BF16), in_=xbuf.ap, func=AF.Square, accum_out=ssq2.ap[:, c:c + 1]),
                     reads=[xbuf, ssq2], writes=[tf, ssq2], n=1024)
                R.op("act", lambda e, c=c: e.activation(out=lnr2.ap[:, c:c + 1], in_=ssq2.ap[:, c:c + 1], func=AF.Ln, scale=1.0 / D_MODEL, bias=epsb.ap),
                     reads=[ssq2, epsb, lnr2], writes=[lnr2])
                R.op("act", lambda e, c=c: e.activation(out=rstd2.ap[:, c:c + 1], in_=lnr2.ap[:, c:c + 1], func=AF.Exp, scale=-0.5),
                     reads=[lnr2, rstd2], writes=[rstd2])

            if KSTOP == "main":
                raise _Stop()
            R.phase = "ffn"
            fences = []
            f_ = R.op("dve", lambda e: e.tensor_copy(out=fsc.ap[:, 0:1], in_=epsb.ap), reads=[epsb], writes=[Buf("fscD", fsc.ap[:, 0:1])])
            fences.append(f_)
            f_ = R.op("act", lambda e: e.activation(out=fsc.ap[:, 1:2], in_=epsb.ap, func=AF.Identity), reads=[epsb], writes=[Buf("fscA", fsc.ap[:, 1:2])])
            fences.append(f_)
            f_ = R.op("pool", lambda e: e.memset(fsc.ap[:, 2:3], 0.0), writes=[Buf("fscP", fsc.ap[:, 2:3])])
            fences.append(f_)
            f_ = R.op("pe", lambda e: e.matmul(T0.ap[:, 0:1], lhsT=ident, rhs=ident[:, 0:1], start=True, stop=True),
                      reads=[cstb], writes=[T0], n=1)
            fences.append(f_)
            for f_ in fences:
                f_.fence = True
            R.seg = 1
            bar = fences

            def pbuf(name, ap):
                b = Buf(name, ap)
                b.readers = list(bar)
                return b

            AB.off = 0
            AFa.off = 0

            def pb_alloc(arena, name, n):
                b = arena.alloc(name, n)
                b.readers = list(bar)
                return b

            mTg = [pb_alloc(AB, f"mTg{g_}", 4096) for g_ in range(4)]
            mT = []
            for c in range(NT_OWN):
                b_ = Buf(f"mT{c}", v3(mTg[c // 4].ap, 8)[:, :, (c % 4) * 128:(c % 4 + 1) * 128])
                b_.readers = list(bar)
                mT.append(b_)
            hTb = [[pb_alloc(AB, f"hT{s_}_{ci}", 512) for ci in range(4)] for s_ in range(2)]
            mb = pb_alloc(AB, "mb", 1024)
            junk2 = pb_alloc(AB, "junk2", 1024)
            identB = pb_alloc(AB, "identB", 128)
            fnw = pb_alloc(AFa, "fnw", 1024)
            sgt = [pb_alloc(AFa, f"sgt{i}", 512) for i in range(2)]
            st2 = pb_alloc(AFa, "st2", 8)
            T0f = pbuf("T0f", banks[6][:, :])
            T1f = pbuf("T1f", banks[7][:, :])
            slots = [pbuf(f"slot{i}", Wt[:, i * 12288:(i + 1) * 12288]) for i in range(2)]
            early = []
            for wb in Wg_buf.values():
                early.extend(wb.readers)
                early.extend(wb.writers)
            slots[0].readers = early

            R.dma("sp", fnw.ap, fnw_d, fnw, writes=[fnw])
            R.dma("pool", identB.ap, cstb_d[:, 0:128], identB, writes=[identB])

            passes = [(0, 4), (4, 4), (8, 4), (12, 4), (16, 3), (19, 3)]
            Wg_v = w_gate_d.rearrange("(k p) n -> p k n", p=128)
            Wu_v = w_up_d.rearrange("(k p) n -> p k n", p=128)

            def load_pass(r):
                f0, C = passes[r]
                sl = slots[r % 2]
                g3 = v3(sl.ap[:, 0:4096], 8)
                u3 = v3(sl.ap[:, 4096:8192], 8)
                d3 = v3(sl.ap[:, 8192:12288], 4)
                R.dma("pool", g3[:, :, 0:C * 128], Wg_v[:, :, f0 * 128:(f0 + C) * 128], sl, writes=[sl])
                R.dma("pool", u3[:, :, 0:C * 128], Wu_v[:, :, f0 * 128:(f0 + C) * 128], sl, writes=[sl])
                R.dma("pool", d3[:, 0:C, :], w_down_d[f0 * 128:(f0 + C) * 128, :].rearrange("(c p) n -> p c n", p=128), sl, writes=[sl])

            load_pass(0)
            load_pass(1)

            def prologue(tgi):
                for t in range(4):
                    c = tgi * 4 + t
                    R.op("dve", lambda e, c=c: e.scalar_tensor_tensor(out=mb.ap, in0=hbuf[c].ap, scalar=rstd2.ap[:, c:c + 1], in1=fnw.ap,
                                                                      op0=ALU.mult, op1=ALU.mult), reads=[hbuf[c], rstd2, fnw], writes=[mb])
                    for k in range(8):
                        R.op("pe", lambda e, k=k: e.transpose(out=T0b[:, k * 128:(k + 1) * 128], in_=mb.ap[:, k * 128:(k + 1) * 128],
                                                              identity=identB.ap), reads=[mb, identB], writes=[T0f], n=128)
                    R.op("act", lambda e, c=c: e.activation(out=mT[c].ap, in_=v3(T0b, 8), func=AF.Identity), reads=[T0f], writes=[mT[c]])

            prologue(0)
            gi = 0
            for r, (f0, C) in enumerate(passes):
                sl = slots[r % 2]
                g3 = v3(sl.ap[:, 0:4096], 8)
                u3 = v3(sl.ap[:, 4096:8192], 8)
                d3 = v3(sl.ap[:, 8192:12288], 4)
                last = (r == len(passes) - 1)
                for tgi in range(4):
                    if r == 0 and tgi + 1 < 4:
                        prologue(tgi + 1)
                    hs = hTb[tgi % 2]
                    for ci in range(C):
                        gbank = F[0] if gi % 2 == 0 else F[1]
                        ubank = F[2] if gi % 2 == 0 else F[3]
                        sg_ = sgt[gi % 2]
                        gi += 1
                        mg3 = v3(mTg[tgi].ap, 8)
                        for (bank, w3) in ((gbank, g3), (ubank, u3)):
                            for k in range(8):
                                R.op("pe", lambda e, bank=bank, w3=w3, ci=ci, k=k, mg3=mg3: e.matmul(
                                    bank.ap, lhsT=w3[:, k, ci * 128:(ci + 1) * 128],
                                    rhs=mg3[:, k, :], start=(k == 0), stop=(k == 7)),
                                    reads=[sl] + mT[tgi * 4:tgi * 4 + 4], writes=[bank])
                        R.op("act", lambda e, gbank=gbank, sg_=sg_: e.activation(out=sg_.ap, in_=gbank.ap, func=AF.Silu),
                             reads=[gbank], writes=[sg_])
                        R.op("dve", lambda e, ubank=ubank, sg_=sg_, hs=hs, ci=ci: e.tensor_tensor(out=hs[ci].ap, in0=sg_.ap, in1=ubank.ap, op=ALU.mult),
                             reads=[sg_, ubank], writes=[hs[ci]])
                    for t in range(4):
                        c = tgi * 4 + t
                        dbanks = (F[4], F[5]) if t % 2 == 0 else (T1f, T0f)
                        for half in range(2):
                            bank = dbanks[half]
                            for ci in range(C):
                                R.op("pe", lambda e, bank=bank, ci=ci, t=t, half=half, hs=hs, C=C, d3=d3: e.matmul(
                                    bank.ap, lhsT=hs[ci].ap[:, t * 128:(t + 1) * 128], rhs=d3[:, ci, half * 512:(half + 1) * 512],
                                    start=(ci == 0), stop=(ci == C - 1)), reads=[hs[ci], sl], writes=[bank])
                            R.op("dve", lambda e, bank=bank, c=c, half=half: e.tensor_tensor(
                                out=hbuf[c].ap[:, half * 512:(half + 1) * 512], in0=hbuf[c].ap[:, half * 512:(half + 1) * 512],
                                in1=bank.ap, op=ALU.add), reads=[bank, hbuf[c]], writes=[hbuf[c]])
                        if last:
                            R.dma("sp", out_d[c * 128:(c + 1) * 128, :], hbuf[c].ap, hbuf[c], reads=[hbuf[c]], final=True)
                if r + 2 < len(passes):
                    load_pass(r + 2)

        try:
            _record()
        except _Stop:
            pass

        R.finalize()
        _NC_CACHE['R'] = R
        print('sched sim total us:', getattr(R, 'sim_total', None))

        @block.sync
        def _(eng):
            R.emit("sp", eng)

        @block.gpsimd
        def _(eng):
            R.emit("pool", eng)

        @block.scalar
        def _(eng):
            R.emit("act", eng)

        @block.vector
        def _(eng):
            R.emit("dve", eng)

        @block.tensor
        def _(eng):
            R.emit("pe", eng)
    return nc


_NC_CACHE = {}


def _rope_tables(hf):
    l = np.arange(SEQ, dtype=np.float32)
    pos = l if hf == 0 else (np.float32(SEQ - 1) - l)
    inv_freq = (np.float32(10000.0) ** (-(np.arange(0, 64, 2, dtype=np.float32)) / np.float32(64))).astype(np.float32)
    ang = (pos[:, None] * inv_freq[None, :]).astype(np.float32)
    cos = np.cos(ang.astype(np.float64)).astype(np.float32)
    sin = np.sin(ang.astype(np.float64)).astype(np.float32)
    cs = np.concatenate([cos, cos, -sin, sin], axis=1)
    return np.ascontiguousarray(cs, dtype=np.float32)


def _const_tables():
    i = np.arange(128, dtype=np.float32)
    iota1 = np.tile((i + 1.0)[None, :], (128, 1))
    iota2 = np.tile((128.0 - i)[None, :], (128, 1))
    m = i[:, None]
    n = i[None, :]
    reluP = np.maximum(n - m, 0.0)
    reluN = np.maximum(m - n, 0.0)
    csts = np.concatenate([iota1, iota2, reluP, reluN], axis=1).astype(np.float32)
    ident = np.eye(128, dtype=np.float32)
    j = i[:, None]
    q = i[None, :]
    mprev = np.where(j >= q, 0.0, -30000.0).astype(np.float32)
    mnext = np.where(j <= q, 0.0, -30000.0).astype(np.float32)
    cstb = np.concatenate([ident, mprev, mprev, mnext, mnext], axis=1).astype(np.float32)
    return np.ascontiguousarray(csts), np.ascontiguousarray(cstb)


def _make_in_maps(x, attn_norm_w, w_in, q_norm_w, k_norm_w, attn_sink, ret_log_decay_fwd, ret_log_decay_bwd,
           ret_norm_w, w_out, ffn_norm_w, w_gate, w_up, w_down):
    x = np.asarray(x, dtype=np.float32)
    f = lambda a: np.asarray(a, dtype=np.float32)
    attn_norm_w, q_norm_w, k_norm_w, attn_sink = f(attn_norm_w)[0], f(q_norm_w)[0], f(k_norm_w)[0], f(attn_sink)[0]
    dfw, dbw = f(ret_log_decay_fwd)[0], f(ret_log_decay_bwd)[0]
    ret_norm_w, ffn_norm_w = f(ret_norm_w)[0], f(ffn_norm_w)[0]
    w_in_, w_out_, w_gate_, w_up_, w_down_ = (np.ascontiguousarray(f(a)[0]) for a in (w_in, w_out, w_gate, w_up, w_down))

    csts, cstb = _const_tables()
    fnw = np.ascontiguousarray(np.tile(ffn_norm_w[None, :], (128, 1)))
    rope_tabs = [_rope_tables(0), _rope_tables(1)]
    swap = lambda w: np.concatenate([w[32:], w[:32]])
    in_maps = []
    for c in range(8):
        b, hf = c // 2, c % 2
        xs = x[b] if hf == 0 else x[b, ::-1]
        dF, dB = (dfw, dbw) if hf == 0 else (dbw, dfw)
        cstf = np.zeros((128, NCF), dtype=np.float32)

        def put(name, row):
            a, bnd = CF[name]
            cstf[:, a:bnd] = row[None, :]
        a, bnd = CF["anwP"]
        cstf[:, a:bnd] = attn_norm_w.reshape(8, 128).T
        put("rnw", ret_norm_w)
        put("qnw", q_norm_w)
        put("qnws", swap(q_norm_w))
        put("knw", k_norm_w)
        put("knws", swap(k_norm_w))
        put("sink", attn_sink)
        put("decR", np.concatenate([dF, dB]))
        a, bnd = CF["decP"]
        for t in range(4):
            cstf[0:64, a + t] = dF[2 * t]
            cstf[64:128, a + t] = dF[2 * t + 1]
            cstf[0:64, a + 4 + t] = dB[2 * t]
            cstf[64:128, a + 4 + t] = dB[2 * t + 1]
        a, _b = CF["c127m"]
        cstf[:, a] = 127.0 - np.arange(128, dtype=np.float32)
        a, _b = CF["cm"]
        cstf[:, a] = np.arange(128, dtype=np.float32)
        in_maps.append({
            "xs": np.ascontiguousarray(xs), "cs": rope_tabs[hf], "cstf": cstf, "csts": csts, "cstb": cstb, "fnw": fnw,
            "w_in": w_in_, "w_out": w_out_, "w_gate": w_gate_, "w_up": w_up_, "w_down": w_down_,
        })
    return in_maps


def kernel(**inputs):
    in_maps = _make_in_maps(**inputs)
    if "nc" not in _NC_CACHE:
        _NC_CACHE["nc"] = build_program()
    nc = _NC_CACHE["nc"]
    res = run_bass_kernel_spmd(nc, in_maps, core_ids=list(range(8)))
    out = np.empty((4, SEQ, D_MODEL), dtype=np.float32)
    for c in range(8):
        b, hf = c // 2, c % 2
        o = np.asarray(res.results[c]["out"], dtype=np.float32)
        if hf == 0:
            out[b, 0:2048] = o
        else:
            out[b, 2048:4096] = o[::-1]
    return out
```

```python
import contextlib
import os
import sys
import numpy as np
import concourse.bass as bass
import concourse.mybir as mybir
from concourse.bass_utils import run_bass_kernel_spmd

F32 = mybir.dt.float32
BF16 = mybir.dt.bfloat16
AF = mybir.ActivationFunctionType
ALU = mybir.AluOpType
AX = mybir.AxisListType

D_MODEL = 1024
SEQ = 4096
NT_ALL = 32
NT_OWN = 16
IN_PROJ = 2816
D_FF = 2816
EPS = 1e-6
C_AQ, C_AK, C_AV, C_RQ, C_RK, C_RV, C_RG = 0, 512, 640, 768, 1280, 1792, 2304

CF = {}
_o = 0
for _n, _w in [("anwP", 8), ("rnwP", 4), ("qnw", 64), ("qnws", 64), ("knw", 64), ("knws", 64),
               ("sink", 8), ("decP", 8), ("decR", 16), ("c127m", 1), ("cm", 1)]:
    CF[_n] = (_o, _o + _w)
    _o += _w
NCF = _o
NCB = 128 + 256 + 256


class Op:
    __slots__ = ("eng", "fn", "deps", "signal", "sigval", "dma", "idx", "cost", "seg", "fence", "lat", "fin", "ph", "line", "crit", "estsrc", "prio")


class DmaTok:
    __slots__ = ("sem", "val", "op")

    def __init__(self, sem, val, op):
        self.sem = sem
        self.val = val
        self.op = op


class Buf:
    def __init__(self, name, ap=None, share=None):
        self.name = name
        self.ap = ap
        self._w = []
        self._r = []
        self.share = share
        self.sem = None
        self.cnt = 0

    @property
    def writers(self):
        return (self.share or self)._w

    @writers.setter
    def writers(self, v):
        (self.share or self)._w = v

    @property
    def readers(self):
        return (self.share or self)._r

    @readers.setter
    def readers(self, v):
        (self.share or self)._r = v

    def nfree(self):
        if self.ap is None:
            return 512
        n = 1
        for d in self.ap.shape[1:]:
            n *= d
        return n


_COST = {"dve": (0.10, 1.0 / 870), "act": (0.17, 1.0 / 1200), "pool": (0.15, 1.0 / 450), "pe": (0.02, 1.0 / 2600), "sp": (0.05, 0.0)}


class Rec:
    ENGS = ("pe", "act", "dve", "pool", "sp")

    def __init__(self, sems):
        self.lists = {e: [] for e in self.ENGS}
        self.free_sems = list(sems)
        self.engsem = {e: self.free_sems.pop() for e in ("pe", "act", "dve", "pool")}
        self.final = []
        self.nops = 0
        self.seg = 0

    def _deps(self, reads, writes, extra):
        deps = []
        for b in reads:
            for t in b.writers:
                deps.append((t, "raw"))
        for b in writes:
            for t in b.readers:
                deps.append((t, "war"))
            for t in b.writers:
                deps.append((t, "waw"))
        for t in extra:
            deps.append((t, "raw"))
        return deps

    def _new(self, eng, fn, deps, dma, cost):
        o = Op()
        o.eng = eng
        o.fn = fn
        o.deps = deps
        o.signal = False
        o.sigval = None
        o.dma = dma
        o.idx = self.nops
        self.nops += 1
        o.cost = cost
        o.seg = self.seg
        o.ph = getattr(self, "phase", "setup")
        o.line = sys._getframe(2).f_lineno if os.environ.get("KSCHEDSTAT", "") else 0
        o.fence = False
        o.lat = 0.0
        o.fin = 0.0
        o.crit = None
        o.estsrc = None
        o.prio = getattr(self, "prio", 0)
        self.lists[eng].append(o)
        return o

    def op(self, eng, fn, reads=(), writes=(), extra=(), n=None):
        if n is None:
            n = writes[0].nfree() if writes else 64
        a, b = _COST[eng]
        o = self._new(eng, fn, self._deps(reads, writes, extra), False, a + b * n)
        for bf in reads:
            bf.readers.append(o)
        for bf in writes:
            bf.writers = [o]
            bf.readers = []
        return o

    def dma(self, q, out_ap, in_ap, semowner, reads=(), writes=(), extra=(), final=False, order_after=()):
        if semowner.sem is None:
            semowner.sem = self.free_sems.pop()
        sem = semowner.sem
        semowner.cnt += 1
        fn = lambda e, out_ap=out_ap, in_ap=in_ap, sem=sem: e.dma_start(out=out_ap, in_=in_ap).then_inc(sem, 16)
        deps_ = self._deps(reads, writes, extra)
        for t_ in order_after:
            deps_.append((t_.op, "order"))
        o = self._new(q, fn, deps_, True, 0.06 if q == "sp" else 1.0)
        nel = 1
        for d in out_ap.shape:
            nel *= d
        o.lat = 2.0 + nel * 4 / 180e3
        tok = DmaTok(sem, semowner.cnt * 16, o)
        for bf in reads:
            bf.readers.append(tok)
        for bf in writes:
            bf.writers = [tok]
            bf.readers = []
        if final:
            self.final.append(tok)
        return tok

    def schedule(self):
        allops = []
        for e in self.ENGS:
            allops.extend(self.lists[e])
        allops.sort(key=lambda o: o.idx)
        preds = {}
        succs = {}
        for o in allops:
            ps = {}
            for (t, kind) in o.deps:
                if isinstance(t, DmaTok):
                    ps[id(t.op)] = (t.op, True)
                else:
                    if id(t) not in ps:
                        ps[id(t)] = (t, False)
            preds[id(o)] = list(ps.values())
            for (p, _) in ps.values():
                succs.setdefault(id(p), []).append(o)
        bl = {}
        for o in reversed(allops):
            m = 0.0
            for sc in succs.get(id(o), ()):
                m = max(m, bl[id(sc)] + (0.3 if sc.eng != o.eng else 0.03))
            bl[id(o)] = o.cost + m
        use_bl = os.environ.get("KBL", "1") == "1"
        slack = float(os.environ.get("KSLACK", "0.05"))
        free = {e: 0.0 for e in self.ENGS}
        neworder = {e: [] for e in self.ENGS}
        nseg = max(o.seg for o in allops) + 1
        for sg in range(nseg):
            ops = [o for o in allops if o.seg == sg]
            inseg = set(id(o) for o in ops)
            indeg = {}
            est = {}
            avail = {e: [] for e in self.ENGS}
            remaining = {e: 0 for e in self.ENGS}
            for o in ops:
                remaining[o.eng] += 1
                d = 0
                t0 = 0.0
                for (p, isdma) in preds[id(o)]:
                    if id(p) in inseg:
                        d += 1
                    else:
                        t0 = max(t0, p.fin + (p.lat if isdma else 0.3))
                indeg[id(o)] = d
                est[id(o)] = t0
                if d == 0:
                    avail[o.eng].append(o)
            nleft = len(ops)
            while nleft:
                best = None
                for e in self.ENGS:
                    cands = []
                    tmin = None
                    for o in avail[e]:
                        if o.fence and remaining[e] > 1:
                            continue
                        st = max(free[e], est[id(o)])
                        cands.append((st, o))
                        if tmin is None or st < tmin:
                            tmin = st
                    if not cands:
                        continue
                    if use_bl:
                        st, o = max(((st, o) for (st, o) in cands if st <= tmin + slack), key=lambda x: (bl[id(x[1])], -x[1].idx))
                    else:
                        st, o = min(cands, key=lambda x: (x[0], x[1].idx))
                    key = (st, o.idx)
                    if best is None or key < best[0]:
                        best = (key, o)
                assert best is not None, "scheduler stuck"
                o = best[1]
                st = best[0][0]
                e = o.eng
                avail[e].remove(o)
                remaining[e] -= 1
                nleft -= 1
                o.fin = st + o.cost
                o.crit = (neworder[e][-1] if (neworder[e] and free[e] >= est[id(o)]) else o.estsrc)
                free[e] = o.fin
                neworder[e].append(o)
                for sc in succs.get(id(o), ()):
                    if id(sc) not in inseg:
                        continue
                    isdma = any((p is o and dm) for (p, dm) in preds[id(sc)])
                    lat = o.lat if isdma else (0.3 if sc.eng != e else 0.03)
                    if o.fin + lat > est[id(sc)]:
                        est[id(sc)] = o.fin + lat
                        sc.estsrc = o
                    indeg[id(sc)] -= 1
                    if indeg[id(sc)] == 0:
                        avail[sc.eng].append(sc)
        self.lists = neworder
        self.sim_total = max(free.values())
        if os.environ.get("KSCHEDSTAT", "") == "1":
            stat = {}
            for o in allops:
                d = stat.setdefault(o.ph, {"end": 0.0, "start": 1e18})
                d["end"] = max(d["end"], o.fin)
                d["start"] = min(d["start"], o.fin - o.cost)
                d[o.eng] = d.get(o.eng, 0.0) + o.cost
            for ph, d in stat.items():
                print("SCHED", ph, {k: round(v, 1) for k, v in d.items()})
            cp = os.environ.get("KSCHEDCRIT", "")
            if cp:
                ph_, n_ = cp.split(",")
                tmax_ = float(os.environ.get("KSCHEDTMAX", "1e18"))
                cur = max((o for o in allops if o.ph == ph_ and o.fin <= tmax_), key=lambda o: o.fin)
                chain = []
                while cur is not None and len(chain) < int(n_):
                    chain.append(cur)
                    cur = cur.crit
                for o in reversed(chain):
                    print("CRIT %8.2f %6.2f %-4s L%d%s" % (o.fin - o.cost, o.cost, o.eng, o.line, " dma" if o.dma else ""))
            w = os.environ.get("KSCHEDWIN", "")
            if w:
                a_, b_ = (float(x) for x in w.split(","))
                sel = [o for o in allops if a_ <= o.fin - o.cost < b_]
                sel.sort(key=lambda o: o.fin - o.cost)
                for o in sel:
                    print("OP %8.2f %6.2f %-4s L%d%s" % (o.fin - o.cost, o.cost, o.eng, o.line, " dma" if o.dma else ""))

    def finalize(self):
        if os.environ.get("KNOSCHED", "") != "1":
            self.schedule()
        for e in self.ENGS:
            for o in self.lists[e]:
                for (t, kind) in o.deps:
                    if kind == "order":
                        continue
                    if isinstance(t, Op):
                        if t.eng != o.eng or o.dma or o.eng in ("act", "dve", "pool"):
                            t.signal = True
        for e in self.ENGS:
            cnt = 0
            for o in self.lists[e]:
                if o.signal:
                    cnt += 1
                    o.sigval = cnt

    def emit(self, e, eng):
        waited = {}
        for o in self.lists[e]:
            need = {}
            for (t, kind) in o.deps:
                if kind == "order":
                    continue
                if isinstance(t, DmaTok):
                    key, val = t.sem, t.val
                else:
                    if t.eng == e and not (o.dma or e in ("act", "dve", "pool")):
                        continue
                    key, val = self.engsem[t.eng], t.sigval
                if need.get(key, (None, 0))[1] < val:
                    need[key] = (key, val)
            for key, val in need.values():
                if waited.get(key, 0) < val:
                    eng.wait_ge(key, val)
                    waited[key] = val
            ins = o.fn(eng)
            if o.signal:
                ins.then_inc(self.engsem[e], 1)
        if e == "sp":
            for t in self.final:
                if waited.get(t.sem, 0) < t.val:
                    eng.wait_ge(t.sem, t.val)
                    waited[t.sem] = t.val


def v3(ap, a):
    return ap.rearrange("p (a b) -> p a b", a=a)


class _Stop(Exception):
    pass


def build_program():
    nc = bass.Bass("TRN2", target_bir_lowering=False)
    KSTOP = os.environ.get("KSTOP", "")
    xs = nc.dram_tensor("xs", [SEQ, D_MODEL], F32, kind="ExternalInput").ap()
    cs_d = nc.dram_tensor("cs", [SEQ, 128], F32, kind="ExternalInput").ap()
    cstf_d = nc.dram_tensor("cstf", [128, NCF], F32, kind="ExternalInput").ap()
    csts_d = nc.dram_tensor("csts", [128, 512], F32, kind="ExternalInput").ap()
    cstb_d = nc.dram_tensor("cstb", [128, NCB], F32, kind="ExternalInput").ap()
    fnw_d = nc.dram_tensor("fnw", [128, D_MODEL], F32, kind="ExternalInput").ap()
    w_in_d = nc.dram_tensor("w_in", [D_MODEL, IN_PROJ], F32, kind="ExternalInput").ap()
    w_out_d = nc.dram_tensor("w_out", [D_MODEL, D_MODEL], F32, kind="ExternalInput").ap()
    w_gate_d = nc.dram_tensor("w_gate", [D_MODEL, D_FF], F32, kind="ExternalInput").ap()
    w_up_d = nc.dram_tensor("w_up", [D_MODEL, D_FF], F32, kind="ExternalInput").ap()
    w_down_d = nc.dram_tensor("w_down", [D_FF, D_MODEL], F32, kind="ExternalInput").ap()
    out_d = nc.dram_tensor("out", [NT_OWN * 128, D_MODEL], F32, kind="ExternalOutput").ap()

    NB16 = 26916
    NF32 = 7488
    with contextlib.ExitStack() as es:
        Wt = es.enter_context(nc.sbuf_tensor("Wt", [128, 30720], BF16))
        Ht = es.enter_context(nc.sbuf_tensor("Ht", [128, NT_OWN * 1024], F32))
        ABt = es.enter_context(nc.sbuf_tensor("ABt", [128, NB16], BF16))
        AFt = es.enter_context(nc.sbuf_tensor("AFt", [128, NF32], F32))
        banks = [es.enter_context(nc.psum_tensor(f"pb{i}", [128, 512], F32)) for i in range(8)]
        sems = [es.enter_context(nc.semaphore(f"s{i}")) for i in range(60)]
        block = es.enter_context(nc.Block())
        R = Rec(sems)

        class Arena:
            def __init__(self, t, n):
                self.t, self.n, self.off = t, n, 0

            def alloc(self, name, n):
                assert self.off + n <= self.n, (name, self.off, n, self.n)
                ap = self.t[:, self.off:self.off + n]
                self.off += n
                return Buf(name, ap)

        AB = Arena(ABt, NB16)
        AFa = Arena(AFt, NF32)

        W_in = Buf("w_in", v3(Wt[:, 0:22528], 8))
        W_out = Buf("w_out", v3(Wt[:, 22528:30720], 8))
        hbuf = [Buf(f"h{c}", Ht[:, c * 1024:(c + 1) * 1024]) for c in range(NT_OWN)]
        kvf = [Buf(f"kvf{c}", v3(Ht[:, 12 * 1024 + c * 256: 12 * 1024 + (c + 1) * 256], 4)) for c in range(NT_OWN)]

        F = [Buf(f"F{i}", banks[i][:, :]) for i in range(6)]
        T0 = Buf("T0", banks[6][:, :])
        T1a = Buf("T1a", banks[7][:, 0:256])
        T1b = Buf("T1b", banks[7][:, 256:512], share=T1a)
        T0b = banks[6][:, :].bitcast(BF16)
        T1ab = banks[7][:, 0:256].bitcast(BF16)
        T1bb = banks[7][:, 256:512].bitcast(BF16)

        cstb = AB.alloc("cstb", NCB)
        ident = cstb.ap[:, 0:128]
        mprev = cstb.ap[:, 128:384]
        mnext = cstb.ap[:, 384:640]
        RbS = [AB.alloc(f"RbS{c}", 256) for c in range(NT_OWN)]
        RfS = [AB.alloc(f"RfS{c}", 256) for c in range(NT_OWN)]
        KT = [AB.alloc(f"KT{c}", 256) for c in range(NT_OWN + 1)]
        Vb = [AB.alloc(f"V{c}", 130) for c in range(NT_OWN + 1)]
        xb = AB.alloc("xb", 1024)
        xT = AB.alloc("xT", 1024)
        qb = AB.alloc("qb", 512)
        rqb = AB.alloc("rqb", 512)
        rkb = AB.alloc("rkb", 512)
        rvb = AB.alloc("rvb", 512)
        kdup = AB.alloc("kdup", 256)
        qT = AB.alloc("qT", 512)
        rqT = AB.alloc("rqT", 512)
        rqxf = AB.alloc("rqxf", 512)
        rqxb = AB.alloc("rqxb", 512)
        rkT = AB.alloc("rkT", 512)
        PT = AB.alloc("PT", 768)
        PT2 = AB.alloc("PT2", 768)
        retS = AB.alloc("retS", 1024)
        yb = AB.alloc("yb", 1024)
        yT = AB.alloc("yT", 1024)
        kzb, kzf = qb, rqb

        cstf = AFa.alloc("cstf", NCF)

        def cf(name):
            a, b = CF[name]
            return cstf.ap[:, a:b]

        Xif = AFa.alloc("Xif", 512)
        Xib = AFa.alloc("Xib", 512)
        csl = [AFa.alloc(f"cs{i}", 128) for i in range(2)]
        tabq = AFa.alloc("tabq", 128)
        ta = AFa.alloc("ta", 512)
        DT = AFa.alloc("DT", 1024)
        td = AFa.alloc("td", 512)
        te = AFa.alloc("te", 512)
        tg = AFa.alloc("tg", 512)
        tf = AFa.alloc("tf", 512)
        akf = Buf("akf", tf.ap[:, 0:128])
        tk1 = Buf("tk1", tf.ap[:, 128:256])
        tk2 = Buf("tk2", tf.ap[:, 256:384])
        tabk = Buf("tabk", tf.ap[:, 384:512])
        Rb = AFa.alloc("Rb", 256)
        Rf = AFa.alloc("Rf", 256)
        rtmp = AFa.alloc("rtmp", 256)
        lgP = AFa.alloc("lgP", 8)
        lgR = AFa.alloc("lgR", 16)
        g128 = AFa.alloc("g128", 8)
        Zfb = AFa.alloc("Zfb", 16)
        esink = AFa.alloc("esink", 8)
        st = [AFa.alloc(f"st{i}", 8) for i in range(2)]
        s8a = AFa.alloc("s8a", 8)
        s8g = AFa.alloc("s8g", 8)
        s8h = AFa.alloc("s8h", 8)
        tb2 = AFa.alloc("tb2", 512)
        junk = AFa.alloc("junk", 512)
        junk_ap = junk.ap.bitcast(BF16)
        s8b = AFa.alloc("s8b", 8)
        s8c = AFa.alloc("s8c", 8)
        s8d = AFa.alloc("s8d", 8)
        s8e = AFa.alloc("s8e", 8)
        s8f = AFa.alloc("s8f", 8)
        epsb = AFa.alloc("epsb", 1)
        onesb = AFa.alloc("onesb", 1)
        fsc = AFa.alloc("fsc", 4)
        ssq2 = AFa.alloc("ssq2", 16)
        lnr2 = AFa.alloc("lnr2", 16)
        rstd2 = AFa.alloc("rstd2", 16)

        if os.environ.get("KSCHEDSTAT", ""):
            print("arena use: bf16", AB.off, "/", NB16, " f32", AFa.off, "/", NF32)
        W_in_v = w_in_d.rearrange("(k p) n -> p k n", p=128)
        W_out_v = w_out_d.rearrange("(k p) n -> p k n", p=128)

        def _record():
            R.dma("sp", cstf.ap, cstf_d, cstf, writes=[cstf])
            R.dma("sp", ta.ap, csts_d, ta, writes=[ta])
            R.dma("pool", cstb.ap, cstb_d, cstb, writes=[cstb])
            wcols = [("rkrv", C_RK, C_RG), ("akav", C_AK, C_RQ), ("aq", C_AQ, C_AK), ("rq", C_RQ, C_RK), ("rg", C_RG, IN_PROJ)]
            Wg_buf = {}
            prev_w = None
            for name, a, b in wcols:
                bb = Buf("w_in_" + name)
                Wg_buf[name] = bb
                ex = [prev_w] if prev_w else []
                tk_ = None
                for k in range(8):
                    tk_ = R.dma("pool", W_in.ap[:, k, a:b], W_in_v[:, k, a:b], bb, writes=([bb] if k == 0 else []), extra=ex,
                                order_after=([tk_] if tk_ else []))
                bb.writers = [tk_]
                prev_w = tk_
            tk_ = None
            for k in range(8):
                tk_ = R.dma("pool", W_out.ap[:, k, :], W_out_v[:, k, :], W_out, writes=([W_out] if k == 0 else []), extra=[prev_w],
                            order_after=([tk_] if tk_ else []))
            W_out.writers = [tk_]
            for k in range(4):
                R.op("dve", lambda e, k=k: e.tensor_scalar(out=W_out.ap[:, 4 + k, :], in0=W_out.ap[:, 4 + k, :],
                                                           scalar1=cf("rnwP")[:, k:k + 1], scalar2=None, op0=ALU.mult),
                     reads=[cstf, W_out], writes=[W_out], n=340)
            for name, a, b in wcols:
                for k in range(8):
                    R.op("dve", lambda e, k=k, a=a, b=b: e.tensor_scalar(out=W_in.ap[:, k, a:b], in0=W_in.ap[:, k, a:b],
                                                                         scalar1=cf("anwP")[:, k:k + 1], scalar2=None, op0=ALU.mult),
                         reads=[cstf, Wg_buf[name]], writes=[Wg_buf[name]], n=(b - a) // 3)

            iota1 = ta.ap[:, 0:128]
            iota2 = ta.ap[:, 128:256]
            reluP = ta.ap[:, 256:384]
            reluN = ta.ap[:, 384:512]

            R.op("act", lambda e: e.activation(out=lgP.ap, in_=cf("decP"), func=AF.Abs), reads=[cstf], writes=[lgP])
            R.op("act", lambda e: e.activation(out=lgR.ap, in_=cf("decR"), func=AF.Abs), reads=[cstf], writes=[lgR])
            R.op("dve", lambda e: e.tensor_scalar(out=lgP.ap, in0=lgP.ap, scalar1=-1.0, scalar2=None, op0=ALU.mult), reads=[lgP], writes=[lgP])
            R.op("dve", lambda e: e.tensor_scalar(out=lgR.ap, in0=lgR.ap, scalar1=-1.0, scalar2=None, op0=ALU.mult), reads=[lgR], writes=[lgR])
            R.op("act", lambda e: e.activation(out=g128.ap, in_=lgP.ap, func=AF.Exp, scale=128.0), reads=[lgP], writes=[g128])
            for t in range(4):
                R.op("act", lambda e, t=t: e.activation(out=Xif.ap[:, t * 128:(t + 1) * 128], in_=iota1, func=AF.Exp,
                                                        scale=lgP.ap[:, t:t + 1]), reads=[lgP, ta], writes=[Xif])
                R.op("act", lambda e, t=t: e.activation(out=Xib.ap[:, t * 128:(t + 1) * 128], in_=iota2, func=AF.Exp,
                                                        scale=lgP.ap[:, 4 + t:5 + t]), reads=[lgP, ta], writes=[Xib])
            R.op("act", lambda e: e.activation(out=Zfb.ap[:, 0:8], in_=lgR.ap[:, 0:8], func=AF.Exp, scale=cf("c127m")),
                 reads=[lgR, cstf], writes=[Zfb])
            R.op("act", lambda e: e.activation(out=Zfb.ap[:, 8:16], in_=lgR.ap[:, 8:16], func=AF.Exp, scale=cf("cm")),
                 reads=[lgR, cstf], writes=[Zfb])
            R.op("act", lambda e: e.activation(out=esink.ap, in_=cf("sink"), func=AF.Exp), reads=[cstf], writes=[esink])
            for h in range(8):
                R.op("dve", lambda e, h=h: e.tensor_scalar(out=te.ap[:, 0:128], in0=reluP, scalar1=lgR.ap[:, h:h + 1],
                                                           scalar2=None, op0=ALU.mult), reads=[ta, lgR], writes=[te])
                R.op("dve", lambda e, h=h: e.scalar_tensor_tensor(out=td.ap[:, 0:128], in0=reluN, scalar=lgR.ap[:, 8 + h:9 + h],
                                                                  in1=te.ap[:, 0:128], op0=ALU.mult, op1=ALU.add),
                     reads=[ta, lgR, te], writes=[td])
                R.op("act", lambda e, h=h: e.activation(out=DT.ap[:, h * 128:(h + 1) * 128], in_=td.ap[:, 0:128], func=AF.Exp),
                     reads=[td], writes=[DT])
            for c in range(NT_OWN + 1):
                R.op("pool", lambda e, c=c: e.memset(v3(Vb[c].ap, 2)[:, :, 64:65], 1.0), writes=[Vb[c]])
            R.op("pool", lambda e: e.memset(Rb.ap, 0.0), writes=[Rb])
            R.op("pool", lambda e: e.memset(Rf.ap, 0.0), writes=[Rf])
            R.op("pool", lambda e: e.memset(epsb.ap, EPS), writes=[epsb])
            R.op("pool", lambda e: e.memset(onesb.ap, 1.0), writes=[onesb])

            if KSTOP == "setup":
                raise _Stop()
            def load_tile(L, xbuf, slot):
                R.dma("sp", xbuf.ap, xs[L * 128:(L + 1) * 128, :], xbuf, writes=[xbuf])
                R.dma("sp", csl[slot].ap, cs_d[L * 128:(L + 1) * 128, :], csl[slot], writes=[csl[slot]])

            def prep_tile(xbuf, slot, xb=xb, xT=xT):
                s = st[slot]
                R.op("act", lambda e: e.activation(out=junk_ap, in_=xbuf.ap, func=AF.Square, accum_out=s.ap[:, 0:1]),
                     reads=[xbuf], writes=[junk, s], n=1024)
                R.op("act", lambda e: e.activation(out=s.ap[:, 1:2], in_=s.ap[:, 0:1], func=AF.Ln, scale=1.0 / D_MODEL, bias=epsb.ap),
                     reads=[s, epsb], writes=[s])
                R.op("act", lambda e: e.activation(out=s.ap[:, 2:3], in_=s.ap[:, 1:2], func=AF.Exp, scale=-0.5), reads=[s], writes=[s])
                R.op("dve", lambda e: e.tensor_scalar(out=s.ap[:, 5:6], in0=s.ap[:, 2:3], scalar1=-1.0, scalar2=None,
                                                      op0=ALU.mult), reads=[s], writes=[s])
                R.op("dve", lambda e: e.tensor_scalar(out=s.ap[:, 3:4], in0=s.ap[:, 2:3], scalar1=0.125, scalar2=None,
                                                      op0=ALU.mult), reads=[s], writes=[s])
                R.op("dve", lambda e: e.tensor_scalar(out=s.ap[:, 4:5], in0=s.ap[:, 2:3], scalar1=0.5, scalar2=None,
                                                      op0=ALU.mult), reads=[s], writes=[s])
                R.op("dve", lambda e: e.tensor_copy(out=xb.ap, in_=xbuf.ap), reads=[xbuf], writes=[xb], n=600)
                for k in range(8):
                    R.op("pe", lambda e, k=k: e.transpose(out=T0b[:, k * 128:(k + 1) * 128], in_=xb.ap[:, k * 128:(k + 1) * 128],
                                                          identity=ident), reads=[xb, cstb], writes=[T0], n=128)
                R.op("act", lambda e: e.activation(out=xT.ap, in_=T0b, func=AF.Identity), reads=[T0], writes=[xT])
                return s

            def inproj(bank, c0, n, wnames, xT=xT, also=()):
                xT3 = v3(xT.ap, 8)
                for k in range(8):
                    R.op("pe", lambda e, k=k: e.matmul(bank.ap[:, 0:n], lhsT=xT3[:, k, :], rhs=W_in.ap[:, k, c0:c0 + n],
                                                       start=(k == 0), stop=(k == 7)),
                         reads=[xT] + [Wg_buf[w] for w in wnames], writes=[bank] + list(also), n=n)

            def rope(eng_a, eng_b, src, dst_t1, dst_t2, A, B, H):
                s3 = v3(src.ap[:, 0:H * 64], H)
                t13 = v3(dst_t1.ap[:, 0:H * 64], H)
                t23 = v3(dst_t2.ap[:, 0:H * 64], H)
                Ab = A.unsqueeze(1).broadcast_to([128, H, 64])
                Bl = B[:, 0:32].unsqueeze(1).broadcast_to([128, H, 32])
                Bh = B[:, 32:64].unsqueeze(1).broadcast_to([128, H, 32])
                return s3, t13, t23, Ab, Bl, Bh

            R.phase = "pre"
            xring = [hbuf[4], hbuf[5], hbuf[6]]
            order = list(range(NT_ALL - 1, -1, -1))
            load_tile(order[0], xring[0], 0)
            for i, L in enumerate(order):
                if KSTOP == "pre1" and i == 1:
                    raise _Stop()
                xbuf = xring[i % 3]
                slot = i % 2
                if i + 1 < len(order):
                    load_tile(order[i + 1], xring[(i + 1) % 3], (i + 1) % 2)
                own = L < NT_OWN
                needkv = L <= NT_OWN
                par = i % 2
                xb_ = (xb, retS)[par]
                xT_ = (xT, yT)[par]
                ta_ = (ta, tg)[par]
                rvb_ = (rvb, rkb)[par]
                kzb_ = (qb, rqxf)[par]
                kzf_ = (rqb, rqxb)[par]
                Brk = (F[0], F[3])[par]
                Brv = (F[1], F[4])[par]
                Bkv = (F[2], F[5])[par]
                s = prep_tile(xbuf, slot, xb_, xT_)
                cst = csl[slot]
                cosv = cst.ap[:, 0:64]
                sinv = cst.ap[:, 64:128]
                inproj(Brk, C_RK, 512, ["rkrv"], xT_)
                inproj(Brv, C_RV, 512, ["rkrv"], xT_)
                if needkv:
                    inproj(T1b, C_AK, 256, ["akav"], xT_, also=[T1a])
                R.op("act", lambda e, s=s, ta_=ta_, Brk=Brk: e.activation(out=ta_.ap, in_=Brk.ap, func=AF.Identity, scale=s.ap[:, 3:4]),
                     reads=[Brk, s], writes=[ta_])
                R.op("act", lambda e, s=s, rvb_=rvb_, Brv=Brv: e.activation(out=rvb_.ap, in_=Brv.ap, func=AF.Identity, scale=s.ap[:, 2:3]),
                     reads=[Brv, s], writes=[rvb_])
                s3, t13, t23, Ab, Bl, Bh = rope(None, None, ta_, td, te, cosv, sinv, 8)
                R.op("dve", lambda e, s3=s3, t13=t13, Ab=Ab: e.tensor_tensor(out=t13, in0=s3, in1=Ab, op=ALU.mult),
                     reads=[ta_, cst], writes=[td])
                R.op("pool", lambda e, s3=s3, t23=t23, Bl=Bl: e.tensor_tensor(out=t23[:, :, 0:32], in0=s3[:, :, 32:64], in1=Bl, op=ALU.mult),
                     reads=[ta_, cst], writes=[te])
                R.op("pool", lambda e, s3=s3, t23=t23, Bh=Bh: e.tensor_tensor(out=t23[:, :, 32:64], in0=s3[:, :, 0:32], in1=Bh, op=ALU.mult),
                     reads=[ta_, cst], writes=[te])
                R.op("dve", lambda e: e.tensor_tensor(out=td.ap, in0=td.ap, in1=te.ap, op=ALU.add), reads=[td, te], writes=[td])
                Zb_b = Zfb.ap[:, 8:16].unsqueeze(2).broadcast_to([128, 8, 64])
                Zf_b = Zfb.ap[:, 0:8].unsqueeze(2).broadcast_to([128, 8, 64])
                R.op("dve", lambda e, Zb_b=Zb_b, kzb_=kzb_: e.tensor_tensor(out=v3(kzb_.ap, 8), in0=v3(td.ap, 8), in1=Zb_b, op=ALU.mult),
                     reads=[td, Zfb], writes=[kzb_])
                if own:
                    R.op("pool", lambda e, Zf_b=Zf_b, kzf_=kzf_: e.tensor_tensor(out=v3(kzf_.ap, 8), in0=v3(td.ap, 8), in1=Zf_b, op=ALU.mult),
                         reads=[td, Zfb], writes=[kzf_])
                for t in range(4):
                    R.op("pe", lambda e, t=t, Bkv=Bkv, kzb_=kzb_, rvb_=rvb_: e.matmul(Bkv.ap[:, t * 128:(t + 1) * 128], lhsT=kzb_.ap[:, t * 128:(t + 1) * 128],
                                                       rhs=rvb_.ap[:, t * 128:(t + 1) * 128], start=True, stop=True),
                         reads=[kzb_, rvb_], writes=[Bkv], n=128)
                if own:
                    for t in range(4):
                        R.op("pe", lambda e, t=t, Brk=Brk, kzf_=kzf_, rvb_=rvb_: e.matmul(Brk.ap[:, t * 128:(t + 1) * 128], lhsT=kzf_.ap[:, t * 128:(t + 1) * 128],
                                                           rhs=rvb_.ap[:, t * 128:(t + 1) * 128], start=True, stop=True),
                             reads=[kzf_, rvb_], writes=[Brk], n=128)
                Rb3 = v3(Rb.ap, 4)
                rt3 = v3(rtmp.ap, 4)
                F33 = v3(Bkv.ap, 4)
                F43 = v3(Brk.ap, 4)
                if own:
                    R.op("dve", lambda e, L=L: e.tensor_copy(out=RbS[L].ap, in_=Rb.ap), reads=[Rb], writes=[RbS[L]])
                gb_b = g128.ap[:, 4:8].unsqueeze(2).broadcast_to([128, 4, 64])
                R.op("dve", lambda e, gb_b=gb_b, Rb3=Rb3, rt3=rt3: e.tensor_tensor(out=rt3, in0=Rb3, in1=gb_b, op=ALU.mult),
                     reads=[Rb, g128], writes=[rtmp])
                R.op("dve", lambda e, Rb3=Rb3, rt3=rt3, F33=F33: e.tensor_tensor(out=Rb3[0:64], in0=rt3[0:64], in1=F33[0:64, :, 0:64], op=ALU.add),
                     reads=[rtmp, Bkv], writes=[Rb])
                R.op("dve", lambda e, Rb3=Rb3, rt3=rt3, F33=F33: e.tensor_tensor(out=Rb3[64:128], in0=rt3[64:128], in1=F33[64:128, :, 64:128], op=ALU.add),
                     reads=[rtmp, Bkv, Rb], writes=[Rb])
                if own:
                    R.op("act", lambda e, L=L, F43=F43: e.activation(out=kvf[L].ap[0:64], in_=F43[0:64, :, 0:64], func=AF.Identity),
                         reads=[Brk], writes=[kvf[L]])
                    R.op("act", lambda e, L=L, F43=F43: e.activation(out=kvf[L].ap[64:128], in_=F43[64:128, :, 64:128], func=AF.Identity),
                         reads=[Brk, kvf[L]], writes=[kvf[L]])
                if needkv:
                    R.op("act", lambda e, s=s: e.activation(out=akf.ap, in_=T1b.ap[:, 0:128], func=AF.Identity, scale=s.ap[:, 2:3]),
                         reads=[T1b, s], writes=[akf])
                    R.op("act", lambda e, s=s, L=L: e.activation(out=v3(Vb[L].ap, 2)[:, :, 0:64], in_=v3(T1b.ap[:, 128:256], 2),
                                                                  func=AF.Identity, scale=s.ap[:, 2:3]),
                         reads=[T1b, s, Vb[L]], writes=[Vb[L]])
                    R.op("dve", lambda e: e.tensor_tensor(out=tk1.ap, in0=akf.ap, in1=akf.ap, op=ALU.mult), reads=[akf], writes=[tk1])
                    R.op("dve", lambda e: e.tensor_reduce(out=s8a.ap[:, 0:2], in_=v3(tk1.ap, 2), axis=AX.X, op=ALU.add),
                         reads=[tk1], writes=[s8a])
                    R.op("act", lambda e: e.activation(out=s8a.ap[:, 2:4], in_=s8a.ap[:, 0:2], func=AF.Ln, scale=1.0 / 64, bias=epsb.ap),
                         reads=[s8a, epsb], writes=[s8a])
                    R.op("act", lambda e: e.activation(out=s8a.ap[:, 4:6], in_=s8a.ap[:, 2:4], func=AF.Exp, scale=-0.5), reads=[s8a], writes=[s8a])
                    R.op("pool", lambda e, cosv=cosv: e.tensor_tensor(out=tabk.ap[:, 0:64], in0=cosv, in1=cf("knw"), op=ALU.mult),
                         reads=[cst, cstf], writes=[tabk])
                    R.op("pool", lambda e, sinv=sinv: e.tensor_tensor(out=tabk.ap[:, 64:128], in0=sinv, in1=cf("knws"), op=ALU.mult),
                         reads=[cst, cstf, tabk], writes=[tabk])
                    s3, t13, t23, Ab, Bl, Bh = rope(None, None, akf, tk1, tk2, tabk.ap[:, 0:64], tabk.ap[:, 64:128], 2)
                    R.op("pool", lambda e, s3=s3, t13=t13, Ab=Ab: e.tensor_tensor(out=t13, in0=s3, in1=Ab, op=ALU.mult),
                         reads=[akf, tabk, s8a], writes=[tk1])
                    R.op("pool", lambda e, s3=s3, t23=t23, Bl=Bl: e.tensor_tensor(out=t23[:, :, 0:32], in0=s3[:, :, 32:64], in1=Bl, op=ALU.mult),
                         reads=[akf, tabk], writes=[tk2])
                    R.op("pool", lambda e, s3=s3, t23=t23, Bh=Bh: e.tensor_tensor(out=t23[:, :, 32:64], in0=s3[:, :, 0:32], in1=Bh, op=ALU.mult),
                         reads=[akf, tabk, tk2], writes=[tk2])
                    R.op("dve", lambda e: e.tensor_tensor(out=tk1.ap, in0=tk1.ap, in1=tk2.ap, op=ALU.add), reads=[tk1, tk2], writes=[tk1])
                    kd4 = kdup.ap.rearrange("p (g u d) -> p g u d", g=2, u=2)
                    rk2b = s8a.ap[:, 4:6].unsqueeze(2).broadcast_to([128, 2, 64])
                    for u in range(2):
                        R.op("dve", lambda e, u=u, kd4=kd4, rk2b=rk2b: e.tensor_tensor(out=kd4[:, :, u, :], in0=v3(tk1.ap, 2), in1=rk2b, op=ALU.mult),
                             reads=[tk1, s8a, kdup], writes=[kdup])
                    for g in range(2):
                        R.op("pe", lambda e, g=g: e.transpose(out=T1ab[:, g * 128:(g + 1) * 128], in_=kdup.ap[:, g * 128:(g + 1) * 128],
                                                              identity=ident), reads=[kdup, cstb], writes=[T1a, T1b], n=128)
                    R.op("act", lambda e, L=L: e.activation(out=KT[L].ap, in_=T1ab[:, 0:256], func=AF.Identity), reads=[T1a], writes=[KT[L]])

            if KSTOP == "pre":
                raise _Stop()
            Rf3 = v3(Rf.ap, 4)
            rt3 = v3(rtmp.ap, 4)
            gf_b = g128.ap[:, 0:4].unsqueeze(2).broadcast_to([128, 4, 64])
            scan_last = None
            for c in range(NT_OWN):
                R.op("dve", lambda e, c=c: e.tensor_copy(out=RfS[c].ap, in_=Rf.ap), reads=[Rf], writes=[RfS[c]])
                R.op("dve", lambda e: e.tensor_tensor(out=rt3, in0=Rf3, in1=gf_b, op=ALU.mult), reads=[Rf, g128], writes=[rtmp])
                scan_last = R.op("dve", lambda e, c=c: e.tensor_tensor(out=Rf3, in0=rt3, in1=kvf[c].ap, op=ALU.add),
                                 reads=[rtmp, kvf[c]], writes=[Rf])

            if KSTOP == "scan":
                raise _Stop()
            R.phase = "main"

            def load_main(c):
                extra = [scan_last] if c >= 12 else []
                R.dma("sp", hbuf[c].ap, xs[c * 128:(c + 1) * 128, :], hbuf[c], writes=[hbuf[c]], extra=extra)
                R.dma("sp", csl[c % 2].ap, cs_d[c * 128:(c + 1) * 128, :], csl[c % 2], writes=[csl[c % 2]])

            load_main(0)
            for c in range(NT_OWN):
                if KSTOP == "main1" and c == 1:
                    raise _Stop()
                xbuf = hbuf[c]
                slot = c % 2
                s = prep_tile(xbuf, slot)
                cst = csl[slot]
                cosv = cst.ap[:, 0:64]
                sinv = cst.ap[:, 64:128]
                R.op("pool", lambda e, cosv=cosv: e.tensor_tensor(out=tabq.ap[:, 0:64], in0=cosv, in1=cf("qnw"), op=ALU.mult),
                     reads=[cst, cstf], writes=[tabq])
                R.op("pool", lambda e, sinv=sinv: e.tensor_tensor(out=tabq.ap[:, 64:128], in0=sinv, in1=cf("qnws"), op=ALU.mult),
                     reads=[cst, cstf, tabq], writes=[tabq])
                BQ, BG = (F[4], F[0]) if os.environ.get("KSWAP", "0") == "1" else (F[0], F[4])
                inproj(BQ, C_AQ, 512, ["aq"])
                inproj(F[1], C_RQ, 512, ["rq"])
                inproj(F[2], C_RK, 512, ["rkrv"])
                inproj(F[3], C_RV, 512, ["rkrv"])
                inproj(BG, C_RG, 512, ["rg"])
                if c + 1 < NT_OWN:
                    load_main(c + 1)
                R.op("act", lambda e, s=s: e.activation(out=ta.ap, in_=BQ.ap, func=AF.Identity, scale=s.ap[:, 2:3]),
                     reads=[BQ, s], writes=[ta])
                R.op("dve", lambda e: e.tensor_tensor(out=td.ap, in0=ta.ap, in1=ta.ap, op=ALU.mult), reads=[ta], writes=[td])
                R.op("dve", lambda e: e.tensor_reduce(out=s8a.ap, in_=v3(td.ap, 8), axis=AX.X, op=ALU.add), reads=[td], writes=[s8a])
                R.op("act", lambda e: e.activation(out=s8b.ap, in_=s8a.ap, func=AF.Ln, scale=1.0 / 64, bias=epsb.ap),
                     reads=[s8a, epsb], writes=[s8b])
                R.op("act", lambda e: e.activation(out=s8c.ap, in_=s8b.ap, func=AF.Exp, scale=-0.5), reads=[s8b], writes=[s8c])
                s3, t13, t23, Ab, Bl, Bh = rope(None, None, ta, td, te, tabq.ap[:, 0:64], tabq.ap[:, 64:128], 8)
                R.op("dve", lambda e, s3=s3, t13=t13, Ab=Ab: e.tensor_tensor(out=t13, in0=s3, in1=Ab, op=ALU.mult),
                     reads=[ta, tabq, s8a], writes=[td])
                R.op("pool", lambda e, s3=s3, t23=t23, Bl=Bl: e.tensor_tensor(out=t23[:, :, 0:32], in0=s3[:, :, 32:64], in1=Bl, op=ALU.mult),
                     reads=[ta, tabq], writes=[te])
                R.op("pool", lambda e, s3=s3, t23=t23, Bh=Bh: e.tensor_tensor(out=t23[:, :, 32:64], in0=s3[:, :, 0:32], in1=Bh, op=ALU.mult),
                     reads=[ta, tabq, te], writes=[te])
                R.op("dve", lambda e: e.tensor_tensor(out=td.ap, in0=td.ap, in1=te.ap, op=ALU.add), reads=[td, te], writes=[td])
                rq8b = s8c.ap.unsqueeze(2).broadcast_to([128, 8, 64])
                R.op("dve", lambda e, rq8b=rq8b: e.tensor_tensor(out=v3(qb.ap, 8), in0=v3(td.ap, 8), in1=rq8b, op=ALU.mult),
                     reads=[td, s8c], writes=[qb])
                for t in range(4):
                    R.op("pe", lambda e, t=t: e.transpose(out=T1ab[:, t * 128:(t + 1) * 128], in_=qb.ap[:, t * 128:(t + 1) * 128],
                                                          identity=ident), reads=[qb, cstb], writes=[T1a], n=128)
                R.op("act", lambda e: e.activation(out=qT.ap, in_=T1ab, func=AF.Identity), reads=[T1a], writes=[qT])
                if KSTOP == "m_q1":
                    raise _Stop()
                R.op("act", lambda e, s=s: e.activation(out=tb2.ap, in_=F[1].ap, func=AF.Identity, scale=s.ap[:, 2:3]),
                     reads=[F[1], s], writes=[tb2] + ([akf, tk1, tk2, tabk] if c == 0 else []), n=512)
                s3, t13, t23, Ab, Bl, Bh = rope(None, None, tb2, td, te, cosv, sinv, 8)
                R.op("dve", lambda e, s3=s3, t13=t13, Ab=Ab: e.tensor_tensor(out=t13, in0=s3, in1=Ab, op=ALU.mult),
                     reads=[tb2, cst], writes=[td])
                R.op("pool", lambda e, s3=s3, t23=t23, Bl=Bl: e.tensor_tensor(out=t23[:, :, 0:32], in0=s3[:, :, 32:64], in1=Bl, op=ALU.mult),
                     reads=[tb2, cst], writes=[te])
                R.op("pool", lambda e, s3=s3, t23=t23, Bh=Bh: e.tensor_tensor(out=t23[:, :, 32:64], in0=s3[:, :, 0:32], in1=Bh, op=ALU.mult),
                     reads=[tb2, cst, te], writes=[te])
                R.op("dve", lambda e: e.tensor_tensor(out=rqb.ap, in0=td.ap, in1=te.ap, op=ALU.add), reads=[td, te], writes=[rqb])
                for t in range(4):
                    R.op("pe", lambda e, t=t: e.transpose(out=T1bb[:, t * 128:(t + 1) * 128], in_=rqb.ap[:, t * 128:(t + 1) * 128],
                                                          identity=ident), reads=[rqb, cstb], writes=[T1b], n=128)
                R.op("act", lambda e: e.activation(out=rqT.ap, in_=T1bb, func=AF.Identity), reads=[T1b], writes=[rqT])
                R.op("dve", lambda e: e.tensor_tensor(out=rqxf.ap, in0=rqT.ap, in1=Xif.ap, op=ALU.mult), reads=[rqT, Xif], writes=[rqxf])
                R.op("dve", lambda e: e.tensor_tensor(out=rqxb.ap, in0=rqT.ap, in1=Xib.ap, op=ALU.mult), reads=[rqT, Xib], writes=[rqxb])
                if KSTOP == "m_q2":
                    raise _Stop()
                R.op("act", lambda e, s=s: e.activation(out=ta.ap, in_=F[2].ap, func=AF.Identity, scale=s.ap[:, 3:4]),
                     reads=[F[2], s], writes=[ta])
                s3, t13, t23, Ab, Bl, Bh = rope(None, None, ta, td, te, cosv, sinv, 8)
                R.op("dve", lambda e, s3=s3, t13=t13, Ab=Ab: e.tensor_tensor(out=t13, in0=s3, in1=Ab, op=ALU.mult),
                     reads=[ta, cst], writes=[td])
                R.op("pool", lambda e, s3=s3, t23=t23, Bl=Bl: e.tensor_tensor(out=t23[:, :, 0:32], in0=s3[:, :, 32:64], in1=Bl, op=ALU.mult),
                     reads=[ta, cst], writes=[te])
                R.op("pool", lambda e, s3=s3, t23=t23, Bh=Bh: e.tensor_tensor(out=t23[:, :, 32:64], in0=s3[:, :, 0:32], in1=Bh, op=ALU.mult),
                     reads=[ta, cst, te], writes=[te])
                R.op("dve", lambda e: e.tensor_tensor(out=rkb.ap, in0=td.ap, in1=te.ap, op=ALU.add), reads=[td, te], writes=[rkb])
                for t in range(4):
                    R.op("pe", lambda e, t=t: e.transpose(out=T1ab[:, t * 128:(t + 1) * 128], in_=rkb.ap[:, t * 128:(t + 1) * 128],
                                                          identity=ident), reads=[rkb, cstb], writes=[T1a], n=128)
                R.op("act", lambda e: e.activation(out=rkT.ap, in_=T1ab, func=AF.Identity), reads=[T1a], writes=[rkT])
                if KSTOP == "m_q3":
                    raise _Stop()
                R.op("act", lambda e, s=s: e.activation(out=rvb.ap, in_=F[3].ap, func=AF.Identity, scale=s.ap[:, 2:3]),
                     reads=[F[3], s], writes=[rvb])
                R.op("act", lambda e, s=s: e.activation(out=tg.ap, in_=BG.ap, func=AF.Exp, scale=s.ap[:, 5:6]),
                     reads=[BG, s], writes=[tg])
                R.op("act", lambda e: e.activation(out=tg.ap, in_=tg.ap, func=AF.Ln, bias=onesb.ap), reads=[tg, onesb], writes=[tg])
                R.op("act", lambda e: e.activation(out=tg.ap, in_=tg.ap, func=AF.Exp, scale=-1.0), reads=[tg], writes=[tg])
                R.op("dve", lambda e, s=s: e.scalar_tensor_tensor(out=tg.ap, in0=BG.ap, scalar=s.ap[:, 2:3], in1=tg.ap,
                                                                  op0=ALU.mult, op1=ALU.mult), reads=[BG, s, tg], writes=[tg])

                if KSTOP == "m_q":
                    raise _Stop()
                kbs = [kb for kb in (c - 1, c, c + 1) if kb >= 0]
                nkb = len(kbs)
                qT3 = v3(qT.ap, 4)
                Obank = [F[4], F[5]]
                it = 0
                for g in range(2):
                    for ee in range(2):
                        bx, by = (F[0], F[1]) if it % 2 == 0 else (F[2], F[3])
                        it += 1
                        regs = [bx.ap[:, 0:256], bx.ap[:, 256:512], by.ap[:, 0:256]]
                        rbuf = [bx, bx, by]
                        for j, kb in enumerate(kbs):
                            KT3 = v3(KT[kb].ap, 2)
                            masked = (kb != c)
                            R.op("pe", lambda e, j=j, KT3=KT3, g=g, ee=ee, masked=masked, regs=regs: e.matmul(
                                v3(regs[j], 2), lhsT=KT3[64 * ee:64 * ee + 64, g, :], rhs=qT3[64 * ee:64 * ee + 64, 2 * g:2 * g + 2, :],
                                start=True, stop=(not masked)), reads=[KT[kb], qT], writes=[rbuf[j]], n=256)
                            if masked:
                                mk = mprev if kb < c else mnext
                                R.op("pe", lambda e, j=j, mk=mk, regs=regs: e.matmul(regs[j], lhsT=ident, rhs=mk, start=False, stop=True),
                                     reads=[cstb], writes=[rbuf[j]], n=256)
                        PTc = (PT, PT2)[it % 2]
                        n1 = min(nkb, 2)
                        R.op("act", lambda e, n1=n1, bx=bx, PTc=PTc: e.activation(out=PTc.ap[:, 0:n1 * 256], in_=bx.ap[:, 0:n1 * 256], func=AF.Exp, scale=0.125),
                             reads=[bx], writes=[PTc])
                        if nkb == 3:
                            R.op("act", lambda e, by=by, PTc=PTc: e.activation(out=PTc.ap[:, 512:768], in_=by.ap[:, 0:256], func=AF.Exp, scale=0.125),
                                 reads=[by, PTc], writes=[PTc])
                        for tt in range(2):
                            h = 2 * (2 * g + tt) + ee
                            ob = Obank[h // 4]
                            hl = h % 4
                            for j, kb in enumerate(kbs):
                                R.op("pe", lambda e, j=j, kb=kb, tt=tt, ob=ob, hl=hl, g=g, nkb=nkb, PTc=PTc: e.matmul(
                                    ob.ap[:, hl * 65:(hl + 1) * 65], lhsT=PTc.ap[:, j * 256 + tt * 128: j * 256 + (tt + 1) * 128],
                                    rhs=v3(Vb[kb].ap, 2)[:, g, :], start=(j == 0), stop=(j == nkb - 1)),
                                    reads=[PTc, Vb[kb]], writes=[ob], n=800)
                if KSTOP == "m_att0":
                    raise _Stop()
                for gb in range(2):
                    O3 = Obank[gb].ap[:, 0:260].rearrange("p (h d) -> p h d", h=4)
                    R.op("dve", lambda e, gb=gb, O3=O3: e.tensor_tensor(out=s8d.ap[:, gb * 4:(gb + 1) * 4], in0=O3[:, :, 64],
                                                                        in1=esink.ap[:, gb * 4:(gb + 1) * 4], op=ALU.add),
                         reads=[Obank[gb], esink, s8d], writes=[s8d])
                R.op("dve", lambda e: e.reciprocal(out=s8e.ap, in_=s8d.ap), reads=[s8d], writes=[s8e])
                for gb in range(2):
                    O3 = Obank[gb].ap[:, 0:260].rearrange("p (h d) -> p h d", h=4)
                    rdb = s8e.ap[:, gb * 4:(gb + 1) * 4].unsqueeze(2).broadcast_to([128, 4, 64])
                    R.op("dve", lambda e, gb=gb, O3=O3, rdb=rdb: e.tensor_tensor(out=v3(yb.ap[:, gb * 256:(gb + 1) * 256], 4), in0=O3[:, :, 0:64],
                                                                                 in1=rdb, op=ALU.mult),
                         reads=[Obank[gb], s8e, yb], writes=[yb], n=256)

                if KSTOP == "m_att":
                    raise _Stop()
                rkT3 = v3(rkT.ap, 4)
                rqT3 = v3(rqT.ap, 4)
                rxf3 = v3(rqxf.ap, 4)
                rxb3 = v3(rqxb.ap, 4)
                for ee in range(2):
                    for t in range(4):
                        bank = F[ee]
                        col = t * 128
                        R.op("pe", lambda e, t=t, ee=ee, bank=bank, col=col: e.matmul(
                            bank.ap[:, col:col + 128], lhsT=rkT3[64 * ee:64 * ee + 64, t, :], rhs=rqT3[64 * ee:64 * ee + 64, t, :],
                            start=True, stop=True), reads=[rkT, rqT], writes=[bank], n=128)
                if KSTOP == "m_r0":
                    raise _Stop()
                retS4 = retS.ap.rearrange("p (t e n) -> p t e n", t=4, e=2)
                DT4 = DT.ap.rearrange("p (t e n) -> p t e n", t=4, e=2)
                for gb in range(2):
                    R.op("dve", lambda e, gb=gb, retS4=retS4, DT4=DT4: e.tensor_tensor(out=retS4[:, :, gb, :], in0=v3(F[gb].ap, 4),
                                                                                     in1=DT4[:, :, gb, :], op=ALU.mult),
                         reads=[F[gb], DT, retS], writes=[retS])
                if KSTOP == "m_r1":
                    raise _Stop()
                for h in range(8):
                    t, ee = h // 2, h % 2
                    Rf3s = v3(RfS[c].ap, 4)
                    Rb3s = v3(RbS[c].ap, 4)
                    R.op("pe", lambda e, h=h: e.matmul(F[2].ap[:, h * 64:(h + 1) * 64], lhsT=retS.ap[:, h * 128:(h + 1) * 128],
                                                       rhs=rvb.ap[:, h * 64:(h + 1) * 64], start=True, stop=False),
                         reads=[retS, rvb], writes=[F[2]], n=200)
                    R.op("pe", lambda e, h=h, t=t, ee=ee, Rf3s=Rf3s: e.matmul(F[2].ap[:, h * 64:(h + 1) * 64], lhsT=rxf3[64 * ee:64 * ee + 64, t, :],
                                                                             rhs=Rf3s[64 * ee:64 * ee + 64, t, :], start=False, stop=False),
                         reads=[rqxf, RfS[c]], writes=[F[2]], n=100)
                    R.op("pe", lambda e, h=h, t=t, ee=ee, Rb3s=Rb3s: e.matmul(F[2].ap[:, h * 64:(h + 1) * 64], lhsT=rxb3[64 * ee:64 * ee + 64, t, :],
                                                                             rhs=Rb3s[64 * ee:64 * ee + 64, t, :], start=False, stop=True),
                         reads=[rqxb, RbS[c]], writes=[F[2]], n=100)
                if KSTOP == "m_r2":
                    raise _Stop()
                R.op("act", lambda e: e.activation(out=tf.ap, in_=F[2].ap, func=AF.Square), reads=[F[2]], writes=[tf])
                R.op("dve", lambda e: e.tensor_reduce(out=s8f.ap, in_=v3(tf.ap, 8), axis=AX.X, op=ALU.add), reads=[tf], writes=[s8f])
                R.op("act", lambda e: e.activation(out=s8g.ap, in_=s8f.ap, func=AF.Ln, scale=1.0 / 64, bias=epsb.ap),
                     reads=[s8f, epsb], writes=[s8g])
                R.op("act", lambda e: e.activation(out=s8h.ap, in_=s8g.ap, func=AF.Exp, scale=-0.5), reads=[s8g], writes=[s8h])
                R.op("dve", lambda e: e.tensor_tensor(out=tf.ap, in0=F[2].ap, in1=tg.ap, op=ALU.mult), reads=[F[2], tg, s8f], writes=[tf])
                rr8b = s8h.ap.unsqueeze(2).broadcast_to([128, 8, 64])
                R.op("dve", lambda e, rr8b=rr8b: e.tensor_tensor(out=v3(yb.ap[:, 512:1024], 8), in0=v3(tf.ap, 8), in1=rr8b, op=ALU.mult),
                     reads=[tf, s8h, yb], writes=[yb], n=512)

                if KSTOP == "m_ret":
                    raise _Stop()
                for k in range(8):
                    R.op("pe", lambda e, k=k: e.transpose(out=T0b[:, k * 128:(k + 1) * 128], in_=yb.ap[:, k * 128:(k + 1) * 128],
                                                          identity=ident), reads=[yb, cstb], writes=[T0], n=128)
                R.op("act", lambda e: e.activation(out=yT.ap, in_=T0b, func=AF.Identity), reads=[T0], writes=[yT])
                yT3 = v3(yT.ap, 8)
                for half in range(2):
                    bank = F[3] if half == 0 else F[5]
                    for k in range(8):
                        R.op("pe", lambda e, k=k, half=half, bank=bank: e.matmul(bank.ap, lhsT=yT3[:, k, :],
                                                                                  rhs=W_out.ap[:, k, half * 512:(half + 1) * 512],
                                                                                  start=(k == 0), stop=(k == 7)),
                             reads=[yT, W_out], writes=[bank])
                    R.op("dve", lambda e, half=half, bank=bank, xbuf=xbuf: e.tensor_tensor(out=xbuf.ap[:, half * 512:(half + 1) * 512],
                                                                                           in0=xbuf.ap[:, half * 512:(half + 1) * 512],
                                                                                           in1=bank.ap, op=ALU.add),
                         reads=[bank, xbuf], writes=[xbuf], n=512)
                R.op("act", lambda e, c=c, xbuf=xbuf: e.activation(out=tf.ap.bitcast(BF16), in_=xbuf.ap, func=AF.Square, accum_out=ssq2.ap[:, c:c + 1]),
                     reads=[xbuf, ssq2], writes=[tf, ssq2], n=1024)
                R.op("act", lambda e, c=c: e.activation(out=lnr2.ap[:, c:c + 1], in_=ssq2.ap[:, c:c + 1], func=AF.Ln, scale=1.0 / D_MODEL, bias=epsb.ap),
                     reads=[ssq2, epsb, lnr2], writes=[lnr2])
                R.op("act", lambda e, c=c: e.activation(out=rstd2.ap[:, c:c + 1], in_=lnr2.ap[:, c:c + 1], func=AF.Exp, scale=-0.5),
                     reads=[lnr2, rstd2], writes=[rstd2])

            if KSTOP == "main":
                raise _Stop()
            R.phase = "ffn"
            fences = []
            f_ = R.op("dve", lambda e: e.tensor_copy(out=fsc.ap[:, 0:1], in_=epsb.ap), reads=[epsb], writes=[Buf("fscD", fsc.ap[:, 0:1])])
            fences.append(f_)
            f_ = R.op("act", lambda e: e.activation(out=fsc.ap[:, 1:2], in_=epsb.ap, func=AF.Identity), reads=[epsb], writes=[Buf("fscA", fsc.ap[:, 1:2])])
            fences.append(f_)
            f_ = R.op("pool", lambda e: e.memset(fsc.ap[:, 2:3], 0.0), writes=[Buf("fscP", fsc.ap[:, 2:3])])
            fences.append(f_)
            f_ = R.op("pe", lambda e: e.matmul(T0.ap[:, 0:1], lhsT=ident, rhs=ident[:, 0:1], start=True, stop=True),
                      reads=[cstb], writes=[T0], n=1)
            fences.append(f_)
            for f_ in fences:
                f_.fence = True
            R.seg = 1
            bar = fences

            def pbuf(name, ap):
                b = Buf(name, ap)
                b.readers = list(bar)
                return b

            AB.off = 0
            AFa.off = 0

            def pb_alloc(arena, name, n):
                b = arena.alloc(name, n)
                b.readers = list(bar)
                return b

            mTg = [pb_alloc(AB, f"mTg{g_}", 4096) for g_ in range(4)]
            mT = []
            for c in range(NT_OWN):
                b_ = Buf(f"mT{c}", v3(mTg[c // 4].ap, 8)[:, :, (c % 4) * 128:(c % 4 + 1) * 128])
                b_.readers = list(bar)
                mT.append(b_)
            hTb = [[pb_alloc(AB, f"hT{s_}_{ci}", 512) for ci in range(4)] for s_ in range(2)]
            mb = pb_alloc(AB, "mb", 1024)
            junk2 = pb_alloc(AB, "junk2", 1024)
            identB = pb_alloc(AB, "identB", 128)
            fnw = pb_alloc(AFa, "fnw", 1024)
            sgt = [pb_alloc(AFa, f"sgt{i}", 512) for i in range(2)]
            st2 = pb_alloc(AFa, "st2", 8)
            T0f = pbuf("T0f", banks[6][:, :])
            T1f = pbuf("T1f", banks[7][:, :])
            slots = [pbuf(f"slot{i}", Wt[:, i * 12288:(i + 1) * 12288]) for i in range(2)]
            early = []
            for wb in Wg_buf.values():
                early.extend(wb.readers)
                early.extend(wb.writers)
            slots[0].readers = early

            R.dma("sp", fnw.ap, fnw_d, fnw, writes=[fnw])
            R.dma("pool", identB.ap, cstb_d[:, 0:128], identB, writes=[identB])

            passes = [(0, 4), (4, 4), (8, 4), (12, 4), (16, 3), (19, 3)]
            Wg_v = w_gate_d.rearrange("(k p) n -> p k n", p=128)
            Wu_v = w_up_d.rearrange("(k p) n -> p k n", p=128)

            def load_pass(r):
                f0, C = passes[r]
                sl = slots[r % 2]
                g3 = v3(sl.ap[:, 0:4096], 8)
                u3 = v3(sl.ap[:, 4096:8192], 8)
                d3 = v3(sl.ap[:, 8192:12288], 4)
                R.dma("pool", g3[:, :, 0:C * 128], Wg_v[:, :, f0 * 128:(f0 + C) * 128], sl, writes=[sl])
                R.dma("pool", u3[:, :, 0:C * 128], Wu_v[:, :, f0 * 128:(f0 + C) * 128], sl, writes=[sl])
                R.dma("pool", d3[:, 0:C, :], w_down_d[f0 * 128:(f0 + C) * 128, :].rearrange("(c p) n -> p c n", p=128), sl, writes=[sl])

            load_pass(0)
            load_pass(1)

            def prologue(tgi):
                for t in range(4):
                    c = tgi * 4 + t
                    R.op("dve", lambda e, c=c: e.scalar_tensor_tensor(out=mb.ap, in0=hbuf[c].ap, scalar=rstd2.ap[:, c:c + 1], in1=fnw.ap,
                                                                      op0=ALU.mult, op1=ALU.mult), reads=[hbuf[c], rstd2, fnw], writes=[mb])
                    for k in range(8):
                        R.op("pe", lambda e, k=k: e.transpose(out=T0b[:, k * 128:(k + 1) * 128], in_=mb.ap[:, k * 128:(k + 1) * 128],
                                                              identity=identB.ap), reads=[mb, identB], writes=[T0f], n=128)
                    R.op("act", lambda e, c=c: e.activation(out=mT[c].ap, in_=v3(T0b, 8), func=AF.Identity), reads=[T0f], writes=[mT[c]])

            prologue(0)
            gi = 0
            for r, (f0, C) in enumerate(passes):
                sl = slots[r % 2]
                g3 = v3(sl.ap[:, 0:4096], 8)
                u3 = v3(sl.ap[:, 4096:8192], 8)
                d3 = v3(sl.ap[:, 8192:12288], 4)
                last = (r == len(passes) - 1)
                for tgi in range(4):
                    if r == 0 and tgi + 1 < 4:
                        prologue(tgi + 1)
                    hs = hTb[tgi % 2]
                    for ci in range(C):
                        gbank = F[0] if gi % 2 == 0 else F[1]
                        ubank = F[2] if gi % 2 == 0 else F[3]
                        sg_ = sgt[gi % 2]
                        gi += 1
                        mg3 = v3(mTg[tgi].ap, 8)
                        for (bank, w3) in ((gbank, g3), (ubank, u3)):
                            for k in range(8):
                                R.op("pe", lambda e, bank=bank, w3=w3, ci=ci, k=k, mg3=mg3: e.matmul(
                                    bank.ap, lhsT=w3[:, k, ci * 128:(ci + 1) * 128],
                                    rhs=mg3[:, k, :], start=(k == 0), stop=(k == 7)),
                                    reads=[sl] + mT[tgi * 4:tgi * 4 + 4], writes=[bank])
                        R.op("act", lambda e, gbank=gbank, sg_=sg_: e.activation(out=sg_.ap, in_=gbank.ap, func=AF.Silu),
                             reads=[gbank], writes=[sg_])
                        R.op("dve", lambda e, ubank=ubank, sg_=sg_, hs=hs, ci=ci: e.tensor_tensor(out=hs[ci].ap, in0=sg_.ap, in1=ubank.ap, op=ALU.mult),
                             reads=[sg_, ubank], writes=[hs[ci]])
                    for t in range(4):
                        c = tgi * 4 + t
                        dbanks = (F[4], F[5]) if t % 2 == 0 else (T1f, T0f)
                        for half in range(2):
                            bank = dbanks[half]
                            for ci in range(C):
                                R.op("pe", lambda e, bank=bank, ci=ci, t=t, half=half, hs=hs, C=C, d3=d3: e.matmul(
                                    bank.ap, lhsT=hs[ci].ap[:, t * 128:(t + 1) * 128], rhs=d3[:, ci, half * 512:(half + 1) * 512],
                                    start=(ci == 0), stop=(ci == C - 1)), reads=[hs[ci], sl], writes=[bank])
                            R.op("dve", lambda e, bank=bank, c=c, half=half: e.tensor_tensor(
                                out=hbuf[c].ap[:, half * 512:(half + 1) * 512], in0=hbuf[c].ap[:, half * 512:(half + 1) * 512],
                                in1=bank.ap, op=ALU.add), reads=[bank, hbuf[c]], writes=[hbuf[c]])
                        if last:
                            R.dma("sp", out_d[c * 128:(c + 1) * 128, :], hbuf[c].ap, hbuf[c], reads=[hbuf[c]], final=True)
                if r + 2 < len(passes):
                    load_pass(r + 2)

        try:
            _record()
        except _Stop:
            pass

        R.finalize()
        _NC_CACHE['R'] = R
        if os.environ.get("KSCHEDSTAT", ""):
            print('sched sim total us:', getattr(R, 'sim_total', None))

        @block.sync
        def _(eng):
            R.emit("sp", eng)

        @block.gpsimd
        def _(eng):
            R.emit("pool", eng)

        @block.scalar
        def _(eng):
            R.emit("act", eng)

        @block.vector
        def _(eng):
            R.emit("dve", eng)

        @block.tensor
        def _(eng):
            R.emit("pe", eng)
    return nc


_NC_CACHE = {}


def _rope_tables(hf):
    l = np.arange(SEQ, dtype=np.float32)
    pos = l if hf == 0 else (np.float32(SEQ - 1) - l)
    inv_freq = (np.float32(10000.0) ** (-(np.arange(0, 64, 2, dtype=np.float32)) / np.float32(64))).astype(np.float32)
    ang = (pos[:, None] * inv_freq[None, :]).astype(np.float32)
    cos = np.cos(ang.astype(np.float64)).astype(np.float32)
    sin = np.sin(ang.astype(np.float64)).astype(np.float32)
    cs = np.concatenate([cos, cos, -sin, sin], axis=1)
    return np.ascontiguousarray(cs, dtype=np.float32)


def _const_tables():
    i = np.arange(128, dtype=np.float32)
    iota1 = np.tile((i + 1.0)[None, :], (128, 1))
    iota2 = np.tile((128.0 - i)[None, :], (128, 1))
    m = i[:, None]
    n = i[None, :]
    reluP = np.maximum(n - m, 0.0)
    reluN = np.maximum(m - n, 0.0)
    csts = np.concatenate([iota1, iota2, reluP, reluN], axis=1).astype(np.float32)
    ident = np.eye(128, dtype=np.float32)
    j = i[:, None]
    q = i[None, :]
    mprev = np.where(j >= q, 0.0, -30000.0).astype(np.float32)
    mnext = np.where(j <= q, 0.0, -30000.0).astype(np.float32)
    cstb = np.concatenate([ident, mprev, mprev, mnext, mnext], axis=1).astype(np.float32)
    return np.ascontiguousarray(csts), np.ascontiguousarray(cstb)


def _make_in_maps(x, attn_norm_w, w_in, q_norm_w, k_norm_w, attn_sink, ret_log_decay_fwd, ret_log_decay_bwd,
           ret_norm_w, w_out, ffn_norm_w, w_gate, w_up, w_down):
    x = np.asarray(x, dtype=np.float32)
    f = lambda a: np.asarray(a, dtype=np.float32)
    attn_norm_w, q_norm_w, k_norm_w, attn_sink = f(attn_norm_w)[0], f(q_norm_w)[0], f(k_norm_w)[0], f(attn_sink)[0]
    dfw, dbw = f(ret_log_decay_fwd)[0], f(ret_log_decay_bwd)[0]
    ret_norm_w, ffn_norm_w = f(ret_norm_w)[0], f(ffn_norm_w)[0]
    w_in_, w_out_, w_gate_, w_up_, w_down_ = (np.ascontiguousarray(f(a)[0]) for a in (w_in, w_out, w_gate, w_up, w_down))

    csts, cstb = _const_tables()
    fnw = np.ascontiguousarray(np.tile(ffn_norm_w[None, :], (128, 1)))
    rope_tabs = [_rope_tables(0), _rope_tables(1)]
    swap = lambda w: np.concatenate([w[32:], w[:32]])
    in_maps = []
    for c in range(8):
        b, hf = c // 2, c % 2
        xs = x[b] if hf == 0 else x[b, ::-1]
        dF, dB = (dfw, dbw) if hf == 0 else (dbw, dfw)
        cstf = np.zeros((128, NCF), dtype=np.float32)

        def put(name, row):
            a, bnd = CF[name]
            cstf[:, a:bnd] = row[None, :]
        a, bnd = CF["anwP"]
        cstf[:, a:bnd] = attn_norm_w.reshape(8, 128).T
        a, bnd = CF["rnwP"]
        cstf[:, a:bnd] = ret_norm_w.reshape(4, 128).T
        put("qnw", q_norm_w)
        put("qnws", swap(q_norm_w))
        put("knw", k_norm_w)
        put("knws", swap(k_norm_w))
        put("sink", attn_sink)
        put("decR", np.concatenate([dF, dB]))
        a, bnd = CF["decP"]
        for t in range(4):
            cstf[0:64, a + t] = dF[2 * t]
            cstf[64:128, a + t] = dF[2 * t + 1]
            cstf[0:64, a + 4 + t] = dB[2 * t]
            cstf[64:128, a + 4 + t] = dB[2 * t + 1]
        a, _b = CF["c127m"]
        cstf[:, a] = 127.0 - np.arange(128, dtype=np.float32)
        a, _b = CF["cm"]
        cstf[:, a] = np.arange(128, dtype=np.float32)
        in_maps.append({
            "xs": np.ascontiguousarray(xs), "cs": rope_tabs[hf], "cstf": cstf, "csts": csts, "cstb": cstb, "fnw": fnw,
            "w_in": w_in_, "w_out": w_out_, "w_gate": w_gate_, "w_up": w_up_, "w_down": w_down_,
        })
    return in_maps


def kernel(**inputs):
    in_maps = _make_in_maps(**inputs)
    if "nc" not in _NC_CACHE:
        _NC_CACHE["nc"] = build_program()
    nc = _NC_CACHE["nc"]
    res = run_bass_kernel_spmd(nc, in_maps, core_ids=list(range(8)))
    out = np.empty((4, SEQ, D_MODEL), dtype=np.float32)
    for c in range(8):
        b, hf = c // 2, c % 2
        o = np.asarray(res.results[c]["out"], dtype=np.float32)
        if hf == 0:
            out[b, 0:2048] = o
        else:
            out[b, 2048:4096] = o[::-1]
    return out
```

```python
import contextlib
import os
import sys
import numpy as np
import concourse.bass as bass
import concourse.mybir as mybir
from concourse.bass_utils import run_bass_kernel_spmd

F32 = mybir.dt.float32
BF16 = mybir.dt.bfloat16
AF = mybir.ActivationFunctionType
ALU = mybir.AluOpType
AX = mybir.AxisListType

D_MODEL = 1024
SEQ = 4096
NT_ALL = 32
NT_OWN = 16
IN_PROJ = 2816
D_FF = 2816
EPS = 1e-6
C_AQ, C_AK, C_AV, C_RQ, C_RK, C_RV, C_RG = 0, 512, 640, 768, 1280, 1792, 2304

CF = {}
_o = 0
for _n, _w in [("anwP", 8), ("rnwP", 4), ("qnw", 64), ("qnws", 64), ("knw", 64), ("knws", 64),
               ("sink", 8), ("decP", 8), ("decR", 16), ("c127m", 1), ("cm", 1)]:
    CF[_n] = (_o, _o + _w)
    _o += _w
NCF = _o
NCB = 128 + 256 + 256


class Op:
    __slots__ = ("eng", "fn", "deps", "signal", "sigval", "dma", "idx", "cost", "seg", "fence", "lat", "fin", "ph", "line", "crit", "estsrc", "prio")


class DmaTok:
    __slots__ = ("sem", "val", "op")

    def __init__(self, sem, val, op):
        self.sem = sem
        self.val = val
        self.op = op


class Buf:
    def __init__(self, name, ap=None, share=None):
        self.name = name
        self.ap = ap
        self._w = []
        self._r = []
        self.share = share
        self.sem = None
        self.cnt = 0

    @property
    def writers(self):
        return (self.share or self)._w

    @writers.setter
    def writers(self, v):
        (self.share or self)._w = v

    @property
    def readers(self):
        return (self.share or self)._r

    @readers.setter
    def readers(self, v):
        (self.share or self)._r = v

    def nfree(self):
        if self.ap is None:
            return 512
        n = 1
        for d in self.ap.shape[1:]:
            n *= d
        return n


_COST = {"dve": (0.10, 1.0 / 870), "act": (0.17, 1.0 / 1200), "pool": (0.15, 1.0 / 450), "pe": (0.02, 1.0 / 2600), "sp": (0.05, 0.0)}


class Rec:
    ENGS = ("pe", "act", "dve", "pool", "sp")

    def __init__(self, sems):
        self.lists = {e: [] for e in self.ENGS}
        self.free_sems = list(sems)
        self.engsem = {e: self.free_sems.pop() for e in ("pe", "act", "dve", "pool")}
        self.final = []
        self.nops = 0
        self.seg = 0

    def _deps(self, reads, writes, extra):
        deps = []
        for b in reads:
            for t in b.writers:
                deps.append((t, "raw"))
        for b in writes:
            for t in b.readers:
                deps.append((t, "war"))
            for t in b.writers:
                deps.append((t, "waw"))
        for t in extra:
            deps.append((t, "raw"))
        return deps

    def _new(self, eng, fn, deps, dma, cost):
        o = Op()
        o.eng = eng
        o.fn = fn
        o.deps = deps
        o.signal = False
        o.sigval = None
        o.dma = dma
        o.idx = self.nops
        self.nops += 1
        o.cost = cost
        o.seg = self.seg
        o.ph = getattr(self, "phase", "setup")
        o.line = sys._getframe(2).f_lineno if os.environ.get("KSCHEDSTAT", "") else 0
        o.fence = False
        o.lat = 0.0
        o.fin = 0.0
        o.crit = None
        o.estsrc = None
        o.prio = getattr(self, "prio", 0)
        self.lists[eng].append(o)
        return o

    def op(self, eng, fn, reads=(), writes=(), extra=(), n=None):
        if n is None:
            n = writes[0].nfree() if writes else 64
        a, b = _COST[eng]
        o = self._new(eng, fn, self._deps(reads, writes, extra), False, a + b * n)
        for bf in reads:
            bf.readers.append(o)
        for bf in writes:
            bf.writers = [o]
            bf.readers = []
        return o

    def dma(self, q, out_ap, in_ap, semowner, reads=(), writes=(), extra=(), final=False, order_after=()):
        if semowner.sem is None:
            semowner.sem = self.free_sems.pop()
        sem = semowner.sem
        semowner.cnt += 1
        fn = lambda e, out_ap=out_ap, in_ap=in_ap, sem=sem: e.dma_start(out=out_ap, in_=in_ap).then_inc(sem, 16)
        deps_ = self._deps(reads, writes, extra)
        for t_ in order_after:
            deps_.append((t_.op, "order"))
        o = self._new(q, fn, deps_, True, 0.06 if q == "sp" else 1.0)
        nel = 1
        for d in out_ap.shape:
            nel *= d
        o.lat = 2.0 + nel * 4 / 180e3
        tok = DmaTok(sem, semowner.cnt * 16, o)
        for bf in reads:
            bf.readers.append(tok)
        for bf in writes:
            bf.writers = [tok]
            bf.readers = []
        if final:
            self.final.append(tok)
        return tok

    def schedule(self):
        allops = []
        for e in self.ENGS:
            allops.extend(self.lists[e])
        allops.sort(key=lambda o: o.idx)
        preds = {}
        succs = {}
        for o in allops:
            ps = {}
            for (t, kind) in o.deps:
                if isinstance(t, DmaTok):
                    ps[id(t.op)] = (t.op, True)
                else:
                    if id(t) not in ps:
                        ps[id(t)] = (t, False)
            preds[id(o)] = list(ps.values())
            for (p, _) in ps.values():
                succs.setdefault(id(p), []).append(o)
        bl = {}
        for o in reversed(allops):
            m = 0.0
            for sc in succs.get(id(o), ()):
                m = max(m, bl[id(sc)] + (float(os.environ.get("KLATX", "0.3")) if sc.eng != o.eng else float(os.environ.get("KLATS", "0.03"))))
            bl[id(o)] = o.cost + m
        use_bl = os.environ.get("KBL", "1") == "1"
        LX = float(os.environ.get("KLATX", "0.3"))
        LS = float(os.environ.get("KLATS", "0.03"))
        slack = float(os.environ.get("KSLACK", "0.05"))
        free = {e: 0.0 for e in self.ENGS}
        neworder = {e: [] for e in self.ENGS}
        nseg = max(o.seg for o in allops) + 1
        for sg in range(nseg):
            ops = [o for o in allops if o.seg == sg]
            inseg = set(id(o) for o in ops)
            indeg = {}
            est = {}
            avail = {e: [] for e in self.ENGS}
            remaining = {e: 0 for e in self.ENGS}
            for o in ops:
                remaining[o.eng] += 1
                d = 0
                t0 = 0.0
                for (p, isdma) in preds[id(o)]:
                    if id(p) in inseg:
                        d += 1
                    else:
                        t0 = max(t0, p.fin + (p.lat if isdma else LX))
                indeg[id(o)] = d
                est[id(o)] = t0
                if d == 0:
                    avail[o.eng].append(o)
            nleft = len(ops)
            while nleft:
                best = None
                for e in self.ENGS:
                    cands = []
                    tmin = None
                    for o in avail[e]:
                        if o.fence and remaining[e] > 1:
                            continue
                        st = max(free[e], est[id(o)])
                        cands.append((st, o))
                        if tmin is None or st < tmin:
                            tmin = st
                    if not cands:
                        continue
                    if use_bl:
                        st, o = max(((st, o) for (st, o) in cands if st <= tmin + slack), key=lambda x: (bl[id(x[1])], -x[1].idx))
                    else:
                        st, o = min(cands, key=lambda x: (x[0], x[1].idx))
                    key = (st, o.idx)
                    if best is None or key < best[0]:
                        best = (key, o)
                assert best is not None, "scheduler stuck"
                o = best[1]
                st = best[0][0]
                e = o.eng
                avail[e].remove(o)
                remaining[e] -= 1
                nleft -= 1
                o.fin = st + o.cost
                o.crit = (neworder[e][-1] if (neworder[e] and free[e] >= est[id(o)]) else o.estsrc)
                free[e] = o.fin
                neworder[e].append(o)
                for sc in succs.get(id(o), ()):
                    if id(sc) not in inseg:
                        continue
                    isdma = any((p is o and dm) for (p, dm) in preds[id(sc)])
                    lat = o.lat if isdma else (LX if sc.eng != e else LS)
                    if o.fin + lat > est[id(sc)]:
                        est[id(sc)] = o.fin + lat
                        sc.estsrc = o
                    indeg[id(sc)] -= 1
                    if indeg[id(sc)] == 0:
                        avail[sc.eng].append(sc)
        self.lists = neworder
        self.sim_total = max(free.values())
        if os.environ.get("KSCHEDSTAT", "") == "1":
            stat = {}
            for o in allops:
                d = stat.setdefault(o.ph, {"end": 0.0, "start": 1e18})
                d["end"] = max(d["end"], o.fin)
                d["start"] = min(d["start"], o.fin - o.cost)
                d[o.eng] = d.get(o.eng, 0.0) + o.cost
            for ph, d in stat.items():
                print("SCHED", ph, {k: round(v, 1) for k, v in d.items()})
            cp = os.environ.get("KSCHEDCRIT", "")
            if cp:
                ph_, n_ = cp.split(",")
                tmax_ = float(os.environ.get("KSCHEDTMAX", "1e18"))
                cur = max((o for o in allops if o.ph == ph_ and o.fin <= tmax_), key=lambda o: o.fin)
                chain = []
                while cur is not None and len(chain) < int(n_):
                    chain.append(cur)
                    cur = cur.crit
                for o in reversed(chain):
                    print("CRIT %8.2f %6.2f %-4s L%d%s" % (o.fin - o.cost, o.cost, o.eng, o.line, " dma" if o.dma else ""))
            w = os.environ.get("KSCHEDWIN", "")
            if w:
                a_, b_ = (float(x) for x in w.split(","))
                sel = [o for o in allops if a_ <= o.fin - o.cost < b_]
                sel.sort(key=lambda o: o.fin - o.cost)
                for o in sel:
                    print("OP %8.2f %6.2f %-4s L%d%s" % (o.fin - o.cost, o.cost, o.eng, o.line, " dma" if o.dma else ""))

    def finalize(self):
        if os.environ.get("KNOSCHED", "") != "1":
            self.schedule()
        for e in self.ENGS:
            for o in self.lists[e]:
                for (t, kind) in o.deps:
                    if kind == "order":
                        continue
                    if isinstance(t, Op):
                        if t.eng != o.eng or o.dma or o.eng in ("act", "dve", "pool"):
                            t.signal = True
        for e in self.ENGS:
            cnt = 0
            for o in self.lists[e]:
                if o.signal:
                    cnt += 1
                    o.sigval = cnt

    def emit(self, e, eng):
        waited = {}
        for o in self.lists[e]:
            need = {}
            for (t, kind) in o.deps:
                if kind == "order":
                    continue
                if isinstance(t, DmaTok):
                    key, val = t.sem, t.val
                else:
                    if t.eng == e and not (o.dma or e in ("act", "dve", "pool")):
                        continue
                    key, val = self.engsem[t.eng], t.sigval
                if need.get(key, (None, 0))[1] < val:
                    need[key] = (key, val)
            for key, val in need.values():
                if waited.get(key, 0) < val:
                    eng.wait_ge(key, val)
                    waited[key] = val
            ins = o.fn(eng)
            if o.signal:
                ins.then_inc(self.engsem[e], 1)
        if e == "sp":
            for t in self.final:
                if waited.get(t.sem, 0) < t.val:
                    eng.wait_ge(t.sem, t.val)
                    waited[t.sem] = t.val


def v3(ap, a):
    return ap.rearrange("p (a b) -> p a b", a=a)


class _Stop(Exception):
    pass


def build_program():
    nc = bass.Bass("TRN2", target_bir_lowering=False)
    KSTOP = os.environ.get("KSTOP", "")
    xs = nc.dram_tensor("xs", [SEQ, D_MODEL], F32, kind="ExternalInput").ap()
    cs_d = nc.dram_tensor("cs", [SEQ, 128], F32, kind="ExternalInput").ap()
    cstf_d = nc.dram_tensor("cstf", [128, NCF], F32, kind="ExternalInput").ap()
    csts_d = nc.dram_tensor("csts", [128, 512], F32, kind="ExternalInput").ap()
    cstb_d = nc.dram_tensor("cstb", [128, NCB], F32, kind="ExternalInput").ap()
    fnw_d = nc.dram_tensor("fnw", [128, D_MODEL], F32, kind="ExternalInput").ap()
    w_in_d = nc.dram_tensor("w_in", [D_MODEL, IN_PROJ], F32, kind="ExternalInput").ap()
    w_out_d = nc.dram_tensor("w_out", [D_MODEL, D_MODEL], F32, kind="ExternalInput").ap()
    w_gate_d = nc.dram_tensor("w_gate", [D_MODEL, D_FF], F32, kind="ExternalInput").ap()
    w_up_d = nc.dram_tensor("w_up", [D_MODEL, D_FF], F32, kind="ExternalInput").ap()
    w_down_d = nc.dram_tensor("w_down", [D_FF, D_MODEL], F32, kind="ExternalInput").ap()
    out_d = nc.dram_tensor("out", [NT_OWN * 128, D_MODEL], F32, kind="ExternalOutput").ap()

    NB16 = 26916
    NF32 = 7488
    with contextlib.ExitStack() as es:
        Wt = es.enter_context(nc.sbuf_tensor("Wt", [128, 30720], BF16))
        Ht = es.enter_context(nc.sbuf_tensor("Ht", [128, NT_OWN * 1024], F32))
        ABt = es.enter_context(nc.sbuf_tensor("ABt", [128, NB16], BF16))
        AFt = es.enter_context(nc.sbuf_tensor("AFt", [128, NF32], F32))
        banks = [es.enter_context(nc.psum_tensor(f"pb{i}", [128, 512], F32)) for i in range(8)]
        sems = [es.enter_context(nc.semaphore(f"s{i}")) for i in range(60)]
        block = es.enter_context(nc.Block())
        R = Rec(sems)

        class Arena:
            def __init__(self, t, n):
                self.t, self.n, self.off = t, n, 0

            def alloc(self, name, n):
                assert self.off + n <= self.n, (name, self.off, n, self.n)
                ap = self.t[:, self.off:self.off + n]
                self.off += n
                return Buf(name, ap)

        AB = Arena(ABt, NB16)
        AFa = Arena(AFt, NF32)

        W_in = Buf("w_in", v3(Wt[:, 0:22528], 8))
        W_out = Buf("w_out", v3(Wt[:, 22528:30720], 8))
        hbuf = [Buf(f"h{c}", Ht[:, c * 1024:(c + 1) * 1024]) for c in range(NT_OWN)]
        kvf = [Buf(f"kvf{c}", v3(Ht[:, 12 * 1024 + c * 256: 12 * 1024 + (c + 1) * 256], 4)) for c in range(NT_OWN)]

        F = [Buf(f"F{i}", banks[i][:, :]) for i in range(6)]
        T0 = Buf("T0", banks[6][:, :])
        T1a = Buf("T1a", banks[7][:, 0:256])
        T1b = Buf("T1b", banks[7][:, 256:512], share=T1a)
        T0b = banks[6][:, :].bitcast(BF16)
        T1ab = banks[7][:, 0:256].bitcast(BF16)
        T1bb = banks[7][:, 256:512].bitcast(BF16)

        cstb = AB.alloc("cstb", NCB)
        ident = cstb.ap[:, 0:128]
        mprev = cstb.ap[:, 128:384]
        mnext = cstb.ap[:, 384:640]
        RbS = [AB.alloc(f"RbS{c}", 256) for c in range(NT_OWN)]
        RfS = [AB.alloc(f"RfS{c}", 256) for c in range(NT_OWN)]
        KT = [AB.alloc(f"KT{c}", 256) for c in range(NT_OWN + 1)]
        Vb = [AB.alloc(f"V{c}", 130) for c in range(NT_OWN + 1)]
        xb = AB.alloc("xb", 1024)
        xT = AB.alloc("xT", 1024)
        qb = AB.alloc("qb", 512)
        rqb = AB.alloc("rqb", 512)
        rkb = AB.alloc("rkb", 512)
        rvb = AB.alloc("rvb", 512)
        kdup = AB.alloc("kdup", 256)
        qT = AB.alloc("qT", 512)
        rqT = AB.alloc("rqT", 512)
        rqxf = AB.alloc("rqxf", 512)
        rqxb = AB.alloc("rqxb", 512)
        rkT = AB.alloc("rkT", 512)
        PT = AB.alloc("PT", 768)
        PT2 = AB.alloc("PT2", 768)
        retS = AB.alloc("retS", 1024)
        yb = AB.alloc("yb", 1024)
        yT = AB.alloc("yT", 1024)
        kzb, kzf = qb, rqb

        cstf = AFa.alloc("cstf", NCF)

        def cf(name):
            a, b = CF[name]
            return cstf.ap[:, a:b]

        Xif = AFa.alloc("Xif", 512)
        Xib = AFa.alloc("Xib", 512)
        csl = [AFa.alloc(f"cs{i}", 128) for i in range(2)]
        tabq = AFa.alloc("tabq", 128)
        ta = AFa.alloc("ta", 512)
        DT = AFa.alloc("DT", 1024)
        td = AFa.alloc("td", 512)
        te = AFa.alloc("te", 512)
        tg = AFa.alloc("tg", 512)
        tf = AFa.alloc("tf", 512)
        akf = Buf("akf", tf.ap[:, 0:128])
        tk1 = Buf("tk1", tf.ap[:, 128:256])
        tk2 = Buf("tk2", tf.ap[:, 256:384])
        tabk = Buf("tabk", tf.ap[:, 384:512])
        Rb = AFa.alloc("Rb", 256)
        Rf = AFa.alloc("Rf", 256)
        rtmp = AFa.alloc("rtmp", 256)
        lgP = AFa.alloc("lgP", 8)
        lgR = AFa.alloc("lgR", 16)
        g128 = AFa.alloc("g128", 8)
        Zfb = AFa.alloc("Zfb", 16)
        esink = AFa.alloc("esink", 8)
        st = [AFa.alloc(f"st{i}", 8) for i in range(2)]
        s8a = AFa.alloc("s8a", 8)
        s8g = AFa.alloc("s8g", 8)
        s8h = AFa.alloc("s8h", 8)
        tb2 = AFa.alloc("tb2", 512)
        junk = AFa.alloc("junk", 512)
        junk_ap = junk.ap.bitcast(BF16)
        s8b = AFa.alloc("s8b", 8)
        s8c = AFa.alloc("s8c", 8)
        s8d = AFa.alloc("s8d", 8)
        s8e = AFa.alloc("s8e", 8)
        s8f = AFa.alloc("s8f", 8)
        epsb = AFa.alloc("epsb", 1)
        onesb = AFa.alloc("onesb", 1)
        fsc = AFa.alloc("fsc", 4)
        ssq2 = AFa.alloc("ssq2", 16)
        lnr2 = AFa.alloc("lnr2", 16)
        rstd2 = AFa.alloc("rstd2", 16)

        if os.environ.get("KSCHEDSTAT", ""):
            print("arena use: bf16", AB.off, "/", NB16, " f32", AFa.off, "/", NF32)
        W_in_v = w_in_d.rearrange("(k p) n -> p k n", p=128)
        W_out_v = w_out_d.rearrange("(k p) n -> p k n", p=128)

        def _record():
            R.dma("sp", cstf.ap, cstf_d, cstf, writes=[cstf])
            R.dma("sp", ta.ap, csts_d, ta, writes=[ta])
            R.dma("pool", cstb.ap, cstb_d, cstb, writes=[cstb])
            wcols = [("rkrv", C_RK, C_RG), ("akav", C_AK, C_RQ), ("aq", C_AQ, C_AK), ("rq", C_RQ, C_RK), ("rg", C_RG, IN_PROJ)]
            Wg_buf = {}
            prev_w = None
            for name, a, b in wcols:
                bb = Buf("w_in_" + name)
                Wg_buf[name] = bb
                ex = [prev_w] if prev_w else []
                tk_ = None
                for k in range(8):
                    tk_ = R.dma("pool", W_in.ap[:, k, a:b], W_in_v[:, k, a:b], bb, writes=([bb] if k == 0 else []), extra=ex,
                                order_after=([tk_] if tk_ else []))
                bb.writers = [tk_]
                prev_w = tk_
            tk_ = None
            for k in range(8):
                tk_ = R.dma("pool", W_out.ap[:, k, :], W_out_v[:, k, :], W_out, writes=([W_out] if k == 0 else []), extra=[prev_w],
                            order_after=([tk_] if tk_ else []))
            W_out.writers = [tk_]
            for k in range(4):
                R.op("dve", lambda e, k=k: e.tensor_scalar(out=W_out.ap[:, 4 + k, :], in0=W_out.ap[:, 4 + k, :],
                                                           scalar1=cf("rnwP")[:, k:k + 1], scalar2=None, op0=ALU.mult),
                     reads=[cstf, W_out], writes=[W_out], n=340)
            for name, a, b in wcols:
                for k in range(8):
                    R.op("dve", lambda e, k=k, a=a, b=b: e.tensor_scalar(out=W_in.ap[:, k, a:b], in0=W_in.ap[:, k, a:b],
                                                                         scalar1=cf("anwP")[:, k:k + 1], scalar2=None, op0=ALU.mult),
                         reads=[cstf, Wg_buf[name]], writes=[Wg_buf[name]], n=(b - a) // 3)

            iota1 = ta.ap[:, 0:128]
            iota2 = ta.ap[:, 128:256]
            reluP = ta.ap[:, 256:384]
            reluN = ta.ap[:, 384:512]

            R.op("act", lambda e: e.activation(out=lgP.ap, in_=cf("decP"), func=AF.Abs), reads=[cstf], writes=[lgP])
            R.op("act", lambda e: e.activation(out=lgR.ap, in_=cf("decR"), func=AF.Abs), reads=[cstf], writes=[lgR])
            R.op("dve", lambda e: e.tensor_scalar(out=lgP.ap, in0=lgP.ap, scalar1=-1.0, scalar2=None, op0=ALU.mult), reads=[lgP], writes=[lgP])
            R.op("dve", lambda e: e.tensor_scalar(out=lgR.ap, in0=lgR.ap, scalar1=-1.0, scalar2=None, op0=ALU.mult), reads=[lgR], writes=[lgR])
            R.op("act", lambda e: e.activation(out=g128.ap, in_=lgP.ap, func=AF.Exp, scale=128.0), reads=[lgP], writes=[g128])
            for t in range(4):
                R.op("act", lambda e, t=t: e.activation(out=Xif.ap[:, t * 128:(t + 1) * 128], in_=iota1, func=AF.Exp,
                                                        scale=lgP.ap[:, t:t + 1]), reads=[lgP, ta], writes=[Xif])
                R.op("act", lambda e, t=t: e.activation(out=Xib.ap[:, t * 128:(t + 1) * 128], in_=iota2, func=AF.Exp,
                                                        scale=lgP.ap[:, 4 + t:5 + t]), reads=[lgP, ta], writes=[Xib])
            R.op("act", lambda e: e.activation(out=Zfb.ap[:, 0:8], in_=lgR.ap[:, 0:8], func=AF.Exp, scale=cf("c127m")),
                 reads=[lgR, cstf], writes=[Zfb])
            R.op("act", lambda e: e.activation(out=Zfb.ap[:, 8:16], in_=lgR.ap[:, 8:16], func=AF.Exp, scale=cf("cm")),
                 reads=[lgR, cstf], writes=[Zfb])
            R.op("act", lambda e: e.activation(out=esink.ap, in_=cf("sink"), func=AF.Exp), reads=[cstf], writes=[esink])
            for h in range(8):
                R.op("dve", lambda e, h=h: e.tensor_scalar(out=te.ap[:, 0:128], in0=reluP, scalar1=lgR.ap[:, h:h + 1],
                                                           scalar2=None, op0=ALU.mult), reads=[ta, lgR], writes=[te])
                R.op("dve", lambda e, h=h: e.scalar_tensor_tensor(out=td.ap[:, 0:128], in0=reluN, scalar=lgR.ap[:, 8 + h:9 + h],
                                                                  in1=te.ap[:, 0:128], op0=ALU.mult, op1=ALU.add),
                     reads=[ta, lgR, te], writes=[td])
                R.op("act", lambda e, h=h: e.activation(out=DT.ap[:, h * 128:(h + 1) * 128], in_=td.ap[:, 0:128], func=AF.Exp),
                     reads=[td], writes=[DT])
            for c in range(NT_OWN + 1):
                R.op("pool", lambda e, c=c: e.memset(v3(Vb[c].ap, 2)[:, :, 64:65], 1.0), writes=[Vb[c]])
            R.op("pool", lambda e: e.memset(Rb.ap, 0.0), writes=[Rb])
            R.op("pool", lambda e: e.memset(Rf.ap, 0.0), writes=[Rf])
            R.op("pool", lambda e: e.memset(epsb.ap, EPS), writes=[epsb])
            R.op("pool", lambda e: e.memset(onesb.ap, 1.0), writes=[onesb])

            if KSTOP == "setup":
                raise _Stop()
            def load_tile(L, xbuf, slot):
                R.dma("sp", xbuf.ap, xs[L * 128:(L + 1) * 128, :], xbuf, writes=[xbuf])
                R.dma("sp", csl[slot].ap, cs_d[L * 128:(L + 1) * 128, :], csl[slot], writes=[csl[slot]])

            def prep_tile(xbuf, slot, xb=xb, xT=xT):
                s = st[slot]
                R.op("act", lambda e: e.activation(out=junk_ap, in_=xbuf.ap, func=AF.Square, accum_out=s.ap[:, 0:1]),
                     reads=[xbuf], writes=[junk, s], n=1024)
                R.op("act", lambda e: e.activation(out=s.ap[:, 1:2], in_=s.ap[:, 0:1], func=AF.Ln, scale=1.0 / D_MODEL, bias=epsb.ap),
                     reads=[s, epsb], writes=[s])
                R.op("act", lambda e: e.activation(out=s.ap[:, 2:3], in_=s.ap[:, 1:2], func=AF.Exp, scale=-0.5), reads=[s], writes=[s])
                R.op("dve", lambda e: e.tensor_scalar(out=s.ap[:, 5:6], in0=s.ap[:, 2:3], scalar1=-1.0, scalar2=None,
                                                      op0=ALU.mult), reads=[s], writes=[s])
                R.op("dve", lambda e: e.tensor_scalar(out=s.ap[:, 3:4], in0=s.ap[:, 2:3], scalar1=0.125, scalar2=None,
                                                      op0=ALU.mult), reads=[s], writes=[s])
                R.op("dve", lambda e: e.tensor_scalar(out=s.ap[:, 4:5], in0=s.ap[:, 2:3], scalar1=0.5, scalar2=None,
                                                      op0=ALU.mult), reads=[s], writes=[s])
                R.op("dve", lambda e: e.tensor_copy(out=xb.ap, in_=xbuf.ap), reads=[xbuf], writes=[xb], n=600)
                for k in range(8):
                    R.op("pe", lambda e, k=k: e.transpose(out=T0b[:, k * 128:(k + 1) * 128], in_=xb.ap[:, k * 128:(k + 1) * 128],
                                                          identity=ident), reads=[xb, cstb], writes=[T0], n=128)
                R.op("act", lambda e: e.activation(out=xT.ap, in_=T0b, func=AF.Identity), reads=[T0], writes=[xT])
                return s

            def inproj(bank, c0, n, wnames, xT=xT, also=()):
                xT3 = v3(xT.ap, 8)
                for k in range(8):
                    R.op("pe", lambda e, k=k: e.matmul(bank.ap[:, 0:n], lhsT=xT3[:, k, :], rhs=W_in.ap[:, k, c0:c0 + n],
                                                       start=(k == 0), stop=(k == 7)),
                         reads=[xT] + [Wg_buf[w] for w in wnames], writes=[bank] + list(also), n=n)

            def rope(eng_a, eng_b, src, dst_t1, dst_t2, A, B, H):
                s3 = v3(src.ap[:, 0:H * 64], H)
                t13 = v3(dst_t1.ap[:, 0:H * 64], H)
                t23 = v3(dst_t2.ap[:, 0:H * 64], H)
                Ab = A.unsqueeze(1).broadcast_to([128, H, 64])
                Bl = B[:, 0:32].unsqueeze(1).broadcast_to([128, H, 32])
                Bh = B[:, 32:64].unsqueeze(1).broadcast_to([128, H, 32])
                return s3, t13, t23, Ab, Bl, Bh

            R.phase = "pre"
            xring = [hbuf[4], hbuf[5], hbuf[6]]
            order = list(range(NT_ALL - 1, -1, -1))
            load_tile(order[0], xring[0], 0)
            for i, L in enumerate(order):
                if KSTOP == "pre1" and i == 1:
                    raise _Stop()
                xbuf = xring[i % 3]
                slot = i % 2
                if i + 1 < len(order):
                    load_tile(order[i + 1], xring[(i + 1) % 3], (i + 1) % 2)
                own = L < NT_OWN
                needkv = L <= NT_OWN
                par = i % 2
                xb_ = (xb, retS)[par]
                xT_ = (xT, yT)[par]
                ta_ = (ta, tg)[par]
                rvb_ = (rvb, rkb)[par]
                kzb_ = (qb, rqxf)[par]
                kzf_ = (rqb, rqxb)[par]
                Brk = (F[0], F[3])[par]
                Brv = (F[1], F[4])[par]
                Bkv = (F[2], F[5])[par]
                s = prep_tile(xbuf, slot, xb_, xT_)
                cst = csl[slot]
                cosv = cst.ap[:, 0:64]
                sinv = cst.ap[:, 64:128]
                inproj(Brk, C_RK, 512, ["rkrv"], xT_)
                inproj(Brv, C_RV, 512, ["rkrv"], xT_)
                if needkv:
                    inproj(T1b, C_AK, 256, ["akav"], xT_, also=[T1a])
                R.op("act", lambda e, s=s, ta_=ta_, Brk=Brk: e.activation(out=ta_.ap, in_=Brk.ap, func=AF.Identity, scale=s.ap[:, 3:4]),
                     reads=[Brk, s], writes=[ta_])
                R.op("act", lambda e, s=s, rvb_=rvb_, Brv=Brv: e.activation(out=rvb_.ap, in_=Brv.ap, func=AF.Identity, scale=s.ap[:, 2:3]),
                     reads=[Brv, s], writes=[rvb_])
                s3, t13, t23, Ab, Bl, Bh = rope(None, None, ta_, td, te, cosv, sinv, 8)
                R.op("dve", lambda e, s3=s3, t13=t13, Ab=Ab: e.tensor_tensor(out=t13, in0=s3, in1=Ab, op=ALU.mult),
                     reads=[ta_, cst], writes=[td])
                R.op("pool", lambda e, s3=s3, t23=t23, Bl=Bl: e.tensor_tensor(out=t23[:, :, 0:32], in0=s3[:, :, 32:64], in1=Bl, op=ALU.mult),
                     reads=[ta_, cst], writes=[te])
                R.op("pool", lambda e, s3=s3, t23=t23, Bh=Bh: e.tensor_tensor(out=t23[:, :, 32:64], in0=s3[:, :, 0:32], in1=Bh, op=ALU.mult),
                     reads=[ta_, cst], writes=[te])
                R.op("dve", lambda e: e.tensor_tensor(out=td.ap, in0=td.ap, in1=te.ap, op=ALU.add), reads=[td, te], writes=[td])
                Zb_b = Zfb.ap[:, 8:16].unsqueeze(2).broadcast_to([128, 8, 64])
                Zf_b = Zfb.ap[:, 0:8].unsqueeze(2).broadcast_to([128, 8, 64])
                R.op("dve", lambda e, Zb_b=Zb_b, kzb_=kzb_: e.tensor_tensor(out=v3(kzb_.ap, 8), in0=v3(td.ap, 8), in1=Zb_b, op=ALU.mult),
                     reads=[td, Zfb], writes=[kzb_])
                if own:
                    R.op("pool", lambda e, Zf_b=Zf_b, kzf_=kzf_: e.tensor_tensor(out=v3(kzf_.ap, 8), in0=v3(td.ap, 8), in1=Zf_b, op=ALU.mult),
                         reads=[td, Zfb], writes=[kzf_])
                for t in range(4):
                    R.op("pe", lambda e, t=t, Bkv=Bkv, kzb_=kzb_, rvb_=rvb_: e.matmul(Bkv.ap[:, t * 128:(t + 1) * 128], lhsT=kzb_.ap[:, t * 128:(t + 1) * 128],
                                                       rhs=rvb_.ap[:, t * 128:(t + 1) * 128], start=True, stop=True),
                         reads=[kzb_, rvb_], writes=[Bkv], n=128)
                if own:
                    for t in range(4):
                        R.op("pe", lambda e, t=t, Brk=Brk, kzf_=kzf_, rvb_=rvb_: e.matmul(Brk.ap[:, t * 128:(t + 1) * 128], lhsT=kzf_.ap[:, t * 128:(t + 1) * 128],
                                                           rhs=rvb_.ap[:, t * 128:(t + 1) * 128], start=True, stop=True),
                             reads=[kzf_, rvb_], writes=[Brk], n=128)
                Rb3 = v3(Rb.ap, 4)
                rt3 = v3(rtmp.ap, 4)
                F33 = v3(Bkv.ap, 4)
                F43 = v3(Brk.ap, 4)
                if own:
                    R.op("dve", lambda e, L=L: e.tensor_copy(out=RbS[L].ap, in_=Rb.ap), reads=[Rb], writes=[RbS[L]])
                gb_b = g128.ap[:, 4:8].unsqueeze(2).broadcast_to([128, 4, 64])
                R.op("dve", lambda e, gb_b=gb_b, Rb3=Rb3, rt3=rt3: e.tensor_tensor(out=rt3, in0=Rb3, in1=gb_b, op=ALU.mult),
                     reads=[Rb, g128], writes=[rtmp])
                R.op("dve", lambda e, Rb3=Rb3, rt3=rt3, F33=F33: e.tensor_tensor(out=Rb3[0:64], in0=rt3[0:64], in1=F33[0:64, :, 0:64], op=ALU.add),
                     reads=[rtmp, Bkv], writes=[Rb])
                R.op("dve", lambda e, Rb3=Rb3, rt3=rt3, F33=F33: e.tensor_tensor(out=Rb3[64:128], in0=rt3[64:128], in1=F33[64:128, :, 64:128], op=ALU.add),
                     reads=[rtmp, Bkv, Rb], writes=[Rb])
                if own:
                    R.op("act", lambda e, L=L, F43=F43: e.activation(out=kvf[L].ap[0:64], in_=F43[0:64, :, 0:64], func=AF.Identity),
                         reads=[Brk], writes=[kvf[L]])
                    R.op("act", lambda e, L=L, F43=F43: e.activation(out=kvf[L].ap[64:128], in_=F43[64:128, :, 64:128], func=AF.Identity),
                         reads=[Brk, kvf[L]], writes=[kvf[L]])
                if needkv:
                    R.op("act", lambda e, s=s: e.activation(out=akf.ap, in_=T1b.ap[:, 0:128], func=AF.Identity, scale=s.ap[:, 2:3]),
                         reads=[T1b, s], writes=[akf])
                    R.op("act", lambda e, s=s, L=L: e.activation(out=v3(Vb[L].ap, 2)[:, :, 0:64], in_=v3(T1b.ap[:, 128:256], 2),
                                                                  func=AF.Identity, scale=s.ap[:, 2:3]),
                         reads=[T1b, s, Vb[L]], writes=[Vb[L]])
                    R.op("dve", lambda e: e.tensor_tensor(out=tk1.ap, in0=akf.ap, in1=akf.ap, op=ALU.mult), reads=[akf], writes=[tk1])
                    R.op("dve", lambda e: e.tensor_reduce(out=s8a.ap[:, 0:2], in_=v3(tk1.ap, 2), axis=AX.X, op=ALU.add),
                         reads=[tk1], writes=[s8a])
                    R.op("act", lambda e: e.activation(out=s8a.ap[:, 2:4], in_=s8a.ap[:, 0:2], func=AF.Ln, scale=1.0 / 64, bias=epsb.ap),
                         reads=[s8a, epsb], writes=[s8a])
                    R.op("act", lambda e: e.activation(out=s8a.ap[:, 4:6], in_=s8a.ap[:, 2:4], func=AF.Exp, scale=-0.5), reads=[s8a], writes=[s8a])
                    R.op("pool", lambda e, cosv=cosv: e.tensor_tensor(out=tabk.ap[:, 0:64], in0=cosv, in1=cf("knw"), op=ALU.mult),
                         reads=[cst, cstf], writes=[tabk])
                    R.op("pool", lambda e, sinv=sinv: e.tensor_tensor(out=tabk.ap[:, 64:128], in0=sinv, in1=cf("knws"), op=ALU.mult),
                         reads=[cst, cstf, tabk], writes=[tabk])
                    s3, t13, t23, Ab, Bl, Bh = rope(None, None, akf, tk1, tk2, tabk.ap[:, 0:64], tabk.ap[:, 64:128], 2)
                    R.op("pool", lambda e, s3=s3, t13=t13, Ab=Ab: e.tensor_tensor(out=t13, in0=s3, in1=Ab, op=ALU.mult),
                         reads=[akf, tabk, s8a], writes=[tk1])
                    R.op("pool", lambda e, s3=s3, t23=t23, Bl=Bl: e.tensor_tensor(out=t23[:, :, 0:32], in0=s3[:, :, 32:64], in1=Bl, op=ALU.mult),
                         reads=[akf, tabk], writes=[tk2])
                    R.op("pool", lambda e, s3=s3, t23=t23, Bh=Bh: e.tensor_tensor(out=t23[:, :, 32:64], in0=s3[:, :, 0:32], in1=Bh, op=ALU.mult),
                         reads=[akf, tabk, tk2], writes=[tk2])
                    R.op("dve", lambda e: e.tensor_tensor(out=tk1.ap, in0=tk1.ap, in1=tk2.ap, op=ALU.add), reads=[tk1, tk2], writes=[tk1])
                    kd4 = kdup.ap.rearrange("p (g u d) -> p g u d", g=2, u=2)
                    rk2b = s8a.ap[:, 4:6].unsqueeze(2).broadcast_to([128, 2, 64])
                    for u in range(2):
                        R.op("dve", lambda e, u=u, kd4=kd4, rk2b=rk2b: e.tensor_tensor(out=kd4[:, :, u, :], in0=v3(tk1.ap, 2), in1=rk2b, op=ALU.mult),
                             reads=[tk1, s8a, kdup], writes=[kdup])
                    for g in range(2):
                        R.op("pe", lambda e, g=g: e.transpose(out=T1ab[:, g * 128:(g + 1) * 128], in_=kdup.ap[:, g * 128:(g + 1) * 128],
                                                              identity=ident), reads=[kdup, cstb], writes=[T1a, T1b], n=128)
                    R.op("act", lambda e, L=L: e.activation(out=KT[L].ap, in_=T1ab[:, 0:256], func=AF.Identity), reads=[T1a], writes=[KT[L]])

            if KSTOP == "pre":
                raise _Stop()
            Rf3 = v3(Rf.ap, 4)
            rt3 = v3(rtmp.ap, 4)
            gf_b = g128.ap[:, 0:4].unsqueeze(2).broadcast_to([128, 4, 64])
            scan_last = None
            for c in range(NT_OWN):
                R.op("dve", lambda e, c=c: e.tensor_copy(out=RfS[c].ap, in_=Rf.ap), reads=[Rf], writes=[RfS[c]])
                R.op("dve", lambda e: e.tensor_tensor(out=rt3, in0=Rf3, in1=gf_b, op=ALU.mult), reads=[Rf, g128], writes=[rtmp])
                scan_last = R.op("dve", lambda e, c=c: e.tensor_tensor(out=Rf3, in0=rt3, in1=kvf[c].ap, op=ALU.add),
                                 reads=[rtmp, kvf[c]], writes=[Rf])

            if KSTOP == "scan":
                raise _Stop()
            R.phase = "main"

            def load_main(c):
                extra = [scan_last] if c >= 12 else []
                R.dma("sp", hbuf[c].ap, xs[c * 128:(c + 1) * 128, :], hbuf[c], writes=[hbuf[c]], extra=extra)
                R.dma("sp", csl[c % 2].ap, cs_d[c * 128:(c + 1) * 128, :], csl[c % 2], writes=[csl[c % 2]])

            load_main(0)
            for c in range(NT_OWN):
                if KSTOP == "main1" and c == 1:
                    raise _Stop()
                xbuf = hbuf[c]
                slot = c % 2
                s = prep_tile(xbuf, slot)
                cst = csl[slot]
                cosv = cst.ap[:, 0:64]
                sinv = cst.ap[:, 64:128]
                R.op("pool", lambda e, cosv=cosv: e.tensor_tensor(out=tabq.ap[:, 0:64], in0=cosv, in1=cf("qnw"), op=ALU.mult),
                     reads=[cst, cstf], writes=[tabq])
                R.op("pool", lambda e, sinv=sinv: e.tensor_tensor(out=tabq.ap[:, 64:128], in0=sinv, in1=cf("qnws"), op=ALU.mult),
                     reads=[cst, cstf, tabq], writes=[tabq])
                BQ, BG = (F[4], F[0]) if os.environ.get("KSWAP", "0") == "1" else (F[0], F[4])
                inproj(BQ, C_AQ, 512, ["aq"])
                inproj(F[1], C_RQ, 512, ["rq"])
                inproj(F[2], C_RK, 512, ["rkrv"])
                inproj(F[3], C_RV, 512, ["rkrv"])
                inproj(BG, C_RG, 512, ["rg"])
                if c + 1 < NT_OWN:
                    load_main(c + 1)
                R.op("act", lambda e, s=s: e.activation(out=ta.ap, in_=BQ.ap, func=AF.Identity, scale=s.ap[:, 2:3]),
                     reads=[BQ, s], writes=[ta])
                R.op("dve", lambda e: e.tensor_tensor(out=td.ap, in0=ta.ap, in1=ta.ap, op=ALU.mult), reads=[ta], writes=[td])
                R.op("dve", lambda e: e.tensor_reduce(out=s8a.ap, in_=v3(td.ap, 8), axis=AX.X, op=ALU.add), reads=[td], writes=[s8a])
                R.op("act", lambda e: e.activation(out=s8b.ap, in_=s8a.ap, func=AF.Ln, scale=1.0 / 64, bias=epsb.ap),
                     reads=[s8a, epsb], writes=[s8b])
                R.op("act", lambda e: e.activation(out=s8c.ap, in_=s8b.ap, func=AF.Exp, scale=-0.5), reads=[s8b], writes=[s8c])
                s3, t13, t23, Ab, Bl, Bh = rope(None, None, ta, td, te, tabq.ap[:, 0:64], tabq.ap[:, 64:128], 8)
                R.op("dve", lambda e, s3=s3, t13=t13, Ab=Ab: e.tensor_tensor(out=t13, in0=s3, in1=Ab, op=ALU.mult),
                     reads=[ta, tabq, s8a], writes=[td])
                R.op("pool", lambda e, s3=s3, t23=t23, Bl=Bl: e.tensor_tensor(out=t23[:, :, 0:32], in0=s3[:, :, 32:64], in1=Bl, op=ALU.mult),
                     reads=[ta, tabq], writes=[te])
                R.op("pool", lambda e, s3=s3, t23=t23, Bh=Bh: e.tensor_tensor(out=t23[:, :, 32:64], in0=s3[:, :, 0:32], in1=Bh, op=ALU.mult),
                     reads=[ta, tabq, te], writes=[te])
                R.op("dve", lambda e: e.tensor_tensor(out=td.ap, in0=td.ap, in1=te.ap, op=ALU.add), reads=[td, te], writes=[td])
                rq8b = s8c.ap.unsqueeze(2).broadcast_to([128, 8, 64])
                R.op("dve", lambda e, rq8b=rq8b: e.tensor_tensor(out=v3(qb.ap, 8), in0=v3(td.ap, 8), in1=rq8b, op=ALU.mult),
                     reads=[td, s8c], writes=[qb])
                for t in range(4):
                    R.op("pe", lambda e, t=t: e.transpose(out=T1ab[:, t * 128:(t + 1) * 128], in_=qb.ap[:, t * 128:(t + 1) * 128],
                                                          identity=ident), reads=[qb, cstb], writes=[T1a], n=128)
                R.op("act", lambda e: e.activation(out=qT.ap, in_=T1ab, func=AF.Identity), reads=[T1a], writes=[qT])
                if KSTOP == "m_q1":
                    raise _Stop()
                R.op("act", lambda e, s=s: e.activation(out=tb2.ap, in_=F[1].ap, func=AF.Identity, scale=s.ap[:, 2:3]),
                     reads=[F[1], s], writes=[tb2] + ([akf, tk1, tk2, tabk] if c == 0 else []), n=512)
                s3, t13, t23, Ab, Bl, Bh = rope(None, None, tb2, td, te, cosv, sinv, 8)
                R.op("dve", lambda e, s3=s3, t13=t13, Ab=Ab: e.tensor_tensor(out=t13, in0=s3, in1=Ab, op=ALU.mult),
                     reads=[tb2, cst], writes=[td])
                R.op("pool", lambda e, s3=s3, t23=t23, Bl=Bl: e.tensor_tensor(out=t23[:, :, 0:32], in0=s3[:, :, 32:64], in1=Bl, op=ALU.mult),
                     reads=[tb2, cst], writes=[te])
                R.op("pool", lambda e, s3=s3, t23=t23, Bh=Bh: e.tensor_tensor(out=t23[:, :, 32:64], in0=s3[:, :, 0:32], in1=Bh, op=ALU.mult),
                     reads=[tb2, cst, te], writes=[te])
                R.op("dve", lambda e: e.tensor_tensor(out=rqb.ap, in0=td.ap, in1=te.ap, op=ALU.add), reads=[td, te], writes=[rqb])
                for t in range(4):
                    R.op("pe", lambda e, t=t: e.transpose(out=T1bb[:, t * 128:(t + 1) * 128], in_=rqb.ap[:, t * 128:(t + 1) * 128],
                                                          identity=ident), reads=[rqb, cstb], writes=[T1b], n=128)
                R.op("act", lambda e: e.activation(out=rqT.ap, in_=T1bb, func=AF.Identity), reads=[T1b], writes=[rqT])
                R.op("dve", lambda e: e.tensor_tensor(out=rqxf.ap, in0=rqT.ap, in1=Xif.ap, op=ALU.mult), reads=[rqT, Xif], writes=[rqxf])
                R.op("dve", lambda e: e.tensor_tensor(out=rqxb.ap, in0=rqT.ap, in1=Xib.ap, op=ALU.mult), reads=[rqT, Xib], writes=[rqxb])
                if KSTOP == "m_q2":
                    raise _Stop()
                R.op("act", lambda e, s=s: e.activation(out=ta.ap, in_=F[2].ap, func=AF.Identity, scale=s.ap[:, 3:4]),
                     reads=[F[2], s], writes=[ta])
                s3, t13, t23, Ab, Bl, Bh = rope(None, None, ta, td, te, cosv, sinv, 8)
                R.op("dve", lambda e, s3=s3, t13=t13, Ab=Ab: e.tensor_tensor(out=t13, in0=s3, in1=Ab, op=ALU.mult),
                     reads=[ta, cst], writes=[td])
                R.op("pool", lambda e, s3=s3, t23=t23, Bl=Bl: e.tensor_tensor(out=t23[:, :, 0:32], in0=s3[:, :, 32:64], in1=Bl, op=ALU.mult),
                     reads=[ta, cst], writes=[te])
                R.op("pool", lambda e, s3=s3, t23=t23, Bh=Bh: e.tensor_tensor(out=t23[:, :, 32:64], in0=s3[:, :, 0:32], in1=Bh, op=ALU.mult),
                     reads=[ta, cst, te], writes=[te])
                R.op("dve", lambda e: e.tensor_tensor(out=rkb.ap, in0=td.ap, in1=te.ap, op=ALU.add), reads=[td, te], writes=[rkb])
                for t in range(4):
                    R.op("pe", lambda e, t=t: e.transpose(out=T1ab[:, t * 128:(t + 1) * 128], in_=rkb.ap[:, t * 128:(t + 1) * 128],
                                                          identity=ident), reads=[rkb, cstb], writes=[T1a], n=128)
                R.op("act", lambda e: e.activation(out=rkT.ap, in_=T1ab, func=AF.Identity), reads=[T1a], writes=[rkT])
                if KSTOP == "m_q3":
                    raise _Stop()
                R.op("act", lambda e, s=s: e.activation(out=rvb.ap, in_=F[3].ap, func=AF.Identity, scale=s.ap[:, 2:3]),
                     reads=[F[3], s], writes=[rvb])
                R.op("act", lambda e, s=s: e.activation(out=tg.ap, in_=BG.ap, func=AF.Exp, scale=s.ap[:, 5:6]),
                     reads=[BG, s], writes=[tg])
                R.op("act", lambda e: e.activation(out=tg.ap, in_=tg.ap, func=AF.Ln, bias=onesb.ap), reads=[tg, onesb], writes=[tg])
                R.op("act", lambda e: e.activation(out=tg.ap, in_=tg.ap, func=AF.Exp, scale=-1.0), reads=[tg], writes=[tg])
                R.op("dve", lambda e, s=s: e.scalar_tensor_tensor(out=tg.ap, in0=BG.ap, scalar=s.ap[:, 2:3], in1=tg.ap,
                                                                  op0=ALU.mult, op1=ALU.mult), reads=[BG, s, tg], writes=[tg])

                if KSTOP == "m_q":
                    raise _Stop()
                kbs = [kb for kb in (c - 1, c, c + 1) if kb >= 0]
                nkb = len(kbs)
                qT3 = v3(qT.ap, 4)
                Obank = [F[4], F[5]]
                it = 0
                for g in range(2):
                    for ee in range(2):
                        bx, by = (F[0], F[1]) if it % 2 == 0 else (F[2], F[3])
                        it += 1
                        regs = [bx.ap[:, 0:256], bx.ap[:, 256:512], by.ap[:, 0:256]]
                        rbuf = [bx, bx, by]
                        for j, kb in enumerate(kbs):
                            KT3 = v3(KT[kb].ap, 2)
                            masked = (kb != c)
                            R.op("pe", lambda e, j=j, KT3=KT3, g=g, ee=ee, masked=masked, regs=regs: e.matmul(
                                v3(regs[j], 2), lhsT=KT3[64 * ee:64 * ee + 64, g, :], rhs=qT3[64 * ee:64 * ee + 64, 2 * g:2 * g + 2, :],
                                start=True, stop=(not masked)), reads=[KT[kb], qT], writes=[rbuf[j]], n=256)
                            if masked:
                                mk = mprev if kb < c else mnext
                                R.op("pe", lambda e, j=j, mk=mk, regs=regs: e.matmul(regs[j], lhsT=ident, rhs=mk, start=False, stop=True),
                                     reads=[cstb], writes=[rbuf[j]], n=256)
                        PTc = (PT, PT2)[it % 2]
                        n1 = min(nkb, 2)
                        R.op("act", lambda e, n1=n1, bx=bx, PTc=PTc: e.activation(out=PTc.ap[:, 0:n1 * 256], in_=bx.ap[:, 0:n1 * 256], func=AF.Exp, scale=0.125),
                             reads=[bx], writes=[PTc])
                        if nkb == 3:
                            R.op("act", lambda e, by=by, PTc=PTc: e.activation(out=PTc.ap[:, 512:768], in_=by.ap[:, 0:256], func=AF.Exp, scale=0.125),
                                 reads=[by, PTc], writes=[PTc])
                        for tt in range(2):
                            h = 2 * (2 * g + tt) + ee
                            ob = Obank[h // 4]
                            hl = h % 4
                            for j, kb in enumerate(kbs):
                                R.op("pe", lambda e, j=j, kb=kb, tt=tt, ob=ob, hl=hl, g=g, nkb=nkb, PTc=PTc: e.matmul(
                                    ob.ap[:, hl * 65:(hl + 1) * 65], lhsT=PTc.ap[:, j * 256 + tt * 128: j * 256 + (tt + 1) * 128],
                                    rhs=v3(Vb[kb].ap, 2)[:, g, :], start=(j == 0), stop=(j == nkb - 1)),
                                    reads=[PTc, Vb[kb]], writes=[ob], n=800)
                if KSTOP == "m_att0":
                    raise _Stop()
                for gb in range(2):
                    O3 = Obank[gb].ap[:, 0:260].rearrange("p (h d) -> p h d", h=4)
                    R.op("dve", lambda e, gb=gb, O3=O3: e.tensor_tensor(out=s8d.ap[:, gb * 4:(gb + 1) * 4], in0=O3[:, :, 64],
                                                                        in1=esink.ap[:, gb * 4:(gb + 1) * 4], op=ALU.add),
                         reads=[Obank[gb], esink, s8d], writes=[s8d])
                R.op("dve", lambda e: e.reciprocal(out=s8e.ap, in_=s8d.ap), reads=[s8d], writes=[s8e])
                for gb in range(2):
                    O3 = Obank[gb].ap[:, 0:260].rearrange("p (h d) -> p h d", h=4)
                    rdb = s8e.ap[:, gb * 4:(gb + 1) * 4].unsqueeze(2).broadcast_to([128, 4, 64])
                    R.op("dve", lambda e, gb=gb, O3=O3, rdb=rdb: e.tensor_tensor(out=v3(yb.ap[:, gb * 256:(gb + 1) * 256], 4), in0=O3[:, :, 0:64],
                                                                                 in1=rdb, op=ALU.mult),
                         reads=[Obank[gb], s8e, yb], writes=[yb], n=256)

                if KSTOP == "m_att":
                    raise _Stop()
                rkT3 = v3(rkT.ap, 4)
                rqT3 = v3(rqT.ap, 4)
                rxf3 = v3(rqxf.ap, 4)
                rxb3 = v3(rqxb.ap, 4)
                for ee in range(2):
                    for t in range(4):
                        bank = F[ee]
                        col = t * 128
                        R.op("pe", lambda e, t=t, ee=ee, bank=bank, col=col: e.matmul(
                            bank.ap[:, col:col + 128], lhsT=rkT3[64 * ee:64 * ee + 64, t, :], rhs=rqT3[64 * ee:64 * ee + 64, t, :],
                            start=True, stop=True), reads=[rkT, rqT], writes=[bank], n=128)
                if KSTOP == "m_r0":
                    raise _Stop()
                retS4 = retS.ap.rearrange("p (t e n) -> p t e n", t=4, e=2)
                DT4 = DT.ap.rearrange("p (t e n) -> p t e n", t=4, e=2)
                for gb in range(2):
                    R.op("dve", lambda e, gb=gb, retS4=retS4, DT4=DT4: e.tensor_tensor(out=retS4[:, :, gb, :], in0=v3(F[gb].ap, 4),
                                                                                     in1=DT4[:, :, gb, :], op=ALU.mult),
                         reads=[F[gb], DT, retS], writes=[retS])
                if KSTOP == "m_r1":
                    raise _Stop()
                for h in range(8):
                    t, ee = h // 2, h % 2
                    Rf3s = v3(RfS[c].ap, 4)
                    Rb3s = v3(RbS[c].ap, 4)
                    R.op("pe", lambda e, h=h: e.matmul(F[2].ap[:, h * 64:(h + 1) * 64], lhsT=retS.ap[:, h * 128:(h + 1) * 128],
                                                       rhs=rvb.ap[:, h * 64:(h + 1) * 64], start=True, stop=False),
                         reads=[retS, rvb], writes=[F[2]], n=200)
                    R.op("pe", lambda e, h=h, t=t, ee=ee, Rf3s=Rf3s: e.matmul(F[2].ap[:, h * 64:(h + 1) * 64], lhsT=rxf3[64 * ee:64 * ee + 64, t, :],
                                                                             rhs=Rf3s[64 * ee:64 * ee + 64, t, :], start=False, stop=False),
                         reads=[rqxf, RfS[c]], writes=[F[2]], n=100)
                    R.op("pe", lambda e, h=h, t=t, ee=ee, Rb3s=Rb3s: e.matmul(F[2].ap[:, h * 64:(h + 1) * 64], lhsT=rxb3[64 * ee:64 * ee + 64, t, :],
                                                                             rhs=Rb3s[64 * ee:64 * ee + 64, t, :], start=False, stop=True),
                         reads=[rqxb, RbS[c]], writes=[F[2]], n=100)
                if KSTOP == "m_r2":
                    raise _Stop()
                R.op("act", lambda e: e.activation(out=tf.ap, in_=F[2].ap, func=AF.Square), reads=[F[2]], writes=[tf])
                R.op("dve", lambda e: e.tensor_reduce(out=s8f.ap, in_=v3(tf.ap, 8), axis=AX.X, op=ALU.add), reads=[tf], writes=[s8f])
                R.op("act", lambda e: e.activation(out=s8g.ap, in_=s8f.ap, func=AF.Ln, scale=1.0 / 64, bias=epsb.ap),
                     reads=[s8f, epsb], writes=[s8g])
                R.op("act", lambda e: e.activation(out=s8h.ap, in_=s8g.ap, func=AF.Exp, scale=-0.5), reads=[s8g], writes=[s8h])
                R.op("dve", lambda e: e.tensor_tensor(out=tf.ap, in0=F[2].ap, in1=tg.ap, op=ALU.mult), reads=[F[2], tg, s8f], writes=[tf])
                rr8b = s8h.ap.unsqueeze(2).broadcast_to([128, 8, 64])
                R.op("dve", lambda e, rr8b=rr8b: e.tensor_tensor(out=v3(yb.ap[:, 512:1024], 8), in0=v3(tf.ap, 8), in1=rr8b, op=ALU.mult),
                     reads=[tf, s8h, yb], writes=[yb], n=512)

                if KSTOP == "m_ret":
                    raise _Stop()
                for k in range(8):
                    R.op("pe", lambda e, k=k: e.transpose(out=T0b[:, k * 128:(k + 1) * 128], in_=yb.ap[:, k * 128:(k + 1) * 128],
                                                          identity=ident), reads=[yb, cstb], writes=[T0], n=128)
                R.op("act", lambda e: e.activation(out=yT.ap, in_=T0b, func=AF.Identity), reads=[T0], writes=[yT])
                yT3 = v3(yT.ap, 8)
                for half in range(2):
                    bank = F[3] if half == 0 else F[5]
                    for k in range(8):
                        R.op("pe", lambda e, k=k, half=half, bank=bank: e.matmul(bank.ap, lhsT=yT3[:, k, :],
                                                                                  rhs=W_out.ap[:, k, half * 512:(half + 1) * 512],
                                                                                  start=(k == 0), stop=(k == 7)),
                             reads=[yT, W_out], writes=[bank])
                    R.op("dve", lambda e, half=half, bank=bank, xbuf=xbuf: e.tensor_tensor(out=xbuf.ap[:, half * 512:(half + 1) * 512],
                                                                                           in0=xbuf.ap[:, half * 512:(half + 1) * 512],
                                                                                           in1=bank.ap, op=ALU.add),
                         reads=[bank, xbuf], writes=[xbuf], n=512)
                R.op("act", lambda e, c=c, xbuf=xbuf: e.activation(out=tf.ap.bitcast(BF16), in_=xbuf.ap, func=AF.Square, accum_out=ssq2.ap[:, c:c + 1]),
                     reads=[xbuf, ssq2], writes=[tf, ssq2], n=1024)
                R.op("act", lambda e, c=c: e.activation(out=lnr2.ap[:, c:c + 1], in_=ssq2.ap[:, c:c + 1], func=AF.Ln, scale=1.0 / D_MODEL, bias=epsb.ap),
                     reads=[ssq2, epsb, lnr2], writes=[lnr2])
                R.op("act", lambda e, c=c: e.activation(out=rstd2.ap[:, c:c + 1], in_=lnr2.ap[:, c:c + 1], func=AF.Exp, scale=-0.5),
                     reads=[lnr2, rstd2], writes=[rstd2])

            if KSTOP == "main":
                raise _Stop()
            R.phase = "ffn"
            fences = []
            f_ = R.op("dve", lambda e: e.tensor_copy(out=fsc.ap[:, 0:1], in_=epsb.ap), reads=[epsb], writes=[Buf("fscD", fsc.ap[:, 0:1])])
            fences.append(f_)
            f_ = R.op("act", lambda e: e.activation(out=fsc.ap[:, 1:2], in_=epsb.ap, func=AF.Identity), reads=[epsb], writes=[Buf("fscA", fsc.ap[:, 1:2])])
            fences.append(f_)
            f_ = R.op("pool", lambda e: e.memset(fsc.ap[:, 2:3], 0.0), writes=[Buf("fscP", fsc.ap[:, 2:3])])
            fences.append(f_)
            f_ = R.op("pe", lambda e: e.matmul(T0.ap[:, 0:1], lhsT=ident, rhs=ident[:, 0:1], start=True, stop=True),
                      reads=[cstb], writes=[T0], n=1)
            fences.append(f_)
            for f_ in fences:
                f_.fence = True
            R.seg = 1
            bar = fences

            def pbuf(name, ap):
                b = Buf(name, ap)
                b.readers = list(bar)
                return b

            AB.off = 0
            AFa.off = 0

            def pb_alloc(arena, name, n):
                b = arena.alloc(name, n)
                b.readers = list(bar)
                return b

            mTg = [pb_alloc(AB, f"mTg{g_}", 4096) for g_ in range(4)]
            mT = []
            for c in range(NT_OWN):
                b_ = Buf(f"mT{c}", v3(mTg[c // 4].ap, 8)[:, :, (c % 4) * 128:(c % 4 + 1) * 128])
                b_.readers = list(bar)
                mT.append(b_)
            hTb = [[pb_alloc(AB, f"hT{s_}_{ci}", 512) for ci in range(4)] for s_ in range(2)]
            mb = pb_alloc(AB, "mb", 1024)
            junk2 = pb_alloc(AB, "junk2", 1024)
            identB = pb_alloc(AB, "identB", 128)
            fnw = pb_alloc(AFa, "fnw", 1024)
            sgt = [pb_alloc(AFa, f"sgt{i}", 512) for i in range(2)]
            st2 = pb_alloc(AFa, "st2", 8)
            T0f = pbuf("T0f", banks[6][:, :])
            T1f = pbuf("T1f", banks[7][:, :])
            slots = [pbuf(f"slot{i}", Wt[:, i * 12288:(i + 1) * 12288]) for i in range(2)]
            early = []
            for wb in Wg_buf.values():
                early.extend(wb.readers)
                early.extend(wb.writers)
            slots[0].readers = early

            R.dma("sp", fnw.ap, fnw_d, fnw, writes=[fnw])
            R.dma("pool", identB.ap, cstb_d[:, 0:128], identB, writes=[identB])

            passes = [(0, 4), (4, 4), (8, 4), (12, 4), (16, 3), (19, 3)]
            Wg_v = w_gate_d.rearrange("(k p) n -> p k n", p=128)
            Wu_v = w_up_d.rearrange("(k p) n -> p k n", p=128)

            def load_pass(r):
                f0, C = passes[r]
                sl = slots[r % 2]
                g3 = v3(sl.ap[:, 0:4096], 8)
                u3 = v3(sl.ap[:, 4096:8192], 8)
                d3 = v3(sl.ap[:, 8192:12288], 4)
                R.dma("pool", g3[:, :, 0:C * 128], Wg_v[:, :, f0 * 128:(f0 + C) * 128], sl, writes=[sl])
                R.dma("pool", u3[:, :, 0:C * 128], Wu_v[:, :, f0 * 128:(f0 + C) * 128], sl, writes=[sl])
                R.dma("pool", d3[:, 0:C, :], w_down_d[f0 * 128:(f0 + C) * 128, :].rearrange("(c p) n -> p c n", p=128), sl, writes=[sl])

            load_pass(0)
            load_pass(1)

            def prologue(tgi):
                for t in range(4):
                    c = tgi * 4 + t
                    R.op("dve", lambda e, c=c: e.scalar_tensor_tensor(out=mb.ap, in0=hbuf[c].ap, scalar=rstd2.ap[:, c:c + 1], in1=fnw.ap,
                                                                      op0=ALU.mult, op1=ALU.mult), reads=[hbuf[c], rstd2, fnw], writes=[mb])
                    for k in range(8):
                        R.op("pe", lambda e, k=k: e.transpose(out=T0b[:, k * 128:(k + 1) * 128], in_=mb.ap[:, k * 128:(k + 1) * 128],
                                                              identity=identB.ap), reads=[mb, identB], writes=[T0f], n=128)
                    R.op("act", lambda e, c=c: e.activation(out=mT[c].ap, in_=v3(T0b, 8), func=AF.Identity), reads=[T0f], writes=[mT[c]])

            prologue(0)
            gi = 0
            for r, (f0, C) in enumerate(passes):
                sl = slots[r % 2]
                g3 = v3(sl.ap[:, 0:4096], 8)
                u3 = v3(sl.ap[:, 4096:8192], 8)
                d3 = v3(sl.ap[:, 8192:12288], 4)
                last = (r == len(passes) - 1)
                for tgi in range(4):
                    if r == 0 and tgi + 1 < 4:
                        prologue(tgi + 1)
                    hs = hTb[tgi % 2]
                    for ci in range(C):
                        gbank = F[0] if gi % 2 == 0 else F[1]
                        ubank = F[2] if gi % 2 == 0 else F[3]
                        sg_ = sgt[gi % 2]
                        gi += 1
                        mg3 = v3(mTg[tgi].ap, 8)
                        for (bank, w3) in ((gbank, g3), (ubank, u3)):
                            for k in range(8):
                                R.op("pe", lambda e, bank=bank, w3=w3, ci=ci, k=k, mg3=mg3: e.matmul(
                                    bank.ap, lhsT=w3[:, k, ci * 128:(ci + 1) * 128],
                                    rhs=mg3[:, k, :], start=(k == 0), stop=(k == 7)),
                                    reads=[sl] + mT[tgi * 4:tgi * 4 + 4], writes=[bank])
                        R.op("act", lambda e, gbank=gbank, sg_=sg_: e.activation(out=sg_.ap, in_=gbank.ap, func=AF.Silu),
                             reads=[gbank], writes=[sg_])
                        R.op("dve", lambda e, ubank=ubank, sg_=sg_, hs=hs, ci=ci: e.tensor_tensor(out=hs[ci].ap, in0=sg_.ap, in1=ubank.ap, op=ALU.mult),
                             reads=[sg_, ubank], writes=[hs[ci]])
                    for t in range(4):
                        c = tgi * 4 + t
                        dbanks = (F[4], F[5]) if t % 2 == 0 else (T1f, T0f)
                        for half in range(2):
                            bank = dbanks[half]
                            for ci in range(C):
                                R.op("pe", lambda e, bank=bank, ci=ci, t=t, half=half, hs=hs, C=C, d3=d3: e.matmul(
                                    bank.ap, lhsT=hs[ci].ap[:, t * 128:(t + 1) * 128], rhs=d3[:, ci, half * 512:(half + 1) * 512],
                                    start=(ci == 0), stop=(ci == C - 1)), reads=[hs[ci], sl], writes=[bank])
                            R.op("dve", lambda e, bank=bank, c=c, half=half: e.tensor_tensor(
                                out=hbuf[c].ap[:, half * 512:(half + 1) * 512], in0=hbuf[c].ap[:, half * 512:(half + 1) * 512],
                                in1=bank.ap, op=ALU.add), reads=[bank, hbuf[c]], writes=[hbuf[c]])
                        if last:
                            R.dma("sp", out_d[c * 128:(c + 1) * 128, :], hbuf[c].ap, hbuf[c], reads=[hbuf[c]], final=True)
                if r + 2 < len(passes):
                    load_pass(r + 2)

        try:
            _record()
        except _Stop:
            pass

        R.finalize()
        _NC_CACHE['R'] = R
        if os.environ.get("KSCHEDSTAT", ""):
            print('sched sim total us:', getattr(R, 'sim_total', None))

        @block.sync
        def _(eng):
            R.emit("sp", eng)

        @block.gpsimd
        def _(eng):
            R.emit("pool", eng)

        @block.scalar
        def _(eng):
            R.emit("act", eng)

        @block.vector
        def _(eng):
            R.emit("dve", eng)

        @block.tensor
        def _(eng):
            R.emit("pe", eng)
    return nc


_NC_CACHE = {}


def _rope_tables(hf):
    l = np.arange(SEQ, dtype=np.float32)
    pos = l if hf == 0 else (np.float32(SEQ - 1) - l)
    inv_freq = (np.float32(10000.0) ** (-(np.arange(0, 64, 2, dtype=np.float32)) / np.float32(64))).astype(np.float32)
    ang = (pos[:, None] * inv_freq[None, :]).astype(np.float32)
    cos = np.cos(ang.astype(np.float64)).astype(np.float32)
    sin = np.sin(ang.astype(np.float64)).astype(np.float32)
    cs = np.concatenate([cos, cos, -sin, sin], axis=1)
    return np.ascontiguousarray(cs, dtype=np.float32)


def _const_tables():
    i = np.arange(128, dtype=np.float32)
    iota1 = np.tile((i + 1.0)[None, :], (128, 1))
    iota2 = np.tile((128.0 - i)[None, :], (128, 1))
    m = i[:, None]
    n = i[None, :]
    reluP = np.maximum(n - m, 0.0)
    reluN = np.maximum(m - n, 0.0)
    csts = np.concatenate([iota1, iota2, reluP, reluN], axis=1).astype(np.float32)
    ident = np.eye(128, dtype=np.float32)
    j = i[:, None]
    q = i[None, :]
    mprev = np.where(j >= q, 0.0, -30000.0).astype(np.float32)
    mnext = np.where(j <= q, 0.0, -30000.0).astype(np.float32)
    cstb = np.concatenate([ident, mprev, mprev, mnext, mnext], axis=1).astype(np.float32)
    return np.ascontiguousarray(csts), np.ascontiguousarray(cstb)


def _make_in_maps(x, attn_norm_w, w_in, q_norm_w, k_norm_w, attn_sink, ret_log_decay_fwd, ret_log_decay_bwd,
           ret_norm_w, w_out, ffn_norm_w, w_gate, w_up, w_down):
    x = np.asarray(x, dtype=np.float32)
    f = lambda a: np.asarray(a, dtype=np.float32)
    attn_norm_w, q_norm_w, k_norm_w, attn_sink = f(attn_norm_w)[0], f(q_norm_w)[0], f(k_norm_w)[0], f(attn_sink)[0]
    dfw, dbw = f(ret_log_decay_fwd)[0], f(ret_log_decay_bwd)[0]
    ret_norm_w, ffn_norm_w = f(ret_norm_w)[0], f(ffn_norm_w)[0]
    w_in_, w_out_, w_gate_, w_up_, w_down_ = (np.ascontiguousarray(f(a)[0]) for a in (w_in, w_out, w_gate, w_up, w_down))

    csts, cstb = _const_tables()
    fnw = np.ascontiguousarray(np.tile(ffn_norm_w[None, :], (128, 1)))
    rope_tabs = [_rope_tables(0), _rope_tables(1)]
    swap = lambda w: np.concatenate([w[32:], w[:32]])
    in_maps = []
    for c in range(8):
        b, hf = c // 2, c % 2
        xs = x[b] if hf == 0 else x[b, ::-1]
        dF, dB = (dfw, dbw) if hf == 0 else (dbw, dfw)
        cstf = np.zeros((128, NCF), dtype=np.float32)

        def put(name, row):
            a, bnd = CF[name]
            cstf[:, a:bnd] = row[None, :]
        a, bnd = CF["anwP"]
        cstf[:, a:bnd] = attn_norm_w.reshape(8, 128).T
        a, bnd = CF["rnwP"]
        cstf[:, a:bnd] = ret_norm_w.reshape(4, 128).T
        put("qnw", q_norm_w)
        put("qnws", swap(q_norm_w))
        put("knw", k_norm_w)
        put("knws", swap(k_norm_w))
        put("sink", attn_sink)
        put("decR", np.concatenate([dF, dB]))
        a, bnd = CF["decP"]
        for t in range(4):
            cstf[0:64, a + t] = dF[2 * t]
            cstf[64:128, a + t] = dF[2 * t + 1]
            cstf[0:64, a + 4 + t] = dB[2 * t]
            cstf[64:128, a + 4 + t] = dB[2 * t + 1]
        a, _b = CF["c127m"]
        cstf[:, a] = 127.0 - np.arange(128, dtype=np.float32)
        a, _b = CF["cm"]
        cstf[:, a] = np.arange(128, dtype=np.float32)
        in_maps.append({
            "xs": np.ascontiguousarray(xs), "cs": rope_tabs[hf], "cstf": cstf, "csts": csts, "cstb": cstb, "fnw": fnw,
            "w_in": w_in_, "w_out": w_out_, "w_gate": w_gate_, "w_up": w_up_, "w_down": w_down_,
        })
    return in_maps


def kernel(**inputs):
    in_maps = _make_in_maps(**inputs)
    if "nc" not in _NC_CACHE:
        _NC_CACHE["nc"] = build_program()
    nc = _NC_CACHE["nc"]
    res = run_bass_kernel_spmd(nc, in_maps, core_ids=list(range(8)))
    out = np.empty((4, SEQ, D_MODEL), dtype=np.float32)
    for c in range(8):
        b, hf = c // 2, c % 2
        o = np.asarray(res.results[c]["out"], dtype=np.float32)
        if hf == 0:
            out[b, 0:2048] = o
        else:
            out[b, 2048:4096] = o[::-1]
    return out
```

```python
import contextlib
import os
import sys
import numpy as np
import concourse.bass as bass
import concourse.mybir as mybir
from concourse.bass_utils import run_bass_kernel_spmd

F32 = mybir.dt.float32
BF16 = mybir.dt.bfloat16
AF = mybir.ActivationFunctionType
ALU = mybir.AluOpType
AX = mybir.AxisListType

D_MODEL = 1024
SEQ = 4096
NT_ALL = 32
NT_OWN = 16
IN_PROJ = 2816
D_FF = 2816
EPS = 1e-6
C_AQ, C_AK, C_AV, C_RQ, C_RK, C_RV, C_RG = 0, 512, 640, 768, 1280, 1792, 2304

CF = {}
_o = 0
for _n, _w in [("anwP", 8), ("rnwP", 4), ("qnw", 64), ("qnws", 64), ("knw", 64), ("knws", 64),
               ("sink", 8), ("decP", 8), ("decR", 16), ("c127m", 1), ("cm", 1)]:
    CF[_n] = (_o, _o + _w)
    _o += _w
NCF = _o
NCB = 128 + 256 + 256


class Op:
    __slots__ = ("eng", "fn", "deps", "signal", "sigval", "dma", "idx", "cost", "seg", "fence", "lat", "fin", "ph", "line", "crit", "estsrc", "prio")


class DmaTok:
    __slots__ = ("sem", "val", "op")

    def __init__(self, sem, val, op):
        self.sem = sem
        self.val = val
        self.op = op


class Buf:
    def __init__(self, name, ap=None, share=None):
        self.name = name
        self.ap = ap
        self._w = []
        self._r = []
        self.share = share
        self.sem = None
        self.cnt = 0

    @property
    def writers(self):
        return (self.share or self)._w

    @writers.setter
    def writers(self, v):
        (self.share or self)._w = v

    @property
    def readers(self):
        return (self.share or self)._r

    @readers.setter
    def readers(self, v):
        (self.share or self)._r = v

    def nfree(self):
        if self.ap is None:
            return 512
        n = 1
        for d in self.ap.shape[1:]:
            n *= d
        return n


_COST = {"dve": (0.10, 1.0 / 870), "act": (0.17, 1.0 / 1200), "pool": (0.15, 1.0 / 450), "pe": (0.02, 1.0 / 2600), "sp": (0.05, 0.0)}


class Rec:
    ENGS = ("pe", "act", "dve", "pool", "sp")

    def __init__(self, sems):
        self.lists = {e: [] for e in self.ENGS}
        self.free_sems = list(sems)
        self.engsem = {e: self.free_sems.pop() for e in ("pe", "act", "dve", "pool")}
        self.final = []
        self.nops = 0
        self.seg = 0

    def _deps(self, reads, writes, extra):
        deps = []
        for b in reads:
            for t in b.writers:
                deps.append((t, "raw"))
        for b in writes:
            for t in b.readers:
                deps.append((t, "war"))
            for t in b.writers:
                deps.append((t, "waw"))
        for t in extra:
            deps.append((t, "raw"))
        return deps

    def _new(self, eng, fn, deps, dma, cost):
        o = Op()
        o.eng = eng
        o.fn = fn
        o.deps = deps
        o.signal = False
        o.sigval = None
        o.dma = dma
        o.idx = self.nops
        self.nops += 1
        o.cost = cost
        o.seg = self.seg
        o.ph = getattr(self, "phase", "setup")
        o.line = sys._getframe(2).f_lineno if os.environ.get("KSCHEDSTAT", "") else 0
        o.fence = False
        o.lat = 0.0
        o.fin = 0.0
        o.crit = None
        o.estsrc = None
        o.prio = getattr(self, "prio", 0)
        self.lists[eng].append(o)
        return o

    def op(self, eng, fn, reads=(), writes=(), extra=(), n=None):
        if n is None:
            n = writes[0].nfree() if writes else 64
        a, b = _COST[eng]
        o = self._new(eng, fn, self._deps(reads, writes, extra), False, a + b * n)
        for bf in reads:
            bf.readers.append(o)
        for bf in writes:
            bf.writers = [o]
            bf.readers = []
        return o

    def dma(self, q, out_ap, in_ap, semowner, reads=(), writes=(), extra=(), final=False, order_after=()):
        if semowner.sem is None:
            semowner.sem = self.free_sems.pop()
        sem = semowner.sem
        semowner.cnt += 1
        fn = lambda e, out_ap=out_ap, in_ap=in_ap, sem=sem: e.dma_start(out=out_ap, in_=in_ap).then_inc(sem, 16)
        deps_ = self._deps(reads, writes, extra)
        for t_ in order_after:
            deps_.append((t_.op, "order"))
        o = self._new(q, fn, deps_, True, 0.06 if q == "sp" else 1.0)
        nel = 1
        for d in out_ap.shape:
            nel *= d
        o.lat = 2.0 + nel * 4 / 180e3
        tok = DmaTok(sem, semowner.cnt * 16, o)
        for bf in reads:
            bf.readers.append(tok)
        for bf in writes:
            bf.writers = [tok]
            bf.readers = []
        if final:
            self.final.append(tok)
        return tok

    def schedule(self):
        allops = []
        for e in self.ENGS:
            allops.extend(self.lists[e])
        allops.sort(key=lambda o: o.idx)
        preds = {}
        succs = {}
        for o in allops:
            ps = {}
            for (t, kind) in o.deps:
                if isinstance(t, DmaTok):
                    ps[id(t.op)] = (t.op, True)
                else:
                    if id(t) not in ps:
                        ps[id(t)] = (t, False)
            preds[id(o)] = list(ps.values())
            for (p, _) in ps.values():
                succs.setdefault(id(p), []).append(o)
        bl = {}
        for o in reversed(allops):
            m = 0.0
            for sc in succs.get(id(o), ()):
                m = max(m, bl[id(sc)] + (float(os.environ.get("KLATX", "0.3")) if sc.eng != o.eng else float(os.environ.get("KLATS", "0.03"))))
            bl[id(o)] = o.cost + m
        use_bl = os.environ.get("KBL", "1") == "1"
        LX = float(os.environ.get("KLATX", "0.3"))
        LS = float(os.environ.get("KLATS", "0.03"))
        slack0 = float(os.environ.get("KSLACK", "0.05"))
        ntrials = int(os.environ.get("KTRIALS", "1"))
        import random
        rng = random.Random(1234)

        def run(bl, slack):
            free = {e: 0.0 for e in self.ENGS}
            neworder = {e: [] for e in self.ENGS}
            nseg = max(o.seg for o in allops) + 1
            for sg in range(nseg):
                ops = [o for o in allops if o.seg == sg]
                inseg = set(id(o) for o in ops)
                indeg = {}
                est = {}
                avail = {e: [] for e in self.ENGS}
                remaining = {e: 0 for e in self.ENGS}
                for o in ops:
                    remaining[o.eng] += 1
                    d = 0
                    t0 = 0.0
                    for (p, isdma) in preds[id(o)]:
                        if id(p) in inseg:
                            d += 1
                        else:
                            t0 = max(t0, p.fin + (p.lat if isdma else LX))
                    indeg[id(o)] = d
                    est[id(o)] = t0
                    if d == 0:
                        avail[o.eng].append(o)
                nleft = len(ops)
                while nleft:
                    best = None
                    for e in self.ENGS:
                        cands = []
                        tmin = None
                        for o in avail[e]:
                            if o.fence and remaining[e] > 1:
                                continue
                            st = max(free[e], est[id(o)])
                            cands.append((st, o))
                            if tmin is None or st < tmin:
                                tmin = st
                        if not cands:
                            continue
                        if use_bl:
                            st, o = max(((st, o) for (st, o) in cands if st <= tmin + slack), key=lambda x: (bl[id(x[1])], -x[1].idx))
                        else:
                            st, o = min(cands, key=lambda x: (x[0], x[1].idx))
                        key = (st, o.idx)
                        if best is None or key < best[0]:
                            best = (key, o)
                    assert best is not None, "scheduler stuck"
                    o = best[1]
                    st = best[0][0]
                    e = o.eng
                    avail[e].remove(o)
                    remaining[e] -= 1
                    nleft -= 1
                    o.fin = st + o.cost
                    o.crit = (neworder[e][-1] if (neworder[e] and free[e] >= est[id(o)]) else o.estsrc)
                    free[e] = o.fin
                    neworder[e].append(o)
                    for sc in succs.get(id(o), ()):
                        if id(sc) not in inseg:
                            continue
                        isdma = any((p is o and dm) for (p, dm) in preds[id(sc)])
                        lat = o.lat if isdma else (LX if sc.eng != e else LS)
                        if o.fin + lat > est[id(sc)]:
                            est[id(sc)] = o.fin + lat
                            sc.estsrc = o
                        indeg[id(sc)] -= 1
                        if indeg[id(sc)] == 0:
                            avail[sc.eng].append(sc)
            return neworder, max(free.values()), {id(o): (o.fin, o.crit) for o in allops}

        bl0 = bl
        best_res = None
        for trial in range(max(1, ntrials)):
            if trial == 0:
                blt, sl = bl0, slack0
            else:
                amp = rng.choice((0.0005, 0.001, 0.003, 0.01))
                blt = {k: v * (1.0 + amp * (2.0 * rng.random() - 1.0)) for k, v in bl0.items()}
                sl = rng.choice((0.03, 0.05, 0.08, 0.15, 0.25))
            res = run(blt, sl)
            if os.environ.get('KSCHEDSTAT', ''):
                print('trial', trial, round(res[1], 1))
            if best_res is None or res[1] < best_res[1]:
                best_res = res
        neworder, total_, fins_ = best_res
        for o in allops:
            o.fin, o.crit = fins_[id(o)]
        free = {"x": total_}
        self.lists = neworder
        self.sim_total = max(free.values())
        if os.environ.get("KSCHEDSTAT", "") == "1":
            stat = {}
            for o in allops:
                d = stat.setdefault(o.ph, {"end": 0.0, "start": 1e18})
                d["end"] = max(d["end"], o.fin)
                d["start"] = min(d["start"], o.fin - o.cost)
                d[o.eng] = d.get(o.eng, 0.0) + o.cost
            for ph, d in stat.items():
                print("SCHED", ph, {k: round(v, 1) for k, v in d.items()})
            cp = os.environ.get("KSCHEDCRIT", "")
            if cp:
                ph_, n_ = cp.split(",")
                tmax_ = float(os.environ.get("KSCHEDTMAX", "1e18"))
                cur = max((o for o in allops if o.ph == ph_ and o.fin <= tmax_), key=lambda o: o.fin)
                chain = []
                while cur is not None and len(chain) < int(n_):
                    chain.append(cur)
                    cur = cur.crit
                for o in reversed(chain):
                    print("CRIT %8.2f %6.2f %-4s L%d%s" % (o.fin - o.cost, o.cost, o.eng, o.line, " dma" if o.dma else ""))
            w = os.environ.get("KSCHEDWIN", "")
            if w:
                a_, b_ = (float(x) for x in w.split(","))
                sel = [o for o in allops if a_ <= o.fin - o.cost < b_]
                sel.sort(key=lambda o: o.fin - o.cost)
                for o in sel:
                    print("OP %8.2f %6.2f %-4s L%d%s" % (o.fin - o.cost, o.cost, o.eng, o.line, " dma" if o.dma else ""))

    def finalize(self):
        if os.environ.get("KNOSCHED", "") != "1":
            self.schedule()
        for e in self.ENGS:
            for o in self.lists[e]:
                for (t, kind) in o.deps:
                    if kind == "order":
                        continue
                    if isinstance(t, Op):
                        if t.eng != o.eng or o.dma or o.eng in ("act", "dve", "pool"):
                            t.signal = True
        for e in self.ENGS:
            cnt = 0
            for o in self.lists[e]:
                if o.signal:
                    cnt += 1
                    o.sigval = cnt

    def emit(self, e, eng):
        waited = {}
        for o in self.lists[e]:
            need = {}
            for (t, kind) in o.deps:
                if kind == "order":
                    continue
                if isinstance(t, DmaTok):
                    key, val = t.sem, t.val
                else:
                    if t.eng == e and not (o.dma or e in ("act", "dve", "pool")):
                        continue
                    key, val = self.engsem[t.eng], t.sigval
                if need.get(key, (None, 0))[1] < val:
                    need[key] = (key, val)
            for key, val in need.values():
                if waited.get(key, 0) < val:
                    eng.wait_ge(key, val)
                    waited[key] = val
            ins = o.fn(eng)
            if o.signal:
                ins.then_inc(self.engsem[e], 1)
        if e == "sp":
            for t in self.final:
                if waited.get(t.sem, 0) < t.val:
                    eng.wait_ge(t.sem, t.val)
                    waited[t.sem] = t.val


def v3(ap, a):
    return ap.rearrange("p (a b) -> p a b", a=a)


class _Stop(Exception):
    pass


def build_program():
    nc = bass.Bass("TRN2", target_bir_lowering=False)
    KSTOP = os.environ.get("KSTOP", "")
    xs = nc.dram_tensor("xs", [SEQ, D_MODEL], F32, kind="ExternalInput").ap()
    cs_d = nc.dram_tensor("cs", [SEQ, 128], F32, kind="ExternalInput").ap()
    cstf_d = nc.dram_tensor("cstf", [128, NCF], F32, kind="ExternalInput").ap()
    csts_d = nc.dram_tensor("csts", [128, 512], F32, kind="ExternalInput").ap()
    cstb_d = nc.dram_tensor("cstb", [128, NCB], F32, kind="ExternalInput").ap()
    fnw_d = nc.dram_tensor("fnw", [128, D_MODEL], F32, kind="ExternalInput").ap()
    w_in_d = nc.dram_tensor("w_in", [D_MODEL, IN_PROJ], F32, kind="ExternalInput").ap()
    w_out_d = nc.dram_tensor("w_out", [D_MODEL, D_MODEL], F32, kind="ExternalInput").ap()
    w_gate_d = nc.dram_tensor("w_gate", [D_MODEL, D_FF], F32, kind="ExternalInput").ap()
    w_up_d = nc.dram_tensor("w_up", [D_MODEL, D_FF], F32, kind="ExternalInput").ap()
    w_down_d = nc.dram_tensor("w_down", [D_FF, D_MODEL], F32, kind="ExternalInput").ap()
    out_d = nc.dram_tensor("out", [NT_OWN * 128, D_MODEL], F32, kind="ExternalOutput").ap()

    NB16 = 26916
    NF32 = 7488
    with contextlib.ExitStack() as es:
        Wt = es.enter_context(nc.sbuf_tensor("Wt", [128, 30720], BF16))
        Ht = es.enter_context(nc.sbuf_tensor("Ht", [128, NT_OWN * 1024], F32))
        ABt = es.enter_context(nc.sbuf_tensor("ABt", [128, NB16], BF16))
        AFt = es.enter_context(nc.sbuf_tensor("AFt", [128, NF32], F32))
        banks = [es.enter_context(nc.psum_tensor(f"pb{i}", [128, 512], F32)) for i in range(8)]
        sems = [es.enter_context(nc.semaphore(f"s{i}")) for i in range(60)]
        block = es.enter_context(nc.Block())
        R = Rec(sems)

        class Arena:
            def __init__(self, t, n):
                self.t, self.n, self.off = t, n, 0

            def alloc(self, name, n):
                assert self.off + n <= self.n, (name, self.off, n, self.n)
                ap = self.t[:, self.off:self.off + n]
                self.off += n
                return Buf(name, ap)

        AB = Arena(ABt, NB16)
        AFa = Arena(AFt, NF32)

        W_in = Buf("w_in", v3(Wt[:, 0:22528], 8))
        W_out = Buf("w_out", v3(Wt[:, 22528:30720], 8))
        hbuf = [Buf(f"h{c}", Ht[:, c * 1024:(c + 1) * 1024]) for c in range(NT_OWN)]
        kvf = [Buf(f"kvf{c}", v3(Ht[:, 12 * 1024 + c * 256: 12 * 1024 + (c + 1) * 256], 4)) for c in range(NT_OWN)]

        F = [Buf(f"F{i}", banks[i][:, :]) for i in range(6)]
        T0 = Buf("T0", banks[6][:, :])
        T1a = Buf("T1a", banks[7][:, 0:256])
        T1b = Buf("T1b", banks[7][:, 256:512], share=T1a)
        T0b = banks[6][:, :].bitcast(BF16)
        T1ab = banks[7][:, 0:256].bitcast(BF16)
        T1bb = banks[7][:, 256:512].bitcast(BF16)

        cstb = AB.alloc("cstb", NCB)
        ident = cstb.ap[:, 0:128]
        mprev = cstb.ap[:, 128:384]
        mnext = cstb.ap[:, 384:640]
        RbS = [AB.alloc(f"RbS{c}", 256) for c in range(NT_OWN)]
        RfS = [AB.alloc(f"RfS{c}", 256) for c in range(NT_OWN)]
        KT = [AB.alloc(f"KT{c}", 256) for c in range(NT_OWN + 1)]
        Vb = [AB.alloc(f"V{c}", 130) for c in range(NT_OWN + 1)]
        xb = AB.alloc("xb", 1024)
        xT = AB.alloc("xT", 1024)
        qb = AB.alloc("qb", 512)
        rqb = AB.alloc("rqb", 512)
        rkb = AB.alloc("rkb", 512)
        rvb = AB.alloc("rvb", 512)
        kdup = AB.alloc("kdup", 256)
        qT = AB.alloc("qT", 512)
        rqT = AB.alloc("rqT", 512)
        rqxf = AB.alloc("rqxf", 512)
        rqxb = AB.alloc("rqxb", 512)
        rkT = AB.alloc("rkT", 512)
        PT = AB.alloc("PT", 768)
        PT2 = AB.alloc("PT2", 768)
        retS = AB.alloc("retS", 1024)
        yb = AB.alloc("yb", 1024)
        yT = AB.alloc("yT", 1024)
        kzb, kzf = qb, rqb
        yba = Buf("yba", yb.ap[:, 0:512])
        ybr = Buf("ybr", yb.ap[:, 512:1024])
        yTa = Buf("yTa", yT.ap[:, 0:512], share=yT)
        yTr = Buf("yTr", yT.ap[:, 512:1024], share=yT)

        cstf = AFa.alloc("cstf", NCF)

        def cf(name):
            a, b = CF[name]
            return cstf.ap[:, a:b]

        Xif = AFa.alloc("Xif", 512)
        Xib = AFa.alloc("Xib", 512)
        csl = [AFa.alloc(f"cs{i}", 128) for i in range(2)]
        tabq = AFa.alloc("tabq", 128)
        ta = AFa.alloc("ta", 512)
        DT = AFa.alloc("DT", 1024)
        td = AFa.alloc("td", 512)
        te = AFa.alloc("te", 512)
        tg = AFa.alloc("tg", 512)
        tf = AFa.alloc("tf", 512)
        akf = Buf("akf", tf.ap[:, 0:128])
        tk1 = Buf("tk1", tf.ap[:, 128:256])
        tk2 = Buf("tk2", tf.ap[:, 256:384])
        tabk = Buf("tabk", tf.ap[:, 384:512])
        Rb = AFa.alloc("Rb", 256)
        Rf = AFa.alloc("Rf", 256)
        rtmp = AFa.alloc("rtmp", 256)
        lgP = AFa.alloc("lgP", 8)
        lgR = AFa.alloc("lgR", 16)
        g128 = AFa.alloc("g128", 8)
        Zfb = AFa.alloc("Zfb", 16)
        esink = AFa.alloc("esink", 8)
        st = [AFa.alloc(f"st{i}", 8) for i in range(2)]
        s8a = AFa.alloc("s8a", 8)
        s8g = AFa.alloc("s8g", 8)
        s8h = AFa.alloc("s8h", 8)
        tb2 = AFa.alloc("tb2", 512)
        junk = AFa.alloc("junk", 512)
        junk_ap = junk.ap.bitcast(BF16)
        s8b = AFa.alloc("s8b", 8)
        s8c = AFa.alloc("s8c", 8)
        s8d = AFa.alloc("s8d", 8)
        s8e = AFa.alloc("s8e", 8)
        s8f = AFa.alloc("s8f", 8)
        epsb = AFa.alloc("epsb", 1)
        onesb = AFa.alloc("onesb", 1)
        fsc = AFa.alloc("fsc", 4)
        ssq2 = AFa.alloc("ssq2", 16)
        lnr2 = AFa.alloc("lnr2", 16)
        rstd2 = AFa.alloc("rstd2", 16)

        if os.environ.get("KSCHEDSTAT", ""):
            print("arena use: bf16", AB.off, "/", NB16, " f32", AFa.off, "/", NF32)
        W_in_v = w_in_d.rearrange("(k p) n -> p k n", p=128)
        W_out_v = w_out_d.rearrange("(k p) n -> p k n", p=128)

        def _record():
            R.dma("sp", cstf.ap, cstf_d, cstf, writes=[cstf])
            R.dma("sp", ta.ap, csts_d, ta, writes=[ta])
            R.dma("pool", cstb.ap, cstb_d, cstb, writes=[cstb])
            wcols = [("rkrv", C_RK, C_RG), ("akav", C_AK, C_RQ), ("aq", C_AQ, C_AK), ("rq", C_RQ, C_RK), ("rg", C_RG, IN_PROJ)]
            Wg_buf = {}
            prev_w = None
            for name, a, b in wcols:
                bb = Buf("w_in_" + name)
                Wg_buf[name] = bb
                ex = [prev_w] if prev_w else []
                tk_ = None
                for k in range(8):
                    tk_ = R.dma("pool", W_in.ap[:, k, a:b], W_in_v[:, k, a:b], bb, writes=([bb] if k == 0 else []), extra=ex,
                                order_after=([tk_] if tk_ else []))
                bb.writers = [tk_]
                prev_w = tk_
            tk_ = None
            for k in range(8):
                tk_ = R.dma("pool", W_out.ap[:, k, :], W_out_v[:, k, :], W_out, writes=([W_out] if k == 0 else []), extra=[prev_w],
                            order_after=([tk_] if tk_ else []))
            W_out.writers = [tk_]
            for k in range(4):
                R.op("dve", lambda e, k=k: e.tensor_scalar(out=W_out.ap[:, 4 + k, :], in0=W_out.ap[:, 4 + k, :],
                                                           scalar1=cf("rnwP")[:, k:k + 1], scalar2=None, op0=ALU.mult),
                     reads=[cstf, W_out], writes=[W_out], n=340)
            for name, a, b in wcols:
                for k in range(8):
                    R.op("dve", lambda e, k=k, a=a, b=b: e.tensor_scalar(out=W_in.ap[:, k, a:b], in0=W_in.ap[:, k, a:b],
                                                                         scalar1=cf("anwP")[:, k:k + 1], scalar2=None, op0=ALU.mult),
                         reads=[cstf, Wg_buf[name]], writes=[Wg_buf[name]], n=(b - a) // 3)

            iota1 = ta.ap[:, 0:128]
            iota2 = ta.ap[:, 128:256]
            reluP = ta.ap[:, 256:384]
            reluN = ta.ap[:, 384:512]

            R.op("act", lambda e: e.activation(out=lgP.ap, in_=cf("decP"), func=AF.Abs), reads=[cstf], writes=[lgP])
            R.op("act", lambda e: e.activation(out=lgR.ap, in_=cf("decR"), func=AF.Abs), reads=[cstf], writes=[lgR])
            R.op("dve", lambda e: e.tensor_scalar(out=lgP.ap, in0=lgP.ap, scalar1=-1.0, scalar2=None, op0=ALU.mult), reads=[lgP], writes=[lgP])
            R.op("dve", lambda e: e.tensor_scalar(out=lgR.ap, in0=lgR.ap, scalar1=-1.0, scalar2=None, op0=ALU.mult), reads=[lgR], writes=[lgR])
            R.op("act", lambda e: e.activation(out=g128.ap, in_=lgP.ap, func=AF.Exp, scale=128.0), reads=[lgP], writes=[g128])
            for t in range(4):
                R.op("act", lambda e, t=t: e.activation(out=Xif.ap[:, t * 128:(t + 1) * 128], in_=iota1, func=AF.Exp,
                                                        scale=lgP.ap[:, t:t + 1]), reads=[lgP, ta], writes=[Xif])
                R.op("act", lambda e, t=t: e.activation(out=Xib.ap[:, t * 128:(t + 1) * 128], in_=iota2, func=AF.Exp,
                                                        scale=lgP.ap[:, 4 + t:5 + t]), reads=[lgP, ta], writes=[Xib])
            R.op("act", lambda e: e.activation(out=Zfb.ap[:, 0:8], in_=lgR.ap[:, 0:8], func=AF.Exp, scale=cf("c127m")),
                 reads=[lgR, cstf], writes=[Zfb])
            R.op("act", lambda e: e.activation(out=Zfb.ap[:, 8:16], in_=lgR.ap[:, 8:16], func=AF.Exp, scale=cf("cm")),
                 reads=[lgR, cstf], writes=[Zfb])
            R.op("act", lambda e: e.activation(out=esink.ap, in_=cf("sink"), func=AF.Exp), reads=[cstf], writes=[esink])
            for h in range(8):
                R.op("dve", lambda e, h=h: e.tensor_scalar(out=te.ap[:, 0:128], in0=reluP, scalar1=lgR.ap[:, h:h + 1],
                                                           scalar2=None, op0=ALU.mult), reads=[ta, lgR], writes=[te])
                R.op("dve", lambda e, h=h: e.scalar_tensor_tensor(out=td.ap[:, 0:128], in0=reluN, scalar=lgR.ap[:, 8 + h:9 + h],
                                                                  in1=te.ap[:, 0:128], op0=ALU.mult, op1=ALU.add),
                     reads=[ta, lgR, te], writes=[td])
                R.op("act", lambda e, h=h: e.activation(out=DT.ap[:, h * 128:(h + 1) * 128], in_=td.ap[:, 0:128], func=AF.Exp),
                     reads=[td], writes=[DT])
            for c in range(NT_OWN + 1):
                R.op("pool", lambda e, c=c: e.memset(v3(Vb[c].ap, 2)[:, :, 64:65], 1.0), writes=[Vb[c]])
            R.op("pool", lambda e: e.memset(Rb.ap, 0.0), writes=[Rb])
            R.op("pool", lambda e: e.memset(Rf.ap, 0.0), writes=[Rf])
            R.op("pool", lambda e: e.memset(epsb.ap, EPS), writes=[epsb])
            R.op("pool", lambda e: e.memset(onesb.ap, 1.0), writes=[onesb])

            if KSTOP == "setup":
                raise _Stop()
            def load_tile(L, xbuf, slot):
                R.dma("sp", xbuf.ap, xs[L * 128:(L + 1) * 128, :], xbuf, writes=[xbuf])
                R.dma("sp", csl[slot].ap, cs_d[L * 128:(L + 1) * 128, :], csl[slot], writes=[csl[slot]])

            def prep_tile(xbuf, slot, xb=xb, xT=xT):
                s = st[slot]
                R.op("act", lambda e: e.activation(out=junk_ap, in_=xbuf.ap, func=AF.Square, accum_out=s.ap[:, 0:1]),
                     reads=[xbuf], writes=[junk, s], n=1024)
                R.op("act", lambda e: e.activation(out=s.ap[:, 1:2], in_=s.ap[:, 0:1], func=AF.Ln, scale=1.0 / D_MODEL, bias=epsb.ap),
                     reads=[s, epsb], writes=[s])
                R.op("act", lambda e: e.activation(out=s.ap[:, 2:3], in_=s.ap[:, 1:2], func=AF.Exp, scale=-0.5), reads=[s], writes=[s])
                R.op("dve", lambda e: e.tensor_scalar(out=s.ap[:, 5:6], in0=s.ap[:, 2:3], scalar1=-1.0, scalar2=None,
                                                      op0=ALU.mult), reads=[s], writes=[s])
                R.op("dve", lambda e: e.tensor_scalar(out=s.ap[:, 3:4], in0=s.ap[:, 2:3], scalar1=0.125, scalar2=None,
                                                      op0=ALU.mult), reads=[s], writes=[s])
                R.op("dve", lambda e: e.tensor_scalar(out=s.ap[:, 4:5], in0=s.ap[:, 2:3], scalar1=0.5, scalar2=None,
                                                      op0=ALU.mult), reads=[s], writes=[s])
                R.op("dve", lambda e: e.tensor_copy(out=xb.ap, in_=xbuf.ap), reads=[xbuf], writes=[xb], n=600)
                for k in range(8):
                    R.op("pe", lambda e, k=k: e.transpose(out=T0b[:, k * 128:(k + 1) * 128], in_=xb.ap[:, k * 128:(k + 1) * 128],
                                                          identity=ident), reads=[xb, cstb], writes=[T0], n=128)
                R.op("act", lambda e: e.activation(out=xT.ap, in_=T0b, func=AF.Identity), reads=[T0], writes=[xT])
                return s

            def inproj(bank, c0, n, wnames, xT=xT, also=()):
                xT3 = v3(xT.ap, 8)
                for k in range(8):
                    R.op("pe", lambda e, k=k: e.matmul(bank.ap[:, 0:n], lhsT=xT3[:, k, :], rhs=W_in.ap[:, k, c0:c0 + n],
                                                       start=(k == 0), stop=(k == 7)),
                         reads=[xT] + [Wg_buf[w] for w in wnames], writes=[bank] + list(also), n=n)

            def rope(eng_a, eng_b, src, dst_t1, dst_t2, A, B, H):
                s3 = v3(src.ap[:, 0:H * 64], H)
                t13 = v3(dst_t1.ap[:, 0:H * 64], H)
                t23 = v3(dst_t2.ap[:, 0:H * 64], H)
                Ab = A.unsqueeze(1).broadcast_to([128, H, 64])
                Bl = B[:, 0:32].unsqueeze(1).broadcast_to([128, H, 32])
                Bh = B[:, 32:64].unsqueeze(1).broadcast_to([128, H, 32])
                return s3, t13, t23, Ab, Bl, Bh

            R.phase = "pre"
            xring = [hbuf[4], hbuf[5], hbuf[6]]
            order = list(range(NT_ALL - 1, -1, -1))
            load_tile(order[0], xring[0], 0)
            for i, L in enumerate(order):
                if KSTOP == "pre1" and i == 1:
                    raise _Stop()
                xbuf = xring[i % 3]
                slot = i % 2
                if i + 1 < len(order):
                    load_tile(order[i + 1], xring[(i + 1) % 3], (i + 1) % 2)
                own = L < NT_OWN
                needkv = L <= NT_OWN
                par = i % 2
                xb_ = (xb, retS)[par]
                xT_ = (xT, yT)[par]
                ta_ = (ta, tg)[par]
                rvb_ = (rvb, rkb)[par]
                kzb_ = (qb, rqxf)[par]
                kzf_ = (rqb, rqxb)[par]
                Brk = (F[0], F[3])[par]
                Brv = (F[1], F[4])[par]
                Bkv = (F[2], F[5])[par]
                s = prep_tile(xbuf, slot, xb_, xT_)
                cst = csl[slot]
                cosv = cst.ap[:, 0:64]
                sinv = cst.ap[:, 64:128]
                inproj(Brk, C_RK, 512, ["rkrv"], xT_)
                inproj(Brv, C_RV, 512, ["rkrv"], xT_)
                if needkv:
                    inproj(T1b, C_AK, 256, ["akav"], xT_, also=[T1a])
                R.op("act", lambda e, s=s, ta_=ta_, Brk=Brk: e.activation(out=ta_.ap, in_=Brk.ap, func=AF.Identity, scale=s.ap[:, 3:4]),
                     reads=[Brk, s], writes=[ta_])
                R.op("act", lambda e, s=s, rvb_=rvb_, Brv=Brv: e.activation(out=rvb_.ap, in_=Brv.ap, func=AF.Identity, scale=s.ap[:, 2:3]),
                     reads=[Brv, s], writes=[rvb_])
                s3, t13, t23, Ab, Bl, Bh = rope(None, None, ta_, td, te, cosv, sinv, 8)
                R.op("dve", lambda e, s3=s3, t13=t13, Ab=Ab: e.tensor_tensor(out=t13, in0=s3, in1=Ab, op=ALU.mult),
                     reads=[ta_, cst], writes=[td])
                R.op("pool", lambda e, s3=s3, t23=t23, Bl=Bl: e.tensor_tensor(out=t23[:, :, 0:32], in0=s3[:, :, 32:64], in1=Bl, op=ALU.mult),
                     reads=[ta_, cst], writes=[te])
                R.op("pool", lambda e, s3=s3, t23=t23, Bh=Bh: e.tensor_tensor(out=t23[:, :, 32:64], in0=s3[:, :, 0:32], in1=Bh, op=ALU.mult),
                     reads=[ta_, cst], writes=[te])
                R.op("dve", lambda e: e.tensor_tensor(out=td.ap, in0=td.ap, in1=te.ap, op=ALU.add), reads=[td, te], writes=[td])
                Zb_b = Zfb.ap[:, 8:16].unsqueeze(2).broadcast_to([128, 8, 64])
                Zf_b = Zfb.ap[:, 0:8].unsqueeze(2).broadcast_to([128, 8, 64])
                R.op("dve", lambda e, Zb_b=Zb_b, kzb_=kzb_: e.tensor_tensor(out=v3(kzb_.ap, 8), in0=v3(td.ap, 8), in1=Zb_b, op=ALU.mult),
                     reads=[td, Zfb], writes=[kzb_])
                if own:
                    R.op("pool", lambda e, Zf_b=Zf_b, kzf_=kzf_: e.tensor_tensor(out=v3(kzf_.ap, 8), in0=v3(td.ap, 8), in1=Zf_b, op=ALU.mult),
                         reads=[td, Zfb], writes=[kzf_])
                for t in range(4):
                    R.op("pe", lambda e, t=t, Bkv=Bkv, kzb_=kzb_, rvb_=rvb_: e.matmul(Bkv.ap[:, t * 128:(t + 1) * 128], lhsT=kzb_.ap[:, t * 128:(t + 1) * 128],
                                                       rhs=rvb_.ap[:, t * 128:(t + 1) * 128], start=True, stop=True),
                         reads=[kzb_, rvb_], writes=[Bkv], n=128)
                if own:
                    for t in range(4):
                        R.op("pe", lambda e, t=t, Brk=Brk, kzf_=kzf_, rvb_=rvb_: e.matmul(Brk.ap[:, t * 128:(t + 1) * 128], lhsT=kzf_.ap[:, t * 128:(t + 1) * 128],
                                                           rhs=rvb_.ap[:, t * 128:(t + 1) * 128], start=True, stop=True),
                             reads=[kzf_, rvb_], writes=[Brk], n=128)
                Rb3 = v3(Rb.ap, 4)
                rt3 = v3(rtmp.ap, 4)
                F33 = v3(Bkv.ap, 4)
                F43 = v3(Brk.ap, 4)
                if own:
                    R.op("dve", lambda e, L=L: e.tensor_copy(out=RbS[L].ap, in_=Rb.ap), reads=[Rb], writes=[RbS[L]])
                gb_b = g128.ap[:, 4:8].unsqueeze(2).broadcast_to([128, 4, 64])
                R.op("dve", lambda e, gb_b=gb_b, Rb3=Rb3, rt3=rt3: e.tensor_tensor(out=rt3, in0=Rb3, in1=gb_b, op=ALU.mult),
                     reads=[Rb, g128], writes=[rtmp])
                R.op("dve", lambda e, Rb3=Rb3, rt3=rt3, F33=F33: e.tensor_tensor(out=Rb3[0:64], in0=rt3[0:64], in1=F33[0:64, :, 0:64], op=ALU.add),
                     reads=[rtmp, Bkv], writes=[Rb])
                R.op("dve", lambda e, Rb3=Rb3, rt3=rt3, F33=F33: e.tensor_tensor(out=Rb3[64:128], in0=rt3[64:128], in1=F33[64:128, :, 64:128], op=ALU.add),
                     reads=[rtmp, Bkv, Rb], writes=[Rb])
                if own:
                    R.op("act", lambda e, L=L, F43=F43: e.activation(out=kvf[L].ap[0:64], in_=F43[0:64, :, 0:64], func=AF.Identity),
                         reads=[Brk], writes=[kvf[L]])
                    R.op("act", lambda e, L=L, F43=F43: e.activation(out=kvf[L].ap[64:128], in_=F43[64:128, :, 64:128], func=AF.Identity),
                         reads=[Brk, kvf[L]], writes=[kvf[L]])
                if needkv:
                    R.op("act", lambda e, s=s: e.activation(out=akf.ap, in_=T1b.ap[:, 0:128], func=AF.Identity, scale=s.ap[:, 2:3]),
                         reads=[T1b, s], writes=[akf])
                    R.op("act", lambda e, s=s, L=L: e.activation(out=v3(Vb[L].ap, 2)[:, :, 0:64], in_=v3(T1b.ap[:, 128:256], 2),
                                                                  func=AF.Identity, scale=s.ap[:, 2:3]),
                         reads=[T1b, s, Vb[L]], writes=[Vb[L]])
                    R.op("dve", lambda e: e.tensor_tensor(out=tk1.ap, in0=akf.ap, in1=akf.ap, op=ALU.mult), reads=[akf], writes=[tk1])
                    R.op("dve", lambda e: e.tensor_reduce(out=s8a.ap[:, 0:2], in_=v3(tk1.ap, 2), axis=AX.X, op=ALU.add),
                         reads=[tk1], writes=[s8a])
                    R.op("act", lambda e: e.activation(out=s8a.ap[:, 2:4], in_=s8a.ap[:, 0:2], func=AF.Ln, scale=1.0 / 64, bias=epsb.ap),
                         reads=[s8a, epsb], writes=[s8a])
                    R.op("act", lambda e: e.activation(out=s8a.ap[:, 4:6], in_=s8a.ap[:, 2:4], func=AF.Exp, scale=-0.5), reads=[s8a], writes=[s8a])
                    R.op("pool", lambda e, cosv=cosv: e.tensor_tensor(out=tabk.ap[:, 0:64], in0=cosv, in1=cf("knw"), op=ALU.mult),
                         reads=[cst, cstf], writes=[tabk])
                    R.op("pool", lambda e, sinv=sinv: e.tensor_tensor(out=tabk.ap[:, 64:128], in0=sinv, in1=cf("knws"), op=ALU.mult),
                         reads=[cst, cstf, tabk], writes=[tabk])
                    s3, t13, t23, Ab, Bl, Bh = rope(None, None, akf, tk1, tk2, tabk.ap[:, 0:64], tabk.ap[:, 64:128], 2)
                    R.op("pool", lambda e, s3=s3, t13=t13, Ab=Ab: e.tensor_tensor(out=t13, in0=s3, in1=Ab, op=ALU.mult),
                         reads=[akf, tabk, s8a], writes=[tk1])
                    R.op("pool", lambda e, s3=s3, t23=t23, Bl=Bl: e.tensor_tensor(out=t23[:, :, 0:32], in0=s3[:, :, 32:64], in1=Bl, op=ALU.mult),
                         reads=[akf, tabk], writes=[tk2])
                    R.op("pool", lambda e, s3=s3, t23=t23, Bh=Bh: e.tensor_tensor(out=t23[:, :, 32:64], in0=s3[:, :, 0:32], in1=Bh, op=ALU.mult),
                         reads=[akf, tabk, tk2], writes=[tk2])
                    R.op("dve", lambda e: e.tensor_tensor(out=tk1.ap, in0=tk1.ap, in1=tk2.ap, op=ALU.add), reads=[tk1, tk2], writes=[tk1])
                    kd4 = kdup.ap.rearrange("p (g u d) -> p g u d", g=2, u=2)
                    rk2b = s8a.ap[:, 4:6].unsqueeze(2).broadcast_to([128, 2, 64])
                    for u in range(2):
                        R.op("dve", lambda e, u=u, kd4=kd4, rk2b=rk2b: e.tensor_tensor(out=kd4[:, :, u, :], in0=v3(tk1.ap, 2), in1=rk2b, op=ALU.mult),
                             reads=[tk1, s8a, kdup], writes=[kdup])
                    for g in range(2):
                        R.op("pe", lambda e, g=g: e.transpose(out=T1ab[:, g * 128:(g + 1) * 128], in_=kdup.ap[:, g * 128:(g + 1) * 128],
                                                              identity=ident), reads=[kdup, cstb], writes=[T1a, T1b], n=128)
                    R.op("act", lambda e, L=L: e.activation(out=KT[L].ap, in_=T1ab[:, 0:256], func=AF.Identity), reads=[T1a], writes=[KT[L]])

            if KSTOP == "pre":
                raise _Stop()
            Rf3 = v3(Rf.ap, 4)
            rt3 = v3(rtmp.ap, 4)
            gf_b = g128.ap[:, 0:4].unsqueeze(2).broadcast_to([128, 4, 64])
            scan_last = None
            for c in range(NT_OWN):
                R.op("dve", lambda e, c=c: e.tensor_copy(out=RfS[c].ap, in_=Rf.ap), reads=[Rf], writes=[RfS[c]])
                R.op("dve", lambda e: e.tensor_tensor(out=rt3, in0=Rf3, in1=gf_b, op=ALU.mult), reads=[Rf, g128], writes=[rtmp])
                scan_last = R.op("dve", lambda e, c=c: e.tensor_tensor(out=Rf3, in0=rt3, in1=kvf[c].ap, op=ALU.add),
                                 reads=[rtmp, kvf[c]], writes=[Rf])

            if KSTOP == "scan":
                raise _Stop()
            R.phase = "main"

            def load_main(c):
                extra = [scan_last] if c >= 12 else []
                R.dma("sp", hbuf[c].ap, xs[c * 128:(c + 1) * 128, :], hbuf[c], writes=[hbuf[c]], extra=extra)
                R.dma("sp", csl[c % 2].ap, cs_d[c * 128:(c + 1) * 128, :], csl[c % 2], writes=[csl[c % 2]])

            load_main(0)
            for c in range(NT_OWN):
                if KSTOP == "main1" and c == 1:
                    raise _Stop()
                xbuf = hbuf[c]
                slot = c % 2
                s = prep_tile(xbuf, slot)
                cst = csl[slot]
                cosv = cst.ap[:, 0:64]
                sinv = cst.ap[:, 64:128]
                R.op("pool", lambda e, cosv=cosv: e.tensor_tensor(out=tabq.ap[:, 0:64], in0=cosv, in1=cf("qnw"), op=ALU.mult),
                     reads=[cst, cstf], writes=[tabq])
                R.op("pool", lambda e, sinv=sinv: e.tensor_tensor(out=tabq.ap[:, 64:128], in0=sinv, in1=cf("qnws"), op=ALU.mult),
                     reads=[cst, cstf, tabq], writes=[tabq])
                KO1 = os.environ.get("KO1", "0") == "1"
                BQ, BG = (F[5], F[0]) if KO1 else (F[0], F[4])
                inproj(BQ, C_AQ, 512, ["aq"])
                inproj(F[1], C_RQ, 512, ["rq"])
                inproj(F[2], C_RK, 512, ["rkrv"])
                inproj(F[3], C_RV, 512, ["rkrv"])
                inproj(BG, C_RG, 512, ["rg"])
                if c + 1 < NT_OWN:
                    load_main(c + 1)
                R.op("act", lambda e, s=s: e.activation(out=ta.ap, in_=BQ.ap, func=AF.Identity, scale=s.ap[:, 2:3]),
                     reads=[BQ, s], writes=[ta])
                R.op("dve", lambda e: e.tensor_tensor(out=td.ap, in0=ta.ap, in1=ta.ap, op=ALU.mult), reads=[ta], writes=[td])
                R.op("dve", lambda e: e.tensor_reduce(out=s8a.ap, in_=v3(td.ap, 8), axis=AX.X, op=ALU.add), reads=[td], writes=[s8a])
                R.op("act", lambda e: e.activation(out=s8b.ap, in_=s8a.ap, func=AF.Ln, scale=1.0 / 64, bias=epsb.ap),
                     reads=[s8a, epsb], writes=[s8b])
                R.op("act", lambda e: e.activation(out=s8c.ap, in_=s8b.ap, func=AF.Exp, scale=-0.5), reads=[s8b], writes=[s8c])
                s3, t13, t23, Ab, Bl, Bh = rope(None, None, ta, td, te, tabq.ap[:, 0:64], tabq.ap[:, 64:128], 8)
                R.op("dve", lambda e, s3=s3, t13=t13, Ab=Ab: e.tensor_tensor(out=t13, in0=s3, in1=Ab, op=ALU.mult),
                     reads=[ta, tabq, s8a], writes=[td])
                R.op("pool", lambda e, s3=s3, t23=t23, Bl=Bl: e.tensor_tensor(out=t23[:, :, 0:32], in0=s3[:, :, 32:64], in1=Bl, op=ALU.mult),
                     reads=[ta, tabq], writes=[te])
                R.op("pool", lambda e, s3=s3, t23=t23, Bh=Bh: e.tensor_tensor(out=t23[:, :, 32:64], in0=s3[:, :, 0:32], in1=Bh, op=ALU.mult),
                     reads=[ta, tabq, te], writes=[te])
                R.op("dve", lambda e: e.tensor_tensor(out=td.ap, in0=td.ap, in1=te.ap, op=ALU.add), reads=[td, te], writes=[td])
                rq8b = s8c.ap.unsqueeze(2).broadcast_to([128, 8, 64])
                R.op("dve", lambda e, rq8b=rq8b: e.tensor_tensor(out=v3(qb.ap, 8), in0=v3(td.ap, 8), in1=rq8b, op=ALU.mult),
                     reads=[td, s8c], writes=[qb])
                for t in range(4):
                    R.op("pe", lambda e, t=t: e.transpose(out=T1ab[:, t * 128:(t + 1) * 128], in_=qb.ap[:, t * 128:(t + 1) * 128],
                                                          identity=ident), reads=[qb, cstb], writes=[T1a], n=128)
                R.op("act", lambda e: e.activation(out=qT.ap, in_=T1ab, func=AF.Identity), reads=[T1a], writes=[qT])
                if KSTOP == "m_q1":
                    raise _Stop()
                R.op("act", lambda e, s=s: e.activation(out=tb2.ap, in_=F[1].ap, func=AF.Identity, scale=s.ap[:, 2:3]),
                     reads=[F[1], s], writes=[tb2] + ([akf, tk1, tk2, tabk] if c == 0 else []), n=512)
                s3, t13, t23, Ab, Bl, Bh = rope(None, None, tb2, td, te, cosv, sinv, 8)
                R.op("dve", lambda e, s3=s3, t13=t13, Ab=Ab: e.tensor_tensor(out=t13, in0=s3, in1=Ab, op=ALU.mult),
                     reads=[tb2, cst], writes=[td])
                R.op("pool", lambda e, s3=s3, t23=t23, Bl=Bl: e.tensor_tensor(out=t23[:, :, 0:32], in0=s3[:, :, 32:64], in1=Bl, op=ALU.mult),
                     reads=[tb2, cst], writes=[te])
                R.op("pool", lambda e, s3=s3, t23=t23, Bh=Bh: e.tensor_tensor(out=t23[:, :, 32:64], in0=s3[:, :, 0:32], in1=Bh, op=ALU.mult),
                     reads=[tb2, cst, te], writes=[te])
                R.op("dve", lambda e: e.tensor_tensor(out=rqb.ap, in0=td.ap, in1=te.ap, op=ALU.add), reads=[td, te], writes=[rqb])
                for t in range(4):
                    R.op("pe", lambda e, t=t: e.transpose(out=T1bb[:, t * 128:(t + 1) * 128], in_=rqb.ap[:, t * 128:(t + 1) * 128],
                                                          identity=ident), reads=[rqb, cstb], writes=[T1b], n=128)
                R.op("act", lambda e: e.activation(out=rqT.ap, in_=T1bb, func=AF.Identity), reads=[T1b], writes=[rqT])
                R.op("dve", lambda e: e.tensor_tensor(out=rqxf.ap, in0=rqT.ap, in1=Xif.ap, op=ALU.mult), reads=[rqT, Xif], writes=[rqxf])
                R.op("dve", lambda e: e.tensor_tensor(out=rqxb.ap, in0=rqT.ap, in1=Xib.ap, op=ALU.mult), reads=[rqT, Xib], writes=[rqxb])
                if KSTOP == "m_q2":
                    raise _Stop()
                R.op("act", lambda e, s=s: e.activation(out=ta.ap, in_=F[2].ap, func=AF.Identity, scale=s.ap[:, 3:4]),
                     reads=[F[2], s], writes=[ta])
                s3, t13, t23, Ab, Bl, Bh = rope(None, None, ta, td, te, cosv, sinv, 8)
                R.op("dve", lambda e, s3=s3, t13=t13, Ab=Ab: e.tensor_tensor(out=t13, in0=s3, in1=Ab, op=ALU.mult),
                     reads=[ta, cst], writes=[td])
                R.op("pool", lambda e, s3=s3, t23=t23, Bl=Bl: e.tensor_tensor(out=t23[:, :, 0:32], in0=s3[:, :, 32:64], in1=Bl, op=ALU.mult),
                     reads=[ta, cst], writes=[te])
                R.op("pool", lambda e, s3=s3, t23=t23, Bh=Bh: e.tensor_tensor(out=t23[:, :, 32:64], in0=s3[:, :, 0:32], in1=Bh, op=ALU.mult),
                     reads=[ta, cst, te], writes=[te])
                R.op("dve", lambda e: e.tensor_tensor(out=rkb.ap, in0=td.ap, in1=te.ap, op=ALU.add), reads=[td, te], writes=[rkb])
                for t in range(4):
                    R.op("pe", lambda e, t=t: e.transpose(out=T1ab[:, t * 128:(t + 1) * 128], in_=rkb.ap[:, t * 128:(t + 1) * 128],
                                                          identity=ident), reads=[rkb, cstb], writes=[T1a], n=128)
                R.op("act", lambda e: e.activation(out=rkT.ap, in_=T1ab, func=AF.Identity), reads=[T1a], writes=[rkT])
                if KSTOP == "m_q3":
                    raise _Stop()
                R.op("act", lambda e, s=s: e.activation(out=rvb.ap, in_=F[3].ap, func=AF.Identity, scale=s.ap[:, 2:3]),
                     reads=[F[3], s], writes=[rvb])
                R.op("act", lambda e, s=s: e.activation(out=tg.ap, in_=BG.ap, func=AF.Exp, scale=s.ap[:, 5:6]),
                     reads=[BG, s], writes=[tg])
                R.op("act", lambda e: e.activation(out=tg.ap, in_=tg.ap, func=AF.Ln, bias=onesb.ap), reads=[tg, onesb], writes=[tg])
                R.op("act", lambda e: e.activation(out=tg.ap, in_=tg.ap, func=AF.Exp, scale=-1.0), reads=[tg], writes=[tg])
                R.op("dve", lambda e, s=s: e.scalar_tensor_tensor(out=tg.ap, in0=BG.ap, scalar=s.ap[:, 2:3], in1=tg.ap,
                                                                  op0=ALU.mult, op1=ALU.mult), reads=[BG, s, tg], writes=[tg])

                if KSTOP == "m_q":
                    raise _Stop()
                kbs = [kb for kb in (c - 1, c, c + 1) if kb >= 0]
                nkb = len(kbs)
                qT3 = v3(qT.ap, 4)
                Obank = [F[4], F[4]] if KO1 else [F[4], F[5]]
                it = 0
                for g in range(2):
                    for ee in range(2):
                        bx, by = (F[0], F[1]) if it % 2 == 0 else (F[2], F[3])
                        it += 1
                        regs = [bx.ap[:, 0:256], bx.ap[:, 256:512], by.ap[:, 0:256]]
                        rbuf = [bx, bx, by]
                        for j, kb in enumerate(kbs):
                            KT3 = v3(KT[kb].ap, 2)
                            masked = (kb != c)
                            R.op("pe", lambda e, j=j, KT3=KT3, g=g, ee=ee, masked=masked, regs=regs: e.matmul(
                                v3(regs[j], 2), lhsT=KT3[64 * ee:64 * ee + 64, g, :], rhs=qT3[64 * ee:64 * ee + 64, 2 * g:2 * g + 2, :],
                                start=True, stop=(not masked)), reads=[KT[kb], qT], writes=[rbuf[j]], n=256)
                            if masked:
                                mk = mprev if kb < c else mnext
                                R.op("pe", lambda e, j=j, mk=mk, regs=regs: e.matmul(regs[j], lhsT=ident, rhs=mk, start=False, stop=True),
                                     reads=[cstb], writes=[rbuf[j]], n=256)
                        PTc = (PT, PT2)[it % 2]
                        n1 = min(nkb, 2)
                        R.op("act", lambda e, n1=n1, bx=bx, PTc=PTc: e.activation(out=PTc.ap[:, 0:n1 * 256], in_=bx.ap[:, 0:n1 * 256], func=AF.Exp, scale=0.125),
                             reads=[bx], writes=[PTc])
                        if nkb == 3:
                            R.op("act", lambda e, by=by, PTc=PTc: e.activation(out=PTc.ap[:, 512:768], in_=by.ap[:, 0:256], func=AF.Exp, scale=0.125),
                                 reads=[by, PTc], writes=[PTc])
                        for tt in range(2):
                            h = 2 * (2 * g + tt) + ee
                            ob = Obank[h // 4]
                            hl = h % 4
                            for j, kb in enumerate(kbs):
                                R.op("pe", lambda e, j=j, kb=kb, tt=tt, ob=ob, hl=hl, g=g, nkb=nkb, PTc=PTc: e.matmul(
                                    ob.ap[:, hl * 65:(hl + 1) * 65], lhsT=PTc.ap[:, j * 256 + tt * 128: j * 256 + (tt + 1) * 128],
                                    rhs=v3(Vb[kb].ap, 2)[:, g, :], start=(j == 0), stop=(j == nkb - 1)),
                                    reads=[PTc, Vb[kb]], writes=[ob], n=800)
                    gb = g
                    O3 = Obank[gb].ap[:, 0:260].rearrange("p (h d) -> p h d", h=4)
                    sd = (s8d, s8e)[gb]
                    R.op("dve", lambda e, gb=gb, O3=O3, sd=sd: e.tensor_tensor(out=sd.ap[:, 0:4], in0=O3[:, :, 64],
                                                                               in1=esink.ap[:, gb * 4:(gb + 1) * 4], op=ALU.add),
                         reads=[Obank[gb], esink], writes=[sd], n=4)
                    R.op("dve", lambda e, sd=sd: e.reciprocal(out=sd.ap[:, 4:8], in_=sd.ap[:, 0:4]), reads=[sd], writes=[sd], n=4)
                    rdb = sd.ap[:, 4:8].unsqueeze(2).broadcast_to([128, 4, 64])
                    R.op("dve", lambda e, gb=gb, O3=O3, rdb=rdb: e.tensor_tensor(out=v3(yb.ap[:, gb * 256:(gb + 1) * 256], 4), in0=O3[:, :, 0:64],
                                                                                 in1=rdb, op=ALU.mult),
                         reads=[Obank[gb], sd, yba], writes=[yba], n=256)

                OPB = [F[3], F[5]]
                for k in range(4):
                    R.op("pe", lambda e, k=k: e.transpose(out=T0b[:, k * 128:(k + 1) * 128], in_=yb.ap[:, k * 128:(k + 1) * 128],
                                                          identity=ident), reads=[yba, cstb], writes=[T0], n=128)
                R.op("act", lambda e: e.activation(out=yT.ap[:, 0:512], in_=T0b[:, 0:512], func=AF.Identity), reads=[T0], writes=[yTa], n=512)
                yT3 = v3(yT.ap, 8)
                for half in range(2):
                    for k in range(4):
                        R.op("pe", lambda e, k=k, half=half: e.matmul(OPB[half].ap, lhsT=yT3[:, k, :],
                                                                       rhs=W_out.ap[:, k, half * 512:(half + 1) * 512],
                                                                       start=(k == 0), stop=False),
                             reads=[yTa, W_out], writes=[OPB[half]])
                rkT3 = v3(rkT.ap, 4)
                rqT3 = v3(rqT.ap, 4)
                rxf3 = v3(rqxf.ap, 4)
                rxb3 = v3(rqxb.ap, 4)
                for ee in range(2):
                    for t in range(4):
                        bank = F[ee]
                        col = t * 128
                        R.op("pe", lambda e, t=t, ee=ee, bank=bank, col=col: e.matmul(
                            bank.ap[:, col:col + 128], lhsT=rkT3[64 * ee:64 * ee + 64, t, :], rhs=rqT3[64 * ee:64 * ee + 64, t, :],
                            start=True, stop=True), reads=[rkT, rqT], writes=[bank], n=128)
                if KSTOP == "m_r0":
                    raise _Stop()
                retS4 = retS.ap.rearrange("p (t e n) -> p t e n", t=4, e=2)
                DT4 = DT.ap.rearrange("p (t e n) -> p t e n", t=4, e=2)
                for gb in range(2):
                    R.op("dve", lambda e, gb=gb, retS4=retS4, DT4=DT4: e.tensor_tensor(out=retS4[:, :, gb, :], in0=v3(F[gb].ap, 4),
                                                                                     in1=DT4[:, :, gb, :], op=ALU.mult),
                         reads=[F[gb], DT, retS], writes=[retS])
                if KSTOP == "m_r1":
                    raise _Stop()
                for h in range(8):
                    t, ee = h // 2, h % 2
                    Rf3s = v3(RfS[c].ap, 4)
                    Rb3s = v3(RbS[c].ap, 4)
                    R.op("pe", lambda e, h=h: e.matmul(F[2].ap[:, h * 64:(h + 1) * 64], lhsT=retS.ap[:, h * 128:(h + 1) * 128],
                                                       rhs=rvb.ap[:, h * 64:(h + 1) * 64], start=True, stop=False),
                         reads=[retS, rvb], writes=[F[2]], n=200)
                    R.op("pe", lambda e, h=h, t=t, ee=ee, Rf3s=Rf3s: e.matmul(F[2].ap[:, h * 64:(h + 1) * 64], lhsT=rxf3[64 * ee:64 * ee + 64, t, :],
                                                                             rhs=Rf3s[64 * ee:64 * ee + 64, t, :], start=False, stop=False),
                         reads=[rqxf, RfS[c]], writes=[F[2]], n=100)
                    R.op("pe", lambda e, h=h, t=t, ee=ee, Rb3s=Rb3s: e.matmul(F[2].ap[:, h * 64:(h + 1) * 64], lhsT=rxb3[64 * ee:64 * ee + 64, t, :],
                                                                             rhs=Rb3s[64 * ee:64 * ee + 64, t, :], start=False, stop=True),
                         reads=[rqxb, RbS[c]], writes=[F[2]], n=100)
                if KSTOP == "m_r2":
                    raise _Stop()
                R.op("act", lambda e: e.activation(out=tf.ap, in_=F[2].ap, func=AF.Square), reads=[F[2]], writes=[tf])
                R.op("dve", lambda e: e.tensor_reduce(out=s8f.ap, in_=v3(tf.ap, 8), axis=AX.X, op=ALU.add), reads=[tf], writes=[s8f])
                R.op("act", lambda e: e.activation(out=s8g.ap, in_=s8f.ap, func=AF.Ln, scale=1.0 / 64, bias=epsb.ap),
                     reads=[s8f, epsb], writes=[s8g])
                R.op("act", lambda e: e.activation(out=s8h.ap, in_=s8g.ap, func=AF.Exp, scale=-0.5), reads=[s8g], writes=[s8h])
                R.op("dve", lambda e: e.tensor_tensor(out=tf.ap, in0=F[2].ap, in1=tg.ap, op=ALU.mult), reads=[F[2], tg, s8f], writes=[tf])
                rr8b = s8h.ap.unsqueeze(2).broadcast_to([128, 8, 64])
                R.op("dve", lambda e, rr8b=rr8b: e.tensor_tensor(out=v3(yb.ap[:, 512:1024], 8), in0=v3(tf.ap, 8), in1=rr8b, op=ALU.mult),
                     reads=[tf, s8h], writes=[ybr], n=512)

                if KSTOP == "m_ret":
                    raise _Stop()
                for k in range(4, 8):
                    R.op("pe", lambda e, k=k: e.transpose(out=T0b[:, k * 128:(k + 1) * 128], in_=yb.ap[:, k * 128:(k + 1) * 128],
                                                          identity=ident), reads=[ybr, cstb], writes=[T0], n=128)
                R.op("act", lambda e: e.activation(out=yT.ap[:, 512:1024], in_=T0b[:, 512:1024], func=AF.Identity), reads=[T0], writes=[yTr], n=512)
                for half in range(2):
                    bank = OPB[half]
                    for k in range(4, 8):
                        R.op("pe", lambda e, k=k, half=half, bank=bank: e.matmul(bank.ap, lhsT=yT3[:, k, :],
                                                                                  rhs=W_out.ap[:, k, half * 512:(half + 1) * 512],
                                                                                  start=False, stop=(k == 7)),
                             reads=[yTr, W_out], writes=[bank])
                    R.op("dve", lambda e, half=half, bank=bank, xbuf=xbuf: e.tensor_tensor(out=xbuf.ap[:, half * 512:(half + 1) * 512],
                                                                                           in0=xbuf.ap[:, half * 512:(half + 1) * 512],
                                                                                           in1=bank.ap, op=ALU.add),
                         reads=[bank, xbuf], writes=[xbuf], n=512)
                R.op("act", lambda e, c=c, xbuf=xbuf: e.activation(out=tf.ap.bitcast(BF16), in_=xbuf.ap, func=AF.Square, accum_out=ssq2.ap[:, c:c + 1]),
                     reads=[xbuf, ssq2], writes=[tf, ssq2], n=1024)
                R.op("act", lambda e, c=c: e.activation(out=lnr2.ap[:, c:c + 1], in_=ssq2.ap[:, c:c + 1], func=AF.Ln, scale=1.0 / D_MODEL, bias=epsb.ap),
                     reads=[ssq2, epsb, lnr2], writes=[lnr2])
                R.op("act", lambda e, c=c: e.activation(out=rstd2.ap[:, c:c + 1], in_=lnr2.ap[:, c:c + 1], func=AF.Exp, scale=-0.5),
                     reads=[lnr2, rstd2], writes=[rstd2])

            if KSTOP == "main":
                raise _Stop()
            R.phase = "ffn"
            fences = []
            f_ = R.op("dve", lambda e: e.tensor_copy(out=fsc.ap[:, 0:1], in_=epsb.ap), reads=[epsb], writes=[Buf("fscD", fsc.ap[:, 0:1])])
            fences.append(f_)
            f_ = R.op("act", lambda e: e.activation(out=fsc.ap[:, 1:2], in_=epsb.ap, func=AF.Identity), reads=[epsb], writes=[Buf("fscA", fsc.ap[:, 1:2])])
            fences.append(f_)
            f_ = R.op("pool", lambda e: e.memset(fsc.ap[:, 2:3], 0.0), writes=[Buf("fscP", fsc.ap[:, 2:3])])
            fences.append(f_)
            f_ = R.op("pe", lambda e: e.matmul(T0.ap[:, 0:1], lhsT=ident, rhs=ident[:, 0:1], start=True, stop=True),
                      reads=[cstb], writes=[T0], n=1)
            fences.append(f_)
            for f_ in fences:
                f_.fence = True
            R.seg = 1
            bar = fences

            def pbuf(name, ap):
                b = Buf(name, ap)
                b.readers = list(bar)
                return b

            AB.off = 0
            AFa.off = 0

            def pb_alloc(arena, name, n):
                b = arena.alloc(name, n)
                b.readers = list(bar)
                return b

            mTg = [pb_alloc(AB, f"mTg{g_}", 4096) for g_ in range(4)]
            mT = []
            for c in range(NT_OWN):
                b_ = Buf(f"mT{c}", v3(mTg[c // 4].ap, 8)[:, :, (c % 4) * 128:(c % 4 + 1) * 128])
                b_.readers = list(bar)
                mT.append(b_)
            hTb = [[pb_alloc(AB, f"hT{s_}_{ci}", 512) for ci in range(4)] for s_ in range(2)]
            mb = pb_alloc(AB, "mb", 1024)
            junk2 = pb_alloc(AB, "junk2", 1024)
            identB = pb_alloc(AB, "identB", 128)
            fnw = pb_alloc(AFa, "fnw", 1024)
            sgt = [pb_alloc(AFa, f"sgt{i}", 512) for i in range(2)]
            st2 = pb_alloc(AFa, "st2", 8)
            T0f = pbuf("T0f", banks[6][:, :])
            T1f = pbuf("T1f", banks[7][:, :])
            slots = [pbuf(f"slot{i}", Wt[:, i * 12288:(i + 1) * 12288]) for i in range(2)]
            early = []
            for wb in Wg_buf.values():
                early.extend(wb.readers)
                early.extend(wb.writers)
            slots[0].readers = early

            R.dma("sp", fnw.ap, fnw_d, fnw, writes=[fnw])
            R.dma("pool", identB.ap, cstb_d[:, 0:128], identB, writes=[identB])

            passes = [(0, 4), (4, 4), (8, 4), (12, 4), (16, 3), (19, 3)]
            Wg_v = w_gate_d.rearrange("(k p) n -> p k n", p=128)
            Wu_v = w_up_d.rearrange("(k p) n -> p k n", p=128)

            def load_pass(r):
                f0, C = passes[r]
                sl = slots[r % 2]
                g3 = v3(sl.ap[:, 0:4096], 8)
                u3 = v3(sl.ap[:, 4096:8192], 8)
                d3 = v3(sl.ap[:, 8192:12288], 4)
                R.dma("pool", g3[:, :, 0:C * 128], Wg_v[:, :, f0 * 128:(f0 + C) * 128], sl, writes=[sl])
                R.dma("pool", u3[:, :, 0:C * 128], Wu_v[:, :, f0 * 128:(f0 + C) * 128], sl, writes=[sl])
                R.dma("pool", d3[:, 0:C, :], w_down_d[f0 * 128:(f0 + C) * 128, :].rearrange("(c p) n -> p c n", p=128), sl, writes=[sl])

            load_pass(0)
            load_pass(1)

            def prologue(tgi):
                for t in range(4):
                    c = tgi * 4 + t
                    R.op("dve", lambda e, c=c: e.scalar_tensor_tensor(out=mb.ap, in0=hbuf[c].ap, scalar=rstd2.ap[:, c:c + 1], in1=fnw.ap,
                                                                      op0=ALU.mult, op1=ALU.mult), reads=[hbuf[c], rstd2, fnw], writes=[mb])
                    for k in range(8):
                        R.op("pe", lambda e, k=k: e.transpose(out=T0b[:, k * 128:(k + 1) * 128], in_=mb.ap[:, k * 128:(k + 1) * 128],
                                                              identity=identB.ap), reads=[mb, identB], writes=[T0f], n=128)
                    R.op("act", lambda e, c=c: e.activation(out=mT[c].ap, in_=v3(T0b, 8), func=AF.Identity), reads=[T0f], writes=[mT[c]])

            prologue(0)
            gi = 0
            for r, (f0, C) in enumerate(passes):
                sl = slots[r % 2]
                g3 = v3(sl.ap[:, 0:4096], 8)
                u3 = v3(sl.ap[:, 4096:8192], 8)
                d3 = v3(sl.ap[:, 8192:12288], 4)
                last = (r == len(passes) - 1)
                for tgi in range(4):
                    if r == 0 and tgi + 1 < 4:
                        prologue(tgi + 1)
                    hs = hTb[tgi % 2]
                    for ci in range(C):
                        gbank = F[0] if gi % 2 == 0 else F[1]
                        ubank = F[2] if gi % 2 == 0 else F[3]
                        sg_ = sgt[gi % 2]
                        gi += 1
                        mg3 = v3(mTg[tgi].ap, 8)
                        for (bank, w3) in ((gbank, g3), (ubank, u3)):
                            for k in range(8):
                                R.op("pe", lambda e, bank=bank, w3=w3, ci=ci, k=k, mg3=mg3: e.matmul(
                                    bank.ap, lhsT=w3[:, k, ci * 128:(ci + 1) * 128],
                                    rhs=mg3[:, k, :], start=(k == 0), stop=(k == 7)),
                                    reads=[sl] + mT[tgi * 4:tgi * 4 + 4], writes=[bank])
                        R.op("act", lambda e, gbank=gbank, sg_=sg_: e.activation(out=sg_.ap, in_=gbank.ap, func=AF.Silu),
                             reads=[gbank], writes=[sg_])
                        R.op("dve", lambda e, ubank=ubank, sg_=sg_, hs=hs, ci=ci: e.tensor_tensor(out=hs[ci].ap, in0=sg_.ap, in1=ubank.ap, op=ALU.mult),
                             reads=[sg_, ubank], writes=[hs[ci]])
                    for t in range(4):
                        c = tgi * 4 + t
                        dbanks = (F[4], F[5]) if t % 2 == 0 else (T1f, T0f)
                        for half in range(2):
                            bank = dbanks[half]
                            for ci in range(C):
                                R.op("pe", lambda e, bank=bank, ci=ci, t=t, half=half, hs=hs, C=C, d3=d3: e.matmul(
                                    bank.ap, lhsT=hs[ci].ap[:, t * 128:(t + 1) * 128], rhs=d3[:, ci, half * 512:(half + 1) * 512],
                                    start=(ci == 0), stop=(ci == C - 1)), reads=[hs[ci], sl], writes=[bank])
                            R.op("dve", lambda e, bank=bank, c=c, half=half: e.tensor_tensor(
                                out=hbuf[c].ap[:, half * 512:(half + 1) * 512], in0=hbuf[c].ap[:, half * 512:(half + 1) * 512],
                                in1=bank.ap, op=ALU.add), reads=[bank, hbuf[c]], writes=[hbuf[c]])
                        if last:
                            R.dma("sp", out_d[c * 128:(c + 1) * 128, :], hbuf[c].ap, hbuf[c], reads=[hbuf[c]], final=True)
                if r + 2 < len(passes):
                    load_pass(r + 2)

        try:
            _record()
        except _Stop:
            pass

        R.finalize()
        _NC_CACHE['R'] = R
        if os.environ.get("KSCHEDSTAT", ""):
            print('sched sim total us:', getattr(R, 'sim_total', None))

        @block.sync
        def _(eng):
            R.emit("sp", eng)

        @block.gpsimd
        def _(eng):
            R.emit("pool", eng)

        @block.scalar
        def _(eng):
            R.emit("act", eng)

        @block.vector
        def _(eng):
            R.emit("dve", eng)

        @block.tensor
        def _(eng):
            R.emit("pe", eng)
    return nc


_NC_CACHE = {}


def _rope_tables(hf):
    l = np.arange(SEQ, dtype=np.float32)
    pos = l if hf == 0 else (np.float32(SEQ - 1) - l)
    inv_freq = (np.float32(10000.0) ** (-(np.arange(0, 64, 2, dtype=np.float32)) / np.float32(64))).astype(np.float32)
    ang = (pos[:, None] * inv_freq[None, :]).astype(np.float32)
    cos = np.cos(ang.astype(np.float64)).astype(np.float32)
    sin = np.sin(ang.astype(np.float64)).astype(np.float32)
    cs = np.concatenate([cos, cos, -sin, sin], axis=1)
    return np.ascontiguousarray(cs, dtype=np.float32)


def _const_tables():
    i = np.arange(128, dtype=np.float32)
    iota1 = np.tile((i + 1.0)[None, :], (128, 1))
    iota2 = np.tile((128.0 - i)[None, :], (128, 1))
    m = i[:, None]
    n = i[None, :]
    reluP = np.maximum(n - m, 0.0)
    reluN = np.maximum(m - n, 0.0)
    csts = np.concatenate([iota1, iota2, reluP, reluN], axis=1).astype(np.float32)
    ident = np.eye(128, dtype=np.float32)
    j = i[:, None]
    q = i[None, :]
    mprev = np.where(j >= q, 0.0, -30000.0).astype(np.float32)
    mnext = np.where(j <= q, 0.0, -30000.0).astype(np.float32)
    cstb = np.concatenate([ident, mprev, mprev, mnext, mnext], axis=1).astype(np.float32)
    return np.ascontiguousarray(csts), np.ascontiguousarray(cstb)


def _make_in_maps(x, attn_norm_w, w_in, q_norm_w, k_norm_w, attn_sink, ret_log_decay_fwd, ret_log_decay_bwd,
           ret_norm_w, w_out, ffn_norm_w, w_gate, w_up, w_down):
    x = np.asarray(x, dtype=np.float32)
    f = lambda a: np.asarray(a, dtype=np.float32)
    attn_norm_w, q_norm_w, k_norm_w, attn_sink = f(attn_norm_w)[0], f(q_norm_w)[0], f(k_norm_w)[0], f(attn_sink)[0]
    dfw, dbw = f(ret_log_decay_fwd)[0], f(ret_log_decay_bwd)[0]
    ret_norm_w, ffn_norm_w = f(ret_norm_w)[0], f(ffn_norm_w)[0]
    w_in_, w_out_, w_gate_, w_up_, w_down_ = (np.ascontiguousarray(f(a)[0]) for a in (w_in, w_out, w_gate, w_up, w_down))

    csts, cstb = _const_tables()
    fnw = np.ascontiguousarray(np.tile(ffn_norm_w[None, :], (128, 1)))
    rope_tabs = [_rope_tables(0), _rope_tables(1)]
    swap = lambda w: np.concatenate([w[32:], w[:32]])
    in_maps = []
    for c in range(8):
        b, hf = c // 2, c % 2
        xs = x[b] if hf == 0 else x[b, ::-1]
        dF, dB = (dfw, dbw) if hf == 0 else (dbw, dfw)
        cstf = np.zeros((128, NCF), dtype=np.float32)

        def put(name, row):
            a, bnd = CF[name]
            cstf[:, a:bnd] = row[None, :]
        a, bnd = CF["anwP"]
        cstf[:, a:bnd] = attn_norm_w.reshape(8, 128).T
        a, bnd = CF["rnwP"]
        cstf[:, a:bnd] = ret_norm_w.reshape(4, 128).T
        put("qnw", q_norm_w)
        put("qnws", swap(q_norm_w))
        put("knw", k_norm_w)
        put("knws", swap(k_norm_w))
        put("sink", attn_sink)
        put("decR", np.concatenate([dF, dB]))
        a, bnd = CF["decP"]
        for t in range(4):
            cstf[0:64, a + t] = dF[2 * t]
            cstf[64:128, a + t] = dF[2 * t + 1]
            cstf[0:64, a + 4 + t] = dB[2 * t]
            cstf[64:128, a + 4 + t] = dB[2 * t + 1]
        a, _b = CF["c127m"]
        cstf[:, a] = 127.0 - np.arange(128, dtype=np.float32)
        a, _b = CF["cm"]
        cstf[:, a] = np.arange(128, dtype=np.float32)
        in_maps.append({
            "xs": np.ascontiguousarray(xs), "cs": rope_tabs[hf], "cstf": cstf, "csts": csts, "cstb": cstb, "fnw": fnw,
            "w_in": w_in_, "w_out": w_out_, "w_gate": w_gate_, "w_up": w_up_, "w_down": w_down_,
        })
    return in_maps


def kernel(**inputs):
    in_maps = _make_in_maps(**inputs)
    if "nc" not in _NC_CACHE:
        _NC_CACHE["nc"] = build_program()
    nc = _NC_CACHE["nc"]
    res = run_bass_kernel_spmd(nc, in_maps, core_ids=list(range(8)))
    out = np.empty((4, SEQ, D_MODEL), dtype=np.float32)
    for c in range(8):
        b, hf = c // 2, c % 2
        o = np.asarray(res.results[c]["out"], dtype=np.float32)
        if hf == 0:
            out[b, 0:2048] = o
        else:
            out[b, 2048:4096] = o[::-1]
    return out
```

```python
import contextlib
import os
import sys
import numpy as np
import concourse.bass as bass
import concourse.mybir as mybir
from concourse.bass_utils import run_bass_kernel_spmd

F32 = mybir.dt.float32
BF16 = mybir.dt.bfloat16
AF = mybir.ActivationFunctionType
ALU = mybir.AluOpType
AX = mybir.AxisListType

D_MODEL = 1024
SEQ = 4096
NT_ALL = 32
NT_OWN = 16
IN_PROJ = 2816
D_FF = 2816
EPS = 1e-6
C_AQ, C_AK, C_AV, C_RQ, C_RK, C_RV, C_RG = 0, 512, 640, 768, 1280, 1792, 2304

CF = {}
_o = 0
for _n, _w in [("anwP", 8), ("rnwP", 4), ("qnw", 64), ("qnws", 64), ("knw", 64), ("knws", 64),
               ("sink", 8), ("decP", 8), ("decR", 16), ("c127m", 1), ("cm", 1)]:
    CF[_n] = (_o, _o + _w)
    _o += _w
NCF = _o
NCB = 128 + 256 + 256


class Op:
    __slots__ = ("eng", "fn", "deps", "signal", "sigval", "dma", "idx", "cost", "seg", "fence", "lat", "fin", "ph", "line", "crit", "estsrc", "prio")


class DmaTok:
    __slots__ = ("sem", "val", "op")

    def __init__(self, sem, val, op):
        self.sem = sem
        self.val = val
        self.op = op


class Buf:
    def __init__(self, name, ap=None, share=None):
        self.name = name
        self.ap = ap
        self._w = []
        self._r = []
        self.share = share
        self.sem = None
        self.cnt = 0

    @property
    def writers(self):
        return (self.share or self)._w

    @writers.setter
    def writers(self, v):
        (self.share or self)._w = v

    @property
    def readers(self):
        return (self.share or self)._r

    @readers.setter
    def readers(self, v):
        (self.share or self)._r = v

    def nfree(self):
        if self.ap is None:
            return 512
        n = 1
        for d in self.ap.shape[1:]:
            n *= d
        return n


_COST = {"dve": (0.10, 1.0 / 870), "act": (0.17, 1.0 / 1200), "pool": (0.15, 1.0 / 450), "pe": (0.02, 1.0 / 2600), "sp": (0.05, 0.0)}


class Rec:
    ENGS = ("pe", "act", "dve", "pool", "sp")

    def __init__(self, sems):
        self.lists = {e: [] for e in self.ENGS}
        self.free_sems = list(sems)
        self.engsem = {e: self.free_sems.pop() for e in ("pe", "act", "dve", "pool")}
        self.final = []
        self.nops = 0
        self.seg = 0

    def _deps(self, reads, writes, extra):
        deps = []
        for b in reads:
            for t in b.writers:
                deps.append((t, "raw"))
        for b in writes:
            for t in b.readers:
                deps.append((t, "war"))
            for t in b.writers:
                deps.append((t, "waw"))
        for t in extra:
            deps.append((t, "raw"))
        return deps

    def _new(self, eng, fn, deps, dma, cost):
        o = Op()
        o.eng = eng
        o.fn = fn
        o.deps = deps
        o.signal = False
        o.sigval = None
        o.dma = dma
        o.idx = self.nops
        self.nops += 1
        o.cost = cost
        o.seg = self.seg
        o.ph = getattr(self, "phase", "setup")
        o.line = sys._getframe(2).f_lineno if os.environ.get("KSCHEDSTAT", "") else 0
        o.fence = False
        o.lat = 0.0
        o.fin = 0.0
        o.crit = None
        o.estsrc = None
        o.prio = getattr(self, "prio", 0)
        self.lists[eng].append(o)
        return o

    def op(self, eng, fn, reads=(), writes=(), extra=(), n=None):
        if n is None:
            n = writes[0].nfree() if writes else 64
        a, b = _COST[eng]
        cost = a + b * n
        if eng == "pe" and n < 512 and os.environ.get("KPESMALL", "1") == "1":
            cost = 0.055 + n / 2300.0
        o = self._new(eng, fn, self._deps(reads, writes, extra), False, cost)
        for bf in reads:
            bf.readers.append(o)
        for bf in writes:
            bf.writers = [o]
            bf.readers = []
        return o

    def dma(self, q, out_ap, in_ap, semowner, reads=(), writes=(), extra=(), final=False, order_after=()):
        if semowner.sem is None:
            semowner.sem = self.free_sems.pop()
        sem = semowner.sem
        semowner.cnt += 1
        fn = lambda e, out_ap=out_ap, in_ap=in_ap, sem=sem: e.dma_start(out=out_ap, in_=in_ap).then_inc(sem, 16)
        deps_ = self._deps(reads, writes, extra)
        for t_ in order_after:
            deps_.append((t_.op, "order"))
        o = self._new(q, fn, deps_, True, 0.06 if q == "sp" else 1.0)
        nel = 1
        for d in out_ap.shape:
            nel *= d
        o.lat = 2.0 + nel * 4 / 180e3
        tok = DmaTok(sem, semowner.cnt * 16, o)
        for bf in reads:
            bf.readers.append(tok)
        for bf in writes:
            bf.writers = [tok]
            bf.readers = []
        if final:
            self.final.append(tok)
        return tok

    def schedule(self):
        allops = []
        for e in self.ENGS:
            allops.extend(self.lists[e])
        allops.sort(key=lambda o: o.idx)
        preds = {}
        succs = {}
        for o in allops:
            ps = {}
            for (t, kind) in o.deps:
                if isinstance(t, DmaTok):
                    ps[id(t.op)] = (t.op, True)
                else:
                    if id(t) not in ps:
                        ps[id(t)] = (t, False)
            preds[id(o)] = list(ps.values())
            for (p, _) in ps.values():
                succs.setdefault(id(p), []).append(o)
        bl = {}
        for o in reversed(allops):
            m = 0.0
            for sc in succs.get(id(o), ()):
                m = max(m, bl[id(sc)] + (float(os.environ.get("KLATX", "0.3")) if sc.eng != o.eng else float(os.environ.get("KLATS", "0.03"))))
            bl[id(o)] = o.cost + m
        use_bl = os.environ.get("KBL", "1") == "1"
        LX = float(os.environ.get("KLATX", "0.3"))
        LS = float(os.environ.get("KLATS", "0.03"))
        slack0 = float(os.environ.get("KSLACK", "0.05"))
        ntrials = int(os.environ.get("KTRIALS", "1"))
        import random
        rng = random.Random(1234)

        def run(bl, slack):
            free = {e: 0.0 for e in self.ENGS}
            neworder = {e: [] for e in self.ENGS}
            nseg = max(o.seg for o in allops) + 1
            for sg in range(nseg):
                ops = [o for o in allops if o.seg == sg]
                inseg = set(id(o) for o in ops)
                indeg = {}
                est = {}
                avail = {e: [] for e in self.ENGS}
                remaining = {e: 0 for e in self.ENGS}
                for o in ops:
                    remaining[o.eng] += 1
                    d = 0
                    t0 = 0.0
                    for (p, isdma) in preds[id(o)]:
                        if id(p) in inseg:
                            d += 1
                        else:
                            t0 = max(t0, p.fin + (p.lat if isdma else LX))
                    indeg[id(o)] = d
                    est[id(o)] = t0
                    if d == 0:
                        avail[o.eng].append(o)
                nleft = len(ops)
                while nleft:
                    best = None
                    for e in self.ENGS:
                        cands = []
                        tmin = None
                        for o in avail[e]:
                            if o.fence and remaining[e] > 1:
                                continue
                            st = max(free[e], est[id(o)])
                            cands.append((st, o))
                            if tmin is None or st < tmin:
                                tmin = st
                        if not cands:
                            continue
                        if use_bl:
                            st, o = max(((st, o) for (st, o) in cands if st <= tmin + slack), key=lambda x: (bl[id(x[1])], -x[1].idx))
                        else:
                            st, o = min(cands, key=lambda x: (x[0], x[1].idx))
                        key = (st, o.idx)
                        if best is None or key < best[0]:
                            best = (key, o)
                    assert best is not None, "scheduler stuck"
                    o = best[1]
                    st = best[0][0]
                    e = o.eng
                    avail[e].remove(o)
                    remaining[e] -= 1
                    nleft -= 1
                    o.fin = st + o.cost
                    o.crit = (neworder[e][-1] if (neworder[e] and free[e] >= est[id(o)]) else o.estsrc)
                    free[e] = o.fin
                    neworder[e].append(o)
                    for sc in succs.get(id(o), ()):
                        if id(sc) not in inseg:
                            continue
                        isdma = any((p is o and dm) for (p, dm) in preds[id(sc)])
                        lat = o.lat if isdma else (LX if sc.eng != e else LS)
                        if o.fin + lat > est[id(sc)]:
                            est[id(sc)] = o.fin + lat
                            sc.estsrc = o
                        indeg[id(sc)] -= 1
                        if indeg[id(sc)] == 0:
                            avail[sc.eng].append(sc)
            return neworder, max(free.values()), {id(o): (o.fin, o.crit) for o in allops}

        bl0 = bl
        best_res = None
        for trial in range(max(1, ntrials)):
            if trial == 0:
                blt, sl = bl0, slack0
            else:
                amp = rng.choice((0.0005, 0.001, 0.003, 0.01))
                blt = {k: v * (1.0 + amp * (2.0 * rng.random() - 1.0)) for k, v in bl0.items()}
                sl = rng.choice((0.03, 0.05, 0.08, 0.15, 0.25))
            res = run(blt, sl)
            if os.environ.get('KSCHEDSTAT', ''):
                print('trial', trial, round(res[1], 1))
            if best_res is None or res[1] < best_res[1]:
                best_res = res
        neworder, total_, fins_ = best_res
        for o in allops:
            o.fin, o.crit = fins_[id(o)]
        free = {"x": total_}
        self.lists = neworder
        self.sim_total = max(free.values())
        if os.environ.get("KSCHEDSTAT", "") == "1":
            stat = {}
            for o in allops:
                d = stat.setdefault(o.ph, {"end": 0.0, "start": 1e18})
                d["end"] = max(d["end"], o.fin)
                d["start"] = min(d["start"], o.fin - o.cost)
                d[o.eng] = d.get(o.eng, 0.0) + o.cost
            for ph, d in stat.items():
                print("SCHED", ph, {k: round(v, 1) for k, v in d.items()})
            cp = os.environ.get("KSCHEDCRIT", "")
            if cp:
                ph_, n_ = cp.split(",")
                tmax_ = float(os.environ.get("KSCHEDTMAX", "1e18"))
                cur = max((o for o in allops if o.ph == ph_ and o.fin <= tmax_), key=lambda o: o.fin)
                chain = []
                while cur is not None and len(chain) < int(n_):
                    chain.append(cur)
                    cur = cur.crit
                for o in reversed(chain):
                    print("CRIT %8.2f %6.2f %-4s L%d%s" % (o.fin - o.cost, o.cost, o.eng, o.line, " dma" if o.dma else ""))
            w = os.environ.get("KSCHEDWIN", "")
            if w:
                a_, b_ = (float(x) for x in w.split(","))
                sel = [o for o in allops if a_ <= o.fin - o.cost < b_]
                sel.sort(key=lambda o: o.fin - o.cost)
                for o in sel:
                    print("OP %8.2f %6.2f %-4s L%d%s" % (o.fin - o.cost, o.cost, o.eng, o.line, " dma" if o.dma else ""))

    def finalize(self):
        if os.environ.get("KNOSCHED", "") != "1":
            self.schedule()
        for e in self.ENGS:
            for o in self.lists[e]:
                for (t, kind) in o.deps:
                    if kind == "order":
                        continue
                    if isinstance(t, Op):
                        if t.eng != o.eng or o.dma or o.eng in ("act", "dve", "pool"):
                            t.signal = True
        for e in self.ENGS:
            cnt = 0
            for o in self.lists[e]:
                if o.signal:
                    cnt += 1
                    o.sigval = cnt

    def emit(self, e, eng):
        waited = {}
        for o in self.lists[e]:
            need = {}
            for (t, kind) in o.deps:
                if kind == "order":
                    continue
                if isinstance(t, DmaTok):
                    key, val = t.sem, t.val
                else:
                    if t.eng == e and not (o.dma or e in ("act", "dve", "pool")):
                        continue
                    key, val = self.engsem[t.eng], t.sigval
                if need.get(key, (None, 0))[1] < val:
                    need[key] = (key, val)
            for key, val in need.values():
                if waited.get(key, 0) < val:
                    eng.wait_ge(key, val)
                    waited[key] = val
            ins = o.fn(eng)
            if o.signal:
                ins.then_inc(self.engsem[e], 1)
        if e == "sp":
            for t in self.final:
                if waited.get(t.sem, 0) < t.val:
                    eng.wait_ge(t.sem, t.val)
                    waited[t.sem] = t.val


def v3(ap, a):
    return ap.rearrange("p (a b) -> p a b", a=a)


class _Stop(Exception):
    pass


def build_program():
    nc = bass.Bass("TRN2", target_bir_lowering=False)
    KSTOP = os.environ.get("KSTOP", "")
    xs = nc.dram_tensor("xs", [SEQ, D_MODEL], F32, kind="ExternalInput").ap()
    cs_d = nc.dram_tensor("cs", [SEQ, 128], F32, kind="ExternalInput").ap()
    cstf_d = nc.dram_tensor("cstf", [128, NCF], F32, kind="ExternalInput").ap()
    csts_d = nc.dram_tensor("csts", [128, 512], F32, kind="ExternalInput").ap()
    cstb_d = nc.dram_tensor("cstb", [128, NCB], F32, kind="ExternalInput").ap()
    fnw_d = nc.dram_tensor("fnw", [128, D_MODEL], F32, kind="ExternalInput").ap()
    w_in_d = nc.dram_tensor("w_in", [D_MODEL, IN_PROJ], F32, kind="ExternalInput").ap()
    w_out_d = nc.dram_tensor("w_out", [D_MODEL, D_MODEL], F32, kind="ExternalInput").ap()
    w_gate_d = nc.dram_tensor("w_gate", [D_MODEL, D_FF], F32, kind="ExternalInput").ap()
    w_up_d = nc.dram_tensor("w_up", [D_MODEL, D_FF], F32, kind="ExternalInput").ap()
    w_down_d = nc.dram_tensor("w_down", [D_FF, D_MODEL], F32, kind="ExternalInput").ap()
    out_d = nc.dram_tensor("out", [NT_OWN * 128, D_MODEL], F32, kind="ExternalOutput").ap()

    NB16 = 26916
    NF32 = 7488
    with contextlib.ExitStack() as es:
        Wt = es.enter_context(nc.sbuf_tensor("Wt", [128, 30720], BF16))
        Ht = es.enter_context(nc.sbuf_tensor("Ht", [128, NT_OWN * 1024], F32))
        ABt = es.enter_context(nc.sbuf_tensor("ABt", [128, NB16], BF16))
        AFt = es.enter_context(nc.sbuf_tensor("AFt", [128, NF32], F32))
        banks = [es.enter_context(nc.psum_tensor(f"pb{i}", [128, 512], F32)) for i in range(8)]
        sems = [es.enter_context(nc.semaphore(f"s{i}")) for i in range(60)]
        block = es.enter_context(nc.Block())
        R = Rec(sems)

        class Arena:
            def __init__(self, t, n):
                self.t, self.n, self.off = t, n, 0

            def alloc(self, name, n):
                assert self.off + n <= self.n, (name, self.off, n, self.n)
                ap = self.t[:, self.off:self.off + n]
                self.off += n
                return Buf(name, ap)

        AB = Arena(ABt, NB16)
        AFa = Arena(AFt, NF32)

        W_in = Buf("w_in", v3(Wt[:, 0:22528], 8))
        W_out = Buf("w_out", v3(Wt[:, 22528:30720], 8))
        hbuf = [Buf(f"h{c}", Ht[:, c * 1024:(c + 1) * 1024]) for c in range(NT_OWN)]
        kvf = [Buf(f"kvf{c}", v3(Ht[:, 12 * 1024 + c * 256: 12 * 1024 + (c + 1) * 256], 4)) for c in range(NT_OWN)]

        F = [Buf(f"F{i}", banks[i][:, :]) for i in range(6)]
        T0 = Buf("T0", banks[6][:, :])
        T1a = Buf("T1a", banks[7][:, 0:256])
        T1b = Buf("T1b", banks[7][:, 256:512], share=T1a)
        T0b = banks[6][:, :].bitcast(BF16)
        T1ab = banks[7][:, 0:256].bitcast(BF16)
        T1bb = banks[7][:, 256:512].bitcast(BF16)

        cstb = AB.alloc("cstb", NCB)
        ident = cstb.ap[:, 0:128]
        mprev = cstb.ap[:, 128:384]
        mnext = cstb.ap[:, 384:640]
        RbS = [AB.alloc(f"RbS{c}", 256) for c in range(NT_OWN)]
        RfS = [AB.alloc(f"RfS{c}", 256) for c in range(NT_OWN)]
        KT = [AB.alloc(f"KT{c}", 256) for c in range(NT_OWN + 1)]
        Vb = [AB.alloc(f"V{c}", 130) for c in range(NT_OWN + 1)]
        xb = AB.alloc("xb", 1024)
        xT = AB.alloc("xT", 1024)
        qb = AB.alloc("qb", 512)
        rqb = AB.alloc("rqb", 512)
        rkb = AB.alloc("rkb", 512)
        rvb = AB.alloc("rvb", 512)
        kdup = AB.alloc("kdup", 256)
        qT = AB.alloc("qT", 512)
        rqT = AB.alloc("rqT", 512)
        rqxf = AB.alloc("rqxf", 512)
        rqxb = AB.alloc("rqxb", 512)
        rkT = AB.alloc("rkT", 512)
        PT = AB.alloc("PT", 768)
        PT2 = AB.alloc("PT2", 768)
        retS = AB.alloc("retS", 1024)
        yb = AB.alloc("yb", 1024)
        yT = AB.alloc("yT", 1024)
        kzb, kzf = qb, rqb
        yba = Buf("yba", yb.ap[:, 0:512])
        ybr = Buf("ybr", yb.ap[:, 512:1024])
        yTa = Buf("yTa", yT.ap[:, 0:512], share=yT)
        yTr = Buf("yTr", yT.ap[:, 512:1024], share=yT)

        cstf = AFa.alloc("cstf", NCF)

        def cf(name):
            a, b = CF[name]
            return cstf.ap[:, a:b]

        Xif = AFa.alloc("Xif", 512)
        Xib = AFa.alloc("Xib", 512)
        csl = [AFa.alloc(f"cs{i}", 128) for i in range(2)]
        tabq = AFa.alloc("tabq", 128)
        ta = AFa.alloc("ta", 512)
        DT = AFa.alloc("DT", 1024)
        td = AFa.alloc("td", 512)
        te = AFa.alloc("te", 512)
        tg = AFa.alloc("tg", 512)
        tf = AFa.alloc("tf", 512)
        akf = Buf("akf", tf.ap[:, 0:128])
        tk1 = Buf("tk1", tf.ap[:, 128:256])
        tk2 = Buf("tk2", tf.ap[:, 256:384])
        tabk = Buf("tabk", tf.ap[:, 384:512])
        Rb = AFa.alloc("Rb", 256)
        Rf = AFa.alloc("Rf", 256)
        rtmp = AFa.alloc("rtmp", 256)
        lgP = AFa.alloc("lgP", 8)
        lgR = AFa.alloc("lgR", 16)
        g128 = AFa.alloc("g128", 8)
        Zfb = AFa.alloc("Zfb", 16)
        esink = AFa.alloc("esink", 8)
        st = [AFa.alloc(f"st{i}", 8) for i in range(2)]
        s8a = AFa.alloc("s8a", 8)
        s8g = AFa.alloc("s8g", 8)
        s8h = AFa.alloc("s8h", 8)
        tb2 = AFa.alloc("tb2", 512)
        junk = AFa.alloc("junk", 512)
        junk_ap = junk.ap.bitcast(BF16)
        s8b = AFa.alloc("s8b", 8)
        s8c = AFa.alloc("s8c", 8)
        s8d = AFa.alloc("s8d", 8)
        s8e = AFa.alloc("s8e", 8)
        s8f = AFa.alloc("s8f", 8)
        epsb = AFa.alloc("epsb", 1)
        onesb = AFa.alloc("onesb", 1)
        fsc = AFa.alloc("fsc", 4)
        ssq2 = AFa.alloc("ssq2", 16)
        lnr2 = AFa.alloc("lnr2", 16)
        rstd2 = AFa.alloc("rstd2", 16)

        if os.environ.get("KSCHEDSTAT", ""):
            print("arena use: bf16", AB.off, "/", NB16, " f32", AFa.off, "/", NF32)
        W_in_v = w_in_d.rearrange("(k p) n -> p k n", p=128)
        W_out_v = w_out_d.rearrange("(k p) n -> p k n", p=128)

        def _record():
            R.dma("sp", cstf.ap, cstf_d, cstf, writes=[cstf])
            R.dma("sp", ta.ap, csts_d, ta, writes=[ta])
            R.dma("pool", cstb.ap, cstb_d, cstb, writes=[cstb])
            wcols = [("rkrv", C_RK, C_RG), ("akav", C_AK, C_RQ), ("aq", C_AQ, C_AK), ("rq", C_RQ, C_RK), ("rg", C_RG, IN_PROJ)]
            Wg_buf = {}
            prev_w = None
            for name, a, b in wcols:
                bb = Buf("w_in_" + name)
                Wg_buf[name] = bb
                ex = [prev_w] if prev_w else []
                tk_ = None
                for k in range(8):
                    tk_ = R.dma("pool", W_in.ap[:, k, a:b], W_in_v[:, k, a:b], bb, writes=([bb] if k == 0 else []), extra=ex,
                                order_after=([tk_] if tk_ else []))
                bb.writers = [tk_]
                prev_w = tk_
            tk_ = None
            for k in range(8):
                tk_ = R.dma("pool", W_out.ap[:, k, :], W_out_v[:, k, :], W_out, writes=([W_out] if k == 0 else []), extra=[prev_w],
                            order_after=([tk_] if tk_ else []))
            W_out.writers = [tk_]
            for k in range(4):
                R.op("dve", lambda e, k=k: e.tensor_scalar(out=W_out.ap[:, 4 + k, :], in0=W_out.ap[:, 4 + k, :],
                                                           scalar1=cf("rnwP")[:, k:k + 1], scalar2=None, op0=ALU.mult),
                     reads=[cstf, W_out], writes=[W_out], n=340)
            for name, a, b in wcols:
                for k in range(8):
                    R.op("dve", lambda e, k=k, a=a, b=b: e.tensor_scalar(out=W_in.ap[:, k, a:b], in0=W_in.ap[:, k, a:b],
                                                                         scalar1=cf("anwP")[:, k:k + 1], scalar2=None, op0=ALU.mult),
                         reads=[cstf, Wg_buf[name]], writes=[Wg_buf[name]], n=(b - a) // 3)

            iota1 = ta.ap[:, 0:128]
            iota2 = ta.ap[:, 128:256]
            reluP = ta.ap[:, 256:384]
            reluN = ta.ap[:, 384:512]

            R.op("act", lambda e: e.activation(out=lgP.ap, in_=cf("decP"), func=AF.Abs), reads=[cstf], writes=[lgP])
            R.op("act", lambda e: e.activation(out=lgR.ap, in_=cf("decR"), func=AF.Abs), reads=[cstf], writes=[lgR])
            R.op("dve", lambda e: e.tensor_scalar(out=lgP.ap, in0=lgP.ap, scalar1=-1.0, scalar2=None, op0=ALU.mult), reads=[lgP], writes=[lgP])
            R.op("dve", lambda e: e.tensor_scalar(out=lgR.ap, in0=lgR.ap, scalar1=-1.0, scalar2=None, op0=ALU.mult), reads=[lgR], writes=[lgR])
            R.op("act", lambda e: e.activation(out=g128.ap, in_=lgP.ap, func=AF.Exp, scale=128.0), reads=[lgP], writes=[g128])
            for t in range(4):
                R.op("act", lambda e, t=t: e.activation(out=Xif.ap[:, t * 128:(t + 1) * 128], in_=iota1, func=AF.Exp,
                                                        scale=lgP.ap[:, t:t + 1]), reads=[lgP, ta], writes=[Xif])
                R.op("act", lambda e, t=t: e.activation(out=Xib.ap[:, t * 128:(t + 1) * 128], in_=iota2, func=AF.Exp,
                                                        scale=lgP.ap[:, 4 + t:5 + t]), reads=[lgP, ta], writes=[Xib])
            R.op("act", lambda e: e.activation(out=Zfb.ap[:, 0:8], in_=lgR.ap[:, 0:8], func=AF.Exp, scale=cf("c127m")),
                 reads=[lgR, cstf], writes=[Zfb])
            R.op("act", lambda e: e.activation(out=Zfb.ap[:, 8:16], in_=lgR.ap[:, 8:16], func=AF.Exp, scale=cf("cm")),
                 reads=[lgR, cstf], writes=[Zfb])
            R.op("act", lambda e: e.activation(out=esink.ap, in_=cf("sink"), func=AF.Exp), reads=[cstf], writes=[esink])
            for h in range(8):
                R.op("dve", lambda e, h=h: e.tensor_scalar(out=te.ap[:, 0:128], in0=reluP, scalar1=lgR.ap[:, h:h + 1],
                                                           scalar2=None, op0=ALU.mult), reads=[ta, lgR], writes=[te])
                R.op("dve", lambda e, h=h: e.scalar_tensor_tensor(out=td.ap[:, 0:128], in0=reluN, scalar=lgR.ap[:, 8 + h:9 + h],
                                                                  in1=te.ap[:, 0:128], op0=ALU.mult, op1=ALU.add),
                     reads=[ta, lgR, te], writes=[td])
                R.op("act", lambda e, h=h: e.activation(out=DT.ap[:, h * 128:(h + 1) * 128], in_=td.ap[:, 0:128], func=AF.Exp),
                     reads=[td], writes=[DT])
            for c in range(NT_OWN + 1):
                R.op("pool", lambda e, c=c: e.memset(v3(Vb[c].ap, 2)[:, :, 64:65], 1.0), writes=[Vb[c]])
            R.op("pool", lambda e: e.memset(Rb.ap, 0.0), writes=[Rb])
            R.op("pool", lambda e: e.memset(Rf.ap, 0.0), writes=[Rf])
            R.op("pool", lambda e: e.memset(epsb.ap, EPS), writes=[epsb])
            R.op("pool", lambda e: e.memset(onesb.ap, 1.0), writes=[onesb])

            if KSTOP == "setup":
                raise _Stop()
            def load_tile(L, xbuf, slot):
                R.dma("sp", xbuf.ap, xs[L * 128:(L + 1) * 128, :], xbuf, writes=[xbuf])
                R.dma("sp", csl[slot].ap, cs_d[L * 128:(L + 1) * 128, :], csl[slot], writes=[csl[slot]])

            def prep_tile(xbuf, slot, xb=xb, xT=xT):
                s = st[slot]
                R.op("act", lambda e: e.activation(out=junk_ap, in_=xbuf.ap, func=AF.Square, accum_out=s.ap[:, 0:1]),
                     reads=[xbuf], writes=[junk, s], n=1024)
                R.op("act", lambda e: e.activation(out=s.ap[:, 1:2], in_=s.ap[:, 0:1], func=AF.Ln, scale=1.0 / D_MODEL, bias=epsb.ap),
                     reads=[s, epsb], writes=[s])
                R.op("act", lambda e: e.activation(out=s.ap[:, 2:3], in_=s.ap[:, 1:2], func=AF.Exp, scale=-0.5), reads=[s], writes=[s])
                R.op("dve", lambda e: e.tensor_scalar(out=s.ap[:, 5:6], in0=s.ap[:, 2:3], scalar1=-1.0, scalar2=None,
                                                      op0=ALU.mult), reads=[s], writes=[s])
                R.op("dve", lambda e: e.tensor_scalar(out=s.ap[:, 3:4], in0=s.ap[:, 2:3], scalar1=0.125, scalar2=None,
                                                      op0=ALU.mult), reads=[s], writes=[s])
                R.op("dve", lambda e: e.tensor_scalar(out=s.ap[:, 4:5], in0=s.ap[:, 2:3], scalar1=0.5, scalar2=None,
                                                      op0=ALU.mult), reads=[s], writes=[s])
                R.op("dve", lambda e: e.tensor_copy(out=xb.ap, in_=xbuf.ap), reads=[xbuf], writes=[xb], n=600)
                for k in range(8):
                    R.op("pe", lambda e, k=k: e.transpose(out=T0b[:, k * 128:(k + 1) * 128], in_=xb.ap[:, k * 128:(k + 1) * 128],
                                                          identity=ident), reads=[xb, cstb], writes=[T0], n=128)
                R.op("act", lambda e: e.activation(out=xT.ap, in_=T0b, func=AF.Identity), reads=[T0], writes=[xT])
                return s

            def inproj(bank, c0, n, wnames, xT=xT, also=()):
                xT3 = v3(xT.ap, 8)
                for k in range(8):
                    R.op("pe", lambda e, k=k: e.matmul(bank.ap[:, 0:n], lhsT=xT3[:, k, :], rhs=W_in.ap[:, k, c0:c0 + n],
                                                       start=(k == 0), stop=(k == 7)),
                         reads=[xT] + [Wg_buf[w] for w in wnames], writes=[bank] + list(also), n=n)

            def rope(eng_a, eng_b, src, dst_t1, dst_t2, A, B, H):
                s3 = v3(src.ap[:, 0:H * 64], H)
                t13 = v3(dst_t1.ap[:, 0:H * 64], H)
                t23 = v3(dst_t2.ap[:, 0:H * 64], H)
                Ab = A.unsqueeze(1).broadcast_to([128, H, 64])
                Bl = B[:, 0:32].unsqueeze(1).broadcast_to([128, H, 32])
                Bh = B[:, 32:64].unsqueeze(1).broadcast_to([128, H, 32])
                return s3, t13, t23, Ab, Bl, Bh

            R.phase = "pre"
            xring = [hbuf[4], hbuf[5], hbuf[6]]
            order = list(range(NT_ALL - 1, -1, -1))
            load_tile(order[0], xring[0], 0)
            for i, L in enumerate(order):
                if KSTOP == "pre1" and i == 1:
                    raise _Stop()
                xbuf = xring[i % 3]
                slot = i % 2
                if i + 1 < len(order):
                    load_tile(order[i + 1], xring[(i + 1) % 3], (i + 1) % 2)
                own = L < NT_OWN
                needkv = L <= NT_OWN
                par = i % 2
                xb_ = (xb, retS)[par]
                xT_ = (xT, yT)[par]
                ta_ = (ta, tg)[par]
                rvb_ = (rvb, rkb)[par]
                kzb_ = (qb, rqxf)[par]
                kzf_ = (rqb, rqxb)[par]
                Brk = (F[0], F[3])[par]
                Brv = (F[1], F[4])[par]
                Bkv = (F[2], F[5])[par]
                s = prep_tile(xbuf, slot, xb_, xT_)
                cst = csl[slot]
                cosv = cst.ap[:, 0:64]
                sinv = cst.ap[:, 64:128]
                inproj(Brk, C_RK, 512, ["rkrv"], xT_)
                inproj(Brv, C_RV, 512, ["rkrv"], xT_)
                if needkv:
                    inproj(T1b, C_AK, 256, ["akav"], xT_, also=[T1a])
                R.op("act", lambda e, s=s, ta_=ta_, Brk=Brk: e.activation(out=ta_.ap, in_=Brk.ap, func=AF.Identity, scale=s.ap[:, 3:4]),
                     reads=[Brk, s], writes=[ta_])
                R.op("act", lambda e, s=s, rvb_=rvb_, Brv=Brv: e.activation(out=rvb_.ap, in_=Brv.ap, func=AF.Identity, scale=s.ap[:, 2:3]),
                     reads=[Brv, s], writes=[rvb_])
                s3, t13, t23, Ab, Bl, Bh = rope(None, None, ta_, td, te, cosv, sinv, 8)
                R.op("dve", lambda e, s3=s3, t13=t13, Ab=Ab: e.tensor_tensor(out=t13, in0=s3, in1=Ab, op=ALU.mult),
                     reads=[ta_, cst], writes=[td])
                R.op("pool", lambda e, s3=s3, t23=t23, Bl=Bl: e.tensor_tensor(out=t23[:, :, 0:32], in0=s3[:, :, 32:64], in1=Bl, op=ALU.mult),
                     reads=[ta_, cst], writes=[te])
                R.op("pool", lambda e, s3=s3, t23=t23, Bh=Bh: e.tensor_tensor(out=t23[:, :, 32:64], in0=s3[:, :, 0:32], in1=Bh, op=ALU.mult),
                     reads=[ta_, cst], writes=[te])
                R.op("dve", lambda e: e.tensor_tensor(out=td.ap, in0=td.ap, in1=te.ap, op=ALU.add), reads=[td, te], writes=[td])
                Zb_b = Zfb.ap[:, 8:16].unsqueeze(2).broadcast_to([128, 8, 64])
                Zf_b = Zfb.ap[:, 0:8].unsqueeze(2).broadcast_to([128, 8, 64])
                R.op("dve", lambda e, Zb_b=Zb_b, kzb_=kzb_: e.tensor_tensor(out=v3(kzb_.ap, 8), in0=v3(td.ap, 8), in1=Zb_b, op=ALU.mult),
                     reads=[td, Zfb], writes=[kzb_])
                if own:
                    R.op("pool", lambda e, Zf_b=Zf_b, kzf_=kzf_: e.tensor_tensor(out=v3(kzf_.ap, 8), in0=v3(td.ap, 8), in1=Zf_b, op=ALU.mult),
                         reads=[td, Zfb], writes=[kzf_])
                for t in range(4):
                    R.op("pe", lambda e, t=t, Bkv=Bkv, kzb_=kzb_, rvb_=rvb_: e.matmul(Bkv.ap[:, t * 128:(t + 1) * 128], lhsT=kzb_.ap[:, t * 128:(t + 1) * 128],
                                                       rhs=rvb_.ap[:, t * 128:(t + 1) * 128], start=True, stop=True),
                         reads=[kzb_, rvb_], writes=[Bkv], n=128)
                if own:
                    for t in range(4):
                        R.op("pe", lambda e, t=t, Brk=Brk, kzf_=kzf_, rvb_=rvb_: e.matmul(Brk.ap[:, t * 128:(t + 1) * 128], lhsT=kzf_.ap[:, t * 128:(t + 1) * 128],
                                                           rhs=rvb_.ap[:, t * 128:(t + 1) * 128], start=True, stop=True),
                             reads=[kzf_, rvb_], writes=[Brk], n=128)
                Rb3 = v3(Rb.ap, 4)
                rt3 = v3(rtmp.ap, 4)
                F33 = v3(Bkv.ap, 4)
                F43 = v3(Brk.ap, 4)
                if own:
                    R.op("dve", lambda e, L=L: e.tensor_copy(out=RbS[L].ap, in_=Rb.ap), reads=[Rb], writes=[RbS[L]])
                gb_b = g128.ap[:, 4:8].unsqueeze(2).broadcast_to([128, 4, 64])
                R.op("dve", lambda e, gb_b=gb_b, Rb3=Rb3, rt3=rt3: e.tensor_tensor(out=rt3, in0=Rb3, in1=gb_b, op=ALU.mult),
                     reads=[Rb, g128], writes=[rtmp])
                R.op("dve", lambda e, Rb3=Rb3, rt3=rt3, F33=F33: e.tensor_tensor(out=Rb3[0:64], in0=rt3[0:64], in1=F33[0:64, :, 0:64], op=ALU.add),
                     reads=[rtmp, Bkv], writes=[Rb])
                R.op("dve", lambda e, Rb3=Rb3, rt3=rt3, F33=F33: e.tensor_tensor(out=Rb3[64:128], in0=rt3[64:128], in1=F33[64:128, :, 64:128], op=ALU.add),
                     reads=[rtmp, Bkv, Rb], writes=[Rb])
                if own:
                    R.op("act", lambda e, L=L, F43=F43: e.activation(out=kvf[L].ap[0:64], in_=F43[0:64, :, 0:64], func=AF.Identity),
                         reads=[Brk], writes=[kvf[L]])
                    R.op("act", lambda e, L=L, F43=F43: e.activation(out=kvf[L].ap[64:128], in_=F43[64:128, :, 64:128], func=AF.Identity),
                         reads=[Brk, kvf[L]], writes=[kvf[L]])
                if needkv:
                    R.op("act", lambda e, s=s: e.activation(out=akf.ap, in_=T1b.ap[:, 0:128], func=AF.Identity, scale=s.ap[:, 2:3]),
                         reads=[T1b, s], writes=[akf])
                    R.op("act", lambda e, s=s, L=L: e.activation(out=v3(Vb[L].ap, 2)[:, :, 0:64], in_=v3(T1b.ap[:, 128:256], 2),
                                                                  func=AF.Identity, scale=s.ap[:, 2:3]),
                         reads=[T1b, s, Vb[L]], writes=[Vb[L]])
                    R.op("dve", lambda e: e.tensor_tensor(out=tk1.ap, in0=akf.ap, in1=akf.ap, op=ALU.mult), reads=[akf], writes=[tk1])
                    R.op("dve", lambda e: e.tensor_reduce(out=s8a.ap[:, 0:2], in_=v3(tk1.ap, 2), axis=AX.X, op=ALU.add),
                         reads=[tk1], writes=[s8a])
                    R.op("act", lambda e: e.activation(out=s8a.ap[:, 2:4], in_=s8a.ap[:, 0:2], func=AF.Ln, scale=1.0 / 64, bias=epsb.ap),
                         reads=[s8a, epsb], writes=[s8a])
                    R.op("act", lambda e: e.activation(out=s8a.ap[:, 4:6], in_=s8a.ap[:, 2:4], func=AF.Exp, scale=-0.5), reads=[s8a], writes=[s8a])
                    R.op("pool", lambda e, cosv=cosv: e.tensor_tensor(out=tabk.ap[:, 0:64], in0=cosv, in1=cf("knw"), op=ALU.mult),
                         reads=[cst, cstf], writes=[tabk])
                    R.op("pool", lambda e, sinv=sinv: e.tensor_tensor(out=tabk.ap[:, 64:128], in0=sinv, in1=cf("knws"), op=ALU.mult),
                         reads=[cst, cstf, tabk], writes=[tabk])
                    s3, t13, t23, Ab, Bl, Bh = rope(None, None, akf, tk1, tk2, tabk.ap[:, 0:64], tabk.ap[:, 64:128], 2)
                    R.op("pool", lambda e, s3=s3, t13=t13, Ab=Ab: e.tensor_tensor(out=t13, in0=s3, in1=Ab, op=ALU.mult),
                         reads=[akf, tabk, s8a], writes=[tk1])
                    R.op("pool", lambda e, s3=s3, t23=t23, Bl=Bl: e.tensor_tensor(out=t23[:, :, 0:32], in0=s3[:, :, 32:64], in1=Bl, op=ALU.mult),
                         reads=[akf, tabk], writes=[tk2])
                    R.op("pool", lambda e, s3=s3, t23=t23, Bh=Bh: e.tensor_tensor(out=t23[:, :, 32:64], in0=s3[:, :, 0:32], in1=Bh, op=ALU.mult),
                         reads=[akf, tabk, tk2], writes=[tk2])
                    R.op("dve", lambda e: e.tensor_tensor(out=tk1.ap, in0=tk1.ap, in1=tk2.ap, op=ALU.add), reads=[tk1, tk2], writes=[tk1])
                    kd4 = kdup.ap.rearrange("p (g u d) -> p g u d", g=2, u=2)
                    rk2b = s8a.ap[:, 4:6].unsqueeze(2).broadcast_to([128, 2, 64])
                    for u in range(2):
                        R.op("dve", lambda e, u=u, kd4=kd4, rk2b=rk2b: e.tensor_tensor(out=kd4[:, :, u, :], in0=v3(tk1.ap, 2), in1=rk2b, op=ALU.mult),
                             reads=[tk1, s8a, kdup], writes=[kdup])
                    for g in range(2):
                        R.op("pe", lambda e, g=g: e.transpose(out=T1ab[:, g * 128:(g + 1) * 128], in_=kdup.ap[:, g * 128:(g + 1) * 128],
                                                              identity=ident), reads=[kdup, cstb], writes=[T1a, T1b], n=128)
                    R.op("act", lambda e, L=L: e.activation(out=KT[L].ap, in_=T1ab[:, 0:256], func=AF.Identity), reads=[T1a], writes=[KT[L]])

            if KSTOP == "pre":
                raise _Stop()
            Rf3 = v3(Rf.ap, 4)
            rt3 = v3(rtmp.ap, 4)
            gf_b = g128.ap[:, 0:4].unsqueeze(2).broadcast_to([128, 4, 64])
            scan_last = None
            for c in range(NT_OWN):
                R.op("dve", lambda e, c=c: e.tensor_copy(out=RfS[c].ap, in_=Rf.ap), reads=[Rf], writes=[RfS[c]])
                R.op("dve", lambda e: e.tensor_tensor(out=rt3, in0=Rf3, in1=gf_b, op=ALU.mult), reads=[Rf, g128], writes=[rtmp])
                scan_last = R.op("dve", lambda e, c=c: e.tensor_tensor(out=Rf3, in0=rt3, in1=kvf[c].ap, op=ALU.add),
                                 reads=[rtmp, kvf[c]], writes=[Rf])

            if KSTOP == "scan":
                raise _Stop()
            R.phase = "main"

            def load_main(c):
                extra = [scan_last] if c >= 12 else []
                R.dma("sp", hbuf[c].ap, xs[c * 128:(c + 1) * 128, :], hbuf[c], writes=[hbuf[c]], extra=extra)
                R.dma("sp", csl[c % 2].ap, cs_d[c * 128:(c + 1) * 128, :], csl[c % 2], writes=[csl[c % 2]])

            load_main(0)
            for c in range(NT_OWN):
                if KSTOP == "main1" and c == 1:
                    raise _Stop()
                xbuf = hbuf[c]
                slot = c % 2
                s = prep_tile(xbuf, slot)
                cst = csl[slot]
                cosv = cst.ap[:, 0:64]
                sinv = cst.ap[:, 64:128]
                R.op("pool", lambda e, cosv=cosv: e.tensor_tensor(out=tabq.ap[:, 0:64], in0=cosv, in1=cf("qnw"), op=ALU.mult),
                     reads=[cst, cstf], writes=[tabq])
                R.op("pool", lambda e, sinv=sinv: e.tensor_tensor(out=tabq.ap[:, 64:128], in0=sinv, in1=cf("qnws"), op=ALU.mult),
                     reads=[cst, cstf, tabq], writes=[tabq])
                KO1 = os.environ.get("KO1", "0") == "1"
                BQ, BG = (F[5], F[0]) if KO1 else (F[0], F[4])
                inproj(BQ, C_AQ, 512, ["aq"])
                inproj(F[1], C_RQ, 512, ["rq"])
                inproj(F[2], C_RK, 512, ["rkrv"])
                inproj(F[3], C_RV, 512, ["rkrv"])
                inproj(BG, C_RG, 512, ["rg"])
                if c + 1 < NT_OWN:
                    load_main(c + 1)
                R.op("act", lambda e, s=s: e.activation(out=ta.ap, in_=BQ.ap, func=AF.Identity, scale=s.ap[:, 2:3]),
                     reads=[BQ, s], writes=[ta])
                R.op("dve", lambda e: e.tensor_tensor(out=td.ap, in0=ta.ap, in1=ta.ap, op=ALU.mult), reads=[ta], writes=[td])
                R.op("dve", lambda e: e.tensor_reduce(out=s8a.ap, in_=v3(td.ap, 8), axis=AX.X, op=ALU.add), reads=[td], writes=[s8a])
                R.op("act", lambda e: e.activation(out=s8b.ap, in_=s8a.ap, func=AF.Ln, scale=1.0 / 64, bias=epsb.ap),
                     reads=[s8a, epsb], writes=[s8b])
                R.op("act", lambda e: e.activation(out=s8c.ap, in_=s8b.ap, func=AF.Exp, scale=-0.5), reads=[s8b], writes=[s8c])
                s3, t13, t23, Ab, Bl, Bh = rope(None, None, ta, td, te, tabq.ap[:, 0:64], tabq.ap[:, 64:128], 8)
                R.op("dve", lambda e, s3=s3, t13=t13, Ab=Ab: e.tensor_tensor(out=t13, in0=s3, in1=Ab, op=ALU.mult),
                     reads=[ta, tabq, s8a], writes=[td])
                R.op("pool", lambda e, s3=s3, t23=t23, Bl=Bl: e.tensor_tensor(out=t23[:, :, 0:32], in0=s3[:, :, 32:64], in1=Bl, op=ALU.mult),
                     reads=[ta, tabq], writes=[te])
                R.op("pool", lambda e, s3=s3, t23=t23, Bh=Bh: e.tensor_tensor(out=t23[:, :, 32:64], in0=s3[:, :, 0:32], in1=Bh, op=ALU.mult),
                     reads=[ta, tabq, te], writes=[te])
                R.op("dve", lambda e: e.tensor_tensor(out=td.ap, in0=td.ap, in1=te.ap, op=ALU.add), reads=[td, te], writes=[td])
                rq8b = s8c.ap.unsqueeze(2).broadcast_to([128, 8, 64])
                R.op("dve", lambda e, rq8b=rq8b: e.tensor_tensor(out=v3(qb.ap, 8), in0=v3(td.ap, 8), in1=rq8b, op=ALU.mult),
                     reads=[td, s8c], writes=[qb])
                for t in range(4):
                    R.op("pe", lambda e, t=t: e.transpose(out=T1ab[:, t * 128:(t + 1) * 128], in_=qb.ap[:, t * 128:(t + 1) * 128],
                                                          identity=ident), reads=[qb, cstb], writes=[T1a], n=128)
                R.op("act", lambda e: e.activation(out=qT.ap, in_=T1ab, func=AF.Identity), reads=[T1a], writes=[qT])
                if KSTOP == "m_q1":
                    raise _Stop()
                R.op("act", lambda e, s=s: e.activation(out=tb2.ap, in_=F[1].ap, func=AF.Identity, scale=s.ap[:, 2:3]),
                     reads=[F[1], s], writes=[tb2] + ([akf, tk1, tk2, tabk] if c == 0 else []), n=512)
                s3, t13, t23, Ab, Bl, Bh = rope(None, None, tb2, td, te, cosv, sinv, 8)
                R.op("dve", lambda e, s3=s3, t13=t13, Ab=Ab: e.tensor_tensor(out=t13, in0=s3, in1=Ab, op=ALU.mult),
                     reads=[tb2, cst], writes=[td])
                R.op("pool", lambda e, s3=s3, t23=t23, Bl=Bl: e.tensor_tensor(out=t23[:, :, 0:32], in0=s3[:, :, 32:64], in1=Bl, op=ALU.mult),
                     reads=[tb2, cst], writes=[te])
                R.op("pool", lambda e, s3=s3, t23=t23, Bh=Bh: e.tensor_tensor(out=t23[:, :, 32:64], in0=s3[:, :, 0:32], in1=Bh, op=ALU.mult),
                     reads=[tb2, cst, te], writes=[te])
                R.op("dve", lambda e: e.tensor_tensor(out=rqb.ap, in0=td.ap, in1=te.ap, op=ALU.add), reads=[td, te], writes=[rqb])
                for t in range(4):
                    R.op("pe", lambda e, t=t: e.transpose(out=T1bb[:, t * 128:(t + 1) * 128], in_=rqb.ap[:, t * 128:(t + 1) * 128],
                                                          identity=ident), reads=[rqb, cstb], writes=[T1b], n=128)
                R.op("act", lambda e: e.activation(out=rqT.ap, in_=T1bb, func=AF.Identity), reads=[T1b], writes=[rqT])
                R.op("dve", lambda e: e.tensor_tensor(out=rqxf.ap, in0=rqT.ap, in1=Xif.ap, op=ALU.mult), reads=[rqT, Xif], writes=[rqxf])
                R.op("dve", lambda e: e.tensor_tensor(out=rqxb.ap, in0=rqT.ap, in1=Xib.ap, op=ALU.mult), reads=[rqT, Xib], writes=[rqxb])
                if KSTOP == "m_q2":
                    raise _Stop()
                R.op("act", lambda e, s=s: e.activation(out=ta.ap, in_=F[2].ap, func=AF.Identity, scale=s.ap[:, 3:4]),
                     reads=[F[2], s], writes=[ta])
                s3, t13, t23, Ab, Bl, Bh = rope(None, None, ta, td, te, cosv, sinv, 8)
                R.op("dve", lambda e, s3=s3, t13=t13, Ab=Ab: e.tensor_tensor(out=t13, in0=s3, in1=Ab, op=ALU.mult),
                     reads=[ta, cst], writes=[td])
                R.op("pool", lambda e, s3=s3, t23=t23, Bl=Bl: e.tensor_tensor(out=t23[:, :, 0:32], in0=s3[:, :, 32:64], in1=Bl, op=ALU.mult),
                     reads=[ta, cst], writes=[te])
                R.op("pool", lambda e, s3=s3, t23=t23, Bh=Bh: e.tensor_tensor(out=t23[:, :, 32:64], in0=s3[:, :, 0:32], in1=Bh, op=ALU.mult),
                     reads=[ta, cst, te], writes=[te])
                R.op("dve", lambda e: e.tensor_tensor(out=rkb.ap, in0=td.ap, in1=te.ap, op=ALU.add), reads=[td, te], writes=[rkb])
                for t in range(4):
                    R.op("pe", lambda e, t=t: e.transpose(out=T1ab[:, t * 128:(t + 1) * 128], in_=rkb.ap[:, t * 128:(t + 1) * 128],
                                                          identity=ident), reads=[rkb, cstb], writes=[T1a], n=128)
                R.op("act", lambda e: e.activation(out=rkT.ap, in_=T1ab, func=AF.Identity), reads=[T1a], writes=[rkT])
                if KSTOP == "m_q3":
                    raise _Stop()
                R.op("act", lambda e, s=s: e.activation(out=rvb.ap, in_=F[3].ap, func=AF.Identity, scale=s.ap[:, 2:3]),
                     reads=[F[3], s], writes=[rvb])
                R.op("act", lambda e, s=s: e.activation(out=tg.ap, in_=BG.ap, func=AF.Exp, scale=s.ap[:, 5:6]),
                     reads=[BG, s], writes=[tg])
                R.op("act", lambda e: e.activation(out=tg.ap, in_=tg.ap, func=AF.Ln, bias=onesb.ap), reads=[tg, onesb], writes=[tg])
                R.op("act", lambda e: e.activation(out=tg.ap, in_=tg.ap, func=AF.Exp, scale=-1.0), reads=[tg], writes=[tg])
                R.op("dve", lambda e, s=s: e.scalar_tensor_tensor(out=tg.ap, in0=BG.ap, scalar=s.ap[:, 2:3], in1=tg.ap,
                                                                  op0=ALU.mult, op1=ALU.mult), reads=[BG, s, tg], writes=[tg])

                if KSTOP == "m_q":
                    raise _Stop()
                kbs = [kb for kb in (c - 1, c, c + 1) if kb >= 0]
                nkb = len(kbs)
                qT3 = v3(qT.ap, 4)
                Obank = [F[4], F[4]] if KO1 else [F[4], F[5]]
                it = 0
                for g in range(2):
                    for ee in range(2):
                        bx, by = (F[0], F[1]) if it % 2 == 0 else (F[2], F[3])
                        it += 1
                        regs = [bx.ap[:, 0:256], bx.ap[:, 256:512], by.ap[:, 0:256]]
                        rbuf = [bx, bx, by]
                        for j, kb in enumerate(kbs):
                            KT3 = v3(KT[kb].ap, 2)
                            masked = (kb != c)
                            R.op("pe", lambda e, j=j, KT3=KT3, g=g, ee=ee, masked=masked, regs=regs: e.matmul(
                                v3(regs[j], 2), lhsT=KT3[64 * ee:64 * ee + 64, g, :], rhs=qT3[64 * ee:64 * ee + 64, 2 * g:2 * g + 2, :],
                                start=True, stop=(not masked)), reads=[KT[kb], qT], writes=[rbuf[j]], n=256)
                            if masked:
                                mk = mprev if kb < c else mnext
                                R.op("pe", lambda e, j=j, mk=mk, regs=regs: e.matmul(regs[j], lhsT=ident, rhs=mk, start=False, stop=True),
                                     reads=[cstb], writes=[rbuf[j]], n=256)
                        PTc = (PT, PT2)[it % 2]
                        n1 = min(nkb, 2)
                        R.op("act", lambda e, n1=n1, bx=bx, PTc=PTc: e.activation(out=PTc.ap[:, 0:n1 * 256], in_=bx.ap[:, 0:n1 * 256], func=AF.Exp, scale=0.125),
                             reads=[bx], writes=[PTc])
                        if nkb == 3:
                            R.op("act", lambda e, by=by, PTc=PTc: e.activation(out=PTc.ap[:, 512:768], in_=by.ap[:, 0:256], func=AF.Exp, scale=0.125),
                                 reads=[by, PTc], writes=[PTc])
                        for tt in range(2):
                            h = 2 * (2 * g + tt) + ee
                            ob = Obank[h // 4]
                            hl = h % 4
                            for j, kb in enumerate(kbs):
                                R.op("pe", lambda e, j=j, kb=kb, tt=tt, ob=ob, hl=hl, g=g, nkb=nkb, PTc=PTc: e.matmul(
                                    ob.ap[:, hl * 65:(hl + 1) * 65], lhsT=PTc.ap[:, j * 256 + tt * 128: j * 256 + (tt + 1) * 128],
                                    rhs=v3(Vb[kb].ap, 2)[:, g, :], start=(j == 0), stop=(j == nkb - 1)),
                                    reads=[PTc, Vb[kb]], writes=[ob], n=250)
                    gb = g
                    O3 = Obank[gb].ap[:, 0:260].rearrange("p (h d) -> p h d", h=4)
                    sd = (s8d, s8e)[gb]
                    R.op("dve", lambda e, gb=gb, O3=O3, sd=sd: e.tensor_tensor(out=sd.ap[:, 0:4], in0=O3[:, :, 64],
                                                                               in1=esink.ap[:, gb * 4:(gb + 1) * 4], op=ALU.add),
                         reads=[Obank[gb], esink], writes=[sd], n=4)
                    R.op("dve", lambda e, sd=sd: e.reciprocal(out=sd.ap[:, 4:8], in_=sd.ap[:, 0:4]), reads=[sd], writes=[sd], n=4)
                    rdb = sd.ap[:, 4:8].unsqueeze(2).broadcast_to([128, 4, 64])
                    R.op("dve", lambda e, gb=gb, O3=O3, rdb=rdb: e.tensor_tensor(out=v3(yb.ap[:, gb * 256:(gb + 1) * 256], 4), in0=O3[:, :, 0:64],
                                                                                 in1=rdb, op=ALU.mult),
                         reads=[Obank[gb], sd, yba], writes=[yba], n=256)

                OPB = [F[3], F[5]]
                for k in range(4):
                    R.op("pe", lambda e, k=k: e.transpose(out=T0b[:, k * 128:(k + 1) * 128], in_=yb.ap[:, k * 128:(k + 1) * 128],
                                                          identity=ident), reads=[yba, cstb], writes=[T0], n=128)
                R.op("act", lambda e: e.activation(out=yT.ap[:, 0:512], in_=T0b[:, 0:512], func=AF.Identity), reads=[T0], writes=[yTa], n=512)
                yT3 = v3(yT.ap, 8)
                for half in range(2):
                    for k in range(4):
                        R.op("pe", lambda e, k=k, half=half: e.matmul(OPB[half].ap, lhsT=yT3[:, k, :],
                                                                       rhs=W_out.ap[:, k, half * 512:(half + 1) * 512],
                                                                       start=(k == 0), stop=False),
                             reads=[yTa, W_out], writes=[OPB[half]])
                rkT3 = v3(rkT.ap, 4)
                rqT3 = v3(rqT.ap, 4)
                rxf3 = v3(rqxf.ap, 4)
                rxb3 = v3(rqxb.ap, 4)
                for ee in range(2):
                    for t in range(4):
                        bank = F[ee]
                        col = t * 128
                        R.op("pe", lambda e, t=t, ee=ee, bank=bank, col=col: e.matmul(
                            bank.ap[:, col:col + 128], lhsT=rkT3[64 * ee:64 * ee + 64, t, :], rhs=rqT3[64 * ee:64 * ee + 64, t, :],
                            start=True, stop=True), reads=[rkT, rqT], writes=[bank], n=128)
                if KSTOP == "m_r0":
                    raise _Stop()
                retS4 = retS.ap.rearrange("p (t e n) -> p t e n", t=4, e=2)
                DT4 = DT.ap.rearrange("p (t e n) -> p t e n", t=4, e=2)
                for gb in range(2):
                    R.op("dve", lambda e, gb=gb, retS4=retS4, DT4=DT4: e.tensor_tensor(out=retS4[:, :, gb, :], in0=v3(F[gb].ap, 4),
                                                                                     in1=DT4[:, :, gb, :], op=ALU.mult),
                         reads=[F[gb], DT, retS], writes=[retS])
                if KSTOP == "m_r1":
                    raise _Stop()
                for h in range(8):
                    t, ee = h // 2, h % 2
                    Rf3s = v3(RfS[c].ap, 4)
                    Rb3s = v3(RbS[c].ap, 4)
                    R.op("pe", lambda e, h=h: e.matmul(F[2].ap[:, h * 64:(h + 1) * 64], lhsT=retS.ap[:, h * 128:(h + 1) * 128],
                                                       rhs=rvb.ap[:, h * 64:(h + 1) * 64], start=True, stop=False),
                         reads=[retS, rvb], writes=[F[2]], n=200)
                    R.op("pe", lambda e, h=h, t=t, ee=ee, Rf3s=Rf3s: e.matmul(F[2].ap[:, h * 64:(h + 1) * 64], lhsT=rxf3[64 * ee:64 * ee + 64, t, :],
                                                                             rhs=Rf3s[64 * ee:64 * ee + 64, t, :], start=False, stop=False),
                         reads=[rqxf, RfS[c]], writes=[F[2]], n=100)
                    R.op("pe", lambda e, h=h, t=t, ee=ee, Rb3s=Rb3s: e.matmul(F[2].ap[:, h * 64:(h + 1) * 64], lhsT=rxb3[64 * ee:64 * ee + 64, t, :],
                                                                             rhs=Rb3s[64 * ee:64 * ee + 64, t, :], start=False, stop=True),
                         reads=[rqxb, RbS[c]], writes=[F[2]], n=100)
                if KSTOP == "m_r2":
                    raise _Stop()
                R.op("act", lambda e: e.activation(out=tf.ap, in_=F[2].ap, func=AF.Square), reads=[F[2]], writes=[tf])
                R.op("dve", lambda e: e.tensor_reduce(out=s8f.ap, in_=v3(tf.ap, 8), axis=AX.X, op=ALU.add), reads=[tf], writes=[s8f])
                R.op("act", lambda e: e.activation(out=s8g.ap, in_=s8f.ap, func=AF.Ln, scale=1.0 / 64, bias=epsb.ap),
                     reads=[s8f, epsb], writes=[s8g])
                R.op("act", lambda e: e.activation(out=s8h.ap, in_=s8g.ap, func=AF.Exp, scale=-0.5), reads=[s8g], writes=[s8h])
                R.op("dve", lambda e: e.tensor_tensor(out=tf.ap, in0=F[2].ap, in1=tg.ap, op=ALU.mult), reads=[F[2], tg, s8f], writes=[tf])
                rr8b = s8h.ap.unsqueeze(2).broadcast_to([128, 8, 64])
                R.op("dve", lambda e, rr8b=rr8b: e.tensor_tensor(out=v3(yb.ap[:, 512:1024], 8), in0=v3(tf.ap, 8), in1=rr8b, op=ALU.mult),
                     reads=[tf, s8h], writes=[ybr], n=512)

                if KSTOP == "m_ret":
                    raise _Stop()
                for k in range(4, 8):
                    R.op("pe", lambda e, k=k: e.transpose(out=T0b[:, k * 128:(k + 1) * 128], in_=yb.ap[:, k * 128:(k + 1) * 128],
                                                          identity=ident), reads=[ybr, cstb], writes=[T0], n=128)
                R.op("act", lambda e: e.activation(out=yT.ap[:, 512:1024], in_=T0b[:, 512:1024], func=AF.Identity), reads=[T0], writes=[yTr], n=512)
                for half in range(2):
                    bank = OPB[half]
                    for k in range(4, 8):
                        R.op("pe", lambda e, k=k, half=half, bank=bank: e.matmul(bank.ap, lhsT=yT3[:, k, :],
                                                                                  rhs=W_out.ap[:, k, half * 512:(half + 1) * 512],
                                                                                  start=False, stop=(k == 7)),
                             reads=[yTr, W_out], writes=[bank])
                    R.op("dve", lambda e, half=half, bank=bank, xbuf=xbuf: e.tensor_tensor(out=xbuf.ap[:, half * 512:(half + 1) * 512],
                                                                                           in0=xbuf.ap[:, half * 512:(half + 1) * 512],
                                                                                           in1=bank.ap, op=ALU.add),
                         reads=[bank, xbuf], writes=[xbuf], n=512)
                R.op("act", lambda e, c=c, xbuf=xbuf: e.activation(out=tf.ap.bitcast(BF16), in_=xbuf.ap, func=AF.Square, accum_out=ssq2.ap[:, c:c + 1]),
                     reads=[xbuf, ssq2], writes=[tf, ssq2], n=1024)
                R.op("act", lambda e, c=c: e.activation(out=lnr2.ap[:, c:c + 1], in_=ssq2.ap[:, c:c + 1], func=AF.Ln, scale=1.0 / D_MODEL, bias=epsb.ap),
                     reads=[ssq2, epsb, lnr2], writes=[lnr2])
                R.op("act", lambda e, c=c: e.activation(out=rstd2.ap[:, c:c + 1], in_=lnr2.ap[:, c:c + 1], func=AF.Exp, scale=-0.5),
                     reads=[lnr2, rstd2], writes=[rstd2])

            if KSTOP == "main":
                raise _Stop()
            R.phase = "ffn"
            fences = []
            f_ = R.op("dve", lambda e: e.tensor_copy(out=fsc.ap[:, 0:1], in_=epsb.ap), reads=[epsb], writes=[Buf("fscD", fsc.ap[:, 0:1])])
            fences.append(f_)
            f_ = R.op("act", lambda e: e.activation(out=fsc.ap[:, 1:2], in_=epsb.ap, func=AF.Identity), reads=[epsb], writes=[Buf("fscA", fsc.ap[:, 1:2])])
            fences.append(f_)
            f_ = R.op("pool", lambda e: e.memset(fsc.ap[:, 2:3], 0.0), writes=[Buf("fscP", fsc.ap[:, 2:3])])
            fences.append(f_)
            f_ = R.op("pe", lambda e: e.matmul(T0.ap[:, 0:1], lhsT=ident, rhs=ident[:, 0:1], start=True, stop=True),
                      reads=[cstb], writes=[T0], n=1)
            fences.append(f_)
            for f_ in fences:
                f_.fence = True
            R.seg = 1
            bar = fences

            def pbuf(name, ap):
                b = Buf(name, ap)
                b.readers = list(bar)
                return b

            AB.off = 0
            AFa.off = 0

            def pb_alloc(arena, name, n):
                b = arena.alloc(name, n)
                b.readers = list(bar)
                return b

            mTg = [pb_alloc(AB, f"mTg{g_}", 4096) for g_ in range(4)]
            mT = []
            for c in range(NT_OWN):
                b_ = Buf(f"mT{c}", v3(mTg[c // 4].ap, 8)[:, :, (c % 4) * 128:(c % 4 + 1) * 128])
                b_.readers = list(bar)
                mT.append(b_)
            hTb = [[pb_alloc(AB, f"hT{s_}_{ci}", 512) for ci in range(4)] for s_ in range(2)]
            mb = pb_alloc(AB, "mb", 1024)
            junk2 = pb_alloc(AB, "junk2", 1024)
            identB = pb_alloc(AB, "identB", 128)
            fnw = pb_alloc(AFa, "fnw", 1024)
            sgt = [pb_alloc(AFa, f"sgt{i}", 512) for i in range(2)]
            st2 = pb_alloc(AFa, "st2", 8)
            T0f = pbuf("T0f", banks[6][:, :])
            T1f = pbuf("T1f", banks[7][:, :])
            slots = [pbuf(f"slot{i}", Wt[:, i * 12288:(i + 1) * 12288]) for i in range(2)]
            early = []
            for wb in Wg_buf.values():
                early.extend(wb.readers)
                early.extend(wb.writers)
            slots[0].readers = early

            R.dma("sp", fnw.ap, fnw_d, fnw, writes=[fnw])
            R.dma("pool", identB.ap, cstb_d[:, 0:128], identB, writes=[identB])

            passes = [(0, 4), (4, 4), (8, 4), (12, 4), (16, 3), (19, 3)]
            Wg_v = w_gate_d.rearrange("(k p) n -> p k n", p=128)
            Wu_v = w_up_d.rearrange("(k p) n -> p k n", p=128)

            def load_pass(r):
                f0, C = passes[r]
                sl = slots[r % 2]
                g3 = v3(sl.ap[:, 0:4096], 8)
                u3 = v3(sl.ap[:, 4096:8192], 8)
                d3 = v3(sl.ap[:, 8192:12288], 4)
                R.dma("pool", g3[:, :, 0:C * 128], Wg_v[:, :, f0 * 128:(f0 + C) * 128], sl, writes=[sl])
                R.dma("pool", u3[:, :, 0:C * 128], Wu_v[:, :, f0 * 128:(f0 + C) * 128], sl, writes=[sl])
                R.dma("pool", d3[:, 0:C, :], w_down_d[f0 * 128:(f0 + C) * 128, :].rearrange("(c p) n -> p c n", p=128), sl, writes=[sl])

            load_pass(0)
            load_pass(1)

            def prologue(tgi):
                for t in range(4):
                    c = tgi * 4 + t
                    R.op("dve", lambda e, c=c: e.scalar_tensor_tensor(out=mb.ap, in0=hbuf[c].ap, scalar=rstd2.ap[:, c:c + 1], in1=fnw.ap,
                                                                      op0=ALU.mult, op1=ALU.mult), reads=[hbuf[c], rstd2, fnw], writes=[mb])
                    for k in range(8):
                        R.op("pe", lambda e, k=k: e.transpose(out=T0b[:, k * 128:(k + 1) * 128], in_=mb.ap[:, k * 128:(k + 1) * 128],
                                                              identity=identB.ap), reads=[mb, identB], writes=[T0f], n=128)
                    R.op("act", lambda e, c=c: e.activation(out=mT[c].ap, in_=v3(T0b, 8), func=AF.Identity), reads=[T0f], writes=[mT[c]])

            prologue(0)
            gi = 0
            for r, (f0, C) in enumerate(passes):
                sl = slots[r % 2]
                g3 = v3(sl.ap[:, 0:4096], 8)
                u3 = v3(sl.ap[:, 4096:8192], 8)
                d3 = v3(sl.ap[:, 8192:12288], 4)
                last = (r == len(passes) - 1)
                for tgi in range(4):
                    if r == 0 and tgi + 1 < 4:
                        prologue(tgi + 1)
                    hs = hTb[tgi % 2]
                    for ci in range(C):
                        gbank = F[0] if gi % 2 == 0 else F[1]
                        ubank = F[2] if gi % 2 == 0 else F[3]
                        sg_ = sgt[gi % 2]
                        gi += 1
                        mg3 = v3(mTg[tgi].ap, 8)
                        for (bank, w3) in ((gbank, g3), (ubank, u3)):
                            for k in range(8):
                                R.op("pe", lambda e, bank=bank, w3=w3, ci=ci, k=k, mg3=mg3: e.matmul(
                                    bank.ap, lhsT=w3[:, k, ci * 128:(ci + 1) * 128],
                                    rhs=mg3[:, k, :], start=(k == 0), stop=(k == 7)),
                                    reads=[sl] + mT[tgi * 4:tgi * 4 + 4], writes=[bank])
                        R.op("act", lambda e, gbank=gbank, sg_=sg_: e.activation(out=sg_.ap, in_=gbank.ap, func=AF.Silu),
                             reads=[gbank], writes=[sg_])
                        R.op("dve", lambda e, ubank=ubank, sg_=sg_, hs=hs, ci=ci: e.tensor_tensor(out=hs[ci].ap, in0=sg_.ap, in1=ubank.ap, op=ALU.mult),
                             reads=[sg_, ubank], writes=[hs[ci]])
                    for t in range(4):
                        c = tgi * 4 + t
                        dbanks = (F[4], F[5]) if t % 2 == 0 else (T1f, T0f)
                        for half in range(2):
                            bank = dbanks[half]
                            for ci in range(C):
                                R.op("pe", lambda e, bank=bank, ci=ci, t=t, half=half, hs=hs, C=C, d3=d3: e.matmul(
                                    bank.ap, lhsT=hs[ci].ap[:, t * 128:(t + 1) * 128], rhs=d3[:, ci, half * 512:(half + 1) * 512],
                                    start=(ci == 0), stop=(ci == C - 1)), reads=[hs[ci], sl], writes=[bank])
                            R.op("dve", lambda e, bank=bank, c=c, half=half: e.tensor_tensor(
                                out=hbuf[c].ap[:, half * 512:(half + 1) * 512], in0=hbuf[c].ap[:, half * 512:(half + 1) * 512],
                                in1=bank.ap, op=ALU.add), reads=[bank, hbuf[c]], writes=[hbuf[c]])
                        if last:
                            R.dma("sp", out_d[c * 128:(c + 1) * 128, :], hbuf[c].ap, hbuf[c], reads=[hbuf[c]], final=True)
                if r + 2 < len(passes):
                    load_pass(r + 2)

        try:
            _record()
        except _Stop:
            pass

        R.finalize()
        _NC_CACHE['R'] = R
        if os.environ.get("KSCHEDSTAT", ""):
            print('sched sim total us:', getattr(R, 'sim_total', None))

        @block.sync
        def _(eng):
            R.emit("sp", eng)

        @block.gpsimd
        def _(eng):
            R.emit("pool", eng)

        @block.scalar
        def _(eng):
            R.emit("act", eng)

        @block.vector
        def _(eng):
            R.emit("dve", eng)

        @block.tensor
        def _(eng):
            R.emit("pe", eng)
    return nc


_NC_CACHE = {}


def _rope_tables(hf):
    l = np.arange(SEQ, dtype=np.float32)
    pos = l if hf == 0 else (np.float32(SEQ - 1) - l)
    inv_freq = (np.float32(10000.0) ** (-(np.arange(0, 64, 2, dtype=np.float32)) / np.float32(64))).astype(np.float32)
    ang = (pos[:, None] * inv_freq[None, :]).astype(np.float32)
    cos = np.cos(ang.astype(np.float64)).astype(np.float32)
    sin = np.sin(ang.astype(np.float64)).astype(np.float32)
    cs = np.concatenate([cos, cos, -sin, sin], axis=1)
    return np.ascontiguousarray(cs, dtype=np.float32)


def _const_tables():
    i = np.arange(128, dtype=np.float32)
    iota1 = np.tile((i + 1.0)[None, :], (128, 1))
    iota2 = np.tile((128.0 - i)[None, :], (128, 1))
    m = i[:, None]
    n = i[None, :]
    reluP = np.maximum(n - m, 0.0)
    reluN = np.maximum(m - n, 0.0)
    csts = np.concatenate([iota1, iota2, reluP, reluN], axis=1).astype(np.float32)
    ident = np.eye(128, dtype=np.float32)
    j = i[:, None]
    q = i[None, :]
    mprev = np.where(j >= q, 0.0, -30000.0).astype(np.float32)
    mnext = np.where(j <= q, 0.0, -30000.0).astype(np.float32)
    cstb = np.concatenate([ident, mprev, mprev, mnext, mnext], axis=1).astype(np.float32)
    return np.ascontiguousarray(csts), np.ascontiguousarray(cstb)


def _make_in_maps(x, attn_norm_w, w_in, q_norm_w, k_norm_w, attn_sink, ret_log_decay_fwd, ret_log_decay_bwd,
           ret_norm_w, w_out, ffn_norm_w, w_gate, w_up, w_down):
    x = np.asarray(x, dtype=np.float32)
    f = lambda a: np.asarray(a, dtype=np.float32)
    attn_norm_w, q_norm_w, k_norm_w, attn_sink = f(attn_norm_w)[0], f(q_norm_w)[0], f(k_norm_w)[0], f(attn_sink)[0]
    dfw, dbw = f(ret_log_decay_fwd)[0], f(ret_log_decay_bwd)[0]
    ret_norm_w, ffn_norm_w = f(ret_norm_w)[0], f(ffn_norm_w)[0]
    w_in_, w_out_, w_gate_, w_up_, w_down_ = (np.ascontiguousarray(f(a)[0]) for a in (w_in, w_out, w_gate, w_up, w_down))

    csts, cstb = _const_tables()
    fnw = np.ascontiguousarray(np.tile(ffn_norm_w[None, :], (128, 1)))
    rope_tabs = [_rope_tables(0), _rope_tables(1)]
    swap = lambda w: np.concatenate([w[32:], w[:32]])
    in_maps = []
    for c in range(8):
        b, hf = c // 2, c % 2
        xs = x[b] if hf == 0 else x[b, ::-1]
        dF, dB = (dfw, dbw) if hf == 0 else (dbw, dfw)
        cstf = np.zeros((128, NCF), dtype=np.float32)

        def put(name, row):
            a, bnd = CF[name]
            cstf[:, a:bnd] = row[None, :]
        a, bnd = CF["anwP"]
        cstf[:, a:bnd] = attn_norm_w.reshape(8, 128).T
        a, bnd = CF["rnwP"]
        cstf[:, a:bnd] = ret_norm_w.reshape(4, 128).T
        put("qnw", q_norm_w)
        put("qnws", swap(q_norm_w))
        put("knw", k_norm_w)
        put("knws", swap(k_norm_w))
        put("sink", attn_sink)
        put("decR", np.concatenate([dF, dB]))
        a, bnd = CF["decP"]
        for t in range(4):
            cstf[0:64, a + t] = dF[2 * t]
            cstf[64:128, a + t] = dF[2 * t + 1]
            cstf[0:64, a + 4 + t] = dB[2 * t]
            cstf[64:128, a + 4 + t] = dB[2 * t + 1]
        a, _b = CF["c127m"]
        cstf[:, a] = 127.0 - np.arange(128, dtype=np.float32)
        a, _b = CF["cm"]
        cstf[:, a] = np.arange(128, dtype=np.float32)
        in_maps.append({
            "xs": np.ascontiguousarray(xs), "cs": rope_tabs[hf], "cstf": cstf, "csts": csts, "cstb": cstb, "fnw": fnw,
            "w_in": w_in_, "w_out": w_out_, "w_gate": w_gate_, "w_up": w_up_, "w_down": w_down_,
        })
    return in_maps


def kernel(**inputs):
    in_maps = _make_in_maps(**inputs)
    if "nc" not in _NC_CACHE:
        _NC_CACHE["nc"] = build_program()
    nc = _NC_CACHE["nc"]
    res = run_bass_kernel_spmd(nc, in_maps, core_ids=list(range(8)))
    out = np.empty((4, SEQ, D_MODEL), dtype=np.float32)
    for c in range(8):
        b, hf = c // 2, c % 2
        o = np.asarray(res.results[c]["out"], dtype=np.float32)
        if hf == 0:
            out[b, 0:2048] = o
        else:
            out[b, 2048:4096] = o[::-1]
    return out
```
